# Optimizing a Trainium2 kernel written in Bass

```python
import math
import jax, jax.numpy as jnp
from jax import lax
import numpy as np

D_MODEL = 2048
BATCH = 1
SEQ = 16384
DEPTH = 1
DEC_BATCH = 16
DEC_SEQ = 16
PAST_LEN = 1024

CHUNK = 64
Q_BLOCK = 128
N_HEADS_A = 16
HEAD_DIM_A = 128
IDX_HEADS = 16
IDX_DIM = 64
TOPK_MAX = 256
REL_BUCKETS = 32
REL_MAX_DIST = 128
N_HEADS_B = 16
QK_NOPE_DIM = 128
ROPE_DIM = 64
V_DIM_B = 128
Q_LORA = 512
KV_LORA = 256
ROPE_THETA = 10000.0
MLA_SCALE = (QK_NOPE_DIM + ROPE_DIM) ** -0.5
A_SCALE = HEAD_DIM_A ** -0.5
D_FF = 4 * D_MODEL
EPS = 1e-6
NEG_INF = -1e30
COL_SIZES = (N_HEADS_A * HEAD_DIM_A, N_HEADS_A * HEAD_DIM_A, N_HEADS_A * HEAD_DIM_A,
             IDX_HEADS * IDX_DIM, IDX_DIM, IDX_HEADS,
             Q_LORA, KV_LORA, ROPE_DIM,
             D_MODEL, D_MODEL)
IN_COLS = sum(COL_SIZES)

kernel_name = "hybrid_dsa_mla_streaming_step"


def rmsnorm(x, g):
    xf = x.astype(jnp.float32)
    y = xf * lax.rsqrt(jnp.mean(xf * xf, axis=-1, keepdims=True) + EPS)
    return (y * g.astype(jnp.float32)).astype(x.dtype)


def split_cols(z):
    outs, off = [], 0
    for n in COL_SIZES:
        outs.append(z[..., off:off + n])
        off += n
    return outs


def rope(x, pos):
    half = ROPE_DIM // 2
    inv_freq = jnp.power(ROPE_THETA, -jnp.arange(half, dtype=jnp.float32) / half)
    ang = pos.astype(jnp.float32)[:, None] * inv_freq[None, :]
    shp = (ang.shape[0],) + (1,) * (x.ndim - 3) + (half,)
    cos = jnp.cos(ang).reshape(shp)
    sin = jnp.sin(ang).reshape(shp)
    x1 = x[..., :half].astype(jnp.float32)
    x2 = x[..., half:].astype(jnp.float32)
    return jnp.concatenate([x1 * cos - x2 * sin, x1 * sin + x2 * cos], axis=-1).astype(x.dtype)


def t5_bucket(rel):
    nb = REL_BUCKETS // 2
    ret = (rel > 0).astype(jnp.int32) * nb
    n = jnp.abs(rel)
    max_exact = nb // 2
    large = max_exact + (jnp.log(jnp.maximum(n, 1).astype(jnp.float32) / max_exact)
                         / math.log(REL_MAX_DIST / max_exact) * (nb - max_exact)).astype(jnp.int32)
    large = jnp.minimum(large, nb - 1)
    return ret + jnp.where(n < max_exact, n, large)


def dsa_attend(q, ix_q, ix_w, qpos, k_all, v_all, ixk_all, kpos, rel_table, topk):
    dots = jnp.einsum('bthd,bsd->bths', ix_q, ixk_all)
    score = jnp.einsum('bths,bth->bts', jax.nn.relu(dots), ix_w).astype(jnp.float32)
    admissible = (kpos[None, :] // CHUNK) <= (qpos[:, None] // CHUNK)
    score = jnp.where(admissible[None], score, NEG_INF)
    top_val, top_idx = lax.top_k(score, topk)
    valid = top_val > 0.5 * NEG_INF
    take = jax.vmap(lambda rows, idx: rows[idx])
    k_sel = take(k_all, top_idx)
    v_sel = take(v_all, top_idx)
    logits = jnp.einsum('bthd,btkhd->bhtk', q, k_sel).astype(jnp.float32) * A_SCALE
    rel = kpos[top_idx] - qpos[None, :, None]
    bias = jnp.moveaxis(rel_table[t5_bucket(rel)], -1, 1).astype(jnp.float32)
    logits = jnp.where(valid[:, None], logits + bias, NEG_INF)
    p = jax.nn.softmax(logits, axis=-1).astype(v_sel.dtype)
    out = jnp.einsum('bhtk,btkhd->bthd', p, v_sel)
    return out.reshape(out.shape[0], out.shape[1], -1)


def mla_attend(q_nope, q_rope, qpos, k_nope, k_rope, v, kpos):
    logits = (jnp.einsum('bthd,bshd->bhts', q_nope, k_nope)
              + jnp.einsum('bthr,bsr->bhts', q_rope, k_rope)).astype(jnp.float32) * MLA_SCALE
    visible = (kpos[None, :] // CHUNK) <= (qpos[:, None] // CHUNK)
    logits = jnp.where(visible[None, None], logits, NEG_INF)
    p = jax.nn.softmax(logits, axis=-1).astype(v.dtype)
    out = jnp.einsum('bhts,bshd->bthd', p, v)
    return out.reshape(out.shape[0], out.shape[1], -1)


def to_blocks(a):
    b, l = a.shape[0], a.shape[1]
    return a.reshape((b, l // Q_BLOCK, Q_BLOCK) + a.shape[2:]).swapaxes(0, 1)


def from_blocks(a):
    nb, b, q = a.shape[0], a.shape[1], a.shape[2]
    return a.swapaxes(0, 1).reshape((b, nb * q) + a.shape[3:])


def trunk_layer(x, past, rel_table, norm_mix_g, w_in, q_lora_g, w_uq, kv_lora_g, w_uk, w_uv,
                w_out, norm_ffn_g, w_ff_up, w_ff_down, topk, blocked):
    b, l, _ = x.shape
    p_len = 0 if past is None else past[0].shape[1]
    qpos = p_len + jnp.arange(l, dtype=jnp.int32)
    kpos = jnp.arange(p_len + l, dtype=jnp.int32)

    h = rmsnorm(x, norm_mix_g)
    z = h @ w_in
    a_q, a_k, a_v, ix_q, ix_k, ix_w, cq, ckv, kr, gate_a, gate_b = split_cols(z)
    a_q = a_q.reshape(b, l, N_HEADS_A, HEAD_DIM_A)
    a_k = a_k.reshape(b, l, N_HEADS_A, HEAD_DIM_A)
    a_v = a_v.reshape(b, l, N_HEADS_A, HEAD_DIM_A)
    ix_q = ix_q.reshape(b, l, IDX_HEADS, IDX_DIM)
    ix_w = ix_w * (IDX_HEADS ** -0.5)

    c_q = rmsnorm(cq, q_lora_g)
    qb = jnp.einsum('blc,chd->blhd', c_q, w_uq)
    qb_nope = qb[..., :QK_NOPE_DIM]
    qb_rope = rope(qb[..., QK_NOPE_DIM:], qpos)
    c_kv = rmsnorm(ckv, kv_lora_g)
    k_rope_new = rope(kr, qpos)

    new_state = (a_k, a_v, ix_k, c_kv, k_rope_new)
    if past is None:
        keys = new_state
    else:
        keys = tuple(jnp.concatenate([pr, nw.astype(pr.dtype)], axis=1) for pr, nw in zip(past, new_state))
    k_all, v_all, ixk_all, ckv_all, kr_all = keys
    kb_nope = jnp.einsum('bsc,chd->bshd', ckv_all, w_uk)
    vb = jnp.einsum('bsc,chd->bshd', ckv_all, w_uv)

    if blocked:
        def block_fn(args):
            aq, iq, iw, qp, bqn, bqr = args
            return (dsa_attend(aq, iq, iw, qp, k_all, v_all, ixk_all, kpos, rel_table, topk),
                    mla_attend(bqn, bqr, qp, kb_nope, kr_all, vb, kpos))
        xs = (to_blocks(a_q), to_blocks(ix_q), to_blocks(ix_w), qpos.reshape(-1, Q_BLOCK),
              to_blocks(qb_nope), to_blocks(qb_rope))
        oa_blk, ob_blk = lax.map(block_fn, xs)
        o_a = from_blocks(oa_blk)
        o_b = from_blocks(ob_blk)
    else:
        o_a = dsa_attend(a_q, ix_q, ix_w, qpos, k_all, v_all, ixk_all, kpos, rel_table, topk)
        o_b = mla_attend(qb_nope, qb_rope, qpos, kb_nope, kr_all, vb, kpos)

    mix = jax.nn.sigmoid(gate_a) * o_a + jax.nn.sigmoid(gate_b) * o_b
    x = x + mix @ w_out
    h2 = rmsnorm(x, norm_ffn_g)
    x = x + jnp.square(jax.nn.relu(h2 @ w_ff_up)) @ w_ff_down
    return x, new_state


def setup_inputs(seed: int = 0) -> dict:
    key = jax.random.key(seed)
    ks = jax.random.split(key, 24)
    f32 = jnp.float32
    nrm = lambda k, shp, s=1.0: jax.random.normal(k, shp, f32) * s
    return {
        "x_prompt": nrm(ks[0], (BATCH, SEQ, D_MODEL)),
        "x_sample": nrm(ks[1], (DEC_BATCH, DEC_SEQ, D_MODEL)),
        "cache_a_k": nrm(ks[2], (DEPTH, DEC_BATCH, PAST_LEN, N_HEADS_A, HEAD_DIM_A)),
        "cache_a_v": nrm(ks[3], (DEPTH, DEC_BATCH, PAST_LEN, N_HEADS_A, HEAD_DIM_A)),
        "cache_a_idx_k": nrm(ks[4], (DEPTH, DEC_BATCH, PAST_LEN, IDX_DIM)),
        "cache_b_ckv": nrm(ks[5], (DEPTH, DEC_BATCH, PAST_LEN, KV_LORA)),
        "cache_b_krope": nrm(ks[6], (DEPTH, DEC_BATCH, PAST_LEN, ROPE_DIM)),
        "rel_bias_table": nrm(ks[7], (REL_BUCKETS, N_HEADS_A), 0.5),
        "norm_mix_g": 1.0 + nrm(ks[8], (DEPTH, D_MODEL), 0.05),
        "w_in": nrm(ks[9], (DEPTH, D_MODEL, IN_COLS), D_MODEL ** -0.5),
        "q_lora_g": 1.0 + nrm(ks[10], (DEPTH, Q_LORA), 0.05),
        "w_uq": nrm(ks[11], (DEPTH, Q_LORA, N_HEADS_B, QK_NOPE_DIM + ROPE_DIM), Q_LORA ** -0.5),
        "kv_lora_g": 1.0 + nrm(ks[12], (DEPTH, KV_LORA), 0.05),
        "w_uk": nrm(ks[13], (DEPTH, KV_LORA, N_HEADS_B, QK_NOPE_DIM), KV_LORA ** -0.5),
        "w_uv": nrm(ks[14], (DEPTH, KV_LORA, N_HEADS_B, V_DIM_B), KV_LORA ** -0.5),
        "w_out": nrm(ks[15], (DEPTH, D_MODEL, D_MODEL), D_MODEL ** -0.5),
        "norm_ffn_g": 1.0 + nrm(ks[16], (DEPTH, D_MODEL), 0.05),
        "w_ff_up": nrm(ks[17], (DEPTH, D_MODEL, D_FF), D_MODEL ** -0.5),
        "w_ff_down": nrm(ks[18], (DEPTH, D_FF, D_MODEL), D_FF ** -0.5),
        "final_norm_g": 1.0 + nrm(ks[19], (D_MODEL,), 0.05),
    }


def reference(x_prompt, x_sample, cache_a_k, cache_a_v, cache_a_idx_k, cache_b_ckv, cache_b_krope,
              rel_bias_table, norm_mix_g, w_in, q_lora_g, w_uq, kv_lora_g, w_uk, w_uv, w_out,
              norm_ffn_g, w_ff_up, w_ff_down, final_norm_g):
    topk_p = min(TOPK_MAX, x_prompt.shape[1] // 4)
    topk_s = min(TOPK_MAX, (cache_a_k.shape[2] + x_sample.shape[1]) // 4)
    hp, hs = x_prompt, x_sample
    st_p, st_s = [], []
    for l in range(DEPTH):
        params = (norm_mix_g[l], w_in[l], q_lora_g[l], w_uq[l], kv_lora_g[l], w_uk[l], w_uv[l],
                  w_out[l], norm_ffn_g[l], w_ff_up[l], w_ff_down[l])
        hp, sp = trunk_layer(hp, None, rel_bias_table, *params, topk=topk_p, blocked=True)
        past = (cache_a_k[l], cache_a_v[l], cache_a_idx_k[l], cache_b_ckv[l], cache_b_krope[l])
        hs, ss = trunk_layer(hs, past, rel_bias_table, *params, topk=topk_s, blocked=False)
        st_p.append(sp)
        st_s.append(ss)
    y_prompt = rmsnorm(hp, final_norm_g)
    y_sample = rmsnorm(hs, final_norm_g)
    a_k_p = jnp.stack([s[0] for s in st_p])
    a_v_p = jnp.stack([s[1] for s in st_p])
    a_idx_p = jnp.stack([s[2] for s in st_p])
    b_ckv_p = jnp.stack([s[3] for s in st_p])
    b_kr_p = jnp.stack([s[4] for s in st_p])
    a_k_s = jnp.stack([s[0] for s in st_s])
    a_v_s = jnp.stack([s[1] for s in st_s])
    a_idx_s = jnp.stack([s[2] for s in st_s])
    b_ckv_s = jnp.stack([s[3] for s in st_s])
    b_kr_s = jnp.stack([s[4] for s in st_s])
    return (y_prompt, y_sample, a_k_p, a_v_p, a_idx_p, b_ckv_p, b_kr_p, a_k_s, a_v_s, a_idx_s, b_ckv_s, b_kr_s)
```

```python
import math
import os
from contextlib import ExitStack

import numpy as np
import concourse.bass as bass
import concourse.mybir as mybir
from concourse.bass_utils import run_bass_kernel_spmd

F32 = mybir.dt.float32
BF16 = mybir.dt.bfloat16
AF = mybir.ActivationFunctionType
ALU = mybir.AluOpType
AX = mybir.AxisListType

D = 2048
NH = 16
HD = 128
NCORE = 8
NPT = 136
NXT = 140
NTILE = 158
NTOK = NTILE * 128
NOWN = 20
EPS = 1e-6
A_SCALE = HD ** -0.5
MLA_SCALE = 192 ** -0.5
NEGBIG = -1.0e30
MASKNEG = -30000.0
TOPK = 256
BIS_R = 128.0
BIS_IT = 25
C_AQ, C_AK, C_AV, C_IXQ, C_IXK, C_IXW, C_CQ, C_CKV, C_KR, C_GA, C_GB = (
    0, 2048, 4096, 6144, 7168, 7232, 7248, 7760, 8016, 8080, 10128)


class Buf:
    __slots__ = ("name", "last_w", "readers", "dsem", "dcount", "excl")

    def __init__(self, name):
        self.excl = False
        self.name = name
        self.last_w = None
        self.readers = []
        self.dsem = None
        self.dcount = 0


class Ins:
    __slots__ = ("eng", "fn", "deps", "is_dma", "dbuf", "need_inc", "seq", "cover")

    def __init__(self, eng, fn, deps, is_dma=False, dbuf=None):
        self.eng = eng
        self.fn = fn
        self.deps = deps
        self.is_dma = is_dma
        self.dbuf = dbuf
        self.need_inc = False
        self.seq = 0
        self.cover = 0


COMPUTE = ("pe", "act", "dve", "pool")


class Prog:
    def __init__(self, nc, stack):
        self.nc = nc
        self.stack = stack
        self.engs = {"pe": nc.tensor, "act": nc.scalar, "dve": nc.vector, "pool": nc.gpsimd, "sp": nc.sync}
        self.batch = []
        self.esem = {e: stack.enter_context(nc.semaphore("e_" + e)) for e in COMPUTE}
        self.ecount = {e: 0 for e in COMPUTE}
        self.waited = {e: {} for e in self.engs}
        self.sempool = []
        self.livesems = {}
        self.nsem = 4
        self.n_ins = 0
        self.n_waits = 0
        self.trace = {e: [] for e in self.engs}

    def _getsem(self, buf):
        if self.sempool:
            sem, cnt = self.sempool.pop()
        else:
            self.nsem += 1
            sem = self.stack.enter_context(self.nc.semaphore("d%d" % self.nsem))
            cnt = 0
        buf.dsem = sem
        buf.dcount = cnt
        self.livesems[id(sem)] = [sem, cnt, cnt]

    def release(self, bufs):
        for b in bufs:
            if b.dsem is not None:
                ent = self.livesems.pop(id(b.dsem))
                self.sempool.append((ent[0], ent[1]))
                b.dsem = None

    def _deps(self, eng, reads, writes, is_dma):
        deps = {}

        def add(p, kind):
            if p is None:
                return
            if (not p.is_dma) and (not is_dma) and p.eng == eng:
                if eng == "pe" or kind != "raw":
                    return
            deps[id(p)] = p

        for b in reads:
            add(b.last_w, "raw")
            if b.excl:
                for r in b.readers:
                    if r.eng != eng:
                        add(r, "war")
        for b in writes:
            add(b.last_w, "waw")
            for r in b.readers:
                add(r, "war")
        return list(deps.values())

    def _post(self, ins, reads, writes):
        for b in reads:
            if not ins.is_dma:
                b.readers = [r for r in b.readers if r.is_dma or r.eng != ins.eng]
            b.readers.append(ins)
        for b in writes:
            b.last_w = ins
            b.readers = []
        self.batch.append(ins)

    def op(self, eng, fn, reads=(), writes=()):
        ins = Ins(eng, fn, self._deps(eng, reads, writes, False))
        self._post(ins, reads, writes)

    def dma(self, fn, reads, writes, dbuf, q="sp"):
        ins = Ins(q, fn, self._deps(q, reads, writes, True), True, dbuf)
        if dbuf.dsem is None:
            self._getsem(dbuf)
        dbuf.dcount += 1
        self.livesems[id(dbuf.dsem)][1] = dbuf.dcount
        self._post(ins, reads, writes)

    def flush(self):
        batch = self.batch
        self.batch = []
        for i in batch:
            for p in i.deps:
                if not p.is_dma:
                    p.need_inc = True
        last = {}
        for i in batch:
            if not i.is_dma:
                last[i.eng] = i
        for i in last.values():
            i.need_inc = True
        for i in batch:
            if not i.is_dma and i.need_inc:
                self.ecount[i.eng] += 1
                i.seq = self.ecount[i.eng]
        nxt = {}
        for i in reversed(batch):
            if not i.is_dma:
                if i.need_inc:
                    nxt[i.eng] = i.seq
                i.cover = nxt[i.eng]
        for i in batch:
            h = self.engs[i.eng]
            need = {}
            for p in i.deps:
                if p.is_dma:
                    ent = self.livesems.get(id(p.dbuf.dsem)) if p.dbuf.dsem is not None else None
                    if ent is None:
                        continue
                    key = id(ent[0])
                    sem = ent[0]
                    val = 16 * ent[2]
                else:
                    key = p.eng
                    sem = self.esem[p.eng]
                    val = p.cover
                if key not in need or need[key][1] < val:
                    need[key] = (sem, val)
            w = self.waited[i.eng]
            for key, (sem, val) in need.items():
                if w.get(key, 0) >= val:
                    continue
                w[key] = val
                h.wait_ge(sem, val)
                self.trace[i.eng].append(("w", id(sem), val))
                self.n_waits += 1
            bi = i.fn(h)
            self.n_ins += 1
            if i.is_dma:
                bi.then_inc(i.dbuf.dsem, 16)
                self.trace[i.eng].append(("i", id(i.dbuf.dsem), 16))
                self.livesems[id(i.dbuf.dsem)][2] += 1
            elif i.need_inc:
                bi.then_inc(self.esem[i.eng], 1)
                self.trace[i.eng].append(("i", id(self.esem[i.eng]), 1))
            else:
                self.trace[i.eng].append(("i", None, 0))

    def barrier(self):
        self.flush()
        for e, h in self.engs.items():
            w = self.waited[e]
            for pe in COMPUTE:
                if pe == e:
                    continue
                val = self.ecount[pe]
                if val > 0 and w.get(pe, 0) < val:
                    w[pe] = val
                    h.wait_ge(self.esem[pe], val)
                    self.trace[e].append(("w", id(self.esem[pe]), val))
            for key, ent in self.livesems.items():
                val = 16 * ent[2]
                if val > 0 and w.get(key, 0) < val:
                    w[key] = val
                    h.wait_ge(ent[0], val)
                    self.trace[e].append(("w", id(ent[0]), val))

    def finish(self):
        self.barrier()


class TB:
    def __init__(self, t, name):
        self.t = t
        self.b = Buf(name)


class Rot:
    def __init__(self, items):
        self.items = items
        self.i = 0

    def next(self):
        x = self.items[self.i % len(self.items)]
        self.i += 1
        return x


def weight_groups():
    g = {}
    g["ak"] = dict(K=2048, blocks=[[("w_in", 0, C_AK + 512 * i, 512)] for i in range(4)])
    g["av"] = dict(K=2048, blocks=[[("w_in", 0, C_AV + 512 * i, 512)] for i in range(4)])
    g["misc"] = dict(K=2048, blocks=[[("w_in", 0, C_CKV, 256), ("w_in", 0, C_IXK, 64), ("w_in", 0, C_KR, 64)]])
    g["aq"] = dict(K=2048, blocks=[[("w_in", 0, C_AQ + 512 * i, 512)] for i in range(4)])
    g["ixq"] = dict(K=2048, blocks=[[("w_in", 0, C_IXQ + 512 * i, 512)] for i in range(2)])
    g["cq"] = dict(K=2048, blocks=[[("w_in", 0, C_CQ, 512)], [("w_in", 0, C_IXW, 16)]])
    g["gate"] = dict(K=2048, blocks=[[("w_in", 0, C_GA + 512 * i, 512)] for i in range(8)])
    g["wo"] = dict(K=2048, blocks=[[("w_out", 0, 512 * i, 512)] for i in range(4)])
    g["up"] = dict(K=2048, blocks=[[("w_ff_up", 0, 512 * i, 512)] for i in range(16)])
    g["dn"] = dict(K=2048, blocks=[[("w_ff_down", 2048 * kg, 512 * cb, 512)] for cb in range(4) for kg in range(4)])
    g["uqn"] = dict(K=512, blocks=[[("uqn", 0, i, 0)] for i in range(4)])
    g["uqr"] = dict(K=512, blocks=[[("uqr", 0, i, 0)] for i in range(2)])
    g["uk"] = dict(K=256, blocks=[[("w_uk", 0, 512 * i, 512)] for i in range(4)])
    g["uv"] = dict(K=256, blocks=[[("w_uv", 0, 512 * i, 512)] for i in range(4)])
    return g


class _Stop(Exception):
    pass


class Builder:
    def ck(self, n):
        if int(os.environ.get("MK_KSTOP", "0")) == n:
            raise _Stop()

    def __init__(self, upto="ALL"):
        self.upto = upto
        self.nc = bass.Bass("TRN2", target_bir_lowering=False)
        self.gstack = ExitStack()
        self.P = Prog(self.nc, self.gstack)
        self.inp = {}
        self.out = {}
        self.scr = {}

    def din(self, name, shape, dt=F32):
        self.inp[name] = self.nc.dram_tensor(name, list(shape), dt, kind="ExternalInput").ap()
        return self.inp[name]

    def dout(self, name, shape, dt=F32):
        self.out[name] = self.nc.dram_tensor(name, list(shape), dt, kind="ExternalOutput").ap()
        return self.out[name]

    def dscr(self, name, shape, dt=BF16):
        self.scr[name] = TB(self.nc.dram_tensor(name, list(shape), dt, kind="Internal").ap(), name)
        return self.scr[name]

    def sb(self, st, name, shape, dt=F32):
        self._uid = getattr(self, "_uid", 0) + 1
        name = "%s_u%d" % (name, self._uid)
        return TB(st.enter_context(self.nc.sbuf_tensor(name, list(shape), dt)), name)

    def mm(self, ps, out_ap, lhsT, rhs, start, stop, reads):
        self.P.op("pe", lambda e: e.matmul(out_ap, lhsT, rhs, start=start, stop=stop), reads, [ps.b])

    def load(self, dst, dst_ap, src_ap, reads=(), q="sp"):
        self.P.dma(lambda e: e.dma_start(out=dst_ap, in_=src_ap), list(reads), [dst.b], dst.b, q=q)

    def store(self, src, dst_ap, src_ap, dstbuf=None, q="sp"):
        w = [dstbuf.b] if dstbuf is not None else []
        self.P.dma(lambda e: e.dma_start(out=dst_ap, in_=src_ap), [src.b], w, src.b, q=q)

    def declare(self):
        din, dout, dscr = self.din, self.dout, self.dscr
        din("x_all", [NXT * 128, D])
        din("cs", [NXT * 128, 64])
        din("valid", [128, NXT])
        din("prefmask", [128, 896])
        din("ident", [128, 128])
        din("onehot", [2, 32, 128 * 128])
        din("relb", [32, 16])
        din("constb", [128, 16])
        din("dsa_diag", [2, 128, 128])
        din("mla_mask", [2, 128, 256])
        din("c_akT", [2, 16, 128, 1024])
        din("c_av", [2, 1024, 16, 130])
        din("c_ixkT", [2, 64, 1024])
        din("c_ckvT", [2, 256, 1024])
        din("c_krT", [2, 64, 1024])
        din("w_in", [D, 12176])
        din("w_uq", [512, 16, 192])
        din("w_uk", [256, 2048])
        din("w_uv", [256, 2048])
        din("w_out", [D, D])
        din("w_ff_up", [D, 8192])
        din("w_ff_down", [8192, D])
        din("g_mix", [128, D])
        din("g_ffn", [128, D])
        din("g_fin", [128, D])
        din("g_q", [128, 512])
        din("g_kv", [128, 256])
        dout("y_own", [NOWN * 128, D])
        dout("st_ak", [NOWN * 128, D])
        dout("st_av", [NOWN * 128, D])
        dout("st_ix", [NOWN * 128, 64])
        dout("st_ckv", [NOWN * 128, 256])
        dout("st_kr", [NOWN * 128, 64])
        dscr("KT_A", [NH, 128, NTOK])
        dscr("V_A", [NH, 128, NTILE, 130])
        dscr("IXK_T", [64, NTOK])
        dscr("KT_B", [NH, 128, NTOK])
        dscr("KR_T", [64, NTOK])
        dscr("V_B", [NH, 128, NTILE, 130])
        dscr("QA_T", [5, 128, 16, 512])
        dscr("IXQ_T", [5, 128, 8, 512])
        dscr("IXW", [5, 128, 4, 16], F32)
        dscr("QN_T", [5, 128, 16, 512])
        dscr("QR_T", [5, 128, 8, 512])
        dscr("OA", [NOWN, 128, D], F32)
        dscr("OB", [NOWN, 128, D], F32)
        self.wg = weight_groups()
        for name, g in self.wg.items():
            dscr("W_" + name, [len(g["blocks"]), 128, g["K"] // 128, 512])

    def phase_W(self):
        P = self.P
        inp = self.inp
        for name, g in self.wg.items():
            W = self.scr["W_" + name]
            KC = g["K"] // 128
            for bi, pieces in enumerate(g["blocks"]):
                off = 0
                for (src, r0, c0, ncols) in pieces:
                    if src in ("uqn", "uqr"):
                        i = c0
                        nh_, w_, lo_ = (4, 128, 0) if src == "uqn" else (8, 64, 128)
                        for j in range(nh_):
                            src_ap = inp["w_uq"][:, nh_ * i + j, lo_:lo_ + w_].rearrange("(kc p) d -> p kc d", p=128)
                            dst_ap = W.t[bi][:, :, j * w_:(j + 1) * w_]
                            P.dma(lambda e, d=dst_ap, s=src_ap: e.dma_start(out=d, in_=s), [], [W.b], W.b, q="pool")
                        continue
                    src_ap = inp[src][r0:r0 + g["K"], c0:c0 + ncols].rearrange("(kc p) c -> p kc c", p=128)
                    dst_ap = W.t[bi][:, :, off:off + ncols]
                    off += ncols
                    P.dma(lambda e, d=dst_ap, s=src_ap: e.dma_start(out=d, in_=s), [], [W.b], W.b, q="pool")
        KT_A, V_A, IXK_T, KR_T = self.scr["KT_A"], self.scr["V_A"], self.scr["IXK_T"], self.scr["KR_T"]
        for b in range(2):
            t0 = 140 + 9 * b
            for h in range(NH):
                P.dma(lambda e, b=b, h=h, t0=t0: e.dma_start(out=KT_A.t[h, :, t0 * 128:t0 * 128 + 1024], in_=inp["c_akT"][b, h]),
                      [], [KT_A.b], KT_A.b, q="pool")
                P.dma(lambda e, b=b, h=h, t0=t0: e.dma_start(
                    out=V_A.t[h, :, t0:t0 + 8, :],
                    in_=inp["c_av"][b, :, h, :].rearrange("(kt p) d -> p kt d", p=128)), [], [V_A.b], V_A.b, q="pool")
            P.dma(lambda e, b=b, t0=t0: e.dma_start(out=IXK_T.t[:, t0 * 128:t0 * 128 + 1024], in_=inp["c_ixkT"][b]),
                  [], [IXK_T.b], IXK_T.b, q="pool")
            P.dma(lambda e, b=b, t0=t0: e.dma_start(out=KR_T.t[:, t0 * 128:t0 * 128 + 1024], in_=inp["c_krT"][b]),
                  [], [KR_T.b], KR_T.b, q="pool")

    def norm_rows(self, x_ap, n, g_ap, out_ap, xb, gb, outb, tmp):
        P = self.P
        junk, ss, sd, rstd = tmp["junk"], tmp["ss"], tmp["sd"], tmp["rstd"]
        P.op("act", lambda e: e.activation(out=junk.t[:, 0:n], in_=x_ap, func=AF.Square, accum_out=ss.t[:]),
             [xb], [junk.b, ss.b])
        P.op("act", lambda e: e.activation(out=sd.t[:], in_=ss.t[:], func=AF.Sqrt, bias=EPS, scale=1.0 / n),
             [ss.b], [sd.b])
        P.op("dve", lambda e: e.reciprocal(rstd.t[:], sd.t[:]), [sd.b], [rstd.b])
        P.op("dve", lambda e: e.scalar_tensor_tensor(out=out_ap, in0=x_ap, scalar=rstd.t[:], in1=g_ap,
                                                     op0=ALU.mult, op1=ALU.mult), [xb, rstd.b, gb], [outb])

    def load_norm_T(self, st, tiles, g_tb, hT, xts, hb, tmp, psT, identb, keep_x=None):
        P = self.P
        xall = self.inp["x_all"]
        for j, tile in enumerate(tiles):
            xt = keep_x[j] if keep_x is not None else xts.next()
            self.load(xt, xt.t[:], xall[tile * 128:(tile + 1) * 128, :])
            self.norm_rows(xt.t[:], D, g_tb.t[:], hb.t[:], xt.b, g_tb.b, hb.b, tmp)
            self.transpose_into(hb, hb.t, 16, hT, lambda kc, j=j: hT.t[:, kc, j * 128:(j + 1) * 128], psT, identb)

    def transpose_into(self, src, src_t, nchunks, dst, dst_ap_fn, psT, identb, rows=128):
        P = self.P
        c = 0
        while c < nchunks:
            n = min(4, nchunks - c)
            ps = psT.next()
            for k in range(n):
                self.mm(ps, ps.t[:, k * 128:(k + 1) * 128], src_t[:, (c + k) * 128:(c + k + 1) * 128], identb.t[:],
                        True, True, [src.b, identb.b])
            self._tflip = not getattr(self, "_tflip", False)
            for k in range(n):
                eng = "act" if self._tflip else "dve"
                d_ap = dst_ap_fn(c + k)
                s_ap = ps.t[:, k * 128:(k + 1) * 128]
                if eng == "act":
                    P.op("act", lambda e, d=d_ap, s=s_ap: e.activation(out=d, in_=s, func=AF.Copy), [ps.b], [dst.b])
                else:
                    P.op("dve", lambda e, d=d_ap, s=s_ap: e.tensor_copy(d, s), [ps.b], [dst.b])
            c += n

    def rope(self, out_ap1, out_ap2, x1, x2, cos, sin, xb, csb, outb, tmps):
        P = self.P
        t1, t2 = tmps
        P.op("dve", lambda e: e.tensor_tensor(t1.t_ap, x1, cos, ALU.mult), [xb, csb], [t1.b])
        P.op("dve", lambda e: e.tensor_tensor(t2.t_ap, x2, sin, ALU.mult), [xb, csb], [t2.b])
        P.op("dve", lambda e: e.tensor_tensor(out_ap1, t1.t_ap, t2.t_ap, ALU.subtract), [t1.b, t2.b], [outb])
        P.op("dve", lambda e: e.tensor_tensor(t1.t_ap, x1, sin, ALU.mult), [xb, csb, outb], [t1.b])
        P.op("dve", lambda e: e.tensor_tensor(t2.t_ap, x2, cos, ALU.mult), [xb, csb, outb], [t2.b])
        P.op("dve", lambda e: e.tensor_tensor(out_ap2, t1.t_ap, t2.t_ap, ALU.add), [t1.b, t2.b], [outb])

    def phase_K(self, blocks):
        P = self.P
        nc = self.nc
        inp, out, scr = self.inp, self.out, self.scr
        with ExitStack() as st:
            sb = lambda name, shape, dt=F32: self.sb(st, name, shape, dt)
            g_mix = sb("k_gmix", [128, D])
            g_kv = sb("k_gkv", [128, 256])
            identf = sb("k_identf", [128, 128])
            identb = sb("k_identb", [128, 128], BF16)
            ones16 = sb("k_ones16", [128, 16, 1])
            wuk = [sb("k_wuk%d" % i, [128, 2, 512], BF16) for i in range(4)]
            wuv = [sb("k_wuv%d" % i, [128, 2, 512], BF16) for i in range(4)]
            xts = Rot([sb("k_x%d" % i, [128, D]) for i in range(2)])
            hb = sb("k_hb", [128, D], BF16)
            hT = sb("k_hT", [128, 16, 512], BF16)
            wts = Rot([sb("k_wt%d" % i, [128, 16, 512], BF16) for i in range(2)])
            ksts = Rot([sb("k_kst%d" % i, [128, 512], BF16) for i in range(2)])
            vst = [sb("k_vst%d" % i, [128, 16, 130], BF16) for i in range(4)]
            vbst = [sb("k_vbst%d" % i, [128, 16, 130], BF16) for i in range(4)]
            mst = [sb("k_mst%d" % i, [128, 384]) for i in range(4)]
            mo = [sb("k_mo%d" % i, [128, 384]) for i in range(4)]
            km = sb("k_km", [128, 384], BF16)
            ckvT = sb("k_ckvT", [128, 2, 512], BF16)
            ixkrT = sb("k_ixkrT", [128, 512], BF16)
            osts = Rot([sb("k_ost%d" % i, [128, 512]) for i in range(3)])
            cst = sb("k_cs", [128, 4, 64])
            vld = sb("k_vld", [128, 4])
            tmp = dict(junk=sb("k_junk", [128, D], BF16), ss=sb("k_ss", [128, 1]), sd=sb("k_sd", [128, 1]),
                       rstd=sb("k_rstd", [128, 1]))
            rt1 = sb("k_rt1", [128, 32])
            rt2 = sb("k_rt2", [128, 32])
            rt1.t_ap = rt1.t[:]
            rt2.t_ap = rt2.t[:]
            PS = self.PS
            psT = Rot([PS[0], PS[1]])
            psM = Rot([PS[2], PS[3], PS[4], PS[5]])
            psX = Rot([PS[6], PS[7]])

            self.load(g_mix, g_mix.t[:], inp["g_mix"][:, :])
            self.load(g_kv, g_kv.t[:], inp["g_kv"][:, :])
            self.load(identf, identf.t[:], inp["ident"][:, :])
            P.op("dve", lambda e: e.tensor_copy(identb.t[:], identf.t[:]), [identf.b], [identb.b])
            P.op("dve", lambda e: e.memset(ones16.t[:], 1.0), [], [ones16.b])
            for i in range(4):
                self.load(wuk[i], wuk[i].t[:], scr["W_uk"].t[i], reads=[scr["W_uk"].b])
                self.load(wuv[i], wuv[i].t[:], scr["W_uv"].t[i], reads=[scr["W_uv"].b])

            for kb in blocks:
              try:
                tiles = [4 * kb + j for j in range(4)]
                tok0 = 4 * kb * 128
                if kb == 34:
                    own = {0: 16, 1: 17, 2: 18, 3: 19}
                elif kb % 2 == 1:
                    own = {3: (kb - 1) // 2}
                else:
                    own = {}
                self.load(cst, cst.t[:], inp["cs"][tok0:tok0 + 512, :].rearrange("(j p) c -> p j c", p=128))
                self.load(vld, vld.t[:], inp["valid"][:, 4 * kb:4 * kb + 4])
                self.ck(1)
                self.load_norm_T(st, tiles, g_mix, hT, xts, hb, tmp, psT, identb)
                self.ck(2)
                for cb in range(4):
                    wt = wts.next()
                    self.load(wt, wt.t[:], scr["W_ak"].t[cb], reads=[scr["W_ak"].b])
                    for j in range(4):
                        head = 4 * cb + j
                        ps = psM.next()
                        for kc in range(16):
                            self.mm(ps, ps.t[:, :], wt.t[:, kc, j * 128:(j + 1) * 128], hT.t[:, kc, :], kc == 0, kc == 15,
                                    [wt.b, hT.b])
                        kst = ksts.next()
                        P.op("act", lambda e, kst=kst, ps=ps: e.activation(out=kst.t[:], in_=ps.t[:, :], func=AF.Copy),
                             [ps.b], [kst.b])
                        self.store_cols(kst, scr["KT_A"], lambda a, n, head=head: scr["KT_A"].t[head, :, a:a + n], kst.t, tiles)
                    for pos, o in own.items():
                        ps = psM.next()
                        for kc in range(16):
                            self.mm(ps, ps.t[:, :], hT.t[:, kc, pos * 128:(pos + 1) * 128], wt.t[:, kc, :], kc == 0, kc == 15,
                                    [wt.b, hT.b])
                        ost = osts.next()
                        P.op("act", lambda e, ost=ost, ps=ps: e.activation(out=ost.t[:], in_=ps.t[:, :], func=AF.Copy),
                             [ps.b], [ost.b])
                        self.store(ost, out["st_ak"][o * 128:(o + 1) * 128, cb * 512:(cb + 1) * 512], ost.t[:])
                self.ck(3)
                for cb in range(4):
                    wt = wts.next()
                    self.load(wt, wt.t[:], scr["W_av"].t[cb], reads=[scr["W_av"].b])
                    for pos in range(4):
                        ps = psM.next()
                        for kc in range(16):
                            self.mm(ps, ps.t[:, :], hT.t[:, kc, pos * 128:(pos + 1) * 128], wt.t[:, kc, :], kc == 0, kc == 15,
                                    [wt.b, hT.b])
                        v = vst[pos]
                        P.op("dve", lambda e, v=v, ps=ps, cb=cb: e.tensor_copy(
                            v.t[:, 4 * cb:4 * cb + 4, 0:128], ps.t[:, :].rearrange("p (h d) -> p h d", h=4)), [ps.b], [v.b])
                        if pos in own:
                            o = own[pos]
                            ost = osts.next()
                            P.op("dve", lambda e, ost=ost, ps=ps: e.tensor_copy(ost.t[:], ps.t[:, :]), [ps.b], [ost.b])
                            self.store(ost, out["st_av"][o * 128:(o + 1) * 128, cb * 512:(cb + 1) * 512], ost.t[:])
                for pos in range(4):
                    v = vst[pos]
                    P.op("dve", lambda e, v=v, pos=pos: e.tensor_scalar(v.t[:, :, 128:129], ones16.t[:], vld.t[:, pos:pos + 1], None,
                                                                        ALU.mult), [ones16.b, vld.b], [v.b])
                    self.store(v, scr["V_A"].t[:, :, self.tmap(tiles[pos]), 0:129].rearrange("h p c -> p h c"), v.t[:, :, 0:129],
                               scr["V_A"])
                self.ck(4)
                wt = wts.next()
                self.load(wt, wt.t[:, :, 0:384], scr["W_misc"].t[0][:, :, 0:384], reads=[scr["W_misc"].b])
                for pos in range(4):
                    ps = psX.next()
                    for kc in range(16):
                        self.mm(ps, ps.t[:, 0:384], hT.t[:, kc, pos * 128:(pos + 1) * 128], wt.t[:, kc, 0:384], kc == 0, kc == 15,
                                [wt.b, hT.b])
                    m = mst[pos]
                    P.op("act", lambda e, m=m, ps=ps: e.activation(out=m.t[:], in_=ps.t[:, 0:384], func=AF.Copy), [ps.b], [m.b])
                self.ck(41)
                for pos in range(4):
                    m = mst[pos]
                    o_ = mo[pos]
                    self.norm_rows(m.t[:, 0:256], 256, g_kv.t[:], o_.t[:, 0:256], m.b, g_kv.b, o_.b, tmp)
                    self.ck(42)
                    P.op("act", lambda e, m=m, o_=o_: e.activation(out=o_.t[:, 256:320], in_=m.t[:, 256:320], func=AF.Copy),
                         [m.b], [o_.b])
                    self.rope(o_.t[:, 320:352], o_.t[:, 352:384], m.t[:, 320:352], m.t[:, 352:384],
                              cst.t[:, pos, 0:32], cst.t[:, pos, 32:64], m.b, cst.b, o_.b, (rt1, rt2))
                    self.ck(43)
                    if pos in own:
                        o = own[pos]
                        self.store(o_, out["st_ckv"][o * 128:(o + 1) * 128, :], o_.t[:, 0:256])
                        self.store(o_, out["st_ix"][o * 128:(o + 1) * 128, :], o_.t[:, 256:320])
                        self.store(o_, out["st_kr"][o * 128:(o + 1) * 128, :], o_.t[:, 320:384])
                    P.op("act", lambda e, o_=o_: e.activation(out=km.t[:], in_=o_.t[:], func=AF.Copy), [o_.b], [km.b])
                    self.ck(431)
                    ps = psX.next()
                    for k in range(3):
                        self.mm(ps, ps.t[:, k * 128:(k + 1) * 128], km.t[:, k * 128:(k + 1) * 128], identb.t[:], True, True,
                                [km.b, identb.b])
                    self.ck(432)
                    P.op("dve", lambda e, ps=ps, pos=pos: e.tensor_copy(
                        ckvT.t[:, :, pos * 128:(pos + 1) * 128], ps.t[:, 0:256].rearrange("p (c t) -> p c t", c=2)),
                        [ps.b], [ckvT.b])
                    self.ck(4321)
                    P.op("dve", lambda e, ps=ps, pos=pos: e.tensor_copy(ixkrT.t[:, pos * 128:(pos + 1) * 128], ps.t[:, 256:384]),
                         [ps.b], [ixkrT.b])
                    self.ck(433)
                self.ck(44)
                self.store_cols(ixkrT, scr["IXK_T"], lambda a, n: scr["IXK_T"].t[:, a:a + n], ixkrT.t[0:64], tiles)
                self.store_cols(ixkrT, scr["KR_T"], lambda a, n: scr["KR_T"].t[:, a:a + n], ixkrT.t[64:128], tiles)
                self.ck(5)
                if int(os.environ.get("MK_KSTOP", "0")) == 6:
                    self.mla_kside(ckvT, wuk, wuv, psM, ksts, vbst, tiles, tok0, lambda pos: vld.t[:, pos:pos + 1], vld, ones16)
                self.ck(6)
                self.mla_kside(ckvT, wuk, wuv, psM, ksts, vbst, tiles, tok0, lambda pos: vld.t[:, pos:pos + 1], vld, ones16)
              except _Stop:
                break
            P.barrier()
            self.release_scope(locals())

    def mla_kside(self, ckvT, wuk, wuv, psM, ksts, vbst, tiles, tok0, vld_ap_fn, vld, ones16, ncols=512):
        P = self.P
        scr = self.scr
        for h in range(NH):
            ps = psM.next()
            for cc in range(2):
                self.mm(ps, ps.t[:, 0:ncols], wuk[h // 4].t[:, cc, (h % 4) * 128:(h % 4 + 1) * 128], ckvT.t[:, cc, 0:ncols],
                        cc == 0, cc == 1, [wuk[h // 4].b, ckvT.b])
            kst = ksts.next()
            P.op("act", lambda e, kst=kst, ps=ps: e.activation(out=kst.t[:, 0:ncols], in_=ps.t[:, 0:ncols], func=AF.Copy),
                 [ps.b], [kst.b])
            self.store_cols(kst, scr["KT_B"], lambda a, n, h=h: scr["KT_B"].t[h, :, a:a + n], kst.t, tiles)
        for pos in range(len(tiles)):
            v = vbst[pos % len(vbst)]
            for cb in range(4):
                ps = psM.next()
                for cc in range(2):
                    self.mm(ps, ps.t[:, :], ckvT.t[:, cc, pos * 128:(pos + 1) * 128], wuv[cb].t[:, cc, :], cc == 0, cc == 1,
                            [wuv[cb].b, ckvT.b])
                P.op("dve", lambda e, v=v, ps=ps, cb=cb: e.tensor_copy(
                    v.t[:, 4 * cb:4 * cb + 4, 0:128], ps.t[:, :].rearrange("p (h d) -> p h d", h=4)), [ps.b], [v.b])
            if vld is not None:
                P.op("dve", lambda e, v=v, pos=pos: e.tensor_scalar(v.t[:, :, 128:129], ones16.t[:], vld_ap_fn(pos), None,
                                                                    ALU.mult), [ones16.b, vld.b], [v.b])
                self.store(v, scr["V_B"].t[:, :, self.tmap(tiles[pos]), 0:129].rearrange("h p c -> p h c"), v.t[:, :, 0:129], scr["V_B"])
            else:
                self.store(v, scr["V_B"].t[:, :, self.tmap(tiles[pos]), 0:129].rearrange("h p c -> p h c"), v.t[:, :, 0:129], scr["V_B"])

    def release_scope(self, loc):
        bufs = []
        for v in loc.values():
            if isinstance(v, TB):
                bufs.append(v.b)
            elif isinstance(v, (list, tuple)):
                for x in v:
                    if isinstance(x, TB):
                        bufs.append(x.b)
            elif isinstance(v, Rot):
                for x in v.items:
                    if isinstance(x, TB):
                        bufs.append(x.b)
            elif isinstance(v, dict):
                for x in v.values():
                    if isinstance(x, TB):
                        bufs.append(x.b)
        self.P.release(bufs)

    def phase_KC(self):
        P = self.P
        inp, scr = self.inp, self.scr
        with ExitStack() as st:
            sb = lambda name, shape, dt=F32: self.sb(st, name, shape, dt)
            wuk = [sb("c_wuk%d" % i, [128, 2, 512], BF16) for i in range(4)]
            wuv = [sb("c_wuv%d" % i, [128, 2, 512], BF16) for i in range(4)]
            ckvT = sb("c_ckvT", [128, 2, 512], BF16)
            ksts = Rot([sb("c_kst%d" % i, [128, 512], BF16) for i in range(2)])
            vbst = [sb("c_vbst%d" % i, [128, 16, 130], BF16) for i in range(4)]
            PS = self.PS
            psM = Rot([PS[2], PS[3], PS[4], PS[5]])
            for i in range(4):
                self.load(wuk[i], wuk[i].t[:], scr["W_uk"].t[i], reads=[scr["W_uk"].b])
                self.load(wuv[i], wuv[i].t[:], scr["W_uv"].t[i], reads=[scr["W_uv"].b])
                P.op("dve", lambda e, v=vbst[i]: e.memset(v.t[:, :, 128:130], 1.0), [], [vbst[i].b])
            for b in range(2):
                for half in range(2):
                    t0 = 140 + 9 * b + 4 * half
                    self.load(ckvT, ckvT.t[:], inp["c_ckvT"][b, :, half * 512:(half + 1) * 512].rearrange("(c p) s -> p c s", p=128),
                              q="pool")
                    self.mla_kside(ckvT, wuk, wuv, psM, ksts, vbst, [t0 + j for j in range(4)], t0 * 128, None, None, None)
            P.barrier()
            self.release_scope(locals())


    @staticmethod
    def tmap(t):
        return {136: 148, 137: 157}.get(t, t)

    def store_cols(self, src, dst, dst_row_ap_fn, src_t, tiles):
        mapped = [self.tmap(t) for t in tiles]
        if all(mapped[i] == mapped[0] + i for i in range(len(mapped))):
            self.store(src, dst_row_ap_fn(mapped[0] * 128, len(mapped) * 128), src_t[:, 0:len(mapped) * 128], dst)
        else:
            for i, mt in enumerate(mapped):
                self.store(src, dst_row_ap_fn(mt * 128, 128), src_t[:, i * 128:(i + 1) * 128], dst)

    def own_tiles(self, b):
        return [136 + j for j in range(4)] if b == 4 else [8 * (4 * b + j) + 7 for j in range(4)]

    def phase_Q(self):
        P = self.P
        inp, scr = self.inp, self.scr
        with ExitStack() as st:
            sb = lambda name, shape, dt=F32: self.sb(st, name, shape, dt)
            g_mix = sb("q_gmix", [128, D])
            g_q = sb("q_gq", [128, 512])
            identf = sb("q_identf", [128, 128])
            identb = sb("q_identb", [128, 128], BF16)
            xts = Rot([sb("q_x%d" % i, [128, D]) for i in range(2)])
            hb = sb("q_hb", [128, D], BF16)
            hT = sb("q_hT", [128, 16, 512], BF16)
            wts = Rot([sb("q_wt%d" % i, [128, 16, 512], BF16) for i in range(2)])
            qsts = Rot([sb("q_qst%d" % i, [128, 512], BF16) for i in range(3)])
            wuqn = [sb("q_wuqn%d" % i, [128, 4, 512], BF16) for i in range(4)]
            wuqr = [sb("q_wuqr%d" % i, [128, 4, 512], BF16) for i in range(2)]
            cq_f = sb("q_cqf", [128, 512])
            cqb = sb("q_cqb", [128, 512], BF16)
            cqT = sb("q_cqT", [128, 4, 512], BF16)
            qr_f = sb("q_qrf", [128, 1024])
            qr_o = sb("q_qro", [128, 1024])
            qrb = sb("q_qrb", [128, 1024], BF16)
            qrT = sb("q_qrT", [128, 8, 512], BF16)
            ixw = sb("q_ixw", [128, 4, 16])
            cst = sb("q_cs", [128, 4, 64])
            tmp = dict(junk=sb("q_junk", [128, D], BF16), ss=sb("q_ss", [128, 1]), sd=sb("q_sd", [128, 1]),
                       rstd=sb("q_rstd", [128, 1]))
            rt1 = sb("q_rt1", [128, 32])
            rt2 = sb("q_rt2", [128, 32])
            rt1.t_ap = rt1.t[:]
            rt2.t_ap = rt2.t[:]
            PS = self.PS
            psT = Rot([PS[0], PS[1]])
            psM = Rot([PS[2], PS[3], PS[4], PS[5]])
            psX = Rot([PS[6], PS[7]])
            self.load(g_mix, g_mix.t[:], inp["g_mix"][:, :])
            self.load(g_q, g_q.t[:], inp["g_q"][:, :])
            self.load(identf, identf.t[:], inp["ident"][:, :])
            P.op("dve", lambda e: e.tensor_copy(identb.t[:], identf.t[:]), [identf.b], [identb.b])
            for i in range(4):
                self.load(wuqn[i], wuqn[i].t[:], scr["W_uqn"].t[i], reads=[scr["W_uqn"].b])
            for i in range(2):
                self.load(wuqr[i], wuqr[i].t[:], scr["W_uqr"].t[i], reads=[scr["W_uqr"].b])
            for b in range(5):
                tiles = self.own_tiles(b)
                for j, tile in enumerate(tiles):
                    self.load(cst, cst.t[:, j, :], inp["cs"][tile * 128:(tile + 1) * 128, :])
                self.load_norm_T(st, tiles, g_mix, hT, xts, hb, tmp, psT, identb)
                for grp, nblk, dst in (("aq", 4, "QA_T"), ("ixq", 2, "IXQ_T")):
                    for cb in range(nblk):
                        wt = wts.next()
                        self.load(wt, wt.t[:], scr["W_" + grp].t[cb], reads=[scr["W_" + grp].b])
                        for j in range(4):
                            ps = psM.next()
                            for kc in range(16):
                                self.mm(ps, ps.t[:, :], wt.t[:, kc, j * 128:(j + 1) * 128], hT.t[:, kc, :], kc == 0, kc == 15,
                                        [wt.b, hT.b])
                            q = qsts.next()
                            P.op("act", lambda e, q=q, ps=ps: e.activation(out=q.t[:], in_=ps.t[:, :], func=AF.Copy), [ps.b], [q.b])
                            self.store(q, scr[dst].t[b][:, 4 * cb + j, :], q.t[:], scr[dst])
                wt = wts.next()
                self.load(wt, wt.t[:], scr["W_cq"].t[0], reads=[scr["W_cq"].b])
                for pos in range(4):
                    ps = psM.next()
                    for kc in range(16):
                        self.mm(ps, ps.t[:, :], hT.t[:, kc, pos * 128:(pos + 1) * 128], wt.t[:, kc, :], kc == 0, kc == 15, [wt.b, hT.b])
                    P.op("act", lambda e, ps=ps: e.activation(out=cq_f.t[:], in_=ps.t[:, :], func=AF.Copy), [ps.b], [cq_f.b])
                    self.norm_rows(cq_f.t[:], 512, g_q.t[:], cqb.t[:], cq_f.b, g_q.b, cqb.b, tmp)
                    self.transpose_into(cqb, cqb.t, 4, cqT, lambda c, pos=pos: cqT.t[:, c, pos * 128:(pos + 1) * 128], psT, identb)
                wt = wts.next()
                self.load(wt, wt.t[:, :, 0:16], scr["W_cq"].t[1][:, :, 0:16], reads=[scr["W_cq"].b])
                for pos in range(4):
                    ps = psX.next()
                    for kc in range(16):
                        self.mm(ps, ps.t[:, 0:16], hT.t[:, kc, pos * 128:(pos + 1) * 128], wt.t[:, kc, 0:16], kc == 0, kc == 15,
                                [wt.b, hT.b])
                    P.op("act", lambda e, ps=ps, pos=pos: e.activation(out=ixw.t[:, pos, :], in_=ps.t[:, 0:16], func=AF.Copy, scale=0.25),
                         [ps.b], [ixw.b])
                self.store(ixw, scr["IXW"].t[b], ixw.t[:], scr["IXW"])
                for h in range(NH):
                    ps = psM.next()
                    for cc in range(4):
                        self.mm(ps, ps.t[:, :], wuqn[h // 4].t[:, cc, (h % 4) * 128:(h % 4 + 1) * 128], cqT.t[:, cc, :], cc == 0, cc == 3,
                                [wuqn[h // 4].b, cqT.b])
                    q = qsts.next()
                    P.op("act", lambda e, q=q, ps=ps: e.activation(out=q.t[:], in_=ps.t[:, :], func=AF.Copy), [ps.b], [q.b])
                    self.store(q, scr["QN_T"].t[b][:, h, :], q.t[:], scr["QN_T"])
                for pos in range(4):
                    for blk in range(2):
                        ps = psM.next()
                        for cc in range(4):
                            self.mm(ps, ps.t[:, :], cqT.t[:, cc, pos * 128:(pos + 1) * 128], wuqr[blk].t[:, cc, :], cc == 0, cc == 3,
                                    [wuqr[blk].b, cqT.b])
                        P.op("act", lambda e, ps=ps, blk=blk: e.activation(out=qr_f.t[:, blk * 512:(blk + 1) * 512], in_=ps.t[:, :],
                                                                           func=AF.Copy), [ps.b], [qr_f.b])
                    for h in range(NH):
                        c0 = h * 64
                        self.rope(qr_o.t[:, c0:c0 + 32], qr_o.t[:, c0 + 32:c0 + 64], qr_f.t[:, c0:c0 + 32], qr_f.t[:, c0 + 32:c0 + 64],
                                  cst.t[:, pos, 0:32], cst.t[:, pos, 32:64], qr_f.b, cst.b, qr_o.b, (rt1, rt2))
                    P.op("act", lambda e: e.activation(out=qrb.t[:], in_=qr_o.t[:], func=AF.Copy), [qr_o.b], [qrb.b])
                    self.transpose_into(qrb, qrb.t, 8, qrT, lambda c, pos=pos: qrT.t[:, c, pos * 128:(pos + 1) * 128], psT, identb)
                self.store(qrT, scr["QR_T"].t[b], qrT.t[:], scr["QR_T"])
            P.barrier()
            self.release_scope(locals())

    def slots_all(self):
        P = self.P
        inp, scr = self.inp, self.scr
        with ExitStack() as st:
            sb = lambda name, shape, dt=F32: self.sb(st, name, shape, dt)
            C = {}
            C["identf"] = sb("s_identf", [128, 128])
            C["identb"] = sb("s_identb", [128, 128], BF16)
            C["Tb"] = sb("s_Tb", [128, 16, 2, 128])
            C["constb"] = sb("s_constb", [128, 16])
            C["prefmask"] = sb("s_pref", [128, 896])
            C["dsa_diag"] = sb("s_dsadiag", [128, 2, 128])
            C["mlaf"] = sb("s_mlaf", [128, 2, 256])
            C["mlamask"] = sb("s_mlamask", [128, 2, 256], BF16)
            relb = sb("s_relb", [32, 16])
            oh = sb("s_oh", [32, 4096])
            self.load(C["identf"], C["identf"].t[:], inp["ident"][:, :])
            P.op("dve", lambda e: e.tensor_copy(C["identb"].t[:], C["identf"].t[:]), [C["identf"].b], [C["identb"].b])
            self.load(C["constb"], C["constb"].t[:], inp["constb"][:, :])
            self.load(C["prefmask"], C["prefmask"].t[:], inp["prefmask"][:, :])
            self.load(C["dsa_diag"], C["dsa_diag"].t[:], inp["dsa_diag"].rearrange("y t s -> t y s"))
            self.load(C["mlaf"], C["mlaf"].t[:], inp["mla_mask"].rearrange("y s c -> s y c"))
            P.op("dve", lambda e: e.tensor_copy(C["mlamask"].t[:], C["mlaf"].t[:]), [C["mlaf"].b], [C["mlamask"].b])
            self.load(relb, relb.t[:], inp["relb"][:, :])
            PS = self.PS
            for ty in range(2):
                for tg in range(4):
                    self.load(oh, oh.t[:], inp["onehot"][ty][:, tg * 4096:(tg + 1) * 4096])
                    ps = PS[tg % 2]
                    for t in range(32):
                        self.mm(ps, ps.t[:, t * 16:(t + 1) * 16], oh.t[:, t * 128:(t + 1) * 128], relb.t[:, :], True, True, [oh.b, relb.b])
                    P.op("dve", lambda e, ps=ps, ty=ty, tg=tg: e.tensor_copy(
                        C["Tb"].t[:, :, ty, tg * 32:(tg + 1) * 32], ps.t[:, :].rearrange("p (t h) -> p h t", h=16)), [ps.b], [C["Tb"].b])
            P.barrier()
            nslot = int(os.environ.get("MK_NSLOT", "18"))
            order = []
            for o in range(16):
                order.append((o // 4, o % 4, [(0, 8 * o + 8)], 0))
            order.append((4, 0, [(140, 9)], 1))
            order.append((4, 1, [(149, 9)], 1))
            if nslot < 18:
                order = [order[0], order[16], order[1], order[17]][:nslot]
            for (b, j, runs, ty) in order:
                self.slot(C, b, j, runs, ty)
            self.release_scope(dict(C=C, relb=relb, oh=oh))

    def slot(self, C, b, j, runs, ty):
        P = self.P
        inp, scr = self.inp, self.scr
        PS = self.PS
        o = 4 * b + j
        ktl = []
        for (t0, n) in runs:
            ktl += [t0 + i for i in range(n)]
        nk = len(ktl)
        S = nk * 128
        identb = C["identb"]
        with ExitStack() as st:
            sb = lambda name, shape, dt=F32: self.sb(st, name, shape, dt)
            qa = sb("l_qa", [128, 16, 128], BF16)
            ixq = sb("l_ixq", [128, 8, 128], BF16)
            ixw = sb("l_ixw", [128, 16])
            qn = sb("l_qn", [128, 16, 128], BF16)
            qr = sb("l_qr", [128, 8, 128], BF16)
            maskadd = sb("l_maskadd", [128, S], BF16)
            js = slice(j * 128, (j + 1) * 128)
            self.load(qa, qa.t[:], scr["QA_T"].t[b][:, :, js], reads=[scr["QA_T"].b])
            self.load(ixq, ixq.t[:], scr["IXQ_T"].t[b][:, :, js], reads=[scr["IXQ_T"].b])
            self.load(ixw, ixw.t[:], scr["IXW"].t[b][:, j, :], reads=[scr["IXW"].b])
            self.load(qn, qn.t[:], scr["QN_T"].t[b][:, :, js], reads=[scr["QN_T"].b])
            self.load(qr, qr.t[:], scr["QR_T"].t[b][:, :, js], reads=[scr["QR_T"].b])
            with ExitStack() as st2:
                sb2 = lambda name, shape, dt=F32: self.sb(st2, name, shape, dt)
                row = sb2("i_row", [128, S])
                ixks = Rot([sb2("i_ixk%d" % i, [128, 512], BF16) for i in range(2)])
                rbs = Rot([sb2("i_r%d" % i, [128, 512]) for i in range(3)])
                mx = sb2("i_mx", [128, 1])
                mid = sb2("i_mid", [128, 1])
                cnt = sb2("i_cnt", [128, 1])
                tfl = sb2("i_tfl", [128, 1])
                thr = sb2("i_thr", [128, 1])
                psR = Rot([PS[0], PS[1], PS[2], PS[3]])
                col = 0
                for (t0, n) in runs:
                    k = 0
                    while k < n:
                        g = min(4, n - k)
                        W = g * 128
                        tok = (t0 + k) * 128
                        ixk = ixks.next()
                        self.load(ixk, ixk.t[0:64, 0:W], scr["IXK_T"].t[:, tok:tok + W], reads=[scr["IXK_T"].b])
                        self.load(ixk, ixk.t[64:128, 0:W], scr["IXK_T"].t[:, tok:tok + W], reads=[scr["IXK_T"].b])
                        for h in range(16):
                            hf = h % 2
                            ps = psR.next()
                            self.mm(ps, ps.t[:, 0:W], ixq.t[hf * 64:(hf + 1) * 64, h // 2, :], ixk.t[hf * 64:(hf + 1) * 64, 0:W],
                                    True, True, [ixq.b, ixk.b])
                            r = rbs.next()
                            P.op("act", lambda e, r=r, ps=ps, W=W: e.activation(out=r.t[:, 0:W], in_=ps.t[:, 0:W], func=AF.Relu),
                                 [ps.b], [r.b])
                            if h == 0:
                                P.op("dve", lambda e, r=r, W=W, col=col: e.tensor_scalar(
                                    row.t[:, col:col + W], r.t[:, 0:W], ixw.t[:, 0:1], None, ALU.mult), [r.b, ixw.b], [row.b])
                            else:
                                P.op("dve", lambda e, r=r, W=W, col=col, h=h: e.scalar_tensor_tensor(
                                    out=row.t[:, col:col + W], in0=r.t[:, 0:W], scalar=ixw.t[:, h:h + 1], in1=row.t[:, col:col + W],
                                    op0=ALU.mult, op1=ALU.add), [r.b, ixw.b, row.b], [row.b])
                        col += W
                        k += g
                if ty == 0:
                    P.op("dve", lambda e: e.tensor_tensor(row.t[:, 0:896], row.t[:, 0:896], C["prefmask"].t[:], ALU.add),
                         [row.b, C["prefmask"].b], [row.b])
                P.op("dve", lambda e: e.tensor_tensor(row.t[:, S - 128:S], row.t[:, S - 128:S], C["dsa_diag"].t[:, ty, :], ALU.add),
                     [row.b, C["dsa_diag"].b], [row.b])
                P.op("dve", lambda e: e.tensor_reduce(mx.t[:], row.t[:], AX.X, ALU.max), [row.b], [mx.b])
                P.op("dve", lambda e: e.tensor_scalar(mid.t[:], mx.t[:], -BIS_R / 2, None, ALU.add), [mx.b], [mid.b])
                for it in range(BIS_IT):
                    hw = BIS_R / (2 ** (it + 1))
                    P.op("dve", lambda e: e.tensor_scalar(maskadd.t[:], row.t[:], mid.t[:], None, ALU.is_ge, ALU.add, accum_out=cnt.t[:]),
                         [row.b, mid.b], [maskadd.b, cnt.b])
                    P.op("dve", lambda e, hw=hw: e.tensor_scalar(tfl.t[:], cnt.t[:], float(TOPK) - 0.5, hw, ALU.is_ge, ALU.mult),
                         [cnt.b], [tfl.b])
                    P.op("dve", lambda e, hw=hw: e.scalar_tensor_tensor(out=mid.t[:], in0=tfl.t[:], scalar=-hw / 2, in1=mid.t[:],
                                                                        op0=ALU.add, op1=ALU.add), [tfl.b, mid.b], [mid.b])
                hwK = BIS_R / (2 ** (BIS_IT + 1))
                P.op("dve", lambda e: e.tensor_scalar(thr.t[:], mid.t[:], -hwK, None, ALU.add), [mid.b], [thr.b])
                P.op("dve", lambda e: e.tensor_scalar(maskadd.t[:], row.t[:], thr.t[:], MASKNEG, ALU.is_lt, ALU.mult),
                     [row.b, thr.b], [maskadd.b])
                P.barrier()
                self.release_scope(locals())
            with ExitStack() as st3:
                sb3 = lambda name, shape, dt=F32: self.sb(st3, name, shape, dt)
                kcs = Rot([sb3("a_kc%d" % i, [128, 2048], BF16) for i in range(2)])
                vcs = Rot([sb3("a_vc%d" % i, [128, 16, 130], BF16) for i in range(2)])
                krr = sb3("a_kr", [128, S], BF16)
                pts = Rot([sb3("a_p%d" % i, [128, 512], BF16) for i in range(3)])
                p2s = Rot([sb3("a_p2%d" % i, [128, 256], BF16) for i in range(2)])
                tmpn = sb3("a_tmpn", [128, 256])
                rec = sb3("a_rec", [128, 1])
                ost = sb3("a_ost", [128, D])
                psS = Rot([PS[0], PS[1], PS[2], PS[3]])
                psA = Rot([PS[4], PS[5]])
                col = 0
                for (t0, n) in runs:
                    self.load(krr, krr.t[0:64, col:col + n * 128], scr["KR_T"].t[:, t0 * 128:(t0 + n) * 128], reads=[scr["KR_T"].b])
                    self.load(krr, krr.t[64:128, col:col + n * 128], scr["KR_T"].t[:, t0 * 128:(t0 + n) * 128], reads=[scr["KR_T"].b])
                    col += n * 128
                chunks = []
                gk = 0
                for (t0, n) in runs:
                    k = 0
                    while k < n:
                        m = min(16, n - k)
                        chunks.append((t0 + k, m, gk))
                        gk += m
                        k += m
                for kind in ("dsa", "mla"):
                    KT = scr["KT_A"] if kind == "dsa" else scr["KT_B"]
                    VV = scr["V_A"] if kind == "dsa" else scr["V_B"]
                    for h in range(NH):
                        acc = psA.next()
                        hf = h % 2
                        for (t0, m, gk0) in chunks:
                            kc = kcs.next()
                            vc = vcs.next()
                            self.load(kc, kc.t[:, 0:m * 128], KT.t[h, :, t0 * 128:(t0 + m) * 128], reads=[KT.b])
                            self.load(vc, vc.t[:, 0:m, :], VV.t[h, :, t0:t0 + m, :], reads=[VV.b])
                            k = 0
                            while k < m:
                                gkt = gk0 + k
                                if gkt >= nk - 2:
                                    g = nk - gkt
                                    near = True
                                else:
                                    g = min(4, m - k, nk - 2 - gkt)
                                    near = False
                                Sps = psS.next()
                                for gi in range(g):
                                    cs_ = slice(gi * 128, (gi + 1) * 128)
                                    kl = k + gi
                                    if kind == "dsa":
                                        self.mm(Sps, Sps.t[:, cs_], kc.t[:, kl * 128:(kl + 1) * 128], qa.t[:, h, :], True, False,
                                                [kc.b, qa.b])
                                        self.mm(Sps, Sps.t[:, cs_], maskadd.t[:, (gkt + gi) * 128:(gkt + gi + 1) * 128], identb.t[:],
                                                False, True, [maskadd.b, identb.b])
                                    else:
                                        self.mm(Sps, Sps.t[:, cs_], kc.t[:, kl * 128:(kl + 1) * 128], qn.t[:, h, :], True, False,
                                                [kc.b, qn.b])
                                        self.mm(Sps, Sps.t[:, cs_], krr.t[hf * 64:(hf + 1) * 64, (gkt + gi) * 128:(gkt + gi + 1) * 128],
                                                qr.t[hf * 64:(hf + 1) * 64, h // 2, :], False, True, [krr.b, qr.b])
                                W = g * 128
                                p = pts.next()
                                if kind == "dsa":
                                    if near:
                                        P.op("dve", lambda e, Sps=Sps, h=h: e.scalar_tensor_tensor(
                                            out=tmpn.t[:], in0=Sps.t[:, 0:256], scalar=A_SCALE,
                                            in1=C["Tb"].t[:, h, :, :].rearrange("p y t -> p (y t)"), op0=ALU.mult, op1=ALU.add),
                                            [Sps.b, C["Tb"].b], [tmpn.b])
                                        P.op("act", lambda e, p=p: e.activation(out=p.t[:, 0:256], in_=tmpn.t[:], func=AF.Exp),
                                             [tmpn.b], [p.b])
                                    else:
                                        P.op("act", lambda e, p=p, Sps=Sps, W=W, h=h: e.activation(
                                            out=p.t[:, 0:W], in_=Sps.t[:, 0:W], func=AF.Exp, bias=C["constb"].t[:, h:h + 1], scale=A_SCALE),
                                            [Sps.b, C["constb"].b], [p.b])
                                    pp = p
                                else:
                                    P.op("act", lambda e, p=p, Sps=Sps, W=W: e.activation(out=p.t[:, 0:W], in_=Sps.t[:, 0:W], func=AF.Exp,
                                                                                       scale=MLA_SCALE), [Sps.b], [p.b])
                                    pp = p
                                    if near:
                                        p2 = p2s.next()
                                        P.op("dve", lambda e, p=p, p2=p2: e.tensor_tensor(p2.t[:], p.t[:, 0:256], C["mlamask"].t[:, ty, :],
                                                                                         ALU.mult), [p.b, C["mlamask"].b], [p2.b])
                                        pp = p2
                                for gi in range(g):
                                    kl = k + gi
                                    self.mm(acc, acc.t[:, 0:129], pp.t[:, gi * 128:(gi + 1) * 128], vc.t[:, kl, 0:129],
                                            (gkt + gi) == 0, (gkt + gi) == nk - 1, [pp.b, vc.b])
                                k += g
                        P.op("dve", lambda e, acc=acc: e.reciprocal(rec.t[:], acc.t[:, 128:129]), [acc.b], [rec.b])
                        P.op("dve", lambda e, acc=acc, h=h: e.tensor_scalar(ost.t[:, h * 128:(h + 1) * 128], acc.t[:, 0:128], rec.t[:], None,
                                                                            ALU.mult), [acc.b, rec.b], [ost.b])
                    dst = scr["OA"] if kind == "dsa" else scr["OB"]
                    self.store(ost, dst.t[o], ost.t[:], dst)
                P.barrier()
                self.release_scope(locals())
            self.release_scope(dict(qa=qa, ixq=ixq, ixw=ixw, qn=qn, qr=qr, maskadd=maskadd))

    def phase_M(self):
        P = self.P
        inp, out, scr = self.inp, self.out, self.scr
        PS = self.PS
        nb = int(os.environ.get("MK_NMB", "5"))
        for b in list(range(5))[:nb] if nb >= 5 else [0, 4][:nb]:
            tiles = self.own_tiles(b)
            with ExitStack() as so:
                xk = [self.sb(so, "m_x%d" % i, [128, D]) for i in range(4)]
                with ExitStack() as st:
                    sb = lambda name, shape, dt=F32: self.sb(st, name, shape, dt)
                    g_mix = sb("m_gmix", [128, D])
                    identf = sb("m_identf", [128, 128])
                    identb = sb("m_identb", [128, 128], BF16)
                    hb = sb("m_hb", [128, D], BF16)
                    hT = sb("m_hT", [128, 16, 512], BF16)
                    wts = Rot([sb("m_wt%d" % i, [128, 16, 512], BF16) for i in range(2)])
                    gas = Rot([sb("m_ga%d" % i, [128, 512]) for i in range(2)])
                    gbs = Rot([sb("m_gb%d" % i, [128, 512]) for i in range(2)])
                    oas = Rot([sb("m_oa%d" % i, [128, 512]) for i in range(2)])
                    obs = Rot([sb("m_ob%d" % i, [128, 512]) for i in range(2)])
                    mix = [sb("m_mix%d" % i, [128, D], BF16) for i in range(4)]
                    tmp = dict(junk=sb("m_junk", [128, D], BF16), ss=sb("m_ss", [128, 1]), sd=sb("m_sd", [128, 1]),
                               rstd=sb("m_rstd", [128, 1]))
                    psT = Rot([PS[0], PS[1]])
                    psM = Rot([PS[2], PS[3], PS[4], PS[5]])
                    self.load(g_mix, g_mix.t[:], inp["g_mix"][:, :])
                    self.load(identf, identf.t[:], inp["ident"][:, :])
                    P.op("dve", lambda e: e.tensor_copy(identb.t[:], identf.t[:]), [identf.b], [identb.b])
                    self.load_norm_T(st, tiles, g_mix, hT, None, hb, tmp, psT, identb, keep_x=xk)
                    for i in range(4):
                        wa = wts.next()
                        wb = wts.next()
                        self.load(wa, wa.t[:], scr["W_gate"].t[i], reads=[scr["W_gate"].b])
                        self.load(wb, wb.t[:], scr["W_gate"].t[4 + i], reads=[scr["W_gate"].b])
                        cs_ = slice(i * 512, (i + 1) * 512)
                        for pos in range(4):
                            o = 4 * b + pos
                            pa = psM.next()
                            for kc in range(16):
                                self.mm(pa, pa.t[:, :], hT.t[:, kc, pos * 128:(pos + 1) * 128], wa.t[:, kc, :], kc == 0, kc == 15, [wa.b, hT.b])
                            pb = psM.next()
                            for kc in range(16):
                                self.mm(pb, pb.t[:, :], hT.t[:, kc, pos * 128:(pos + 1) * 128], wb.t[:, kc, :], kc == 0, kc == 15, [wb.b, hT.b])
                            ga, gb, oa, ob = gas.next(), gbs.next(), oas.next(), obs.next()
                            P.op("act", lambda e, ga=ga, pa=pa: e.activation(out=ga.t[:], in_=pa.t[:, :], func=AF.Sigmoid), [pa.b], [ga.b])
                            P.op("act", lambda e, gb=gb, pb=pb: e.activation(out=gb.t[:], in_=pb.t[:, :], func=AF.Sigmoid), [pb.b], [gb.b])
                            self.load(oa, oa.t[:], scr["OA"].t[o][:, cs_], reads=[scr["OA"].b])
                            self.load(ob, ob.t[:], scr["OB"].t[o][:, cs_], reads=[scr["OB"].b])
                            P.op("dve", lambda e, ga=ga, oa=oa: e.tensor_tensor(ga.t[:], ga.t[:], oa.t[:], ALU.mult), [ga.b, oa.b], [ga.b])
                            P.op("dve", lambda e, gb=gb, ob=ob: e.tensor_tensor(gb.t[:], gb.t[:], ob.t[:], ALU.mult), [gb.b, ob.b], [gb.b])
                            P.op("dve", lambda e, ga=ga, gb=gb, pos=pos, cs_=cs_: e.tensor_tensor(mix[pos].t[:, cs_], ga.t[:], gb.t[:], ALU.add),
                                 [ga.b, gb.b], [mix[pos].b])
                    for pos in range(4):
                        self.transpose_into(mix[pos], mix[pos].t, 16, hT, lambda kc, pos=pos: hT.t[:, kc, pos * 128:(pos + 1) * 128], psT, identb)
                    for cb in range(4):
                        wt = wts.next()
                        self.load(wt, wt.t[:], scr["W_wo"].t[cb], reads=[scr["W_wo"].b])
                        cs_ = slice(cb * 512, (cb + 1) * 512)
                        for pos in range(4):
                            ps = psM.next()
                            for kc in range(16):
                                self.mm(ps, ps.t[:, :], hT.t[:, kc, pos * 128:(pos + 1) * 128], wt.t[:, kc, :], kc == 0, kc == 15, [wt.b, hT.b])
                            P.op("dve", lambda e, ps=ps, pos=pos, cs_=cs_: e.tensor_tensor(xk[pos].t[:, cs_], xk[pos].t[:, cs_], ps.t[:, :], ALU.add),
                                 [xk[pos].b, ps.b], [xk[pos].b])
                    P.barrier()
                    self.release_scope(locals())
                with ExitStack() as st:
                    sb = lambda name, shape, dt=F32: self.sb(st, name, shape, dt)
                    g_ffn = sb("n_gffn", [128, D])
                    g_fin = sb("n_gfin", [128, D])
                    identf = sb("n_identf", [128, 128])
                    identb = sb("n_identb", [128, 128], BF16)
                    hb = sb("n_hb", [128, D], BF16)
                    h2T = sb("n_h2T", [128, 16, 512], BF16)
                    uT = sb("n_uT", [128, 64, 512], BF16)
                    wts = Rot([sb("n_wt%d" % i, [128, 16, 512], BF16) for i in range(2)])
                    rrs = Rot([sb("n_rr%d" % i, [128, 512]) for i in range(2)])
                    ys = Rot([sb("n_y%d" % i, [128, D]) for i in range(2)])
                    tmp = dict(junk=sb("n_junk", [128, D], BF16), ss=sb("n_ss", [128, 1]), sd=sb("n_sd", [128, 1]),
                               rstd=sb("n_rstd", [128, 1]))
                    psT = Rot([PS[0], PS[1]])
                    psM = Rot([PS[6], PS[7]])
                    self.load(g_ffn, g_ffn.t[:], inp["g_ffn"][:, :])
                    self.load(g_fin, g_fin.t[:], inp["g_fin"][:, :])
                    self.load(identf, identf.t[:], inp["ident"][:, :])
                    P.op("dve", lambda e: e.tensor_copy(identb.t[:], identf.t[:]), [identf.b], [identb.b])
                    for pos in range(4):
                        self.norm_rows(xk[pos].t[:], D, g_ffn.t[:], hb.t[:], xk[pos].b, g_ffn.b, hb.b, tmp)
                        self.transpose_into(hb, hb.t, 16, h2T, lambda kc, pos=pos: h2T.t[:, kc, pos * 128:(pos + 1) * 128], psT, identb)
                    for cb in range(16):
                        wt = wts.next()
                        self.load(wt, wt.t[:], scr["W_up"].t[cb], reads=[scr["W_up"].b])
                        for jj in range(4):
                            ffc = 4 * cb + jj
                            ps = psM.next()
                            for kc in range(16):
                                self.mm(ps, ps.t[:, :], wt.t[:, kc, jj * 128:(jj + 1) * 128], h2T.t[:, kc, :], kc == 0, kc == 15, [wt.b, h2T.b])
                            rr = rrs.next()
                            P.op("act", lambda e, rr=rr, ps=ps: e.activation(out=rr.t[:], in_=ps.t[:, :], func=AF.Relu), [ps.b], [rr.b])
                            P.op("dve", lambda e, rr=rr, ffc=ffc: e.tensor_tensor(uT.t[:, ffc, :], rr.t[:], rr.t[:], ALU.mult), [rr.b], [uT.b])
                    for cb in range(4):
                        cs_ = slice(cb * 512, (cb + 1) * 512)
                        pss = [PS[2], PS[3], PS[4], PS[5]]
                        for kg in range(4):
                            wt = wts.next()
                            self.load(wt, wt.t[:], scr["W_dn"].t[cb * 4 + kg], reads=[scr["W_dn"].b])
                            for pos in range(4):
                                for kc in range(16):
                                    self.mm(pss[pos], pss[pos].t[:, :], uT.t[:, kg * 16 + kc, pos * 128:(pos + 1) * 128], wt.t[:, kc, :],
                                            kg == 0 and kc == 0, kg == 3 and kc == 15, [wt.b, uT.b])
                        for pos in range(4):
                            P.op("dve", lambda e, pos=pos, cs_=cs_, pss=pss: e.tensor_tensor(xk[pos].t[:, cs_], xk[pos].t[:, cs_], pss[pos].t[:, :],
                                                                                         ALU.add), [xk[pos].b, pss[pos].b], [xk[pos].b])
                    for pos in range(4):
                        o = 4 * b + pos
                        y = ys.next()
                        self.norm_rows(xk[pos].t[:], D, g_fin.t[:], y.t[:], xk[pos].b, g_fin.b, y.b, tmp)
                        self.store(y, out["y_own"][o * 128:(o + 1) * 128, :], y.t[:])
                    P.barrier()
                    self.release_scope(locals())
                self.release_scope(dict(xk=xk))

    def build(self):
        nc = self.nc
        self.declare()
        st = self.gstack
        self.PS = [TB(st.enter_context(nc.psum_tensor("ps%d" % i, [128, 512], F32)), "ps%d" % i) for i in range(8)]
        for p_ in self.PS:
            p_.b.excl = True
        self.phase_W()
        self.P.barrier()
        self.P.release([tb.b for tb in self.scr.values()])
        if self.upto == "W":
            return self.finish()
        kblocks = list(range(35))
        if self.upto == "K1":
            kblocks = [0, 1, 34][:int(os.environ.get("MK_NB", "3"))]
        self.phase_K(kblocks)
        if self.upto in ("K", "K1"):
            return self.finish()
        self.phase_KC()
        if self.upto == "KC":
            return self.finish()
        self.phase_Q()
        if self.upto == "Q":
            return self.finish()
        self.slots_all()
        if self.upto == "S":
            return self.finish()
        self.phase_M()
        return self.finish()

    def finish(self):
        self.P.finish()
        self.gstack.close()
        return self.nc


def t5_bucket_np(rel):
    nb = 16
    ret = (rel > 0).astype(np.int32) * nb
    n = np.abs(rel)
    max_exact = 8
    nf = np.maximum(n, 1).astype(np.float32)
    large = max_exact + (np.log(nf / np.float32(max_exact)) / np.float32(math.log(128 / max_exact))
                         * np.float32(nb - max_exact)).astype(np.int32)
    large = np.minimum(large, nb - 1)
    return ret + np.where(n < max_exact, n, large)


def host_inputs(inputs):
    f32 = np.float32
    xp = np.asarray(inputs["x_prompt"], f32)[0]
    xs = np.asarray(inputs["x_sample"], f32)
    ck = np.asarray(inputs["cache_a_k"], f32)[0]
    cv = np.asarray(inputs["cache_a_v"], f32)[0]
    cix = np.asarray(inputs["cache_a_idx_k"], f32)[0]
    cckv = np.asarray(inputs["cache_b_ckv"], f32)[0]
    ckr = np.asarray(inputs["cache_b_krope"], f32)[0]
    half = 32
    inv_freq = np.power(np.float32(10000.0), -np.arange(half, dtype=f32) / np.float32(half)).astype(f32)
    shared = {
        "ident": np.eye(128, dtype=f32),
        "constb": np.ascontiguousarray(np.broadcast_to(np.asarray(inputs["rel_bias_table"], f32)[15], (128, 16))),
        "relb": np.ascontiguousarray(np.asarray(inputs["rel_bias_table"], f32)),
        "w_in": np.ascontiguousarray(np.asarray(inputs["w_in"], f32)[0]),
        "w_uq": np.ascontiguousarray(np.asarray(inputs["w_uq"], f32)[0]),
        "w_uk": np.ascontiguousarray(np.asarray(inputs["w_uk"], f32)[0].reshape(256, 2048)),
        "w_uv": np.ascontiguousarray(np.asarray(inputs["w_uv"], f32)[0].reshape(256, 2048)),
        "w_out": np.ascontiguousarray(np.asarray(inputs["w_out"], f32)[0]),
        "w_ff_up": np.ascontiguousarray(np.asarray(inputs["w_ff_up"], f32)[0]),
        "w_ff_down": np.ascontiguousarray(np.asarray(inputs["w_ff_down"], f32)[0]),
        "g_mix": np.ascontiguousarray(np.broadcast_to(np.asarray(inputs["norm_mix_g"], f32)[0], (128, D))),
        "g_ffn": np.ascontiguousarray(np.broadcast_to(np.asarray(inputs["norm_ffn_g"], f32)[0], (128, D))),
        "g_fin": np.ascontiguousarray(np.broadcast_to(np.asarray(inputs["final_norm_g"], f32), (128, D))),
        "g_q": np.ascontiguousarray(np.broadcast_to(np.asarray(inputs["q_lora_g"], f32)[0], (128, 512))),
        "g_kv": np.ascontiguousarray(np.broadcast_to(np.asarray(inputs["kv_lora_g"], f32)[0], (128, 256))),
    }
    s = np.arange(128)[None, :]
    t = np.arange(128)[:, None]
    onehot = np.zeros((2, 32, 128, 128), f32)
    for ty, off in enumerate((-128, 0)):
        rel = (s - t + off).astype(np.int32)
        bk = t5_bucket_np(rel)
        for b in range(32):
            onehot[ty, b] = (bk == b)
    shared["onehot"] = onehot.reshape(2, 32, 128 * 128)
    dsa_diag = np.zeros((2, 128, 128), f32)
    dsa_diag[0] = np.where((s // 64) <= (t // 64), 0.0, NEGBIG)
    dsa_diag[1] = np.where(s < 16, 0.0, NEGBIG) * np.ones((128, 1), f32)
    shared["dsa_diag"] = dsa_diag
    mla_mask = np.ones((2, 128, 2, 128), f32)
    ss_ = np.arange(128)[:, None]
    tt_ = np.arange(128)[None, :]
    mla_mask[0, :, 1, :] = ((ss_ // 64) <= (tt_ // 64)).astype(f32)
    mla_mask[1, :, 1, :] = (ss_ < 16).astype(f32) * np.ones((1, 128), f32)
    shared["mla_mask"] = mla_mask.reshape(2, 128, 256)
    maps = []
    for c in range(NCORE):
        m = dict(shared)
        pre = 7 - c
        x_all = np.zeros((NXT * 128, D), f32)
        x_all[pre * 128:pre * 128 + 16384] = xp
        pos = np.zeros((NXT * 128,), f32)
        valid = np.zeros((NXT * 128, 1), f32)
        pos[pre * 128:pre * 128 + 16384] = np.arange(16384, dtype=f32)
        valid[pre * 128:pre * 128 + 16384] = 1.0
        for b in range(2):
            r0 = (136 + b) * 128
            x_all[r0:r0 + 16] = xs[2 * c + b]
            pos[r0:r0 + 16] = 1024 + np.arange(16, dtype=f32)
            valid[r0:r0 + 16] = 1.0
        ang = pos[:, None] * inv_freq[None, :]
        m["x_all"] = x_all
        m["cs"] = np.concatenate([np.cos(ang), np.sin(ang)], axis=1).astype(f32)
        m["valid"] = np.ascontiguousarray(valid.reshape(NXT, 128).T)
        pm = np.zeros((896,), f32)
        pm[:pre * 128] = NEGBIG
        m["prefmask"] = np.ascontiguousarray(np.broadcast_to(pm, (128, 896)))
        sl = slice(2 * c, 2 * c + 2)
        m["c_akT"] = np.ascontiguousarray(ck[sl].transpose(0, 2, 3, 1))
        cve = np.ones((2, 1024, 16, 130), f32)
        cve[..., 0:128] = cv[sl]
        m["c_av"] = cve
        m["c_ixkT"] = np.ascontiguousarray(cix[sl].transpose(0, 2, 1))
        m["c_ckvT"] = np.ascontiguousarray(cckv[sl].transpose(0, 2, 1))
        m["c_krT"] = np.ascontiguousarray(ckr[sl].transpose(0, 2, 1))
        maps.append(m)
    return maps


def assemble(results):
    f32 = np.float32
    y_p = np.zeros((1, 16384, D), f32)
    y_s = np.zeros((16, 16, D), f32)
    a_k_p = np.zeros((1, 1, 16384, 16, 128), f32)
    a_v_p = np.zeros((1, 1, 16384, 16, 128), f32)
    a_ix_p = np.zeros((1, 1, 16384, 64), f32)
    b_ckv_p = np.zeros((1, 1, 16384, 256), f32)
    b_kr_p = np.zeros((1, 1, 16384, 64), f32)
    a_k_s = np.zeros((1, 16, 16, 16, 128), f32)
    a_v_s = np.zeros((1, 16, 16, 16, 128), f32)
    a_ix_s = np.zeros((1, 16, 16, 64), f32)
    b_ckv_s = np.zeros((1, 16, 16, 256), f32)
    b_kr_s = np.zeros((1, 16, 16, 64), f32)
    for c in range(NCORE):
        r = results[c]
        for i in range(16):
            j = c + 8 * i
            rs = slice(i * 128, (i + 1) * 128)
            ps = slice(j * 128, (j + 1) * 128)
            y_p[0, ps] = r["y_own"][rs]
            a_k_p[0, 0, ps] = r["st_ak"][rs].reshape(128, 16, 128)
            a_v_p[0, 0, ps] = r["st_av"][rs].reshape(128, 16, 128)
            a_ix_p[0, 0, ps] = r["st_ix"][rs]
            b_ckv_p[0, 0, ps] = r["st_ckv"][rs]
            b_kr_p[0, 0, ps] = r["st_kr"][rs]
        for b in range(2):
            rs = slice((16 + b) * 128, (16 + b) * 128 + 16)
            sq = 2 * c + b
            y_s[sq] = r["y_own"][rs]
            a_k_s[0, sq] = r["st_ak"][rs].reshape(16, 16, 128)
            a_v_s[0, sq] = r["st_av"][rs].reshape(16, 16, 128)
            a_ix_s[0, sq] = r["st_ix"][rs]
            b_ckv_s[0, sq] = r["st_ckv"][rs]
            b_kr_s[0, sq] = r["st_kr"][rs]
    return (y_p, y_s, a_k_p, a_v_p, a_ix_p, b_ckv_p, b_kr_p, a_k_s, a_v_s, a_ix_s, b_ckv_s, b_kr_s)


def kernel(**inputs):
    upto = os.environ.get("MK_UPTO", "ALL")
    bld = Builder(upto)
    nc = bld.build()
    maps = host_inputs(inputs)
    res = run_bass_kernel_spmd(nc, maps, core_ids=list(range(NCORE)))
    return assemble(res.results)
```

```python
import math
import os
from contextlib import ExitStack

import numpy as np
import concourse.bass as bass
import concourse.mybir as mybir
from concourse.bass_utils import run_bass_kernel_spmd

F32 = mybir.dt.float32
BF16 = mybir.dt.bfloat16
AF = mybir.ActivationFunctionType
ALU = mybir.AluOpType
AX = mybir.AxisListType

D = 2048
NH = 16
HD = 128
NCORE = 8
NPT = 136
NXT = 140
NTILE = 158
NTOK = NTILE * 128
NOWN = 20
EPS = 1e-6
A_SCALE = HD ** -0.5
MLA_SCALE = 192 ** -0.5
NEGBIG = -1.0e30
MASKNEG = -30000.0
TOPK = 256
BIS_R = 128.0
BIS_IT = 20
C_AQ, C_AK, C_AV, C_IXQ, C_IXK, C_IXW, C_CQ, C_CKV, C_KR, C_GA, C_GB = (
    0, 2048, 4096, 6144, 7168, 7232, 7248, 7760, 8016, 8080, 10128)


class Buf:
    __slots__ = ("name", "last_w", "readers", "dsem", "dcount", "excl")

    def __init__(self, name):
        self.excl = False
        self.name = name
        self.last_w = None
        self.readers = []
        self.dsem = None
        self.dcount = 0


class Ins:
    __slots__ = ("eng", "fn", "deps", "is_dma", "dbuf", "need_inc", "seq", "cover")

    def __init__(self, eng, fn, deps, is_dma=False, dbuf=None):
        self.eng = eng
        self.fn = fn
        self.deps = deps
        self.is_dma = is_dma
        self.dbuf = dbuf
        self.need_inc = False
        self.seq = 0
        self.cover = 0


COMPUTE = ("pe", "act", "dve", "pool")


class Prog:
    def __init__(self, nc, stack):
        self.nc = nc
        self.stack = stack
        self.engs = {"pe": nc.tensor, "act": nc.scalar, "dve": nc.vector, "pool": nc.gpsimd, "sp": nc.sync}
        self.batch = []
        self.esem = {e: stack.enter_context(nc.semaphore("e_" + e)) for e in COMPUTE}
        self.ecount = {e: 0 for e in COMPUTE}
        self.waited = {e: {} for e in self.engs}
        self.sempool = []
        self.livesems = {}
        self.nsem = 4
        self.n_ins = 0
        self.n_waits = 0
        self.trace = {e: [] for e in self.engs}

    def _getsem(self, buf):
        if self.sempool:
            sem, cnt = self.sempool.pop()
        else:
            self.nsem += 1
            sem = self.stack.enter_context(self.nc.semaphore("d%d" % self.nsem))
            cnt = 0
        buf.dsem = sem
        buf.dcount = cnt
        self.livesems[id(sem)] = [sem, cnt, cnt]

    def release(self, bufs):
        for b in bufs:
            if b.dsem is not None:
                ent = self.livesems.pop(id(b.dsem))
                self.sempool.append((ent[0], ent[1]))
                b.dsem = None

    def _deps(self, eng, reads, writes, is_dma):
        deps = {}

        def add(p, kind):
            if p is None:
                return
            if (not p.is_dma) and (not is_dma) and p.eng == eng:
                if eng == "pe" or kind != "raw":
                    return
            deps[id(p)] = p

        for b in reads:
            add(b.last_w, "raw")
            if b.excl:
                for r in b.readers:
                    if r.eng != eng:
                        add(r, "war")
        for b in writes:
            add(b.last_w, "waw")
            for r in b.readers:
                add(r, "war")
        return list(deps.values())

    def _post(self, ins, reads, writes):
        for b in reads:
            if not ins.is_dma:
                b.readers = [r for r in b.readers if r.is_dma or r.eng != ins.eng]
            b.readers.append(ins)
        for b in writes:
            b.last_w = ins
            b.readers = []
        self.batch.append(ins)

    def op(self, eng, fn, reads=(), writes=()):
        ins = Ins(eng, fn, self._deps(eng, reads, writes, False))
        self._post(ins, reads, writes)

    def dma(self, fn, reads, writes, dbuf, q="sp"):
        ins = Ins(q, fn, self._deps(q, reads, writes, True), True, dbuf)
        if dbuf.dsem is None:
            self._getsem(dbuf)
        dbuf.dcount += 1
        self.livesems[id(dbuf.dsem)][1] = dbuf.dcount
        self._post(ins, reads, writes)

    def flush(self):
        batch = self.batch
        self.batch = []
        for i in batch:
            for p in i.deps:
                if not p.is_dma:
                    p.need_inc = True
        last = {}
        for i in batch:
            if not i.is_dma:
                last[i.eng] = i
        for i in last.values():
            i.need_inc = True
        for i in batch:
            if not i.is_dma and i.need_inc:
                self.ecount[i.eng] += 1
                i.seq = self.ecount[i.eng]
        nxt = {}
        for i in reversed(batch):
            if not i.is_dma:
                if i.need_inc:
                    nxt[i.eng] = i.seq
                i.cover = nxt[i.eng]
        for i in batch:
            h = self.engs[i.eng]
            need = {}
            for p in i.deps:
                if p.is_dma:
                    ent = self.livesems.get(id(p.dbuf.dsem)) if p.dbuf.dsem is not None else None
                    if ent is None:
                        continue
                    key = id(ent[0])
                    sem = ent[0]
                    val = 16 * ent[2]
                else:
                    key = p.eng
                    sem = self.esem[p.eng]
                    val = p.cover
                if key not in need or need[key][1] < val:
                    need[key] = (sem, val)
            w = self.waited[i.eng]
            for key, (sem, val) in need.items():
                if w.get(key, 0) >= val:
                    continue
                w[key] = val
                h.wait_ge(sem, val)
                self.trace[i.eng].append(("w", id(sem), val))
                self.n_waits += 1
            bi = i.fn(h)
            self.n_ins += 1
            if i.is_dma:
                bi.then_inc(i.dbuf.dsem, 16)
                self.trace[i.eng].append(("i", id(i.dbuf.dsem), 16))
                self.livesems[id(i.dbuf.dsem)][2] += 1
            elif i.need_inc:
                bi.then_inc(self.esem[i.eng], 1)
                self.trace[i.eng].append(("i", id(self.esem[i.eng]), 1))
            else:
                self.trace[i.eng].append(("i", None, 0))

    def barrier(self):
        self.flush()
        for e, h in self.engs.items():
            w = self.waited[e]
            for pe in COMPUTE:
                if pe == e:
                    continue
                val = self.ecount[pe]
                if val > 0 and w.get(pe, 0) < val:
                    w[pe] = val
                    h.wait_ge(self.esem[pe], val)
                    self.trace[e].append(("w", id(self.esem[pe]), val))
            for key, ent in self.livesems.items():
                val = 16 * ent[2]
                if val > 0 and w.get(key, 0) < val:
                    w[key] = val
                    h.wait_ge(ent[0], val)
                    self.trace[e].append(("w", id(ent[0]), val))

    def finish(self):
        self.barrier()


class TB:
    def __init__(self, t, name):
        self.t = t
        self.b = Buf(name)


class Rot:
    def __init__(self, items):
        self.items = items
        self.i = 0

    def next(self):
        x = self.items[self.i % len(self.items)]
        self.i += 1
        return x


def weight_groups():
    g = {}
    g["ak"] = dict(K=2048, blocks=[[("w_in", 0, C_AK + 512 * i, 512)] for i in range(4)])
    g["av"] = dict(K=2048, blocks=[[("w_in", 0, C_AV + 512 * i, 512)] for i in range(4)])
    g["misc"] = dict(K=2048, blocks=[[("w_in", 0, C_CKV, 256), ("w_in", 0, C_IXK, 64), ("w_in", 0, C_KR, 64)]])
    g["aq"] = dict(K=2048, blocks=[[("w_in", 0, C_AQ + 512 * i, 512)] for i in range(4)])
    g["ixq"] = dict(K=2048, blocks=[[("w_in", 0, C_IXQ + 512 * i, 512)] for i in range(2)])
    g["cq"] = dict(K=2048, blocks=[[("w_in", 0, C_CQ, 512)], [("w_in", 0, C_IXW, 16)]])
    g["gate"] = dict(K=2048, blocks=[[("w_in", 0, C_GA + 512 * i, 512)] for i in range(8)])
    g["wo"] = dict(K=2048, blocks=[[("w_out", 0, 512 * i, 512)] for i in range(4)])
    g["up"] = dict(K=2048, blocks=[[("w_ff_up", 0, 512 * i, 512)] for i in range(16)])
    g["dn"] = dict(K=2048, blocks=[[("w_ff_down", 2048 * kg, 512 * cb, 512)] for cb in range(4) for kg in range(4)])
    g["uqn"] = dict(K=512, blocks=[[("uqn", 0, i, 0)] for i in range(4)])
    g["uqr"] = dict(K=512, blocks=[[("uqr", 0, i, 0)] for i in range(2)])
    g["uk"] = dict(K=256, blocks=[[("w_uk", 0, 512 * i, 512)] for i in range(4)])
    g["uv"] = dict(K=256, blocks=[[("w_uv", 0, 512 * i, 512)] for i in range(4)])
    first = ["uk", "uv", "misc", "ak", "av"]
    return {k: g[k] for k in first + [k for k in g if k not in first]}


class _Stop(Exception):
    pass


class Builder:
    def ck(self, n):
        if int(os.environ.get("MK_KSTOP", "0")) == n:
            raise _Stop()

    def __init__(self, upto="ALL"):
        self.upto = upto
        self.nc = bass.Bass("TRN2", target_bir_lowering=False)
        self.gstack = ExitStack()
        self.P = Prog(self.nc, self.gstack)
        self.inp = {}
        self.out = {}
        self.scr = {}

    def din(self, name, shape, dt=F32):
        self.inp[name] = self.nc.dram_tensor(name, list(shape), dt, kind="ExternalInput").ap()
        return self.inp[name]

    def dout(self, name, shape, dt=F32):
        self.out[name] = self.nc.dram_tensor(name, list(shape), dt, kind="ExternalOutput").ap()
        return self.out[name]

    def dscr(self, name, shape, dt=BF16):
        self.scr[name] = TB(self.nc.dram_tensor(name, list(shape), dt, kind="Internal").ap(), name)
        return self.scr[name]

    def sb(self, st, name, shape, dt=F32):
        self._uid = getattr(self, "_uid", 0) + 1
        name = "%s_u%d" % (name, self._uid)
        return TB(st.enter_context(self.nc.sbuf_tensor(name, list(shape), dt)), name)

    def mm(self, ps, out_ap, lhsT, rhs, start, stop, reads):
        self.P.op("pe", lambda e: e.matmul(out_ap, lhsT, rhs, start=start, stop=stop), reads, [ps.b])

    def load(self, dst, dst_ap, src_ap, reads=(), q="sp"):
        self.P.dma(lambda e: e.dma_start(out=dst_ap, in_=src_ap), list(reads), [dst.b], dst.b, q=q)

    def store(self, src, dst_ap, src_ap, dstbuf=None, q="pool"):
        w = [dstbuf.b] if dstbuf is not None else []
        self.P.dma(lambda e: e.dma_start(out=dst_ap, in_=src_ap), [src.b], w, src.b, q=q)

    def declare(self):
        din, dout, dscr = self.din, self.dout, self.dscr
        din("x_all", [NXT * 128, D])
        din("cs", [NXT * 128, 64])
        din("valid", [128, NXT])
        din("prefmask", [128, 896])
        din("ident", [128, 128])
        din("onehot", [2, 32, 128 * 128])
        din("relb", [32, 16])
        din("constb", [128, 16])
        din("dsa_diag", [2, 128, 128])
        din("mla_mask", [2, 128, 256])
        din("c_akT", [2, 16, 128, 1024])
        din("c_av", [2, 1024, 16, 130])
        din("c_ixkT", [2, 64, 1024])
        din("c_ckvT", [2, 256, 1024])
        din("c_krT", [2, 64, 1024])
        din("w_in", [D, 12176])
        din("w_uq", [512, 16, 192])
        din("w_uk", [256, 2048])
        din("w_uv", [256, 2048])
        din("w_out", [D, D])
        din("w_ff_up", [D, 8192])
        din("w_ff_down", [8192, D])
        din("g_mix", [128, D])
        din("g_ffn", [128, D])
        din("g_fin", [128, D])
        din("g_q", [128, 512])
        din("g_kv", [128, 256])
        dout("y_own", [NOWN * 128, D])
        dout("st_ak", [NOWN * 128, D])
        dout("st_av", [NOWN * 128, D])
        dout("st_ix", [NOWN * 128, 64])
        dout("st_ckv", [NOWN * 128, 256])
        dout("st_kr", [NOWN * 128, 64])
        dscr("KT_A", [NH, 128, NTOK])
        dscr("V_A", [NH, 128, NTILE, 130])
        dscr("IXK_T", [64, NTOK])
        dscr("KT_B", [NH, 128, NTOK])
        dscr("KR_T", [64, NTOK])
        dscr("V_B", [NH, 128, NTILE, 130])
        dscr("QA_T", [5, 128, 16, 512])
        dscr("IXQ_T", [5, 128, 8, 512])
        dscr("IXW", [5, 128, 4, 16], F32)
        dscr("QN_T", [5, 128, 16, 512])
        dscr("QR_T", [5, 128, 8, 512])
        dscr("OA", [NOWN, 128, D], F32)
        dscr("OB", [NOWN, 128, D], F32)
        self.wg = weight_groups()
        for name, g in self.wg.items():
            dscr("W_" + name, [len(g["blocks"]), 128, g["K"] // 128, 512])

    def phase_W(self):
        P = self.P
        inp = self.inp
        for name, g in self.wg.items():
            W = self.scr["W_" + name]
            KC = g["K"] // 128
            for bi, pieces in enumerate(g["blocks"]):
                off = 0
                for (src, r0, c0, ncols) in pieces:
                    if src in ("uqn", "uqr"):
                        i = c0
                        nh_, w_, lo_ = (4, 128, 0) if src == "uqn" else (8, 64, 128)
                        for j in range(nh_):
                            src_ap = inp["w_uq"][:, nh_ * i + j, lo_:lo_ + w_].rearrange("(kc p) d -> p kc d", p=128)
                            dst_ap = W.t[bi][:, :, j * w_:(j + 1) * w_]
                            P.dma(lambda e, d=dst_ap, s=src_ap: e.dma_start(out=d, in_=s), [], [W.b], W.b, q="pool")
                        continue
                    src_ap = inp[src][r0:r0 + g["K"], c0:c0 + ncols].rearrange("(kc p) c -> p kc c", p=128)
                    dst_ap = W.t[bi][:, :, off:off + ncols]
                    off += ncols
                    P.dma(lambda e, d=dst_ap, s=src_ap: e.dma_start(out=d, in_=s), [], [W.b], W.b, q="pool")
        KT_A, V_A, IXK_T, KR_T = self.scr["KT_A"], self.scr["V_A"], self.scr["IXK_T"], self.scr["KR_T"]
        for b in range(2):
            t0 = 140 + 9 * b
            for h in range(NH):
                P.dma(lambda e, b=b, h=h, t0=t0: e.dma_start(out=KT_A.t[h, :, t0 * 128:t0 * 128 + 1024], in_=inp["c_akT"][b, h]),
                      [], [KT_A.b], KT_A.b, q="pool")
                P.dma(lambda e, b=b, h=h, t0=t0: e.dma_start(
                    out=V_A.t[h, :, t0:t0 + 8, :],
                    in_=inp["c_av"][b, :, h, :].rearrange("(kt p) d -> p kt d", p=128)), [], [V_A.b], V_A.b, q="pool")
            P.dma(lambda e, b=b, t0=t0: e.dma_start(out=IXK_T.t[:, t0 * 128:t0 * 128 + 1024], in_=inp["c_ixkT"][b]),
                  [], [IXK_T.b], IXK_T.b, q="pool")
            P.dma(lambda e, b=b, t0=t0: e.dma_start(out=KR_T.t[:, t0 * 128:t0 * 128 + 1024], in_=inp["c_krT"][b]),
                  [], [KR_T.b], KR_T.b, q="pool")

    def norm_rows(self, x_ap, n, g_ap, out_ap, xb, gb, outb, tmp):
        P = self.P
        junk, ss, sd, rstd = tmp["junk"], tmp["ss"], tmp["sd"], tmp["rstd"]
        P.op("act", lambda e: e.activation(out=junk.t[:, 0:n], in_=x_ap, func=AF.Square, accum_out=ss.t[:]),
             [xb], [junk.b, ss.b])
        P.op("act", lambda e: e.activation(out=sd.t[:], in_=ss.t[:], func=AF.Sqrt, bias=EPS, scale=1.0 / n),
             [ss.b], [sd.b])
        P.op("dve", lambda e: e.reciprocal(rstd.t[:], sd.t[:]), [sd.b], [rstd.b])
        P.op("dve", lambda e: e.scalar_tensor_tensor(out=out_ap, in0=x_ap, scalar=rstd.t[:], in1=g_ap,
                                                     op0=ALU.mult, op1=ALU.mult), [xb, rstd.b, gb], [outb])

    def load_norm_T(self, st, tiles, g_tb, hT, xts, hb, tmp, psT, identb, keep_x=None):
        P = self.P
        xall = self.inp["x_all"]
        for j, tile in enumerate(tiles):
            xt = keep_x[j] if keep_x is not None else xts.next()
            self.load(xt, xt.t[:], xall[tile * 128:(tile + 1) * 128, :])
            self.norm_rows(xt.t[:], D, g_tb.t[:], hb.t[:], xt.b, g_tb.b, hb.b, tmp)
            self.transpose_into(hb, hb.t, 16, hT, lambda kc, j=j: hT.t[:, kc, j * 128:(j + 1) * 128], psT, identb)

    def transpose_into(self, src, src_t, nchunks, dst, dst_ap_fn, psT, identb, rows=128):
        P = self.P
        c = 0
        while c < nchunks:
            n = min(4, nchunks - c)
            ps = psT.next()
            for k in range(n):
                self.mm(ps, ps.t[:, k * 128:(k + 1) * 128], src_t[:, (c + k) * 128:(c + k + 1) * 128], identb.t[:],
                        True, True, [src.b, identb.b])
            self._tflip = not getattr(self, "_tflip", False)
            for k in range(n):
                eng = "act" if self._tflip else "dve"
                d_ap = dst_ap_fn(c + k)
                s_ap = ps.t[:, k * 128:(k + 1) * 128]
                if eng == "act":
                    P.op("act", lambda e, d=d_ap, s=s_ap: e.activation(out=d, in_=s, func=AF.Copy), [ps.b], [dst.b])
                else:
                    P.op("dve", lambda e, d=d_ap, s=s_ap: e.tensor_copy(d, s), [ps.b], [dst.b])
            c += n

    def rope(self, out_ap1, out_ap2, x1, x2, cos, sin, xb, csb, outb, tmps):
        P = self.P
        t1, t2 = tmps
        P.op("dve", lambda e: e.tensor_tensor(t1.t_ap, x1, cos, ALU.mult), [xb, csb], [t1.b])
        P.op("dve", lambda e: e.tensor_tensor(t2.t_ap, x2, sin, ALU.mult), [xb, csb], [t2.b])
        P.op("dve", lambda e: e.tensor_tensor(out_ap1, t1.t_ap, t2.t_ap, ALU.subtract), [t1.b, t2.b], [outb])
        P.op("dve", lambda e: e.tensor_tensor(t1.t_ap, x1, sin, ALU.mult), [xb, csb, outb], [t1.b])
        P.op("dve", lambda e: e.tensor_tensor(t2.t_ap, x2, cos, ALU.mult), [xb, csb, outb], [t2.b])
        P.op("dve", lambda e: e.tensor_tensor(out_ap2, t1.t_ap, t2.t_ap, ALU.add), [t1.b, t2.b], [outb])

    def phase_K(self, blocks):
        P = self.P
        nc = self.nc
        inp, out, scr = self.inp, self.out, self.scr
        with ExitStack() as st:
            sb = lambda name, shape, dt=F32: self.sb(st, name, shape, dt)
            g_mix = sb("k_gmix", [128, D])
            g_kv = sb("k_gkv", [128, 256])
            identf = sb("k_identf", [128, 128])
            identb = sb("k_identb", [128, 128], BF16)
            ones16 = sb("k_ones16", [128, 16, 1])
            wuk = [sb("k_wuk%d" % i, [128, 2, 512], BF16) for i in range(4)]
            wuv = [sb("k_wuv%d" % i, [128, 2, 512], BF16) for i in range(4)]
            xts = Rot([sb("k_x%d" % i, [128, D]) for i in range(2)])
            hb = sb("k_hb", [128, D], BF16)
            hT = sb("k_hT", [128, 16, 512], BF16)
            wts = Rot([sb("k_wt%d" % i, [128, 16, 512], BF16) for i in range(2)])
            ksts = Rot([sb("k_kst%d" % i, [128, 512], BF16) for i in range(2)])
            vst = [sb("k_vst%d" % i, [128, 16, 130], BF16) for i in range(4)]
            vbst = [sb("k_vbst%d" % i, [128, 16, 130], BF16) for i in range(4)]
            mst = [sb("k_mst%d" % i, [128, 384]) for i in range(4)]
            mo = [sb("k_mo%d" % i, [128, 384]) for i in range(4)]
            km = sb("k_km", [128, 384], BF16)
            ckvT = sb("k_ckvT", [128, 2, 512], BF16)
            ixkrT = sb("k_ixkrT", [128, 512], BF16)
            osts = Rot([sb("k_ost%d" % i, [128, 512]) for i in range(3)])
            cst = sb("k_cs", [128, 4, 64])
            vld = sb("k_vld", [128, 4])
            tmp = dict(junk=sb("k_junk", [128, D], BF16), ss=sb("k_ss", [128, 1]), sd=sb("k_sd", [128, 1]),
                       rstd=sb("k_rstd", [128, 1]))
            rt1 = sb("k_rt1", [128, 32])
            rt2 = sb("k_rt2", [128, 32])
            rt1.t_ap = rt1.t[:]
            rt2.t_ap = rt2.t[:]
            PS = self.PS
            psT = Rot([PS[0], PS[1]])
            psM = Rot([PS[2], PS[3], PS[4], PS[5]])
            psX = Rot([PS[6], PS[7]])

            self.load(g_mix, g_mix.t[:], inp["g_mix"][:, :])
            self.load(g_kv, g_kv.t[:], inp["g_kv"][:, :])
            self.load(identf, identf.t[:], inp["ident"][:, :])
            P.op("dve", lambda e: e.tensor_copy(identb.t[:], identf.t[:]), [identf.b], [identb.b])
            P.op("dve", lambda e: e.memset(ones16.t[:], 1.0), [], [ones16.b])
            for i in range(4):
                self.load(wuk[i], wuk[i].t[:], scr["W_uk"].t[i], reads=[scr["W_uk"].b])
                self.load(wuv[i], wuv[i].t[:], scr["W_uv"].t[i], reads=[scr["W_uv"].b])

            for kb in blocks:
              try:
                tiles = [4 * kb + j for j in range(4)]
                tok0 = 4 * kb * 128
                if kb == 34:
                    own = {0: 16, 1: 17, 2: 18, 3: 19}
                elif kb % 2 == 1:
                    own = {3: (kb - 1) // 2}
                else:
                    own = {}
                self.load(cst, cst.t[:], inp["cs"][tok0:tok0 + 512, :].rearrange("(j p) c -> p j c", p=128))
                self.load(vld, vld.t[:], inp["valid"][:, 4 * kb:4 * kb + 4])
                self.ck(1)
                self.load_norm_T(st, tiles, g_mix, hT, xts, hb, tmp, psT, identb)
                self.ck(2)
                for cb in range(4):
                    wt = wts.next()
                    self.load(wt, wt.t[:], scr["W_ak"].t[cb], reads=[scr["W_ak"].b])
                    for j in range(4):
                        head = 4 * cb + j
                        ps = psM.next()
                        for kc in range(16):
                            self.mm(ps, ps.t[:, :], wt.t[:, kc, j * 128:(j + 1) * 128], hT.t[:, kc, :], kc == 0, kc == 15,
                                    [wt.b, hT.b])
                        kst = ksts.next()
                        P.op("act", lambda e, kst=kst, ps=ps: e.activation(out=kst.t[:], in_=ps.t[:, :], func=AF.Copy),
                             [ps.b], [kst.b])
                        self.store_cols(kst, scr["KT_A"], lambda a, n, head=head: scr["KT_A"].t[head, :, a:a + n], kst.t, tiles)
                    for pos, o in own.items():
                        ps = psM.next()
                        for kc in range(16):
                            self.mm(ps, ps.t[:, :], hT.t[:, kc, pos * 128:(pos + 1) * 128], wt.t[:, kc, :], kc == 0, kc == 15,
                                    [wt.b, hT.b])
                        ost = osts.next()
                        P.op("act", lambda e, ost=ost, ps=ps: e.activation(out=ost.t[:], in_=ps.t[:, :], func=AF.Copy),
                             [ps.b], [ost.b])
                        self.store(ost, out["st_ak"][o * 128:(o + 1) * 128, cb * 512:(cb + 1) * 512], ost.t[:])
                self.ck(3)
                for cb in range(4):
                    wt = wts.next()
                    self.load(wt, wt.t[:], scr["W_av"].t[cb], reads=[scr["W_av"].b])
                    for pos in range(4):
                        ps = psM.next()
                        for kc in range(16):
                            self.mm(ps, ps.t[:, :], hT.t[:, kc, pos * 128:(pos + 1) * 128], wt.t[:, kc, :], kc == 0, kc == 15,
                                    [wt.b, hT.b])
                        v = vst[pos]
                        P.op("dve", lambda e, v=v, ps=ps, cb=cb: e.tensor_copy(
                            v.t[:, 4 * cb:4 * cb + 4, 0:128], ps.t[:, :].rearrange("p (h d) -> p h d", h=4)), [ps.b], [v.b])
                        if pos in own:
                            o = own[pos]
                            ost = osts.next()
                            P.op("dve", lambda e, ost=ost, ps=ps: e.tensor_copy(ost.t[:], ps.t[:, :]), [ps.b], [ost.b])
                            self.store(ost, out["st_av"][o * 128:(o + 1) * 128, cb * 512:(cb + 1) * 512], ost.t[:])
                for pos in range(4):
                    v = vst[pos]
                    P.op("dve", lambda e, v=v, pos=pos: e.tensor_scalar(v.t[:, :, 128:129], ones16.t[:], vld.t[:, pos:pos + 1], None,
                                                                        ALU.mult), [ones16.b, vld.b], [v.b])
                    self.store(v, scr["V_A"].t[:, :, self.tmap(tiles[pos]), 0:129].rearrange("h p c -> p h c"), v.t[:, :, 0:129],
                               scr["V_A"])
                self.ck(4)
                wt = wts.next()
                self.load(wt, wt.t[:, :, 0:384], scr["W_misc"].t[0][:, :, 0:384], reads=[scr["W_misc"].b])
                for pos in range(4):
                    ps = psX.next()
                    for kc in range(16):
                        self.mm(ps, ps.t[:, 0:384], hT.t[:, kc, pos * 128:(pos + 1) * 128], wt.t[:, kc, 0:384], kc == 0, kc == 15,
                                [wt.b, hT.b])
                    m = mst[pos]
                    P.op("act", lambda e, m=m, ps=ps: e.activation(out=m.t[:], in_=ps.t[:, 0:384], func=AF.Copy), [ps.b], [m.b])
                self.ck(41)
                for pos in range(4):
                    m = mst[pos]
                    o_ = mo[pos]
                    self.norm_rows(m.t[:, 0:256], 256, g_kv.t[:], o_.t[:, 0:256], m.b, g_kv.b, o_.b, tmp)
                    self.ck(42)
                    P.op("act", lambda e, m=m, o_=o_: e.activation(out=o_.t[:, 256:320], in_=m.t[:, 256:320], func=AF.Copy),
                         [m.b], [o_.b])
                    self.rope(o_.t[:, 320:352], o_.t[:, 352:384], m.t[:, 320:352], m.t[:, 352:384],
                              cst.t[:, pos, 0:32], cst.t[:, pos, 32:64], m.b, cst.b, o_.b, (rt1, rt2))
                    self.ck(43)
                    if pos in own:
                        o = own[pos]
                        self.store(o_, out["st_ckv"][o * 128:(o + 1) * 128, :], o_.t[:, 0:256])
                        self.store(o_, out["st_ix"][o * 128:(o + 1) * 128, :], o_.t[:, 256:320])
                        self.store(o_, out["st_kr"][o * 128:(o + 1) * 128, :], o_.t[:, 320:384])
                    P.op("act", lambda e, o_=o_: e.activation(out=km.t[:], in_=o_.t[:], func=AF.Copy), [o_.b], [km.b])
                    self.ck(431)
                    ps = psX.next()
                    for k in range(3):
                        self.mm(ps, ps.t[:, k * 128:(k + 1) * 128], km.t[:, k * 128:(k + 1) * 128], identb.t[:], True, True,
                                [km.b, identb.b])
                    self.ck(432)
                    P.op("dve", lambda e, ps=ps, pos=pos: e.tensor_copy(
                        ckvT.t[:, :, pos * 128:(pos + 1) * 128], ps.t[:, 0:256].rearrange("p (c t) -> p c t", c=2)),
                        [ps.b], [ckvT.b])
                    self.ck(4321)
                    P.op("dve", lambda e, ps=ps, pos=pos: e.tensor_copy(ixkrT.t[:, pos * 128:(pos + 1) * 128], ps.t[:, 256:384]),
                         [ps.b], [ixkrT.b])
                    self.ck(433)
                self.ck(44)
                self.store_cols(ixkrT, scr["IXK_T"], lambda a, n: scr["IXK_T"].t[:, a:a + n], ixkrT.t[0:64], tiles)
                self.store_cols(ixkrT, scr["KR_T"], lambda a, n: scr["KR_T"].t[:, a:a + n], ixkrT.t[64:128], tiles)
                self.ck(5)
                if int(os.environ.get("MK_KSTOP", "0")) == 6:
                    self.mla_kside(ckvT, wuk, wuv, psM, ksts, vbst, tiles, tok0, lambda pos: vld.t[:, pos:pos + 1], vld, ones16)
                self.ck(6)
                self.mla_kside(ckvT, wuk, wuv, psM, ksts, vbst, tiles, tok0, lambda pos: vld.t[:, pos:pos + 1], vld, ones16)
              except _Stop:
                break
            P.barrier()
            self.release_scope(locals())

    def mla_kside(self, ckvT, wuk, wuv, psM, ksts, vbst, tiles, tok0, vld_ap_fn, vld, ones16, ncols=512):
        P = self.P
        scr = self.scr
        for h in range(NH):
            ps = psM.next()
            for cc in range(2):
                self.mm(ps, ps.t[:, 0:ncols], wuk[h // 4].t[:, cc, (h % 4) * 128:(h % 4 + 1) * 128], ckvT.t[:, cc, 0:ncols],
                        cc == 0, cc == 1, [wuk[h // 4].b, ckvT.b])
            kst = ksts.next()
            P.op("act", lambda e, kst=kst, ps=ps: e.activation(out=kst.t[:, 0:ncols], in_=ps.t[:, 0:ncols], func=AF.Copy),
                 [ps.b], [kst.b])
            self.store_cols(kst, scr["KT_B"], lambda a, n, h=h: scr["KT_B"].t[h, :, a:a + n], kst.t, tiles)
        for pos in range(len(tiles)):
            v = vbst[pos % len(vbst)]
            for cb in range(4):
                ps = psM.next()
                for cc in range(2):
                    self.mm(ps, ps.t[:, :], ckvT.t[:, cc, pos * 128:(pos + 1) * 128], wuv[cb].t[:, cc, :], cc == 0, cc == 1,
                            [wuv[cb].b, ckvT.b])
                P.op("dve", lambda e, v=v, ps=ps, cb=cb: e.tensor_copy(
                    v.t[:, 4 * cb:4 * cb + 4, 0:128], ps.t[:, :].rearrange("p (h d) -> p h d", h=4)), [ps.b], [v.b])
            if vld is not None:
                P.op("dve", lambda e, v=v, pos=pos: e.tensor_scalar(v.t[:, :, 128:129], ones16.t[:], vld_ap_fn(pos), None,
                                                                    ALU.mult), [ones16.b, vld.b], [v.b])
                self.store(v, scr["V_B"].t[:, :, self.tmap(tiles[pos]), 0:129].rearrange("h p c -> p h c"), v.t[:, :, 0:129], scr["V_B"])
            else:
                self.store(v, scr["V_B"].t[:, :, self.tmap(tiles[pos]), 0:129].rearrange("h p c -> p h c"), v.t[:, :, 0:129], scr["V_B"])

    def release_scope(self, loc):
        bufs = []
        for v in loc.values():
            if isinstance(v, TB):
                bufs.append(v.b)
            elif isinstance(v, (list, tuple)):
                for x in v:
                    if isinstance(x, TB):
                        bufs.append(x.b)
            elif isinstance(v, Rot):
                for x in v.items:
                    if isinstance(x, TB):
                        bufs.append(x.b)
            elif isinstance(v, dict):
                for x in v.values():
                    if isinstance(x, TB):
                        bufs.append(x.b)
        self.P.release(bufs)

    def phase_KC(self):
        P = self.P
        inp, scr = self.inp, self.scr
        with ExitStack() as st:
            sb = lambda name, shape, dt=F32: self.sb(st, name, shape, dt)
            wuk = [sb("c_wuk%d" % i, [128, 2, 512], BF16) for i in range(4)]
            wuv = [sb("c_wuv%d" % i, [128, 2, 512], BF16) for i in range(4)]
            ckvT = sb("c_ckvT", [128, 2, 512], BF16)
            ksts = Rot([sb("c_kst%d" % i, [128, 512], BF16) for i in range(2)])
            vbst = [sb("c_vbst%d" % i, [128, 16, 130], BF16) for i in range(4)]
            PS = self.PS
            psM = Rot([PS[2], PS[3], PS[4], PS[5]])
            for i in range(4):
                self.load(wuk[i], wuk[i].t[:], scr["W_uk"].t[i], reads=[scr["W_uk"].b])
                self.load(wuv[i], wuv[i].t[:], scr["W_uv"].t[i], reads=[scr["W_uv"].b])
                P.op("dve", lambda e, v=vbst[i]: e.memset(v.t[:, :, 128:130], 1.0), [], [vbst[i].b])
            for b in range(2):
                for half in range(2):
                    t0 = 140 + 9 * b + 4 * half
                    self.load(ckvT, ckvT.t[:], inp["c_ckvT"][b, :, half * 512:(half + 1) * 512].rearrange("(c p) s -> p c s", p=128),
                              q="pool")
                    self.mla_kside(ckvT, wuk, wuv, psM, ksts, vbst, [t0 + j for j in range(4)], t0 * 128, None, None, None)
            P.barrier()
            self.release_scope(locals())


    @staticmethod
    def tmap(t):
        return {136: 148, 137: 157}.get(t, t)

    def store_cols(self, src, dst, dst_row_ap_fn, src_t, tiles):
        mapped = [self.tmap(t) for t in tiles]
        if all(mapped[i] == mapped[0] + i for i in range(len(mapped))):
            self.store(src, dst_row_ap_fn(mapped[0] * 128, len(mapped) * 128), src_t[:, 0:len(mapped) * 128], dst)
        else:
            for i, mt in enumerate(mapped):
                self.store(src, dst_row_ap_fn(mt * 128, 128), src_t[:, i * 128:(i + 1) * 128], dst)

    def own_tiles(self, b):
        return [136 + j for j in range(4)] if b == 4 else [8 * (4 * b + j) + 7 for j in range(4)]

    def phase_Q(self):
        P = self.P
        inp, scr = self.inp, self.scr
        with ExitStack() as st:
            sb = lambda name, shape, dt=F32: self.sb(st, name, shape, dt)
            g_mix = sb("q_gmix", [128, D])
            g_q = sb("q_gq", [128, 512])
            identf = sb("q_identf", [128, 128])
            identb = sb("q_identb", [128, 128], BF16)
            xts = Rot([sb("q_x%d" % i, [128, D]) for i in range(2)])
            hb = sb("q_hb", [128, D], BF16)
            hT = sb("q_hT", [128, 16, 512], BF16)
            wts = Rot([sb("q_wt%d" % i, [128, 16, 512], BF16) for i in range(2)])
            qsts = Rot([sb("q_qst%d" % i, [128, 512], BF16) for i in range(3)])
            wuqn = [sb("q_wuqn%d" % i, [128, 4, 512], BF16) for i in range(4)]
            wuqr = [sb("q_wuqr%d" % i, [128, 4, 512], BF16) for i in range(2)]
            cq_f = sb("q_cqf", [128, 512])
            cqb = sb("q_cqb", [128, 512], BF16)
            cqT = sb("q_cqT", [128, 4, 512], BF16)
            qr_f = sb("q_qrf", [128, 1024])
            qr_o = sb("q_qro", [128, 1024])
            qrb = sb("q_qrb", [128, 1024], BF16)
            qrT = sb("q_qrT", [128, 8, 512], BF16)
            ixw = sb("q_ixw", [128, 4, 16])
            cst = sb("q_cs", [128, 4, 64])
            tmp = dict(junk=sb("q_junk", [128, D], BF16), ss=sb("q_ss", [128, 1]), sd=sb("q_sd", [128, 1]),
                       rstd=sb("q_rstd", [128, 1]))
            rt1 = sb("q_rt1", [128, 32])
            rt2 = sb("q_rt2", [128, 32])
            rt1.t_ap = rt1.t[:]
            rt2.t_ap = rt2.t[:]
            PS = self.PS
            psT = Rot([PS[0], PS[1]])
            psM = Rot([PS[2], PS[3], PS[4], PS[5]])
            psX = Rot([PS[6], PS[7]])
            self.load(g_mix, g_mix.t[:], inp["g_mix"][:, :])
            self.load(g_q, g_q.t[:], inp["g_q"][:, :])
            self.load(identf, identf.t[:], inp["ident"][:, :])
            P.op("dve", lambda e: e.tensor_copy(identb.t[:], identf.t[:]), [identf.b], [identb.b])
            for i in range(4):
                self.load(wuqn[i], wuqn[i].t[:], scr["W_uqn"].t[i], reads=[scr["W_uqn"].b])
            for i in range(2):
                self.load(wuqr[i], wuqr[i].t[:], scr["W_uqr"].t[i], reads=[scr["W_uqr"].b])
            for b in range(5):
                tiles = self.own_tiles(b)
                for j, tile in enumerate(tiles):
                    self.load(cst, cst.t[:, j, :], inp["cs"][tile * 128:(tile + 1) * 128, :])
                self.load_norm_T(st, tiles, g_mix, hT, xts, hb, tmp, psT, identb)
                for grp, nblk, dst in (("aq", 4, "QA_T"), ("ixq", 2, "IXQ_T")):
                    for cb in range(nblk):
                        wt = wts.next()
                        self.load(wt, wt.t[:], scr["W_" + grp].t[cb], reads=[scr["W_" + grp].b])
                        for j in range(4):
                            ps = psM.next()
                            for kc in range(16):
                                self.mm(ps, ps.t[:, :], wt.t[:, kc, j * 128:(j + 1) * 128], hT.t[:, kc, :], kc == 0, kc == 15,
                                        [wt.b, hT.b])
                            q = qsts.next()
                            P.op("act", lambda e, q=q, ps=ps: e.activation(out=q.t[:], in_=ps.t[:, :], func=AF.Copy), [ps.b], [q.b])
                            self.store(q, scr[dst].t[b][:, 4 * cb + j, :], q.t[:], scr[dst])
                wt = wts.next()
                self.load(wt, wt.t[:], scr["W_cq"].t[0], reads=[scr["W_cq"].b])
                for pos in range(4):
                    ps = psM.next()
                    for kc in range(16):
                        self.mm(ps, ps.t[:, :], hT.t[:, kc, pos * 128:(pos + 1) * 128], wt.t[:, kc, :], kc == 0, kc == 15, [wt.b, hT.b])
                    P.op("act", lambda e, ps=ps: e.activation(out=cq_f.t[:], in_=ps.t[:, :], func=AF.Copy), [ps.b], [cq_f.b])
                    self.norm_rows(cq_f.t[:], 512, g_q.t[:], cqb.t[:], cq_f.b, g_q.b, cqb.b, tmp)
                    self.transpose_into(cqb, cqb.t, 4, cqT, lambda c, pos=pos: cqT.t[:, c, pos * 128:(pos + 1) * 128], psT, identb)
                wt = wts.next()
                self.load(wt, wt.t[:, :, 0:16], scr["W_cq"].t[1][:, :, 0:16], reads=[scr["W_cq"].b])
                for pos in range(4):
                    ps = psX.next()
                    for kc in range(16):
                        self.mm(ps, ps.t[:, 0:16], hT.t[:, kc, pos * 128:(pos + 1) * 128], wt.t[:, kc, 0:16], kc == 0, kc == 15,
                                [wt.b, hT.b])
                    P.op("act", lambda e, ps=ps, pos=pos: e.activation(out=ixw.t[:, pos, :], in_=ps.t[:, 0:16], func=AF.Copy, scale=0.25),
                         [ps.b], [ixw.b])
                self.store(ixw, scr["IXW"].t[b], ixw.t[:], scr["IXW"])
                for h in range(NH):
                    ps = psM.next()
                    for cc in range(4):
                        self.mm(ps, ps.t[:, :], wuqn[h // 4].t[:, cc, (h % 4) * 128:(h % 4 + 1) * 128], cqT.t[:, cc, :], cc == 0, cc == 3,
                                [wuqn[h // 4].b, cqT.b])
                    q = qsts.next()
                    P.op("act", lambda e, q=q, ps=ps: e.activation(out=q.t[:], in_=ps.t[:, :], func=AF.Copy), [ps.b], [q.b])
                    self.store(q, scr["QN_T"].t[b][:, h, :], q.t[:], scr["QN_T"])
                for pos in range(4):
                    for blk in range(2):
                        ps = psM.next()
                        for cc in range(4):
                            self.mm(ps, ps.t[:, :], cqT.t[:, cc, pos * 128:(pos + 1) * 128], wuqr[blk].t[:, cc, :], cc == 0, cc == 3,
                                    [wuqr[blk].b, cqT.b])
                        P.op("act", lambda e, ps=ps, blk=blk: e.activation(out=qr_f.t[:, blk * 512:(blk + 1) * 512], in_=ps.t[:, :],
                                                                           func=AF.Copy), [ps.b], [qr_f.b])
                    for h in range(NH):
                        c0 = h * 64
                        self.rope(qr_o.t[:, c0:c0 + 32], qr_o.t[:, c0 + 32:c0 + 64], qr_f.t[:, c0:c0 + 32], qr_f.t[:, c0 + 32:c0 + 64],
                                  cst.t[:, pos, 0:32], cst.t[:, pos, 32:64], qr_f.b, cst.b, qr_o.b, (rt1, rt2))
                    P.op("act", lambda e: e.activation(out=qrb.t[:], in_=qr_o.t[:], func=AF.Copy), [qr_o.b], [qrb.b])
                    self.transpose_into(qrb, qrb.t, 8, qrT, lambda c, pos=pos: qrT.t[:, c, pos * 128:(pos + 1) * 128], psT, identb)
                self.store(qrT, scr["QR_T"].t[b], qrT.t[:], scr["QR_T"])
            P.barrier()
            self.release_scope(locals())

    def slots_all(self):
        P = self.P
        inp, scr = self.inp, self.scr
        with ExitStack() as st:
            sb = lambda name, shape, dt=F32: self.sb(st, name, shape, dt)
            C = {}
            C["identf"] = sb("s_identf", [128, 128])
            C["identb"] = sb("s_identb", [128, 128], BF16)
            C["Tb"] = sb("s_Tb", [128, 16, 2, 128])
            C["constb"] = sb("s_constb", [128, 16])
            C["prefmask"] = sb("s_pref", [128, 896])
            C["dsa_diag"] = sb("s_dsadiag", [128, 2, 128])
            C["mlaf"] = sb("s_mlaf", [128, 2, 256])
            C["mlamask"] = sb("s_mlamask", [128, 2, 256], BF16)
            relb = sb("s_relb", [32, 16])
            oh = sb("s_oh", [32, 4096])
            self.load(C["identf"], C["identf"].t[:], inp["ident"][:, :])
            P.op("dve", lambda e: e.tensor_copy(C["identb"].t[:], C["identf"].t[:]), [C["identf"].b], [C["identb"].b])
            self.load(C["constb"], C["constb"].t[:], inp["constb"][:, :])
            self.load(C["prefmask"], C["prefmask"].t[:], inp["prefmask"][:, :])
            self.load(C["dsa_diag"], C["dsa_diag"].t[:], inp["dsa_diag"].rearrange("y t s -> t y s"))
            self.load(C["mlaf"], C["mlaf"].t[:], inp["mla_mask"].rearrange("y s c -> s y c"))
            P.op("dve", lambda e: e.tensor_copy(C["mlamask"].t[:], C["mlaf"].t[:]), [C["mlaf"].b], [C["mlamask"].b])
            self.load(relb, relb.t[:], inp["relb"][:, :])
            PS = self.PS
            for ty in range(2):
                for tg in range(4):
                    self.load(oh, oh.t[:], inp["onehot"][ty][:, tg * 4096:(tg + 1) * 4096])
                    ps = PS[tg % 2]
                    for t in range(32):
                        self.mm(ps, ps.t[:, t * 16:(t + 1) * 16], oh.t[:, t * 128:(t + 1) * 128], relb.t[:, :], True, True, [oh.b, relb.b])
                    P.op("dve", lambda e, ps=ps, ty=ty, tg=tg: e.tensor_copy(
                        C["Tb"].t[:, :, ty, tg * 32:(tg + 1) * 32], ps.t[:, :].rearrange("p (t h) -> p h t", h=16)), [ps.b], [C["Tb"].b])
            P.barrier()
            nslot = int(os.environ.get("MK_NSLOT", "18"))
            order = []
            for o in range(16):
                order.append((o // 4, o % 4, [(0, 8 * o + 8)], 0))
            order.append((4, 0, [(140, 9)], 1))
            order.append((4, 1, [(149, 9)], 1))
            if nslot < 18:
                order = [order[0], order[16], order[1], order[17]][:nslot]
            for (b, j, runs, ty) in order:
                self.slot(C, b, j, runs, ty)
            self.release_scope(dict(C=C, relb=relb, oh=oh))

    def slot(self, C, b, j, runs, ty):
        P = self.P
        inp, scr = self.inp, self.scr
        PS = self.PS
        o = 4 * b + j
        ktl = []
        for (t0, n) in runs:
            ktl += [t0 + i for i in range(n)]
        nk = len(ktl)
        S = nk * 128
        identb = C["identb"]
        with ExitStack() as st:
            sb = lambda name, shape, dt=F32: self.sb(st, name, shape, dt)
            qa = sb("l_qa", [128, 16, 128], BF16)
            ixq = sb("l_ixq", [128, 8, 128], BF16)
            ixw = sb("l_ixw", [128, 16])
            qn = sb("l_qn", [128, 16, 128], BF16)
            qr = sb("l_qr", [128, 8, 128], BF16)
            maskadd = sb("l_maskadd", [128, S], BF16)
            js = slice(j * 128, (j + 1) * 128)
            self.load(qa, qa.t[:], scr["QA_T"].t[b][:, :, js], reads=[scr["QA_T"].b])
            self.load(ixq, ixq.t[:], scr["IXQ_T"].t[b][:, :, js], reads=[scr["IXQ_T"].b])
            self.load(ixw, ixw.t[:], scr["IXW"].t[b][:, j, :], reads=[scr["IXW"].b])
            self.load(qn, qn.t[:], scr["QN_T"].t[b][:, :, js], reads=[scr["QN_T"].b])
            self.load(qr, qr.t[:], scr["QR_T"].t[b][:, :, js], reads=[scr["QR_T"].b])
            with ExitStack() as st2:
                sb2 = lambda name, shape, dt=F32: self.sb(st2, name, shape, dt)
                row = sb2("i_row", [128, S])
                ixks = Rot([sb2("i_ixk%d" % i, [128, 512], BF16) for i in range(2)])
                rbs = Rot([sb2("i_r%d" % i, [128, 512]) for i in range(3)])
                mx = sb2("i_mx", [128, 1])
                mid = sb2("i_mid", [128, 1])
                cnt = sb2("i_cnt", [128, 1])
                tfl = sb2("i_tfl", [128, 1])
                thr = sb2("i_thr", [128, 1])
                psR = Rot([PS[0], PS[1], PS[2], PS[3]])
                col = 0
                for (t0, n) in runs:
                    k = 0
                    while k < n:
                        g = min(4, n - k)
                        W = g * 128
                        tok = (t0 + k) * 128
                        ixk = ixks.next()
                        self.load(ixk, ixk.t[0:64, 0:W], scr["IXK_T"].t[:, tok:tok + W], reads=[scr["IXK_T"].b])
                        self.load(ixk, ixk.t[64:128, 0:W], scr["IXK_T"].t[:, tok:tok + W], reads=[scr["IXK_T"].b])
                        for h in range(16):
                            hf = h % 2
                            ps = psR.next()
                            self.mm(ps, ps.t[:, 0:W], ixq.t[hf * 64:(hf + 1) * 64, h // 2, :], ixk.t[hf * 64:(hf + 1) * 64, 0:W],
                                    True, True, [ixq.b, ixk.b])
                            r = rbs.next()
                            P.op("act", lambda e, r=r, ps=ps, W=W: e.activation(out=r.t[:, 0:W], in_=ps.t[:, 0:W], func=AF.Relu),
                                 [ps.b], [r.b])
                            if h == 0:
                                P.op("dve", lambda e, r=r, W=W, col=col: e.tensor_scalar(
                                    row.t[:, col:col + W], r.t[:, 0:W], ixw.t[:, 0:1], None, ALU.mult), [r.b, ixw.b], [row.b])
                            else:
                                P.op("dve", lambda e, r=r, W=W, col=col, h=h: e.scalar_tensor_tensor(
                                    out=row.t[:, col:col + W], in0=r.t[:, 0:W], scalar=ixw.t[:, h:h + 1], in1=row.t[:, col:col + W],
                                    op0=ALU.mult, op1=ALU.add), [r.b, ixw.b, row.b], [row.b])
                        col += W
                        k += g
                if ty == 0:
                    P.op("dve", lambda e: e.tensor_tensor(row.t[:, 0:896], row.t[:, 0:896], C["prefmask"].t[:], ALU.add),
                         [row.b, C["prefmask"].b], [row.b])
                P.op("dve", lambda e: e.tensor_tensor(row.t[:, S - 128:S], row.t[:, S - 128:S], C["dsa_diag"].t[:, ty, :], ALU.add),
                     [row.b, C["dsa_diag"].b], [row.b])
                P.op("dve", lambda e: e.tensor_reduce(mx.t[:], row.t[:], AX.X, ALU.max), [row.b], [mx.b])
                P.op("dve", lambda e: e.tensor_scalar(mid.t[:], mx.t[:], -BIS_R / 2, None, ALU.add), [mx.b], [mid.b])
                for it in range(BIS_IT):
                    hw = BIS_R / (2 ** (it + 1))
                    P.op("dve", lambda e: e.tensor_scalar(maskadd.t[:], row.t[:], mid.t[:], None, ALU.is_ge, ALU.add, accum_out=cnt.t[:]),
                         [row.b, mid.b], [maskadd.b, cnt.b])
                    P.op("dve", lambda e, hw=hw: e.tensor_scalar(tfl.t[:], cnt.t[:], float(TOPK) - 0.5, hw, ALU.is_ge, ALU.mult),
                         [cnt.b], [tfl.b])
                    P.op("dve", lambda e, hw=hw: e.scalar_tensor_tensor(out=mid.t[:], in0=tfl.t[:], scalar=-hw / 2, in1=mid.t[:],
                                                                        op0=ALU.add, op1=ALU.add), [tfl.b, mid.b], [mid.b])
                hwK = BIS_R / (2 ** (BIS_IT + 1))
                P.op("dve", lambda e: e.tensor_scalar(thr.t[:], mid.t[:], -hwK, None, ALU.add), [mid.b], [thr.b])
                P.op("dve", lambda e: e.tensor_scalar(maskadd.t[:], row.t[:], thr.t[:], MASKNEG, ALU.is_lt, ALU.mult),
                     [row.b, thr.b], [maskadd.b])
                P.barrier()
                self.release_scope(locals())
            with ExitStack() as st3:
                sb3 = lambda name, shape, dt=F32: self.sb(st3, name, shape, dt)
                kcs = Rot([sb3("a_kc%d" % i, [128, 2048], BF16) for i in range(2)])
                vcs = Rot([sb3("a_vc%d" % i, [128, 16, 130], BF16) for i in range(2)])
                krr = sb3("a_kr", [128, S], BF16)
                pts = Rot([sb3("a_p%d" % i, [128, 512], BF16) for i in range(3)])
                p2s = Rot([sb3("a_p2%d" % i, [128, 256], BF16) for i in range(2)])
                tmpn = sb3("a_tmpn", [128, 256])
                rec = sb3("a_rec", [128, 1])
                ost = sb3("a_ost", [128, D])
                psS = Rot([PS[0], PS[1], PS[2], PS[3]])
                psA = Rot([PS[4], PS[5]])
                col = 0
                for (t0, n) in runs:
                    self.load(krr, krr.t[0:64, col:col + n * 128], scr["KR_T"].t[:, t0 * 128:(t0 + n) * 128], reads=[scr["KR_T"].b])
                    self.load(krr, krr.t[64:128, col:col + n * 128], scr["KR_T"].t[:, t0 * 128:(t0 + n) * 128], reads=[scr["KR_T"].b])
                    col += n * 128
                chunks = []
                gk = 0
                for (t0, n) in runs:
                    k = 0
                    while k < n:
                        m = min(16, n - k)
                        chunks.append((t0 + k, m, gk))
                        gk += m
                        k += m
                for kind in ("dsa", "mla"):
                    KT = scr["KT_A"] if kind == "dsa" else scr["KT_B"]
                    VV = scr["V_A"] if kind == "dsa" else scr["V_B"]
                    for h in range(NH):
                        acc = psA.next()
                        hf = h % 2
                        for (t0, m, gk0) in chunks:
                            kc = kcs.next()
                            vc = vcs.next()
                            self.load(kc, kc.t[:, 0:m * 128], KT.t[h, :, t0 * 128:(t0 + m) * 128], reads=[KT.b])
                            self.load(vc, vc.t[:, 0:m, :], VV.t[h, :, t0:t0 + m, :], reads=[VV.b])
                            k = 0
                            while k < m:
                                gkt = gk0 + k
                                if gkt >= nk - 2:
                                    g = nk - gkt
                                    near = True
                                else:
                                    g = min(4, m - k, nk - 2 - gkt)
                                    near = False
                                Sps = psS.next()
                                for gi in range(g):
                                    cs_ = slice(gi * 128, (gi + 1) * 128)
                                    kl = k + gi
                                    if kind == "dsa":
                                        self.mm(Sps, Sps.t[:, cs_], kc.t[:, kl * 128:(kl + 1) * 128], qa.t[:, h, :], True, False,
                                                [kc.b, qa.b])
                                        self.mm(Sps, Sps.t[:, cs_], maskadd.t[:, (gkt + gi) * 128:(gkt + gi + 1) * 128], identb.t[:],
                                                False, True, [maskadd.b, identb.b])
                                    else:
                                        self.mm(Sps, Sps.t[:, cs_], kc.t[:, kl * 128:(kl + 1) * 128], qn.t[:, h, :], True, False,
                                                [kc.b, qn.b])
                                        self.mm(Sps, Sps.t[:, cs_], krr.t[hf * 64:(hf + 1) * 64, (gkt + gi) * 128:(gkt + gi + 1) * 128],
                                                qr.t[hf * 64:(hf + 1) * 64, h // 2, :], False, True, [krr.b, qr.b])
                                W = g * 128
                                p = pts.next()
                                if kind == "dsa":
                                    if near:
                                        P.op("dve", lambda e, Sps=Sps, h=h: e.scalar_tensor_tensor(
                                            out=tmpn.t[:], in0=Sps.t[:, 0:256], scalar=A_SCALE,
                                            in1=C["Tb"].t[:, h, :, :].rearrange("p y t -> p (y t)"), op0=ALU.mult, op1=ALU.add),
                                            [Sps.b, C["Tb"].b], [tmpn.b])
                                        P.op("act", lambda e, p=p: e.activation(out=p.t[:, 0:256], in_=tmpn.t[:], func=AF.Exp),
                                             [tmpn.b], [p.b])
                                    else:
                                        P.op("act", lambda e, p=p, Sps=Sps, W=W, h=h: e.activation(
                                            out=p.t[:, 0:W], in_=Sps.t[:, 0:W], func=AF.Exp, bias=C["constb"].t[:, h:h + 1], scale=A_SCALE),
                                            [Sps.b, C["constb"].b], [p.b])
                                    pp = p
                                else:
                                    P.op("act", lambda e, p=p, Sps=Sps, W=W: e.activation(out=p.t[:, 0:W], in_=Sps.t[:, 0:W], func=AF.Exp,
                                                                                       scale=MLA_SCALE), [Sps.b], [p.b])
                                    pp = p
                                    if near:
                                        p2 = p2s.next()
                                        P.op("dve", lambda e, p=p, p2=p2: e.tensor_tensor(p2.t[:], p.t[:, 0:256], C["mlamask"].t[:, ty, :],
                                                                                         ALU.mult), [p.b, C["mlamask"].b], [p2.b])
                                        pp = p2
                                for gi in range(g):
                                    kl = k + gi
                                    self.mm(acc, acc.t[:, 0:129], pp.t[:, gi * 128:(gi + 1) * 128], vc.t[:, kl, 0:129],
                                            (gkt + gi) == 0, (gkt + gi) == nk - 1, [pp.b, vc.b])
                                k += g
                        P.op("dve", lambda e, acc=acc: e.reciprocal(rec.t[:], acc.t[:, 128:129]), [acc.b], [rec.b])
                        P.op("dve", lambda e, acc=acc, h=h: e.tensor_scalar(ost.t[:, h * 128:(h + 1) * 128], acc.t[:, 0:128], rec.t[:], None,
                                                                            ALU.mult), [acc.b, rec.b], [ost.b])
                    dst = scr["OA"] if kind == "dsa" else scr["OB"]
                    self.store(ost, dst.t[o], ost.t[:], dst)
                P.barrier()
                self.release_scope(locals())
            self.release_scope(dict(qa=qa, ixq=ixq, ixw=ixw, qn=qn, qr=qr, maskadd=maskadd))

    def phase_M(self):
        P = self.P
        inp, out, scr = self.inp, self.out, self.scr
        PS = self.PS
        nb = int(os.environ.get("MK_NMB", "5"))
        for b in list(range(5))[:nb] if nb >= 5 else [0, 4][:nb]:
            tiles = self.own_tiles(b)
            with ExitStack() as so:
                xk = [self.sb(so, "m_x%d" % i, [128, D]) for i in range(4)]
                with ExitStack() as st:
                    sb = lambda name, shape, dt=F32: self.sb(st, name, shape, dt)
                    g_mix = sb("m_gmix", [128, D])
                    identf = sb("m_identf", [128, 128])
                    identb = sb("m_identb", [128, 128], BF16)
                    hb = sb("m_hb", [128, D], BF16)
                    hT = sb("m_hT", [128, 16, 512], BF16)
                    wts = Rot([sb("m_wt%d" % i, [128, 16, 512], BF16) for i in range(2)])
                    gas = Rot([sb("m_ga%d" % i, [128, 512]) for i in range(2)])
                    gbs = Rot([sb("m_gb%d" % i, [128, 512]) for i in range(2)])
                    oas = Rot([sb("m_oa%d" % i, [128, 512]) for i in range(2)])
                    obs = Rot([sb("m_ob%d" % i, [128, 512]) for i in range(2)])
                    mix = [sb("m_mix%d" % i, [128, D], BF16) for i in range(4)]
                    tmp = dict(junk=sb("m_junk", [128, D], BF16), ss=sb("m_ss", [128, 1]), sd=sb("m_sd", [128, 1]),
                               rstd=sb("m_rstd", [128, 1]))
                    psT = Rot([PS[0], PS[1]])
                    psM = Rot([PS[2], PS[3], PS[4], PS[5]])
                    self.load(g_mix, g_mix.t[:], inp["g_mix"][:, :])
                    self.load(identf, identf.t[:], inp["ident"][:, :])
                    P.op("dve", lambda e: e.tensor_copy(identb.t[:], identf.t[:]), [identf.b], [identb.b])
                    self.load_norm_T(st, tiles, g_mix, hT, None, hb, tmp, psT, identb, keep_x=xk)
                    for i in range(4):
                        wa = wts.next()
                        wb = wts.next()
                        self.load(wa, wa.t[:], scr["W_gate"].t[i], reads=[scr["W_gate"].b])
                        self.load(wb, wb.t[:], scr["W_gate"].t[4 + i], reads=[scr["W_gate"].b])
                        cs_ = slice(i * 512, (i + 1) * 512)
                        for pos in range(4):
                            o = 4 * b + pos
                            pa = psM.next()
                            for kc in range(16):
                                self.mm(pa, pa.t[:, :], hT.t[:, kc, pos * 128:(pos + 1) * 128], wa.t[:, kc, :], kc == 0, kc == 15, [wa.b, hT.b])
                            pb = psM.next()
                            for kc in range(16):
                                self.mm(pb, pb.t[:, :], hT.t[:, kc, pos * 128:(pos + 1) * 128], wb.t[:, kc, :], kc == 0, kc == 15, [wb.b, hT.b])
                            ga, gb, oa, ob = gas.next(), gbs.next(), oas.next(), obs.next()
                            P.op("act", lambda e, ga=ga, pa=pa: e.activation(out=ga.t[:], in_=pa.t[:, :], func=AF.Sigmoid), [pa.b], [ga.b])
                            P.op("act", lambda e, gb=gb, pb=pb: e.activation(out=gb.t[:], in_=pb.t[:, :], func=AF.Sigmoid), [pb.b], [gb.b])
                            self.load(oa, oa.t[:], scr["OA"].t[o][:, cs_], reads=[scr["OA"].b])
                            self.load(ob, ob.t[:], scr["OB"].t[o][:, cs_], reads=[scr["OB"].b])
                            P.op("dve", lambda e, ga=ga, oa=oa: e.tensor_tensor(ga.t[:], ga.t[:], oa.t[:], ALU.mult), [ga.b, oa.b], [ga.b])
                            P.op("dve", lambda e, gb=gb, ob=ob: e.tensor_tensor(gb.t[:], gb.t[:], ob.t[:], ALU.mult), [gb.b, ob.b], [gb.b])
                            P.op("dve", lambda e, ga=ga, gb=gb, pos=pos, cs_=cs_: e.tensor_tensor(mix[pos].t[:, cs_], ga.t[:], gb.t[:], ALU.add),
                                 [ga.b, gb.b], [mix[pos].b])
                    for pos in range(4):
                        self.transpose_into(mix[pos], mix[pos].t, 16, hT, lambda kc, pos=pos: hT.t[:, kc, pos * 128:(pos + 1) * 128], psT, identb)
                    for cb in range(4):
                        wt = wts.next()
                        self.load(wt, wt.t[:], scr["W_wo"].t[cb], reads=[scr["W_wo"].b])
                        cs_ = slice(cb * 512, (cb + 1) * 512)
                        for pos in range(4):
                            ps = psM.next()
                            for kc in range(16):
                                self.mm(ps, ps.t[:, :], hT.t[:, kc, pos * 128:(pos + 1) * 128], wt.t[:, kc, :], kc == 0, kc == 15, [wt.b, hT.b])
                            P.op("dve", lambda e, ps=ps, pos=pos, cs_=cs_: e.tensor_tensor(xk[pos].t[:, cs_], xk[pos].t[:, cs_], ps.t[:, :], ALU.add),
                                 [xk[pos].b, ps.b], [xk[pos].b])
                    P.barrier()
                    self.release_scope(locals())
                with ExitStack() as st:
                    sb = lambda name, shape, dt=F32: self.sb(st, name, shape, dt)
                    g_ffn = sb("n_gffn", [128, D])
                    g_fin = sb("n_gfin", [128, D])
                    identf = sb("n_identf", [128, 128])
                    identb = sb("n_identb", [128, 128], BF16)
                    hb = sb("n_hb", [128, D], BF16)
                    h2T = sb("n_h2T", [128, 16, 512], BF16)
                    uT = sb("n_uT", [128, 64, 512], BF16)
                    wts = Rot([sb("n_wt%d" % i, [128, 16, 512], BF16) for i in range(2)])
                    rrs = Rot([sb("n_rr%d" % i, [128, 512]) for i in range(2)])
                    ys = Rot([sb("n_y%d" % i, [128, D]) for i in range(2)])
                    tmp = dict(junk=sb("n_junk", [128, D], BF16), ss=sb("n_ss", [128, 1]), sd=sb("n_sd", [128, 1]),
                               rstd=sb("n_rstd", [128, 1]))
                    psT = Rot([PS[0], PS[1]])
                    psM = Rot([PS[6], PS[7]])
                    self.load(g_ffn, g_ffn.t[:], inp["g_ffn"][:, :])
                    self.load(g_fin, g_fin.t[:], inp["g_fin"][:, :])
                    self.load(identf, identf.t[:], inp["ident"][:, :])
                    P.op("dve", lambda e: e.tensor_copy(identb.t[:], identf.t[:]), [identf.b], [identb.b])
                    for pos in range(4):
                        self.norm_rows(xk[pos].t[:], D, g_ffn.t[:], hb.t[:], xk[pos].b, g_ffn.b, hb.b, tmp)
                        self.transpose_into(hb, hb.t, 16, h2T, lambda kc, pos=pos: h2T.t[:, kc, pos * 128:(pos + 1) * 128], psT, identb)
                    for cb in range(16):
                        wt = wts.next()
                        self.load(wt, wt.t[:], scr["W_up"].t[cb], reads=[scr["W_up"].b])
                        for jj in range(4):
                            ffc = 4 * cb + jj
                            ps = psM.next()
                            for kc in range(16):
                                self.mm(ps, ps.t[:, :], wt.t[:, kc, jj * 128:(jj + 1) * 128], h2T.t[:, kc, :], kc == 0, kc == 15, [wt.b, h2T.b])
                            rr = rrs.next()
                            P.op("act", lambda e, rr=rr, ps=ps: e.activation(out=rr.t[:], in_=ps.t[:, :], func=AF.Relu), [ps.b], [rr.b])
                            P.op("dve", lambda e, rr=rr, ffc=ffc: e.tensor_tensor(uT.t[:, ffc, :], rr.t[:], rr.t[:], ALU.mult), [rr.b], [uT.b])
                    for cb in range(4):
                        cs_ = slice(cb * 512, (cb + 1) * 512)
                        pss = [PS[2], PS[3], PS[4], PS[5]]
                        for kg in range(4):
                            wt = wts.next()
                            self.load(wt, wt.t[:], scr["W_dn"].t[cb * 4 + kg], reads=[scr["W_dn"].b])
                            for pos in range(4):
                                for kc in range(16):
                                    self.mm(pss[pos], pss[pos].t[:, :], uT.t[:, kg * 16 + kc, pos * 128:(pos + 1) * 128], wt.t[:, kc, :],
                                            kg == 0 and kc == 0, kg == 3 and kc == 15, [wt.b, uT.b])
                        for pos in range(4):
                            P.op("dve", lambda e, pos=pos, cs_=cs_, pss=pss: e.tensor_tensor(xk[pos].t[:, cs_], xk[pos].t[:, cs_], pss[pos].t[:, :],
                                                                                         ALU.add), [xk[pos].b, pss[pos].b], [xk[pos].b])
                    for pos in range(4):
                        o = 4 * b + pos
                        y = ys.next()
                        self.norm_rows(xk[pos].t[:], D, g_fin.t[:], y.t[:], xk[pos].b, g_fin.b, y.b, tmp)
                        self.store(y, out["y_own"][o * 128:(o + 1) * 128, :], y.t[:])
                    P.barrier()
                    self.release_scope(locals())
                self.release_scope(dict(xk=xk))

    def build(self):
        nc = self.nc
        self.declare()
        st = self.gstack
        self.PS = [TB(st.enter_context(nc.psum_tensor("ps%d" % i, [128, 512], F32)), "ps%d" % i) for i in range(8)]
        for p_ in self.PS:
            p_.b.excl = True
        self.phase_W()
        if self.upto == "W":
            return self.finish()
        kblocks = list(range(35))
        if self.upto == "K1":
            kblocks = [0, 1, 34][:int(os.environ.get("MK_NB", "3"))]
        self.phase_K(kblocks)
        if self.upto in ("K", "K1"):
            return self.finish()
        self.phase_KC()
        if self.upto == "KC":
            return self.finish()
        self.phase_Q()
        if self.upto == "Q":
            return self.finish()
        self.slots_all()
        if self.upto == "S":
            return self.finish()
        self.phase_M()
        return self.finish()

    def finish(self):
        self.P.finish()
        self.gstack.close()
        return self.nc


def t5_bucket_np(rel):
    nb = 16
    ret = (rel > 0).astype(np.int32) * nb
    n = np.abs(rel)
    max_exact = 8
    nf = np.maximum(n, 1).astype(np.float32)
    large = max_exact + (np.log(nf / np.float32(max_exact)) / np.float32(math.log(128 / max_exact))
                         * np.float32(nb - max_exact)).astype(np.int32)
    large = np.minimum(large, nb - 1)
    return ret + np.where(n < max_exact, n, large)


def host_inputs(inputs):
    f32 = np.float32
    xp = np.asarray(inputs["x_prompt"], f32)[0]
    xs = np.asarray(inputs["x_sample"], f32)
    ck = np.asarray(inputs["cache_a_k"], f32)[0]
    cv = np.asarray(inputs["cache_a_v"], f32)[0]
    cix = np.asarray(inputs["cache_a_idx_k"], f32)[0]
    cckv = np.asarray(inputs["cache_b_ckv"], f32)[0]
    ckr = np.asarray(inputs["cache_b_krope"], f32)[0]
    half = 32
    inv_freq = np.power(np.float32(10000.0), -np.arange(half, dtype=f32) / np.float32(half)).astype(f32)
    shared = {
        "ident": np.eye(128, dtype=f32),
        "constb": np.ascontiguousarray(np.broadcast_to(np.asarray(inputs["rel_bias_table"], f32)[15], (128, 16))),
        "relb": np.ascontiguousarray(np.asarray(inputs["rel_bias_table"], f32)),
        "w_in": np.ascontiguousarray(np.asarray(inputs["w_in"], f32)[0]),
        "w_uq": np.ascontiguousarray(np.asarray(inputs["w_uq"], f32)[0]),
        "w_uk": np.ascontiguousarray(np.asarray(inputs["w_uk"], f32)[0].reshape(256, 2048)),
        "w_uv": np.ascontiguousarray(np.asarray(inputs["w_uv"], f32)[0].reshape(256, 2048)),
        "w_out": np.ascontiguousarray(np.asarray(inputs["w_out"], f32)[0]),
        "w_ff_up": np.ascontiguousarray(np.asarray(inputs["w_ff_up"], f32)[0]),
        "w_ff_down": np.ascontiguousarray(np.asarray(inputs["w_ff_down"], f32)[0]),
        "g_mix": np.ascontiguousarray(np.broadcast_to(np.asarray(inputs["norm_mix_g"], f32)[0], (128, D))),
        "g_ffn": np.ascontiguousarray(np.broadcast_to(np.asarray(inputs["norm_ffn_g"], f32)[0], (128, D))),
        "g_fin": np.ascontiguousarray(np.broadcast_to(np.asarray(inputs["final_norm_g"], f32), (128, D))),
        "g_q": np.ascontiguousarray(np.broadcast_to(np.asarray(inputs["q_lora_g"], f32)[0], (128, 512))),
        "g_kv": np.ascontiguousarray(np.broadcast_to(np.asarray(inputs["kv_lora_g"], f32)[0], (128, 256))),
    }
    s = np.arange(128)[None, :]
    t = np.arange(128)[:, None]
    onehot = np.zeros((2, 32, 128, 128), f32)
    for ty, off in enumerate((-128, 0)):
        rel = (s - t + off).astype(np.int32)
        bk = t5_bucket_np(rel)
        for b in range(32):
            onehot[ty, b] = (bk == b)
    shared["onehot"] = onehot.reshape(2, 32, 128 * 128)
    dsa_diag = np.zeros((2, 128, 128), f32)
    dsa_diag[0] = np.where((s // 64) <= (t // 64), 0.0, NEGBIG)
    dsa_diag[1] = np.where(s < 16, 0.0, NEGBIG) * np.ones((128, 1), f32)
    shared["dsa_diag"] = dsa_diag
    mla_mask = np.ones((2, 128, 2, 128), f32)
    ss_ = np.arange(128)[:, None]
    tt_ = np.arange(128)[None, :]
    mla_mask[0, :, 1, :] = ((ss_ // 64) <= (tt_ // 64)).astype(f32)
    mla_mask[1, :, 1, :] = (ss_ < 16).astype(f32) * np.ones((1, 128), f32)
    shared["mla_mask"] = mla_mask.reshape(2, 128, 256)
    maps = []
    for c in range(NCORE):
        m = dict(shared)
        pre = 7 - c
        x_all = np.zeros((NXT * 128, D), f32)
        x_all[pre * 128:pre * 128 + 16384] = xp
        pos = np.zeros((NXT * 128,), f32)
        valid = np.zeros((NXT * 128, 1), f32)
        pos[pre * 128:pre * 128 + 16384] = np.arange(16384, dtype=f32)
        valid[pre * 128:pre * 128 + 16384] = 1.0
        for b in range(2):
            r0 = (136 + b) * 128
            x_all[r0:r0 + 16] = xs[2 * c + b]
            pos[r0:r0 + 16] = 1024 + np.arange(16, dtype=f32)
            valid[r0:r0 + 16] = 1.0
        ang = pos[:, None] * inv_freq[None, :]
        m["x_all"] = x_all
        m["cs"] = np.concatenate([np.cos(ang), np.sin(ang)], axis=1).astype(f32)
        m["valid"] = np.ascontiguousarray(valid.reshape(NXT, 128).T)
        pm = np.zeros((896,), f32)
        pm[:pre * 128] = NEGBIG
        m["prefmask"] = np.ascontiguousarray(np.broadcast_to(pm, (128, 896)))
        sl = slice(2 * c, 2 * c + 2)
        m["c_akT"] = np.ascontiguousarray(ck[sl].transpose(0, 2, 3, 1))
        cve = np.ones((2, 1024, 16, 130), f32)
        cve[..., 0:128] = cv[sl]
        m["c_av"] = cve
        m["c_ixkT"] = np.ascontiguousarray(cix[sl].transpose(0, 2, 1))
        m["c_ckvT"] = np.ascontiguousarray(cckv[sl].transpose(0, 2, 1))
        m["c_krT"] = np.ascontiguousarray(ckr[sl].transpose(0, 2, 1))
        maps.append(m)
    return maps


def assemble(results):
    f32 = np.float32
    y_p = np.zeros((1, 16384, D), f32)
    y_s = np.zeros((16, 16, D), f32)
    a_k_p = np.zeros((1, 1, 16384, 16, 128), f32)
    a_v_p = np.zeros((1, 1, 16384, 16, 128), f32)
    a_ix_p = np.zeros((1, 1, 16384, 64), f32)
    b_ckv_p = np.zeros((1, 1, 16384, 256), f32)
    b_kr_p = np.zeros((1, 1, 16384, 64), f32)
    a_k_s = np.zeros((1, 16, 16, 16, 128), f32)
    a_v_s = np.zeros((1, 16, 16, 16, 128), f32)
    a_ix_s = np.zeros((1, 16, 16, 64), f32)
    b_ckv_s = np.zeros((1, 16, 16, 256), f32)
    b_kr_s = np.zeros((1, 16, 16, 64), f32)
    for c in range(NCORE):
        r = results[c]
        for i in range(16):
            j = c + 8 * i
            rs = slice(i * 128, (i + 1) * 128)
            ps = slice(j * 128, (j + 1) * 128)
            y_p[0, ps] = r["y_own"][rs]
            a_k_p[0, 0, ps] = r["st_ak"][rs].reshape(128, 16, 128)
            a_v_p[0, 0, ps] = r["st_av"][rs].reshape(128, 16, 128)
            a_ix_p[0, 0, ps] = r["st_ix"][rs]
            b_ckv_p[0, 0, ps] = r["st_ckv"][rs]
            b_kr_p[0, 0, ps] = r["st_kr"][rs]
        for b in range(2):
            rs = slice((16 + b) * 128, (16 + b) * 128 + 16)
            sq = 2 * c + b
            y_s[sq] = r["y_own"][rs]
            a_k_s[0, sq] = r["st_ak"][rs].reshape(16, 16, 128)
            a_v_s[0, sq] = r["st_av"][rs].reshape(16, 16, 128)
            a_ix_s[0, sq] = r["st_ix"][rs]
            b_ckv_s[0, sq] = r["st_ckv"][rs]
            b_kr_s[0, sq] = r["st_kr"][rs]
    return (y_p, y_s, a_k_p, a_v_p, a_ix_p, b_ckv_p, b_kr_p, a_k_s, a_v_s, a_ix_s, b_ckv_s, b_kr_s)


def kernel(**inputs):
    upto = os.environ.get("MK_UPTO", "ALL")
    bld = Builder(upto)
    nc = bld.build()
    maps = host_inputs(inputs)
    res = run_bass_kernel_spmd(nc, maps, core_ids=list(range(NCORE)))
    return assemble(res.results)
```

```python
import math
import os
from contextlib import ExitStack

import numpy as np
import concourse.bass as bass
import concourse.mybir as mybir
from concourse.bass_utils import run_bass_kernel_spmd

F32 = mybir.dt.float32
BF16 = mybir.dt.bfloat16
AF = mybir.ActivationFunctionType
ALU = mybir.AluOpType
AX = mybir.AxisListType

D = 2048
NH = 16
HD = 128
NCORE = 8
NPT = 136
NXT = 140
NTILE = 158
NTOK = NTILE * 128
NOWN = 20
EPS = 1e-6
A_SCALE = HD ** -0.5
MLA_SCALE = 192 ** -0.5
NEGBIG = -1.0e30
MASKNEG = -30000.0
TOPK = 256
BIS_R = 128.0
BIS_IT = 16
C_AQ, C_AK, C_AV, C_IXQ, C_IXK, C_IXW, C_CQ, C_CKV, C_KR, C_GA, C_GB = (
    0, 2048, 4096, 6144, 7168, 7232, 7248, 7760, 8016, 8080, 10128)


class Buf:
    __slots__ = ("name", "last_w", "readers", "dsem", "dcount", "excl")

    def __init__(self, name):
        self.excl = False
        self.name = name
        self.last_w = None
        self.readers = []
        self.dsem = None
        self.dcount = 0


class Ins:
    __slots__ = ("eng", "fn", "deps", "is_dma", "dbuf", "need_inc", "seq", "cover")

    def __init__(self, eng, fn, deps, is_dma=False, dbuf=None):
        self.eng = eng
        self.fn = fn
        self.deps = deps
        self.is_dma = is_dma
        self.dbuf = dbuf
        self.need_inc = False
        self.seq = 0
        self.cover = 0


COMPUTE = ("pe", "act", "dve", "pool")


class Prog:
    def __init__(self, nc, stack):
        self.nc = nc
        self.stack = stack
        self.engs = {"pe": nc.tensor, "act": nc.scalar, "dve": nc.vector, "pool": nc.gpsimd, "sp": nc.sync}
        self.batch = []
        self.esem = {e: stack.enter_context(nc.semaphore("e_" + e)) for e in COMPUTE}
        self.ecount = {e: 0 for e in COMPUTE}
        self.waited = {e: {} for e in self.engs}
        self.sempool = []
        self.livesems = {}
        self.nsem = 4
        self.n_ins = 0
        self.n_waits = 0
        self.trace = {e: [] for e in self.engs}

    def _getsem(self, buf):
        if self.sempool:
            sem, cnt = self.sempool.pop()
        else:
            self.nsem += 1
            sem = self.stack.enter_context(self.nc.semaphore("d%d" % self.nsem))
            cnt = 0
        buf.dsem = sem
        buf.dcount = cnt
        self.livesems[id(sem)] = [sem, cnt, cnt]

    def release(self, bufs):
        for b in bufs:
            if b.dsem is not None:
                ent = self.livesems.pop(id(b.dsem))
                self.sempool.append((ent[0], ent[1]))
                b.dsem = None

    def _deps(self, eng, reads, writes, is_dma):
        deps = {}

        def add(p, kind):
            if p is None:
                return
            if (not p.is_dma) and (not is_dma) and p.eng == eng:
                if eng == "pe" or kind != "raw":
                    return
            deps[id(p)] = p

        for b in reads:
            add(b.last_w, "raw")
            if b.excl:
                for r in b.readers:
                    if r.eng != eng:
                        add(r, "war")
        for b in writes:
            add(b.last_w, "waw")
            for r in b.readers:
                add(r, "war")
        return list(deps.values())

    def _post(self, ins, reads, writes):
        for b in reads:
            if not ins.is_dma:
                b.readers = [r for r in b.readers if r.is_dma or r.eng != ins.eng]
            b.readers.append(ins)
        for b in writes:
            b.last_w = ins
            b.readers = []
        self.batch.append(ins)

    def op(self, eng, fn, reads=(), writes=()):
        ins = Ins(eng, fn, self._deps(eng, reads, writes, False))
        self._post(ins, reads, writes)

    def dma(self, fn, reads, writes, dbuf, q="sp"):
        ins = Ins(q, fn, self._deps(q, reads, writes, True), True, dbuf)
        if dbuf.dsem is None:
            self._getsem(dbuf)
        dbuf.dcount += 1
        self.livesems[id(dbuf.dsem)][1] = dbuf.dcount
        self._post(ins, reads, writes)

    def flush(self):
        batch = self.batch
        self.batch = []
        for i in batch:
            for p in i.deps:
                if not p.is_dma:
                    p.need_inc = True
        last = {}
        for i in batch:
            if not i.is_dma:
                last[i.eng] = i
        for i in last.values():
            i.need_inc = True
        for i in batch:
            if not i.is_dma and i.need_inc:
                self.ecount[i.eng] += 1
                i.seq = self.ecount[i.eng]
        nxt = {}
        for i in reversed(batch):
            if not i.is_dma:
                if i.need_inc:
                    nxt[i.eng] = i.seq
                i.cover = nxt[i.eng]
        for i in batch:
            h = self.engs[i.eng]
            need = {}
            for p in i.deps:
                if p.is_dma:
                    ent = self.livesems.get(id(p.dbuf.dsem)) if p.dbuf.dsem is not None else None
                    if ent is None:
                        continue
                    key = id(ent[0])
                    sem = ent[0]
                    val = 16 * ent[2]
                else:
                    key = p.eng
                    sem = self.esem[p.eng]
                    val = p.cover
                if key not in need or need[key][1] < val:
                    need[key] = (sem, val)
            w = self.waited[i.eng]
            for key, (sem, val) in need.items():
                if w.get(key, 0) >= val:
                    continue
                w[key] = val
                h.wait_ge(sem, val)
                self.trace[i.eng].append(("w", id(sem), val))
                self.n_waits += 1
            bi = i.fn(h)
            self.n_ins += 1
            if i.is_dma:
                bi.then_inc(i.dbuf.dsem, 16)
                self.trace[i.eng].append(("i", id(i.dbuf.dsem), 16))
                self.livesems[id(i.dbuf.dsem)][2] += 1
            elif i.need_inc:
                bi.then_inc(self.esem[i.eng], 1)
                self.trace[i.eng].append(("i", id(self.esem[i.eng]), 1))
            else:
                self.trace[i.eng].append(("i", None, 0))

    def barrier(self):
        self.flush()
        for e, h in self.engs.items():
            w = self.waited[e]
            for pe in COMPUTE:
                if pe == e:
                    continue
                val = self.ecount[pe]
                if val > 0 and w.get(pe, 0) < val:
                    w[pe] = val
                    h.wait_ge(self.esem[pe], val)
                    self.trace[e].append(("w", id(self.esem[pe]), val))
            for key, ent in self.livesems.items():
                val = 16 * ent[2]
                if val > 0 and w.get(key, 0) < val:
                    w[key] = val
                    h.wait_ge(ent[0], val)
                    self.trace[e].append(("w", id(ent[0]), val))

    def finish(self):
        self.barrier()


class TB:
    def __init__(self, t, name):
        self.t = t
        self.b = Buf(name)


class Rot:
    def __init__(self, items):
        self.items = items
        self.i = 0

    def next(self):
        x = self.items[self.i % len(self.items)]
        self.i += 1
        return x


def weight_groups():
    g = {}
    g["ak"] = dict(K=2048, blocks=[[("w_in", 0, C_AK + 512 * i, 512)] for i in range(4)])
    g["av"] = dict(K=2048, blocks=[[("w_in", 0, C_AV + 512 * i, 512)] for i in range(4)])
    g["misc"] = dict(K=2048, blocks=[[("w_in", 0, C_CKV, 256), ("w_in", 0, C_IXK, 64), ("w_in", 0, C_KR, 64)]])
    g["aq"] = dict(K=2048, blocks=[[("w_in", 0, C_AQ + 512 * i, 512)] for i in range(4)])
    g["ixq"] = dict(K=2048, blocks=[[("w_in", 0, C_IXQ + 512 * i, 512)] for i in range(2)])
    g["cq"] = dict(K=2048, blocks=[[("w_in", 0, C_CQ, 512)], [("w_in", 0, C_IXW, 16)]])
    g["gate"] = dict(K=2048, blocks=[[("w_in", 0, C_GA + 512 * i, 512)] for i in range(8)])
    g["wo"] = dict(K=2048, blocks=[[("w_out", 0, 512 * i, 512)] for i in range(4)])
    g["up"] = dict(K=2048, blocks=[[("w_ff_up", 0, 512 * i, 512)] for i in range(16)])
    g["dn"] = dict(K=2048, blocks=[[("w_ff_down", 2048 * kg, 512 * cb, 512)] for cb in range(4) for kg in range(4)])
    g["uqn"] = dict(K=512, blocks=[[("uqn", 0, i, 0)] for i in range(4)])
    g["uqr"] = dict(K=512, blocks=[[("uqr", 0, i, 0)] for i in range(2)])
    g["uk"] = dict(K=256, blocks=[[("w_uk", 0, 512 * i, 512)] for i in range(4)])
    g["uv"] = dict(K=256, blocks=[[("w_uv", 0, 512 * i, 512)] for i in range(4)])
    first = ["uk", "uv", "misc", "ak", "av"]
    return {k: g[k] for k in first + [k for k in g if k not in first]}


class _Stop(Exception):
    pass


class Builder:
    def ck(self, n):
        if int(os.environ.get("MK_KSTOP", "0")) == n:
            raise _Stop()

    def __init__(self, upto="ALL"):
        self.upto = upto
        self.nc = bass.Bass("TRN2", target_bir_lowering=False)
        self.gstack = ExitStack()
        self.P = Prog(self.nc, self.gstack)
        self.inp = {}
        self.out = {}
        self.scr = {}

    def din(self, name, shape, dt=F32):
        self.inp[name] = self.nc.dram_tensor(name, list(shape), dt, kind="ExternalInput").ap()
        return self.inp[name]

    def dout(self, name, shape, dt=F32):
        self.out[name] = self.nc.dram_tensor(name, list(shape), dt, kind="ExternalOutput").ap()
        return self.out[name]

    def dscr(self, name, shape, dt=BF16):
        self.scr[name] = TB(self.nc.dram_tensor(name, list(shape), dt, kind="Internal").ap(), name)
        return self.scr[name]

    def sb(self, st, name, shape, dt=F32):
        self._uid = getattr(self, "_uid", 0) + 1
        name = "%s_u%d" % (name, self._uid)
        return TB(st.enter_context(self.nc.sbuf_tensor(name, list(shape), dt)), name)

    def mm(self, ps, out_ap, lhsT, rhs, start, stop, reads):
        self.P.op("pe", lambda e: e.matmul(out_ap, lhsT, rhs, start=start, stop=stop), reads, [ps.b])

    def load(self, dst, dst_ap, src_ap, reads=(), q="sp"):
        self.P.dma(lambda e: e.dma_start(out=dst_ap, in_=src_ap), list(reads), [dst.b], dst.b, q=q)

    def store(self, src, dst_ap, src_ap, dstbuf=None, q=None):
        q = q or getattr(self, "store_q", "pool")
        w = [dstbuf.b] if dstbuf is not None else []
        self.P.dma(lambda e: e.dma_start(out=dst_ap, in_=src_ap), [src.b], w, src.b, q=q)

    def declare(self):
        din, dout, dscr = self.din, self.dout, self.dscr
        din("x_all", [NXT * 128, D])
        din("cs", [NXT * 128, 64])
        din("valid", [128, NXT])
        din("prefmask", [128, 896])
        din("ident", [128, 128])
        din("onehot", [2, 32, 128 * 128])
        din("relb", [32, 16])
        din("constb", [128, 16])
        din("dsa_diag", [2, 128, 128])
        din("mla_mask", [2, 128, 256])
        din("c_akT", [2, 16, 128, 1024])
        din("c_av", [2, 1024, 16, 130])
        din("c_ixkT", [2, 64, 1024])
        din("c_ckvT", [2, 256, 1024])
        din("c_krT", [2, 64, 1024])
        din("w_in", [D, 12176])
        din("w_uq", [512, 16, 192])
        din("w_uk", [256, 2048])
        din("w_uv", [256, 2048])
        din("w_out", [D, D])
        din("w_ff_up", [D, 8192])
        din("w_ff_down", [8192, D])
        din("g_mix", [128, D])
        din("g_ffn", [128, D])
        din("g_fin", [128, D])
        din("g_q", [128, 512])
        din("g_kv", [128, 256])
        dout("y_own", [NOWN * 128, D])
        dout("st_ak", [NOWN * 128, D])
        dout("st_av", [NOWN * 128, D])
        dout("st_ix", [NOWN * 128, 64])
        dout("st_ckv", [NOWN * 128, 256])
        dout("st_kr", [NOWN * 128, 64])
        dscr("KT_A", [NH, 128, NTOK])
        dscr("V_A", [NH, 128, NTILE, 130])
        dscr("IXK_T", [64, NTOK])
        dscr("KT_B", [NH, 128, NTOK])
        dscr("KR_T", [64, NTOK])
        dscr("V_B", [NH, 128, NTILE, 130])
        dscr("QA_T", [5, 128, 16, 512])
        dscr("IXQ_T", [5, 128, 8, 512])
        dscr("IXW", [5, 128, 4, 16], F32)
        dscr("QN_T", [5, 128, 16, 512])
        dscr("QR_T", [5, 128, 8, 512])
        dscr("OA", [NOWN, 128, D], F32)
        dscr("OB", [NOWN, 128, D], F32)
        self.wg = weight_groups()
        for name, g in self.wg.items():
            dscr("W_" + name, [len(g["blocks"]), 128, g["K"] // 128, 512])

    def phase_W(self):
        P = self.P
        inp = self.inp
        urgent = ["uk", "uv", "misc", "ak", "av"]
        self._precast(urgent)
        KT_A, V_A, IXK_T, KR_T = self.scr["KT_A"], self.scr["V_A"], self.scr["IXK_T"], self.scr["KR_T"]
        for b in range(2):
            t0 = 140 + 9 * b
            for h in range(NH):
                P.dma(lambda e, b=b, h=h, t0=t0: e.dma_start(out=KT_A.t[h, :, t0 * 128:t0 * 128 + 1024], in_=inp["c_akT"][b, h]),
                      [], [KT_A.b], KT_A.b, q="pool")
                P.dma(lambda e, b=b, h=h, t0=t0: e.dma_start(
                    out=V_A.t[h, :, t0:t0 + 8, :],
                    in_=inp["c_av"][b, :, h, :].rearrange("(kt p) d -> p kt d", p=128)), [], [V_A.b], V_A.b, q="pool")
            P.dma(lambda e, b=b, t0=t0: e.dma_start(out=IXK_T.t[:, t0 * 128:t0 * 128 + 1024], in_=inp["c_ixkT"][b]),
                  [], [IXK_T.b], IXK_T.b, q="pool")
            P.dma(lambda e, b=b, t0=t0: e.dma_start(out=KR_T.t[:, t0 * 128:t0 * 128 + 1024], in_=inp["c_krT"][b]),
                  [], [KR_T.b], KR_T.b, q="pool")
        self._precast([k for k in self.wg if k not in urgent])

    def _precast(self, names):
        P = self.P
        inp = self.inp
        for name in names:
            g = self.wg[name]
            W = self.scr["W_" + name]
            KC = g["K"] // 128
            for bi, pieces in enumerate(g["blocks"]):
                off = 0
                for (src, r0, c0, ncols) in pieces:
                    if src in ("uqn", "uqr"):
                        i = c0
                        nh_, w_, lo_ = (4, 128, 0) if src == "uqn" else (8, 64, 128)
                        for j in range(nh_):
                            src_ap = inp["w_uq"][:, nh_ * i + j, lo_:lo_ + w_].rearrange("(kc p) d -> p kc d", p=128)
                            dst_ap = W.t[bi][:, :, j * w_:(j + 1) * w_]
                            P.dma(lambda e, d=dst_ap, s=src_ap: e.dma_start(out=d, in_=s), [], [W.b], W.b, q="pool")
                        continue
                    src_ap = inp[src][r0:r0 + g["K"], c0:c0 + ncols].rearrange("(kc p) c -> p kc c", p=128)
                    dst_ap = W.t[bi][:, :, off:off + ncols]
                    off += ncols
                    P.dma(lambda e, d=dst_ap, s=src_ap: e.dma_start(out=d, in_=s), [], [W.b], W.b, q="pool")

    def norm_rows(self, x_ap, n, g_ap, out_ap, xb, gb, outb, tmp):
        P = self.P
        junk, ss, sd, rstd = tmp["junk"], tmp["ss"], tmp["sd"], tmp["rstd"]
        P.op("act", lambda e: e.activation(out=junk.t[:, 0:n], in_=x_ap, func=AF.Square, accum_out=ss.t[:]),
             [xb], [junk.b, ss.b])
        P.op("act", lambda e: e.activation(out=sd.t[:], in_=ss.t[:], func=AF.Sqrt, bias=EPS, scale=1.0 / n),
             [ss.b], [sd.b])
        P.op("dve", lambda e: e.reciprocal(rstd.t[:], sd.t[:]), [sd.b], [rstd.b])
        P.op("dve", lambda e: e.scalar_tensor_tensor(out=out_ap, in0=x_ap, scalar=rstd.t[:], in1=g_ap,
                                                     op0=ALU.mult, op1=ALU.mult), [xb, rstd.b, gb], [outb])

    def load_norm_T(self, st, tiles, g_tb, hT, xts, hb, tmp, psT, identb, keep_x=None):
        P = self.P
        xall = self.inp["x_all"]
        for j, tile in enumerate(tiles):
            xt = keep_x[j] if keep_x is not None else xts.next()
            self.load(xt, xt.t[:], xall[tile * 128:(tile + 1) * 128, :])
            self.norm_rows(xt.t[:], D, g_tb.t[:], hb.t[:], xt.b, g_tb.b, hb.b, tmp)
            self.transpose_into(hb, hb.t, 16, hT, lambda kc, j=j: hT.t[:, kc, j * 128:(j + 1) * 128], psT, identb)

    def transpose_into(self, src, src_t, nchunks, dst, dst_ap_fn, psT, identb, rows=128):
        P = self.P
        c = 0
        while c < nchunks:
            n = min(4, nchunks - c)
            ps = psT.next()
            for k in range(n):
                self.mm(ps, ps.t[:, k * 128:(k + 1) * 128], src_t[:, (c + k) * 128:(c + k + 1) * 128], identb.t[:],
                        True, True, [src.b, identb.b])
            self._tflip = not getattr(self, "_tflip", False)
            for k in range(n):
                eng = "act" if self._tflip else "dve"
                d_ap = dst_ap_fn(c + k)
                s_ap = ps.t[:, k * 128:(k + 1) * 128]
                if eng == "act":
                    P.op("act", lambda e, d=d_ap, s=s_ap: e.activation(out=d, in_=s, func=AF.Copy), [ps.b], [dst.b])
                else:
                    P.op("dve", lambda e, d=d_ap, s=s_ap: e.tensor_copy(d, s), [ps.b], [dst.b])
            c += n

    def rope(self, out_ap1, out_ap2, x1, x2, cos, sin, xb, csb, outb, tmps):
        P = self.P
        t1, t2 = tmps
        P.op("dve", lambda e: e.tensor_tensor(t1.t_ap, x1, cos, ALU.mult), [xb, csb], [t1.b])
        P.op("dve", lambda e: e.tensor_tensor(t2.t_ap, x2, sin, ALU.mult), [xb, csb], [t2.b])
        P.op("dve", lambda e: e.tensor_tensor(out_ap1, t1.t_ap, t2.t_ap, ALU.subtract), [t1.b, t2.b], [outb])
        P.op("dve", lambda e: e.tensor_tensor(t1.t_ap, x1, sin, ALU.mult), [xb, csb, outb], [t1.b])
        P.op("dve", lambda e: e.tensor_tensor(t2.t_ap, x2, cos, ALU.mult), [xb, csb, outb], [t2.b])
        P.op("dve", lambda e: e.tensor_tensor(out_ap2, t1.t_ap, t2.t_ap, ALU.add), [t1.b, t2.b], [outb])

    def phase_K(self, blocks):
        P = self.P
        self.store_q = "act"
        nc = self.nc
        inp, out, scr = self.inp, self.out, self.scr
        with ExitStack() as st:
            sb = lambda name, shape, dt=F32: self.sb(st, name, shape, dt)
            g_mix = sb("k_gmix", [128, D])
            g_kv = sb("k_gkv", [128, 256])
            identf = sb("k_identf", [128, 128])
            identb = sb("k_identb", [128, 128], BF16)
            ones16 = sb("k_ones16", [128, 16, 1])
            wuk = [sb("k_wuk%d" % i, [128, 2, 512], BF16) for i in range(4)]
            wuv = [sb("k_wuv%d" % i, [128, 2, 512], BF16) for i in range(4)]
            xts = Rot([sb("k_x%d" % i, [128, D]) for i in range(2)])
            hb = sb("k_hb", [128, D], BF16)
            hT = sb("k_hT", [128, 16, 512], BF16)
            wts = Rot([sb("k_wt%d" % i, [128, 16, 512], BF16) for i in range(2)])
            ksts = Rot([sb("k_kst%d" % i, [128, 512], BF16) for i in range(2)])
            vst = [sb("k_vst%d" % i, [128, 16, 130], BF16) for i in range(4)]
            vbst = [sb("k_vbst%d" % i, [128, 16, 130], BF16) for i in range(4)]
            mst = [sb("k_mst%d" % i, [128, 384]) for i in range(4)]
            mo = [sb("k_mo%d" % i, [128, 384]) for i in range(4)]
            km = sb("k_km", [128, 384], BF16)
            ckvT = sb("k_ckvT", [128, 2, 512], BF16)
            ixkrT = sb("k_ixkrT", [128, 512], BF16)
            osts = Rot([sb("k_ost%d" % i, [128, 512]) for i in range(3)])
            cst = sb("k_cs", [128, 4, 64])
            vld = sb("k_vld", [128, 4])
            tmp = dict(junk=sb("k_junk", [128, D], BF16), ss=sb("k_ss", [128, 1]), sd=sb("k_sd", [128, 1]),
                       rstd=sb("k_rstd", [128, 1]))
            rt1 = sb("k_rt1", [128, 32])
            rt2 = sb("k_rt2", [128, 32])
            rt1.t_ap = rt1.t[:]
            rt2.t_ap = rt2.t[:]
            PS = self.PS
            psT = Rot([PS[0], PS[1]])
            psM = Rot([PS[2], PS[3], PS[4], PS[5]])
            psX = Rot([PS[6], PS[7]])

            self.load(g_mix, g_mix.t[:], inp["g_mix"][:, :])
            self.load(g_kv, g_kv.t[:], inp["g_kv"][:, :])
            self.load(identf, identf.t[:], inp["ident"][:, :])
            P.op("dve", lambda e: e.tensor_copy(identb.t[:], identf.t[:]), [identf.b], [identb.b])
            P.op("dve", lambda e: e.memset(ones16.t[:], 1.0), [], [ones16.b])
            for i in range(4):
                self.load(wuk[i], wuk[i].t[:], scr["W_uk"].t[i], reads=[scr["W_uk"].b])
                self.load(wuv[i], wuv[i].t[:], scr["W_uv"].t[i], reads=[scr["W_uv"].b])

            for kb in blocks:
              try:
                tiles = [4 * kb + j for j in range(4)]
                tok0 = 4 * kb * 128
                if kb == 34:
                    own = {0: 16, 1: 17, 2: 18, 3: 19}
                elif kb % 2 == 1:
                    own = {3: (kb - 1) // 2}
                else:
                    own = {}
                self.load(cst, cst.t[:], inp["cs"][tok0:tok0 + 512, :].rearrange("(j p) c -> p j c", p=128))
                self.load(vld, vld.t[:], inp["valid"][:, 4 * kb:4 * kb + 4])
                self.ck(1)
                self.load_norm_T(st, tiles, g_mix, hT, xts, hb, tmp, psT, identb)
                self.ck(2)
                for cb in range(4):
                    wt = wts.next()
                    self.load(wt, wt.t[:], scr["W_ak"].t[cb], reads=[scr["W_ak"].b])
                    for j in range(4):
                        head = 4 * cb + j
                        ps = psM.next()
                        for kc in range(16):
                            self.mm(ps, ps.t[:, :], wt.t[:, kc, j * 128:(j + 1) * 128], hT.t[:, kc, :], kc == 0, kc == 15,
                                    [wt.b, hT.b])
                        kst = ksts.next()
                        P.op("act", lambda e, kst=kst, ps=ps: e.activation(out=kst.t[:], in_=ps.t[:, :], func=AF.Copy),
                             [ps.b], [kst.b])
                        self.store_cols(kst, scr["KT_A"], lambda a, n, head=head: scr["KT_A"].t[head, :, a:a + n], kst.t, tiles)
                    for pos, o in own.items():
                        ps = psM.next()
                        for kc in range(16):
                            self.mm(ps, ps.t[:, :], hT.t[:, kc, pos * 128:(pos + 1) * 128], wt.t[:, kc, :], kc == 0, kc == 15,
                                    [wt.b, hT.b])
                        ost = osts.next()
                        P.op("act", lambda e, ost=ost, ps=ps: e.activation(out=ost.t[:], in_=ps.t[:, :], func=AF.Copy),
                             [ps.b], [ost.b])
                        self.store(ost, out["st_ak"][o * 128:(o + 1) * 128, cb * 512:(cb + 1) * 512], ost.t[:])
                self.ck(3)
                for cb in range(4):
                    wt = wts.next()
                    self.load(wt, wt.t[:], scr["W_av"].t[cb], reads=[scr["W_av"].b])
                    for pos in range(4):
                        ps = psM.next()
                        for kc in range(16):
                            self.mm(ps, ps.t[:, :], hT.t[:, kc, pos * 128:(pos + 1) * 128], wt.t[:, kc, :], kc == 0, kc == 15,
                                    [wt.b, hT.b])
                        v = vst[pos]
                        P.op("dve", lambda e, v=v, ps=ps, cb=cb: e.tensor_copy(
                            v.t[:, 4 * cb:4 * cb + 4, 0:128], ps.t[:, :].rearrange("p (h d) -> p h d", h=4)), [ps.b], [v.b])
                        if pos in own:
                            o = own[pos]
                            ost = osts.next()
                            P.op("dve", lambda e, ost=ost, ps=ps: e.tensor_copy(ost.t[:], ps.t[:, :]), [ps.b], [ost.b])
                            self.store(ost, out["st_av"][o * 128:(o + 1) * 128, cb * 512:(cb + 1) * 512], ost.t[:])
                for pos in range(4):
                    v = vst[pos]
                    P.op("dve", lambda e, v=v, pos=pos: e.tensor_scalar(v.t[:, :, 128:129], ones16.t[:], vld.t[:, pos:pos + 1], None,
                                                                        ALU.mult), [ones16.b, vld.b], [v.b])
                    self.store(v, scr["V_A"].t[:, :, self.tmap(tiles[pos]), 0:129].rearrange("h p c -> p h c"), v.t[:, :, 0:129],
                               scr["V_A"])
                self.ck(4)
                wt = wts.next()
                self.load(wt, wt.t[:, :, 0:384], scr["W_misc"].t[0][:, :, 0:384], reads=[scr["W_misc"].b])
                for pos in range(4):
                    ps = psX.next()
                    for kc in range(16):
                        self.mm(ps, ps.t[:, 0:384], hT.t[:, kc, pos * 128:(pos + 1) * 128], wt.t[:, kc, 0:384], kc == 0, kc == 15,
                                [wt.b, hT.b])
                    m = mst[pos]
                    P.op("act", lambda e, m=m, ps=ps: e.activation(out=m.t[:], in_=ps.t[:, 0:384], func=AF.Copy), [ps.b], [m.b])
                self.ck(41)
                for pos in range(4):
                    m = mst[pos]
                    o_ = mo[pos]
                    self.norm_rows(m.t[:, 0:256], 256, g_kv.t[:], o_.t[:, 0:256], m.b, g_kv.b, o_.b, tmp)
                    self.ck(42)
                    P.op("act", lambda e, m=m, o_=o_: e.activation(out=o_.t[:, 256:320], in_=m.t[:, 256:320], func=AF.Copy),
                         [m.b], [o_.b])
                    self.rope(o_.t[:, 320:352], o_.t[:, 352:384], m.t[:, 320:352], m.t[:, 352:384],
                              cst.t[:, pos, 0:32], cst.t[:, pos, 32:64], m.b, cst.b, o_.b, (rt1, rt2))
                    self.ck(43)
                    if pos in own:
                        o = own[pos]
                        self.store(o_, out["st_ckv"][o * 128:(o + 1) * 128, :], o_.t[:, 0:256])
                        self.store(o_, out["st_ix"][o * 128:(o + 1) * 128, :], o_.t[:, 256:320])
                        self.store(o_, out["st_kr"][o * 128:(o + 1) * 128, :], o_.t[:, 320:384])
                    P.op("act", lambda e, o_=o_: e.activation(out=km.t[:], in_=o_.t[:], func=AF.Copy), [o_.b], [km.b])
                    self.ck(431)
                    ps = psX.next()
                    for k in range(3):
                        self.mm(ps, ps.t[:, k * 128:(k + 1) * 128], km.t[:, k * 128:(k + 1) * 128], identb.t[:], True, True,
                                [km.b, identb.b])
                    self.ck(432)
                    P.op("dve", lambda e, ps=ps, pos=pos: e.tensor_copy(
                        ckvT.t[:, :, pos * 128:(pos + 1) * 128], ps.t[:, 0:256].rearrange("p (c t) -> p c t", c=2)),
                        [ps.b], [ckvT.b])
                    self.ck(4321)
                    P.op("dve", lambda e, ps=ps, pos=pos: e.tensor_copy(ixkrT.t[:, pos * 128:(pos + 1) * 128], ps.t[:, 256:384]),
                         [ps.b], [ixkrT.b])
                    self.ck(433)
                self.ck(44)
                self.store_cols(ixkrT, scr["IXK_T"], lambda a, n: scr["IXK_T"].t[:, a:a + n], ixkrT.t[0:64], tiles)
                self.store_cols(ixkrT, scr["KR_T"], lambda a, n: scr["KR_T"].t[:, a:a + n], ixkrT.t[64:128], tiles)
                self.ck(5)
                if int(os.environ.get("MK_KSTOP", "0")) == 6:
                    self.mla_kside(ckvT, wuk, wuv, psM, ksts, vbst, tiles, tok0, lambda pos: vld.t[:, pos:pos + 1], vld, ones16)
                self.ck(6)
                self.mla_kside(ckvT, wuk, wuv, psM, ksts, vbst, tiles, tok0, lambda pos: vld.t[:, pos:pos + 1], vld, ones16)
              except _Stop:
                break
            self.store_q = "pool"
            P.barrier()
            self.release_scope(locals())

    def mla_kside(self, ckvT, wuk, wuv, psM, ksts, vbst, tiles, tok0, vld_ap_fn, vld, ones16, ncols=512):
        P = self.P
        scr = self.scr
        for h in range(NH):
            ps = psM.next()
            for cc in range(2):
                self.mm(ps, ps.t[:, 0:ncols], wuk[h // 4].t[:, cc, (h % 4) * 128:(h % 4 + 1) * 128], ckvT.t[:, cc, 0:ncols],
                        cc == 0, cc == 1, [wuk[h // 4].b, ckvT.b])
            kst = ksts.next()
            P.op("act", lambda e, kst=kst, ps=ps: e.activation(out=kst.t[:, 0:ncols], in_=ps.t[:, 0:ncols], func=AF.Copy),
                 [ps.b], [kst.b])
            self.store_cols(kst, scr["KT_B"], lambda a, n, h=h: scr["KT_B"].t[h, :, a:a + n], kst.t, tiles)
        for pos in range(len(tiles)):
            v = vbst[pos % len(vbst)]
            for cb in range(4):
                ps = psM.next()
                for cc in range(2):
                    self.mm(ps, ps.t[:, :], ckvT.t[:, cc, pos * 128:(pos + 1) * 128], wuv[cb].t[:, cc, :], cc == 0, cc == 1,
                            [wuv[cb].b, ckvT.b])
                P.op("dve", lambda e, v=v, ps=ps, cb=cb: e.tensor_copy(
                    v.t[:, 4 * cb:4 * cb + 4, 0:128], ps.t[:, :].rearrange("p (h d) -> p h d", h=4)), [ps.b], [v.b])
            if vld is not None:
                P.op("dve", lambda e, v=v, pos=pos: e.tensor_scalar(v.t[:, :, 128:129], ones16.t[:], vld_ap_fn(pos), None,
                                                                    ALU.mult), [ones16.b, vld.b], [v.b])
                self.store(v, scr["V_B"].t[:, :, self.tmap(tiles[pos]), 0:129].rearrange("h p c -> p h c"), v.t[:, :, 0:129], scr["V_B"])
            else:
                self.store(v, scr["V_B"].t[:, :, self.tmap(tiles[pos]), 0:129].rearrange("h p c -> p h c"), v.t[:, :, 0:129], scr["V_B"])

    def release_scope(self, loc):
        bufs = []
        for v in loc.values():
            if isinstance(v, TB):
                bufs.append(v.b)
            elif isinstance(v, (list, tuple)):
                for x in v:
                    if isinstance(x, TB):
                        bufs.append(x.b)
            elif isinstance(v, Rot):
                for x in v.items:
                    if isinstance(x, TB):
                        bufs.append(x.b)
            elif isinstance(v, dict):
                for x in v.values():
                    if isinstance(x, TB):
                        bufs.append(x.b)
        self.P.release(bufs)

    def phase_KC(self):
        P = self.P
        inp, scr = self.inp, self.scr
        with ExitStack() as st:
            sb = lambda name, shape, dt=F32: self.sb(st, name, shape, dt)
            wuk = [sb("c_wuk%d" % i, [128, 2, 512], BF16) for i in range(4)]
            wuv = [sb("c_wuv%d" % i, [128, 2, 512], BF16) for i in range(4)]
            ckvT = sb("c_ckvT", [128, 2, 512], BF16)
            ksts = Rot([sb("c_kst%d" % i, [128, 512], BF16) for i in range(2)])
            vbst = [sb("c_vbst%d" % i, [128, 16, 130], BF16) for i in range(4)]
            PS = self.PS
            psM = Rot([PS[2], PS[3], PS[4], PS[5]])
            for i in range(4):
                self.load(wuk[i], wuk[i].t[:], scr["W_uk"].t[i], reads=[scr["W_uk"].b])
                self.load(wuv[i], wuv[i].t[:], scr["W_uv"].t[i], reads=[scr["W_uv"].b])
                P.op("dve", lambda e, v=vbst[i]: e.memset(v.t[:, :, 128:130], 1.0), [], [vbst[i].b])
            for b in range(2):
                for half in range(2):
                    t0 = 140 + 9 * b + 4 * half
                    self.load(ckvT, ckvT.t[:], inp["c_ckvT"][b, :, half * 512:(half + 1) * 512].rearrange("(c p) s -> p c s", p=128),
                              q="pool")
                    self.mla_kside(ckvT, wuk, wuv, psM, ksts, vbst, [t0 + j for j in range(4)], t0 * 128, None, None, None)
            P.barrier()
            self.release_scope(locals())


    @staticmethod
    def tmap(t):
        return {136: 148, 137: 157}.get(t, t)

    def store_cols(self, src, dst, dst_row_ap_fn, src_t, tiles):
        mapped = [self.tmap(t) for t in tiles]
        if all(mapped[i] == mapped[0] + i for i in range(len(mapped))):
            self.store(src, dst_row_ap_fn(mapped[0] * 128, len(mapped) * 128), src_t[:, 0:len(mapped) * 128], dst)
        else:
            for i, mt in enumerate(mapped):
                self.store(src, dst_row_ap_fn(mt * 128, 128), src_t[:, i * 128:(i + 1) * 128], dst)

    def own_tiles(self, b):
        return [136 + j for j in range(4)] if b == 4 else [8 * (4 * b + j) + 7 for j in range(4)]

    def phase_Q(self):
        P = self.P
        inp, scr = self.inp, self.scr
        with ExitStack() as st:
            sb = lambda name, shape, dt=F32: self.sb(st, name, shape, dt)
            g_mix = sb("q_gmix", [128, D])
            g_q = sb("q_gq", [128, 512])
            identf = sb("q_identf", [128, 128])
            identb = sb("q_identb", [128, 128], BF16)
            xts = Rot([sb("q_x%d" % i, [128, D]) for i in range(2)])
            hb = sb("q_hb", [128, D], BF16)
            hT = sb("q_hT", [128, 16, 512], BF16)
            wts = Rot([sb("q_wt%d" % i, [128, 16, 512], BF16) for i in range(2)])
            qsts = Rot([sb("q_qst%d" % i, [128, 512], BF16) for i in range(3)])
            wuqn = [sb("q_wuqn%d" % i, [128, 4, 512], BF16) for i in range(4)]
            wuqr = [sb("q_wuqr%d" % i, [128, 4, 512], BF16) for i in range(2)]
            cq_f = sb("q_cqf", [128, 512])
            cqb = sb("q_cqb", [128, 512], BF16)
            cqT = sb("q_cqT", [128, 4, 512], BF16)
            qr_f = sb("q_qrf", [128, 1024])
            qr_o = sb("q_qro", [128, 1024])
            qrb = sb("q_qrb", [128, 1024], BF16)
            qrT = sb("q_qrT", [128, 8, 512], BF16)
            ixw = sb("q_ixw", [128, 4, 16])
            cst = sb("q_cs", [128, 4, 64])
            tmp = dict(junk=sb("q_junk", [128, D], BF16), ss=sb("q_ss", [128, 1]), sd=sb("q_sd", [128, 1]),
                       rstd=sb("q_rstd", [128, 1]))
            rt1 = sb("q_rt1", [128, 32])
            rt2 = sb("q_rt2", [128, 32])
            rt1.t_ap = rt1.t[:]
            rt2.t_ap = rt2.t[:]
            PS = self.PS
            psT = Rot([PS[0], PS[1]])
            psM = Rot([PS[2], PS[3], PS[4], PS[5]])
            psX = Rot([PS[6], PS[7]])
            self.load(g_mix, g_mix.t[:], inp["g_mix"][:, :])
            self.load(g_q, g_q.t[:], inp["g_q"][:, :])
            self.load(identf, identf.t[:], inp["ident"][:, :])
            P.op("dve", lambda e: e.tensor_copy(identb.t[:], identf.t[:]), [identf.b], [identb.b])
            for i in range(4):
                self.load(wuqn[i], wuqn[i].t[:], scr["W_uqn"].t[i], reads=[scr["W_uqn"].b])
            for i in range(2):
                self.load(wuqr[i], wuqr[i].t[:], scr["W_uqr"].t[i], reads=[scr["W_uqr"].b])
            for b in range(5):
                tiles = self.own_tiles(b)
                for j, tile in enumerate(tiles):
                    self.load(cst, cst.t[:, j, :], inp["cs"][tile * 128:(tile + 1) * 128, :])
                self.load_norm_T(st, tiles, g_mix, hT, xts, hb, tmp, psT, identb)
                for grp, nblk, dst in (("aq", 4, "QA_T"), ("ixq", 2, "IXQ_T")):
                    for cb in range(nblk):
                        wt = wts.next()
                        self.load(wt, wt.t[:], scr["W_" + grp].t[cb], reads=[scr["W_" + grp].b])
                        for j in range(4):
                            ps = psM.next()
                            for kc in range(16):
                                self.mm(ps, ps.t[:, :], wt.t[:, kc, j * 128:(j + 1) * 128], hT.t[:, kc, :], kc == 0, kc == 15,
                                        [wt.b, hT.b])
                            q = qsts.next()
                            P.op("act", lambda e, q=q, ps=ps: e.activation(out=q.t[:], in_=ps.t[:, :], func=AF.Copy), [ps.b], [q.b])
                            self.store(q, scr[dst].t[b][:, 4 * cb + j, :], q.t[:], scr[dst])
                wt = wts.next()
                self.load(wt, wt.t[:], scr["W_cq"].t[0], reads=[scr["W_cq"].b])
                for pos in range(4):
                    ps = psM.next()
                    for kc in range(16):
                        self.mm(ps, ps.t[:, :], hT.t[:, kc, pos * 128:(pos + 1) * 128], wt.t[:, kc, :], kc == 0, kc == 15, [wt.b, hT.b])
                    P.op("act", lambda e, ps=ps: e.activation(out=cq_f.t[:], in_=ps.t[:, :], func=AF.Copy), [ps.b], [cq_f.b])
                    self.norm_rows(cq_f.t[:], 512, g_q.t[:], cqb.t[:], cq_f.b, g_q.b, cqb.b, tmp)
                    self.transpose_into(cqb, cqb.t, 4, cqT, lambda c, pos=pos: cqT.t[:, c, pos * 128:(pos + 1) * 128], psT, identb)
                wt = wts.next()
                self.load(wt, wt.t[:, :, 0:16], scr["W_cq"].t[1][:, :, 0:16], reads=[scr["W_cq"].b])
                for pos in range(4):
                    ps = psX.next()
                    for kc in range(16):
                        self.mm(ps, ps.t[:, 0:16], hT.t[:, kc, pos * 128:(pos + 1) * 128], wt.t[:, kc, 0:16], kc == 0, kc == 15,
                                [wt.b, hT.b])
                    P.op("act", lambda e, ps=ps, pos=pos: e.activation(out=ixw.t[:, pos, :], in_=ps.t[:, 0:16], func=AF.Copy, scale=0.25),
                         [ps.b], [ixw.b])
                self.store(ixw, scr["IXW"].t[b], ixw.t[:], scr["IXW"])
                for h in range(NH):
                    ps = psM.next()
                    for cc in range(4):
                        self.mm(ps, ps.t[:, :], wuqn[h // 4].t[:, cc, (h % 4) * 128:(h % 4 + 1) * 128], cqT.t[:, cc, :], cc == 0, cc == 3,
                                [wuqn[h // 4].b, cqT.b])
                    q = qsts.next()
                    P.op("act", lambda e, q=q, ps=ps: e.activation(out=q.t[:], in_=ps.t[:, :], func=AF.Copy), [ps.b], [q.b])
                    self.store(q, scr["QN_T"].t[b][:, h, :], q.t[:], scr["QN_T"])
                for pos in range(4):
                    for blk in range(2):
                        ps = psM.next()
                        for cc in range(4):
                            self.mm(ps, ps.t[:, :], cqT.t[:, cc, pos * 128:(pos + 1) * 128], wuqr[blk].t[:, cc, :], cc == 0, cc == 3,
                                    [wuqr[blk].b, cqT.b])
                        P.op("act", lambda e, ps=ps, blk=blk: e.activation(out=qr_f.t[:, blk * 512:(blk + 1) * 512], in_=ps.t[:, :],
                                                                           func=AF.Copy), [ps.b], [qr_f.b])
                    for h in range(NH):
                        c0 = h * 64
                        self.rope(qr_o.t[:, c0:c0 + 32], qr_o.t[:, c0 + 32:c0 + 64], qr_f.t[:, c0:c0 + 32], qr_f.t[:, c0 + 32:c0 + 64],
                                  cst.t[:, pos, 0:32], cst.t[:, pos, 32:64], qr_f.b, cst.b, qr_o.b, (rt1, rt2))
                    P.op("act", lambda e: e.activation(out=qrb.t[:], in_=qr_o.t[:], func=AF.Copy), [qr_o.b], [qrb.b])
                    self.transpose_into(qrb, qrb.t, 8, qrT, lambda c, pos=pos: qrT.t[:, c, pos * 128:(pos + 1) * 128], psT, identb)
                self.store(qrT, scr["QR_T"].t[b], qrT.t[:], scr["QR_T"])
            P.barrier()
            self.release_scope(locals())

    def slots_all(self):
        P = self.P
        inp, scr = self.inp, self.scr
        with ExitStack() as st:
            sb = lambda name, shape, dt=F32: self.sb(st, name, shape, dt)
            C = {}
            C["identf"] = sb("s_identf", [128, 128])
            C["identb"] = sb("s_identb", [128, 128], BF16)
            C["Tb"] = sb("s_Tb", [128, 16, 2, 128])
            C["constb"] = sb("s_constb", [128, 16])
            C["prefmask"] = sb("s_pref", [128, 896])
            C["dsa_diag"] = sb("s_dsadiag", [128, 2, 128])
            C["mlaf"] = sb("s_mlaf", [128, 2, 256])
            C["mlamask"] = sb("s_mlamask", [128, 2, 256], BF16)
            relb = sb("s_relb", [32, 16])
            oh = sb("s_oh", [32, 4096])
            self.load(C["identf"], C["identf"].t[:], inp["ident"][:, :])
            P.op("dve", lambda e: e.tensor_copy(C["identb"].t[:], C["identf"].t[:]), [C["identf"].b], [C["identb"].b])
            self.load(C["constb"], C["constb"].t[:], inp["constb"][:, :])
            self.load(C["prefmask"], C["prefmask"].t[:], inp["prefmask"][:, :])
            self.load(C["dsa_diag"], C["dsa_diag"].t[:], inp["dsa_diag"].rearrange("y t s -> t y s"))
            self.load(C["mlaf"], C["mlaf"].t[:], inp["mla_mask"].rearrange("y s c -> s y c"))
            P.op("dve", lambda e: e.tensor_copy(C["mlamask"].t[:], C["mlaf"].t[:]), [C["mlaf"].b], [C["mlamask"].b])
            self.load(relb, relb.t[:], inp["relb"][:, :])
            PS = self.PS
            for ty in range(2):
                for tg in range(4):
                    self.load(oh, oh.t[:], inp["onehot"][ty][:, tg * 4096:(tg + 1) * 4096])
                    ps = PS[tg % 2]
                    for t in range(32):
                        self.mm(ps, ps.t[:, t * 16:(t + 1) * 16], oh.t[:, t * 128:(t + 1) * 128], relb.t[:, :], True, True, [oh.b, relb.b])
                    P.op("dve", lambda e, ps=ps, ty=ty, tg=tg: e.tensor_copy(
                        C["Tb"].t[:, :, ty, tg * 32:(tg + 1) * 32], ps.t[:, :].rearrange("p (t h) -> p h t", h=16)), [ps.b], [C["Tb"].b])
            P.barrier()
            nslot = int(os.environ.get("MK_NSLOT", "18"))
            order = []
            for o in range(16):
                order.append((o // 4, o % 4, [(0, 8 * o + 8)], 0))
            order.append((4, 0, [(140, 9)], 1))
            order.append((4, 1, [(149, 9)], 1))
            if nslot < 18:
                order = [order[0], order[16], order[1], order[17]][:nslot]
            for (b, j, runs, ty) in order:
                self.slot(C, b, j, runs, ty)
            self.release_scope(dict(C=C, relb=relb, oh=oh))

    def slot(self, C, b, j, runs, ty):
        P = self.P
        inp, scr = self.inp, self.scr
        PS = self.PS
        o = 4 * b + j
        ktl = []
        for (t0, n) in runs:
            ktl += [t0 + i for i in range(n)]
        nk = len(ktl)
        S = nk * 128
        identb = C["identb"]
        with ExitStack() as st:
            sb = lambda name, shape, dt=F32: self.sb(st, name, shape, dt)
            qa = sb("l_qa", [128, 16, 128], BF16)
            ixq = sb("l_ixq", [128, 8, 128], BF16)
            ixw = sb("l_ixw", [128, 16])
            qn = sb("l_qn", [128, 16, 128], BF16)
            qr = sb("l_qr", [128, 8, 128], BF16)
            maskadd = sb("l_maskadd", [128, S], BF16)
            js = slice(j * 128, (j + 1) * 128)
            self.load(qa, qa.t[:], scr["QA_T"].t[b][:, :, js], reads=[scr["QA_T"].b])
            self.load(ixq, ixq.t[:], scr["IXQ_T"].t[b][:, :, js], reads=[scr["IXQ_T"].b])
            self.load(ixw, ixw.t[:], scr["IXW"].t[b][:, j, :], reads=[scr["IXW"].b])
            self.load(qn, qn.t[:], scr["QN_T"].t[b][:, :, js], reads=[scr["QN_T"].b])
            self.load(qr, qr.t[:], scr["QR_T"].t[b][:, :, js], reads=[scr["QR_T"].b])
            with ExitStack() as st2:
                sb2 = lambda name, shape, dt=F32: self.sb(st2, name, shape, dt)
                row = sb2("i_row", [128, S])
                ixks = Rot([sb2("i_ixk%d" % i, [128, 512], BF16) for i in range(2)])
                rbs = Rot([sb2("i_r%d" % i, [128, 512]) for i in range(3)])
                mx = sb2("i_mx", [128, 1])
                mid = sb2("i_mid", [128, 1])
                cnt = sb2("i_cnt", [128, 1])
                tfl = sb2("i_tfl", [128, 1])
                thr = sb2("i_thr", [128, 1])
                psR = Rot([PS[0], PS[1], PS[2], PS[3]])
                col = 0
                for (t0, n) in runs:
                    k = 0
                    while k < n:
                        g = min(4, n - k)
                        W = g * 128
                        tok = (t0 + k) * 128
                        ixk = ixks.next()
                        self.load(ixk, ixk.t[0:64, 0:W], scr["IXK_T"].t[:, tok:tok + W], reads=[scr["IXK_T"].b])
                        self.load(ixk, ixk.t[64:128, 0:W], scr["IXK_T"].t[:, tok:tok + W], reads=[scr["IXK_T"].b])
                        for h in range(16):
                            hf = h % 2
                            ps = psR.next()
                            self.mm(ps, ps.t[:, 0:W], ixq.t[hf * 64:(hf + 1) * 64, h // 2, :], ixk.t[hf * 64:(hf + 1) * 64, 0:W],
                                    True, True, [ixq.b, ixk.b])
                            r = rbs.next()
                            P.op("act", lambda e, r=r, ps=ps, W=W: e.activation(out=r.t[:, 0:W], in_=ps.t[:, 0:W], func=AF.Relu),
                                 [ps.b], [r.b])
                            if h == 0:
                                P.op("dve", lambda e, r=r, W=W, col=col: e.tensor_scalar(
                                    row.t[:, col:col + W], r.t[:, 0:W], ixw.t[:, 0:1], None, ALU.mult), [r.b, ixw.b], [row.b])
                            else:
                                P.op("dve", lambda e, r=r, W=W, col=col, h=h: e.scalar_tensor_tensor(
                                    out=row.t[:, col:col + W], in0=r.t[:, 0:W], scalar=ixw.t[:, h:h + 1], in1=row.t[:, col:col + W],
                                    op0=ALU.mult, op1=ALU.add), [r.b, ixw.b, row.b], [row.b])
                        col += W
                        k += g
                if ty == 0:
                    P.op("dve", lambda e: e.tensor_tensor(row.t[:, 0:896], row.t[:, 0:896], C["prefmask"].t[:], ALU.add),
                         [row.b, C["prefmask"].b], [row.b])
                P.op("dve", lambda e: e.tensor_tensor(row.t[:, S - 128:S], row.t[:, S - 128:S], C["dsa_diag"].t[:, ty, :], ALU.add),
                     [row.b, C["dsa_diag"].b], [row.b])
                P.op("dve", lambda e: e.tensor_reduce(mx.t[:], row.t[:], AX.X, ALU.max), [row.b], [mx.b])
                P.op("dve", lambda e: e.tensor_scalar(mid.t[:], mx.t[:], -BIS_R / 2, None, ALU.add), [mx.b], [mid.b])
                for it in range(BIS_IT):
                    hw = BIS_R / (2 ** (it + 1))
                    P.op("dve", lambda e: e.tensor_scalar(maskadd.t[:], row.t[:], mid.t[:], None, ALU.is_ge, ALU.add, accum_out=cnt.t[:]),
                         [row.b, mid.b], [maskadd.b, cnt.b])
                    P.op("dve", lambda e, hw=hw: e.tensor_scalar(tfl.t[:], cnt.t[:], float(TOPK) - 0.5, hw, ALU.is_ge, ALU.mult),
                         [cnt.b], [tfl.b])
                    P.op("dve", lambda e, hw=hw: e.scalar_tensor_tensor(out=mid.t[:], in0=tfl.t[:], scalar=-hw / 2, in1=mid.t[:],
                                                                        op0=ALU.add, op1=ALU.add), [tfl.b, mid.b], [mid.b])
                hwK = BIS_R / (2 ** (BIS_IT + 1))
                P.op("dve", lambda e: e.tensor_scalar(thr.t[:], mid.t[:], -hwK, None, ALU.add), [mid.b], [thr.b])
                P.op("dve", lambda e: e.tensor_scalar(maskadd.t[:], row.t[:], thr.t[:], MASKNEG, ALU.is_lt, ALU.mult),
                     [row.b, thr.b], [maskadd.b])
                P.barrier()
                self.release_scope(locals())
            with ExitStack() as st3:
                sb3 = lambda name, shape, dt=F32: self.sb(st3, name, shape, dt)
                kcs = Rot([sb3("a_kc%d" % i, [128, 2048], BF16) for i in range(2)])
                vcs = Rot([sb3("a_vc%d" % i, [128, 16, 130], BF16) for i in range(2)])
                krr = sb3("a_kr", [128, S], BF16)
                pts = Rot([sb3("a_p%d" % i, [128, 512], BF16) for i in range(3)])
                p2s = Rot([sb3("a_p2%d" % i, [128, 256], BF16) for i in range(2)])
                tmpn = sb3("a_tmpn", [128, 256])
                rec = sb3("a_rec", [128, 1])
                ost = sb3("a_ost", [128, D])
                psS = Rot([PS[0], PS[1], PS[2], PS[3]])
                psA = Rot([PS[4], PS[5]])
                col = 0
                for (t0, n) in runs:
                    self.load(krr, krr.t[0:64, col:col + n * 128], scr["KR_T"].t[:, t0 * 128:(t0 + n) * 128], reads=[scr["KR_T"].b])
                    self.load(krr, krr.t[64:128, col:col + n * 128], scr["KR_T"].t[:, t0 * 128:(t0 + n) * 128], reads=[scr["KR_T"].b])
                    col += n * 128
                chunks = []
                gk = 0
                for (t0, n) in runs:
                    k = 0
                    while k < n:
                        m = min(16, n - k)
                        chunks.append((t0 + k, m, gk))
                        gk += m
                        k += m
                for kind in ("dsa", "mla"):
                    KT = scr["KT_A"] if kind == "dsa" else scr["KT_B"]
                    VV = scr["V_A"] if kind == "dsa" else scr["V_B"]
                    for h in range(NH):
                        acc = psA.next()
                        hf = h % 2
                        for (t0, m, gk0) in chunks:
                            kc = kcs.next()
                            vc = vcs.next()
                            self.load(kc, kc.t[:, 0:m * 128], KT.t[h, :, t0 * 128:(t0 + m) * 128], reads=[KT.b])
                            self.load(vc, vc.t[:, 0:m, :], VV.t[h, :, t0:t0 + m, :], reads=[VV.b])
                            k = 0
                            while k < m:
                                gkt = gk0 + k
                                if gkt >= nk - 2:
                                    g = nk - gkt
                                    near = True
                                else:
                                    g = min(4, m - k, nk - 2 - gkt)
                                    near = False
                                Sps = psS.next()
                                for gi in range(g):
                                    cs_ = slice(gi * 128, (gi + 1) * 128)
                                    kl = k + gi
                                    if kind == "dsa":
                                        self.mm(Sps, Sps.t[:, cs_], kc.t[:, kl * 128:(kl + 1) * 128], qa.t[:, h, :], True, False,
                                                [kc.b, qa.b])
                                        self.mm(Sps, Sps.t[:, cs_], maskadd.t[:, (gkt + gi) * 128:(gkt + gi + 1) * 128], identb.t[:],
                                                False, True, [maskadd.b, identb.b])
                                    else:
                                        self.mm(Sps, Sps.t[:, cs_], kc.t[:, kl * 128:(kl + 1) * 128], qn.t[:, h, :], True, False,
                                                [kc.b, qn.b])
                                        self.mm(Sps, Sps.t[:, cs_], krr.t[hf * 64:(hf + 1) * 64, (gkt + gi) * 128:(gkt + gi + 1) * 128],
                                                qr.t[hf * 64:(hf + 1) * 64, h // 2, :], False, True, [krr.b, qr.b])
                                W = g * 128
                                p = pts.next()
                                if kind == "dsa":
                                    if near:
                                        P.op("dve", lambda e, Sps=Sps, h=h: e.scalar_tensor_tensor(
                                            out=tmpn.t[:], in0=Sps.t[:, 0:256], scalar=A_SCALE,
                                            in1=C["Tb"].t[:, h, :, :].rearrange("p y t -> p (y t)"), op0=ALU.mult, op1=ALU.add),
                                            [Sps.b, C["Tb"].b], [tmpn.b])
                                        P.op("act", lambda e, p=p: e.activation(out=p.t[:, 0:256], in_=tmpn.t[:], func=AF.Exp),
                                             [tmpn.b], [p.b])
                                    else:
                                        P.op("act", lambda e, p=p, Sps=Sps, W=W, h=h: e.activation(
                                            out=p.t[:, 0:W], in_=Sps.t[:, 0:W], func=AF.Exp, bias=C["constb"].t[:, h:h + 1], scale=A_SCALE),
                                            [Sps.b, C["constb"].b], [p.b])
                                    pp = p
                                else:
                                    P.op("act", lambda e, p=p, Sps=Sps, W=W: e.activation(out=p.t[:, 0:W], in_=Sps.t[:, 0:W], func=AF.Exp,
                                                                                       scale=MLA_SCALE), [Sps.b], [p.b])
                                    pp = p
                                    if near:
                                        p2 = p2s.next()
                                        P.op("dve", lambda e, p=p, p2=p2: e.tensor_tensor(p2.t[:], p.t[:, 0:256], C["mlamask"].t[:, ty, :],
                                                                                         ALU.mult), [p.b, C["mlamask"].b], [p2.b])
                                        pp = p2
                                for gi in range(g):
                                    kl = k + gi
                                    self.mm(acc, acc.t[:, 0:129], pp.t[:, gi * 128:(gi + 1) * 128], vc.t[:, kl, 0:129],
                                            (gkt + gi) == 0, (gkt + gi) == nk - 1, [pp.b, vc.b])
                                k += g
                        P.op("dve", lambda e, acc=acc: e.reciprocal(rec.t[:], acc.t[:, 128:129]), [acc.b], [rec.b])
                        P.op("dve", lambda e, acc=acc, h=h: e.tensor_scalar(ost.t[:, h * 128:(h + 1) * 128], acc.t[:, 0:128], rec.t[:], None,
                                                                            ALU.mult), [acc.b, rec.b], [ost.b])
                    dst = scr["OA"] if kind == "dsa" else scr["OB"]
                    self.store(ost, dst.t[o], ost.t[:], dst)
                P.barrier()
                self.release_scope(locals())
            self.release_scope(dict(qa=qa, ixq=ixq, ixw=ixw, qn=qn, qr=qr, maskadd=maskadd))

    def phase_M(self):
        P = self.P
        inp, out, scr = self.inp, self.out, self.scr
        PS = self.PS
        nb = int(os.environ.get("MK_NMB", "5"))
        for b in list(range(5))[:nb] if nb >= 5 else [0, 4][:nb]:
            tiles = self.own_tiles(b)
            with ExitStack() as so:
                xk = [self.sb(so, "m_x%d" % i, [128, D]) for i in range(4)]
                with ExitStack() as st:
                    sb = lambda name, shape, dt=F32: self.sb(st, name, shape, dt)
                    g_mix = sb("m_gmix", [128, D])
                    identf = sb("m_identf", [128, 128])
                    identb = sb("m_identb", [128, 128], BF16)
                    hb = sb("m_hb", [128, D], BF16)
                    hT = sb("m_hT", [128, 16, 512], BF16)
                    wts = Rot([sb("m_wt%d" % i, [128, 16, 512], BF16) for i in range(2)])
                    gas = Rot([sb("m_ga%d" % i, [128, 512]) for i in range(2)])
                    gbs = Rot([sb("m_gb%d" % i, [128, 512]) for i in range(2)])
                    oas = Rot([sb("m_oa%d" % i, [128, 512]) for i in range(2)])
                    obs = Rot([sb("m_ob%d" % i, [128, 512]) for i in range(2)])
                    mix = [sb("m_mix%d" % i, [128, D], BF16) for i in range(4)]
                    tmp = dict(junk=sb("m_junk", [128, D], BF16), ss=sb("m_ss", [128, 1]), sd=sb("m_sd", [128, 1]),
                               rstd=sb("m_rstd", [128, 1]))
                    psT = Rot([PS[0], PS[1]])
                    psM = Rot([PS[2], PS[3], PS[4], PS[5]])
                    self.load(g_mix, g_mix.t[:], inp["g_mix"][:, :])
                    self.load(identf, identf.t[:], inp["ident"][:, :])
                    P.op("dve", lambda e: e.tensor_copy(identb.t[:], identf.t[:]), [identf.b], [identb.b])
                    self.load_norm_T(st, tiles, g_mix, hT, None, hb, tmp, psT, identb, keep_x=xk)
                    for i in range(4):
                        wa = wts.next()
                        wb = wts.next()
                        self.load(wa, wa.t[:], scr["W_gate"].t[i], reads=[scr["W_gate"].b])
                        self.load(wb, wb.t[:], scr["W_gate"].t[4 + i], reads=[scr["W_gate"].b])
                        cs_ = slice(i * 512, (i + 1) * 512)
                        for pos in range(4):
                            o = 4 * b + pos
                            pa = psM.next()
                            for kc in range(16):
                                self.mm(pa, pa.t[:, :], hT.t[:, kc, pos * 128:(pos + 1) * 128], wa.t[:, kc, :], kc == 0, kc == 15, [wa.b, hT.b])
                            pb = psM.next()
                            for kc in range(16):
                                self.mm(pb, pb.t[:, :], hT.t[:, kc, pos * 128:(pos + 1) * 128], wb.t[:, kc, :], kc == 0, kc == 15, [wb.b, hT.b])
                            ga, gb, oa, ob = gas.next(), gbs.next(), oas.next(), obs.next()
                            P.op("act", lambda e, ga=ga, pa=pa: e.activation(out=ga.t[:], in_=pa.t[:, :], func=AF.Sigmoid), [pa.b], [ga.b])
                            P.op("act", lambda e, gb=gb, pb=pb: e.activation(out=gb.t[:], in_=pb.t[:, :], func=AF.Sigmoid), [pb.b], [gb.b])
                            self.load(oa, oa.t[:], scr["OA"].t[o][:, cs_], reads=[scr["OA"].b])
                            self.load(ob, ob.t[:], scr["OB"].t[o][:, cs_], reads=[scr["OB"].b])
                            P.op("dve", lambda e, ga=ga, oa=oa: e.tensor_tensor(ga.t[:], ga.t[:], oa.t[:], ALU.mult), [ga.b, oa.b], [ga.b])
                            P.op("dve", lambda e, gb=gb, ob=ob: e.tensor_tensor(gb.t[:], gb.t[:], ob.t[:], ALU.mult), [gb.b, ob.b], [gb.b])
                            P.op("dve", lambda e, ga=ga, gb=gb, pos=pos, cs_=cs_: e.tensor_tensor(mix[pos].t[:, cs_], ga.t[:], gb.t[:], ALU.add),
                                 [ga.b, gb.b], [mix[pos].b])
                    for pos in range(4):
                        self.transpose_into(mix[pos], mix[pos].t, 16, hT, lambda kc, pos=pos: hT.t[:, kc, pos * 128:(pos + 1) * 128], psT, identb)
                    for cb in range(4):
                        wt = wts.next()
                        self.load(wt, wt.t[:], scr["W_wo"].t[cb], reads=[scr["W_wo"].b])
                        cs_ = slice(cb * 512, (cb + 1) * 512)
                        for pos in range(4):
                            ps = psM.next()
                            for kc in range(16):
                                self.mm(ps, ps.t[:, :], hT.t[:, kc, pos * 128:(pos + 1) * 128], wt.t[:, kc, :], kc == 0, kc == 15, [wt.b, hT.b])
                            P.op("dve", lambda e, ps=ps, pos=pos, cs_=cs_: e.tensor_tensor(xk[pos].t[:, cs_], xk[pos].t[:, cs_], ps.t[:, :], ALU.add),
                                 [xk[pos].b, ps.b], [xk[pos].b])
                    P.barrier()
                    self.release_scope(locals())
                with ExitStack() as st:
                    sb = lambda name, shape, dt=F32: self.sb(st, name, shape, dt)
                    g_ffn = sb("n_gffn", [128, D])
                    g_fin = sb("n_gfin", [128, D])
                    identf = sb("n_identf", [128, 128])
                    identb = sb("n_identb", [128, 128], BF16)
                    hb = sb("n_hb", [128, D], BF16)
                    h2T = sb("n_h2T", [128, 16, 512], BF16)
                    uT = sb("n_uT", [128, 64, 512], BF16)
                    wts = Rot([sb("n_wt%d" % i, [128, 16, 512], BF16) for i in range(2)])
                    rrs = Rot([sb("n_rr%d" % i, [128, 512]) for i in range(2)])
                    ys = Rot([sb("n_y%d" % i, [128, D]) for i in range(2)])
                    tmp = dict(junk=sb("n_junk", [128, D], BF16), ss=sb("n_ss", [128, 1]), sd=sb("n_sd", [128, 1]),
                               rstd=sb("n_rstd", [128, 1]))
                    psT = Rot([PS[0], PS[1]])
                    psM = Rot([PS[6], PS[7]])
                    self.load(g_ffn, g_ffn.t[:], inp["g_ffn"][:, :])
                    self.load(g_fin, g_fin.t[:], inp["g_fin"][:, :])
                    self.load(identf, identf.t[:], inp["ident"][:, :])
                    P.op("dve", lambda e: e.tensor_copy(identb.t[:], identf.t[:]), [identf.b], [identb.b])
                    for pos in range(4):
                        self.norm_rows(xk[pos].t[:], D, g_ffn.t[:], hb.t[:], xk[pos].b, g_ffn.b, hb.b, tmp)
                        self.transpose_into(hb, hb.t, 16, h2T, lambda kc, pos=pos: h2T.t[:, kc, pos * 128:(pos + 1) * 128], psT, identb)
                    for cb in range(16):
                        wt = wts.next()
                        self.load(wt, wt.t[:], scr["W_up"].t[cb], reads=[scr["W_up"].b])
                        for jj in range(4):
                            ffc = 4 * cb + jj
                            ps = psM.next()
                            for kc in range(16):
                                self.mm(ps, ps.t[:, :], wt.t[:, kc, jj * 128:(jj + 1) * 128], h2T.t[:, kc, :], kc == 0, kc == 15, [wt.b, h2T.b])
                            rr = rrs.next()
                            P.op("act", lambda e, rr=rr, ps=ps: e.activation(out=rr.t[:], in_=ps.t[:, :], func=AF.Relu), [ps.b], [rr.b])
                            P.op("dve", lambda e, rr=rr, ffc=ffc: e.tensor_tensor(uT.t[:, ffc, :], rr.t[:], rr.t[:], ALU.mult), [rr.b], [uT.b])
                    for cb in range(4):
                        cs_ = slice(cb * 512, (cb + 1) * 512)
                        pss = [PS[2], PS[3], PS[4], PS[5]]
                        for kg in range(4):
                            wt = wts.next()
                            self.load(wt, wt.t[:], scr["W_dn"].t[cb * 4 + kg], reads=[scr["W_dn"].b])
                            for pos in range(4):
                                for kc in range(16):
                                    self.mm(pss[pos], pss[pos].t[:, :], uT.t[:, kg * 16 + kc, pos * 128:(pos + 1) * 128], wt.t[:, kc, :],
                                            kg == 0 and kc == 0, kg == 3 and kc == 15, [wt.b, uT.b])
                        for pos in range(4):
                            P.op("dve", lambda e, pos=pos, cs_=cs_, pss=pss: e.tensor_tensor(xk[pos].t[:, cs_], xk[pos].t[:, cs_], pss[pos].t[:, :],
                                                                                         ALU.add), [xk[pos].b, pss[pos].b], [xk[pos].b])
                    for pos in range(4):
                        o = 4 * b + pos
                        y = ys.next()
                        self.norm_rows(xk[pos].t[:], D, g_fin.t[:], y.t[:], xk[pos].b, g_fin.b, y.b, tmp)
                        self.store(y, out["y_own"][o * 128:(o + 1) * 128, :], y.t[:])
                    P.barrier()
                    self.release_scope(locals())
                self.release_scope(dict(xk=xk))

    def build(self):
        nc = self.nc
        self.declare()
        st = self.gstack
        self.PS = [TB(st.enter_context(nc.psum_tensor("ps%d" % i, [128, 512], F32)), "ps%d" % i) for i in range(8)]
        for p_ in self.PS:
            p_.b.excl = True
        self.phase_W()
        if self.upto == "W":
            return self.finish()
        kblocks = list(range(32)) + [34]
        if self.upto == "K1":
            kblocks = [0, 1, 34][:int(os.environ.get("MK_NB", "3"))]
        self.phase_K(kblocks)
        if self.upto in ("K", "K1"):
            return self.finish()
        self.phase_KC()
        if self.upto == "KC":
            return self.finish()
        self.phase_Q()
        if self.upto == "Q":
            return self.finish()
        self.slots_all()
        if self.upto == "S":
            return self.finish()
        self.phase_M()
        return self.finish()

    def finish(self):
        self.P.finish()
        self.gstack.close()
        return self.nc


def t5_bucket_np(rel):
    nb = 16
    ret = (rel > 0).astype(np.int32) * nb
    n = np.abs(rel)
    max_exact = 8
    nf = np.maximum(n, 1).astype(np.float32)
    large = max_exact + (np.log(nf / np.float32(max_exact)) / np.float32(math.log(128 / max_exact))
                         * np.float32(nb - max_exact)).astype(np.int32)
    large = np.minimum(large, nb - 1)
    return ret + np.where(n < max_exact, n, large)


def host_inputs(inputs):
    f32 = np.float32
    xp = np.asarray(inputs["x_prompt"], f32)[0]
    xs = np.asarray(inputs["x_sample"], f32)
    ck = np.asarray(inputs["cache_a_k"], f32)[0]
    cv = np.asarray(inputs["cache_a_v"], f32)[0]
    cix = np.asarray(inputs["cache_a_idx_k"], f32)[0]
    cckv = np.asarray(inputs["cache_b_ckv"], f32)[0]
    ckr = np.asarray(inputs["cache_b_krope"], f32)[0]
    half = 32
    inv_freq = np.power(np.float32(10000.0), -np.arange(half, dtype=f32) / np.float32(half)).astype(f32)
    shared = {
        "ident": np.eye(128, dtype=f32),
        "constb": np.ascontiguousarray(np.broadcast_to(np.asarray(inputs["rel_bias_table"], f32)[15], (128, 16))),
        "relb": np.ascontiguousarray(np.asarray(inputs["rel_bias_table"], f32)),
        "w_in": np.ascontiguousarray(np.asarray(inputs["w_in"], f32)[0]),
        "w_uq": np.ascontiguousarray(np.asarray(inputs["w_uq"], f32)[0]),
        "w_uk": np.ascontiguousarray(np.asarray(inputs["w_uk"], f32)[0].reshape(256, 2048)),
        "w_uv": np.ascontiguousarray(np.asarray(inputs["w_uv"], f32)[0].reshape(256, 2048)),
        "w_out": np.ascontiguousarray(np.asarray(inputs["w_out"], f32)[0]),
        "w_ff_up": np.ascontiguousarray(np.asarray(inputs["w_ff_up"], f32)[0]),
        "w_ff_down": np.ascontiguousarray(np.asarray(inputs["w_ff_down"], f32)[0]),
        "g_mix": np.ascontiguousarray(np.broadcast_to(np.asarray(inputs["norm_mix_g"], f32)[0], (128, D))),
        "g_ffn": np.ascontiguousarray(np.broadcast_to(np.asarray(inputs["norm_ffn_g"], f32)[0], (128, D))),
        "g_fin": np.ascontiguousarray(np.broadcast_to(np.asarray(inputs["final_norm_g"], f32), (128, D))),
        "g_q": np.ascontiguousarray(np.broadcast_to(np.asarray(inputs["q_lora_g"], f32)[0], (128, 512))),
        "g_kv": np.ascontiguousarray(np.broadcast_to(np.asarray(inputs["kv_lora_g"], f32)[0], (128, 256))),
    }
    s = np.arange(128)[None, :]
    t = np.arange(128)[:, None]
    onehot = np.zeros((2, 32, 128, 128), f32)
    for ty, off in enumerate((-128, 0)):
        rel = (s - t + off).astype(np.int32)
        bk = t5_bucket_np(rel)
        for b in range(32):
            onehot[ty, b] = (bk == b)
    shared["onehot"] = onehot.reshape(2, 32, 128 * 128)
    dsa_diag = np.zeros((2, 128, 128), f32)
    dsa_diag[0] = np.where((s // 64) <= (t // 64), 0.0, NEGBIG)
    dsa_diag[1] = np.where(s < 16, 0.0, NEGBIG) * np.ones((128, 1), f32)
    shared["dsa_diag"] = dsa_diag
    mla_mask = np.ones((2, 128, 2, 128), f32)
    ss_ = np.arange(128)[:, None]
    tt_ = np.arange(128)[None, :]
    mla_mask[0, :, 1, :] = ((ss_ // 64) <= (tt_ // 64)).astype(f32)
    mla_mask[1, :, 1, :] = (ss_ < 16).astype(f32) * np.ones((1, 128), f32)
    shared["mla_mask"] = mla_mask.reshape(2, 128, 256)
    maps = []
    for c in range(NCORE):
        m = dict(shared)
        pre = 7 - c
        x_all = np.zeros((NXT * 128, D), f32)
        x_all[pre * 128:pre * 128 + 16384] = xp
        pos = np.zeros((NXT * 128,), f32)
        valid = np.zeros((NXT * 128, 1), f32)
        pos[pre * 128:pre * 128 + 16384] = np.arange(16384, dtype=f32)
        valid[pre * 128:pre * 128 + 16384] = 1.0
        for b in range(2):
            r0 = (136 + b) * 128
            x_all[r0:r0 + 16] = xs[2 * c + b]
            pos[r0:r0 + 16] = 1024 + np.arange(16, dtype=f32)
            valid[r0:r0 + 16] = 1.0
        ang = pos[:, None] * inv_freq[None, :]
        m["x_all"] = x_all
        m["cs"] = np.concatenate([np.cos(ang), np.sin(ang)], axis=1).astype(f32)
        m["valid"] = np.ascontiguousarray(valid.reshape(NXT, 128).T)
        pm = np.zeros((896,), f32)
        pm[:pre * 128] = NEGBIG
        m["prefmask"] = np.ascontiguousarray(np.broadcast_to(pm, (128, 896)))
        sl = slice(2 * c, 2 * c + 2)
        m["c_akT"] = np.ascontiguousarray(ck[sl].transpose(0, 2, 3, 1))
        cve = np.ones((2, 1024, 16, 130), f32)
        cve[..., 0:128] = cv[sl]
        m["c_av"] = cve
        m["c_ixkT"] = np.ascontiguousarray(cix[sl].transpose(0, 2, 1))
        m["c_ckvT"] = np.ascontiguousarray(cckv[sl].transpose(0, 2, 1))
        m["c_krT"] = np.ascontiguousarray(ckr[sl].transpose(0, 2, 1))
        maps.append(m)
    return maps


def assemble(results):
    f32 = np.float32
    y_p = np.zeros((1, 16384, D), f32)
    y_s = np.zeros((16, 16, D), f32)
    a_k_p = np.zeros((1, 1, 16384, 16, 128), f32)
    a_v_p = np.zeros((1, 1, 16384, 16, 128), f32)
    a_ix_p = np.zeros((1, 1, 16384, 64), f32)
    b_ckv_p = np.zeros((1, 1, 16384, 256), f32)
    b_kr_p = np.zeros((1, 1, 16384, 64), f32)
    a_k_s = np.zeros((1, 16, 16, 16, 128), f32)
    a_v_s = np.zeros((1, 16, 16, 16, 128), f32)
    a_ix_s = np.zeros((1, 16, 16, 64), f32)
    b_ckv_s = np.zeros((1, 16, 16, 256), f32)
    b_kr_s = np.zeros((1, 16, 16, 64), f32)
    for c in range(NCORE):
        r = results[c]
        for i in range(16):
            j = c + 8 * i
            rs = slice(i * 128, (i + 1) * 128)
            ps = slice(j * 128, (j + 1) * 128)
            y_p[0, ps] = r["y_own"][rs]
            a_k_p[0, 0, ps] = r["st_ak"][rs].reshape(128, 16, 128)
            a_v_p[0, 0, ps] = r["st_av"][rs].reshape(128, 16, 128)
            a_ix_p[0, 0, ps] = r["st_ix"][rs]
            b_ckv_p[0, 0, ps] = r["st_ckv"][rs]
            b_kr_p[0, 0, ps] = r["st_kr"][rs]
        for b in range(2):
            rs = slice((16 + b) * 128, (16 + b) * 128 + 16)
            sq = 2 * c + b
            y_s[sq] = r["y_own"][rs]
            a_k_s[0, sq] = r["st_ak"][rs].reshape(16, 16, 128)
            a_v_s[0, sq] = r["st_av"][rs].reshape(16, 16, 128)
            a_ix_s[0, sq] = r["st_ix"][rs]
            b_ckv_s[0, sq] = r["st_ckv"][rs]
            b_kr_s[0, sq] = r["st_kr"][rs]
    return (y_p, y_s, a_k_p, a_v_p, a_ix_p, b_ckv_p, b_kr_p, a_k_s, a_v_s, a_ix_s, b_ckv_s, b_kr_s)


def kernel(**inputs):
    upto = os.environ.get("MK_UPTO", "ALL")
    bld = Builder(upto)
    nc = bld.build()
    maps = host_inputs(inputs)
    res = run_bass_kernel_spmd(nc, maps, core_ids=list(range(NCORE)))
    return assemble(res.results)
```

```python
import math
import os
from contextlib import ExitStack

import numpy as np
import concourse.bass as bass
import concourse.mybir as mybir
from concourse.bass_utils import run_bass_kernel_spmd

F32 = mybir.dt.float32
BF16 = mybir.dt.bfloat16
AF = mybir.ActivationFunctionType
ALU = mybir.AluOpType
AX = mybir.AxisListType

D = 2048
NH = 16
HD = 128
NCORE = 8
NPT = 136
NXT = 140
NTILE = 158
NTOK = NTILE * 128
NOWN = 20
EPS = 1e-6
A_SCALE = HD ** -0.5
MLA_SCALE = 192 ** -0.5
NEGBIG = -1.0e30
MASKNEG = -30000.0
TOPK = 256
BIS_R = 128.0
BIS_IT = 16
C_AQ, C_AK, C_AV, C_IXQ, C_IXK, C_IXW, C_CQ, C_CKV, C_KR, C_GA, C_GB = (
    0, 2048, 4096, 6144, 7168, 7232, 7248, 7760, 8016, 8080, 10128)


class Buf:
    __slots__ = ("name", "last_w", "readers", "dsem", "dcount", "excl")

    def __init__(self, name):
        self.excl = False
        self.name = name
        self.last_w = None
        self.readers = []
        self.dsem = None
        self.dcount = 0


class Ins:
    __slots__ = ("eng", "fn", "deps", "is_dma", "dbuf", "need_inc", "seq", "cover")

    def __init__(self, eng, fn, deps, is_dma=False, dbuf=None):
        self.eng = eng
        self.fn = fn
        self.deps = deps
        self.is_dma = is_dma
        self.dbuf = dbuf
        self.need_inc = False
        self.seq = 0
        self.cover = 0


COMPUTE = ("pe", "act", "dve", "pool")


class Prog:
    def __init__(self, nc, stack):
        self.nc = nc
        self.stack = stack
        self.engs = {"pe": nc.tensor, "act": nc.scalar, "dve": nc.vector, "pool": nc.gpsimd, "sp": nc.sync}
        self.batch = []
        self.esem = {e: stack.enter_context(nc.semaphore("e_" + e)) for e in COMPUTE}
        self.ecount = {e: 0 for e in COMPUTE}
        self.waited = {e: {} for e in self.engs}
        self.sempool = []
        self.livesems = {}
        self.nsem = 4
        self.n_ins = 0
        self.n_waits = 0
        self.trace = {e: [] for e in self.engs}

    def _getsem(self, buf):
        if self.sempool:
            sem, cnt = self.sempool.pop()
        else:
            self.nsem += 1
            sem = self.stack.enter_context(self.nc.semaphore("d%d" % self.nsem))
            cnt = 0
        buf.dsem = sem
        buf.dcount = cnt
        self.livesems[id(sem)] = [sem, cnt, cnt]

    def release(self, bufs):
        for b in bufs:
            if b.dsem is not None:
                ent = self.livesems.pop(id(b.dsem))
                self.sempool.append((ent[0], ent[1]))
                b.dsem = None

    def _deps(self, eng, reads, writes, is_dma):
        deps = {}

        def add(p, kind):
            if p is None:
                return
            if (not p.is_dma) and (not is_dma) and p.eng == eng:
                if eng == "pe" or kind != "raw":
                    return
            deps[id(p)] = p

        for b in reads:
            add(b.last_w, "raw")
            if b.excl:
                for r in b.readers:
                    if r.eng != eng:
                        add(r, "war")
        for b in writes:
            add(b.last_w, "waw")
            for r in b.readers:
                add(r, "war")
        return list(deps.values())

    def _post(self, ins, reads, writes):
        for b in reads:
            if not ins.is_dma:
                b.readers = [r for r in b.readers if r.is_dma or r.eng != ins.eng]
            b.readers.append(ins)
        for b in writes:
            b.last_w = ins
            b.readers = []
        self.batch.append(ins)

    def op(self, eng, fn, reads=(), writes=()):
        ins = Ins(eng, fn, self._deps(eng, reads, writes, False))
        self._post(ins, reads, writes)

    def dma(self, fn, reads, writes, dbuf, q="sp"):
        ins = Ins(q, fn, self._deps(q, reads, writes, True), True, dbuf)
        if dbuf.dsem is None:
            self._getsem(dbuf)
        dbuf.dcount += 1
        self.livesems[id(dbuf.dsem)][1] = dbuf.dcount
        self._post(ins, reads, writes)

    def flush(self):
        batch = self.batch
        self.batch = []
        for i in batch:
            for p in i.deps:
                if not p.is_dma:
                    p.need_inc = True
        last = {}
        for i in batch:
            if not i.is_dma:
                last[i.eng] = i
        for i in last.values():
            i.need_inc = True
        for i in batch:
            if not i.is_dma and i.need_inc:
                self.ecount[i.eng] += 1
                i.seq = self.ecount[i.eng]
        nxt = {}
        for i in reversed(batch):
            if not i.is_dma:
                if i.need_inc:
                    nxt[i.eng] = i.seq
                i.cover = nxt[i.eng]
        for i in batch:
            h = self.engs[i.eng]
            need = {}
            for p in i.deps:
                if p.is_dma:
                    ent = self.livesems.get(id(p.dbuf.dsem)) if p.dbuf.dsem is not None else None
                    if ent is None:
                        continue
                    key = id(ent[0])
                    sem = ent[0]
                    val = 16 * ent[2]
                else:
                    key = p.eng
                    sem = self.esem[p.eng]
                    val = p.cover
                if key not in need or need[key][1] < val:
                    need[key] = (sem, val)
            w = self.waited[i.eng]
            for key, (sem, val) in need.items():
                if w.get(key, 0) >= val:
                    continue
                w[key] = val
                h.wait_ge(sem, val)
                self.trace[i.eng].append(("w", id(sem), val))
                self.n_waits += 1
            bi = i.fn(h)
            self.n_ins += 1
            if i.is_dma:
                bi.then_inc(i.dbuf.dsem, 16)
                self.trace[i.eng].append(("i", id(i.dbuf.dsem), 16))
                self.livesems[id(i.dbuf.dsem)][2] += 1
            elif i.need_inc:
                bi.then_inc(self.esem[i.eng], 1)
                self.trace[i.eng].append(("i", id(self.esem[i.eng]), 1))
            else:
                self.trace[i.eng].append(("i", None, 0))

    def barrier(self):
        self.flush()
        for e, h in self.engs.items():
            w = self.waited[e]
            for pe in COMPUTE:
                if pe == e:
                    continue
                val = self.ecount[pe]
                if val > 0 and w.get(pe, 0) < val:
                    w[pe] = val
                    h.wait_ge(self.esem[pe], val)
                    self.trace[e].append(("w", id(self.esem[pe]), val))
            for key, ent in self.livesems.items():
                val = 16 * ent[2]
                if val > 0 and w.get(key, 0) < val:
                    w[key] = val
                    h.wait_ge(ent[0], val)
                    self.trace[e].append(("w", id(ent[0]), val))

    def finish(self):
        self.barrier()


class TB:
    def __init__(self, t, name):
        self.t = t
        self.b = Buf(name)


class Rot:
    def __init__(self, items):
        self.items = items
        self.i = 0

    def next(self):
        x = self.items[self.i % len(self.items)]
        self.i += 1
        return x


def weight_groups():
    g = {}
    g["ak"] = dict(K=2048, blocks=[[("w_in", 0, C_AK + 512 * i, 512)] for i in range(4)])
    g["av"] = dict(K=2048, blocks=[[("w_in", 0, C_AV + 512 * i, 512)] for i in range(4)])
    g["misc"] = dict(K=2048, blocks=[[("w_in", 0, C_CKV, 256), ("w_in", 0, C_IXK, 64), ("w_in", 0, C_KR, 64)]])
    g["aq"] = dict(K=2048, blocks=[[("w_in", 0, C_AQ + 512 * i, 512)] for i in range(4)])
    g["ixq"] = dict(K=2048, blocks=[[("w_in", 0, C_IXQ + 512 * i, 512)] for i in range(2)])
    g["cq"] = dict(K=2048, blocks=[[("w_in", 0, C_CQ, 512)], [("w_in", 0, C_IXW, 16)]])
    g["gate"] = dict(K=2048, blocks=[[("w_in", 0, C_GA + 512 * i, 512)] for i in range(8)])
    g["wo"] = dict(K=2048, blocks=[[("w_out", 0, 512 * i, 512)] for i in range(4)])
    g["up"] = dict(K=2048, blocks=[[("w_ff_up", 0, 512 * i, 512)] for i in range(16)])
    g["dn"] = dict(K=2048, blocks=[[("w_ff_down", 2048 * kg, 512 * cb, 512)] for cb in range(4) for kg in range(4)])
    g["uqn"] = dict(K=512, blocks=[[("uqn", 0, i, 0)] for i in range(4)])
    g["uqr"] = dict(K=512, blocks=[[("uqr", 0, i, 0)] for i in range(2)])
    g["uk"] = dict(K=256, blocks=[[("w_uk", 0, 512 * i, 512)] for i in range(4)])
    g["uv"] = dict(K=256, blocks=[[("w_uv", 0, 512 * i, 512)] for i in range(4)])
    first = ["uk", "uv", "misc", "ak", "av"]
    return {k: g[k] for k in first + [k for k in g if k not in first]}


class _Stop(Exception):
    pass


class Builder:
    def ck(self, n):
        if int(os.environ.get("MK_KSTOP", "0")) == n:
            raise _Stop()

    def __init__(self, upto="ALL"):
        self.upto = upto
        self.nc = bass.Bass("TRN2", target_bir_lowering=False)
        self.gstack = ExitStack()
        self.P = Prog(self.nc, self.gstack)
        self.inp = {}
        self.out = {}
        self.scr = {}

    def din(self, name, shape, dt=F32):
        self.inp[name] = self.nc.dram_tensor(name, list(shape), dt, kind="ExternalInput").ap()
        return self.inp[name]

    def dout(self, name, shape, dt=F32):
        self.out[name] = self.nc.dram_tensor(name, list(shape), dt, kind="ExternalOutput").ap()
        return self.out[name]

    def dscr(self, name, shape, dt=BF16):
        self.scr[name] = TB(self.nc.dram_tensor(name, list(shape), dt, kind="Internal").ap(), name)
        return self.scr[name]

    def sb(self, st, name, shape, dt=F32):
        self._uid = getattr(self, "_uid", 0) + 1
        name = "%s_u%d" % (name, self._uid)
        return TB(st.enter_context(self.nc.sbuf_tensor(name, list(shape), dt)), name)

    def mm(self, ps, out_ap, lhsT, rhs, start, stop, reads):
        self.P.op("pe", lambda e: e.matmul(out_ap, lhsT, rhs, start=start, stop=stop), reads, [ps.b])

    def load(self, dst, dst_ap, src_ap, reads=(), q="sp"):
        self.P.dma(lambda e: e.dma_start(out=dst_ap, in_=src_ap), list(reads), [dst.b], dst.b, q=q)

    def store(self, src, dst_ap, src_ap, dstbuf=None, q=None):
        q = q or getattr(self, "store_q", "pool")
        w = [dstbuf.b] if dstbuf is not None else []
        self.P.dma(lambda e: e.dma_start(out=dst_ap, in_=src_ap), [src.b], w, src.b, q=q)

    def declare(self):
        din, dout, dscr = self.din, self.dout, self.dscr
        din("x_all", [NXT * 128, D])
        din("cs", [NXT * 128, 64])
        din("valid", [128, NXT])
        din("prefmask", [128, 896])
        din("ident", [128, 128])
        din("onehot", [2, 32, 128 * 128])
        din("relb", [32, 16])
        din("constb", [128, 16])
        din("dsa_diag", [2, 128, 128])
        din("mla_mask", [2, 128, 256])
        din("c_akT", [2, 16, 128, 1024])
        din("c_av", [2, 1024, 16, 130])
        din("c_ixkT", [2, 64, 1024])
        din("c_ckvT", [2, 256, 1024])
        din("c_krT", [2, 64, 1024])
        din("w_in", [D, 12176])
        din("w_uq", [512, 16, 192])
        din("w_uk", [256, 2048])
        din("w_uv", [256, 2048])
        din("w_out", [D, D])
        din("w_ff_up", [D, 8192])
        din("w_ff_down", [8192, D])
        din("g_mix", [128, D])
        din("g_ffn", [128, D])
        din("g_fin", [128, D])
        din("g_q", [128, 512])
        din("g_kv", [128, 256])
        dout("y_own", [NOWN * 128, D])
        dout("st_ak", [NOWN * 128, D])
        dout("st_av", [NOWN * 128, D])
        dout("st_ix", [NOWN * 128, 64])
        dout("st_ckv", [NOWN * 128, 256])
        dout("st_kr", [NOWN * 128, 64])
        dscr("KT_A", [NH, 128, NTOK])
        dscr("V_A", [NH, 128, NTILE, 130])
        dscr("IXK_T", [64, NTOK])
        dscr("KT_B", [NH, 128, NTOK])
        dscr("KR_T", [64, NTOK])
        dscr("V_B", [NH, 128, NTILE, 130])
        dscr("QA_T", [5, 128, 16, 512])
        dscr("IXQ_T", [5, 128, 8, 512])
        dscr("IXW", [5, 128, 4, 16], F32)
        dscr("QN_T", [5, 128, 16, 512])
        dscr("QR_T", [5, 128, 8, 512])
        dscr("OA", [NOWN, 128, D], F32)
        dscr("OB", [NOWN, 128, D], F32)
        self.wg = weight_groups()
        for name, g in self.wg.items():
            dscr("W_" + name, [len(g["blocks"]), 128, g["K"] // 128, 512])

    def phase_W(self):
        P = self.P
        inp = self.inp
        urgent = ["uk", "uv", "misc", "ak", "av"]
        self._precast(urgent)
        KT_A, V_A, IXK_T, KR_T = self.scr["KT_A"], self.scr["V_A"], self.scr["IXK_T"], self.scr["KR_T"]
        for b in range(2):
            t0 = 140 + 9 * b
            for h in range(NH):
                P.dma(lambda e, b=b, h=h, t0=t0: e.dma_start(out=KT_A.t[h, :, t0 * 128:t0 * 128 + 1024], in_=inp["c_akT"][b, h]),
                      [], [KT_A.b], KT_A.b, q="pool")
                P.dma(lambda e, b=b, h=h, t0=t0: e.dma_start(
                    out=V_A.t[h, :, t0:t0 + 8, :],
                    in_=inp["c_av"][b, :, h, :].rearrange("(kt p) d -> p kt d", p=128)), [], [V_A.b], V_A.b, q="pool")
            P.dma(lambda e, b=b, t0=t0: e.dma_start(out=IXK_T.t[:, t0 * 128:t0 * 128 + 1024], in_=inp["c_ixkT"][b]),
                  [], [IXK_T.b], IXK_T.b, q="pool")
            P.dma(lambda e, b=b, t0=t0: e.dma_start(out=KR_T.t[:, t0 * 128:t0 * 128 + 1024], in_=inp["c_krT"][b]),
                  [], [KR_T.b], KR_T.b, q="pool")
        self._precast([k for k in self.wg if k not in urgent])

    def _precast(self, names):
        P = self.P
        inp = self.inp
        for name in names:
            g = self.wg[name]
            W = self.scr["W_" + name]
            KC = g["K"] // 128
            for bi, pieces in enumerate(g["blocks"]):
                off = 0
                for (src, r0, c0, ncols) in pieces:
                    if src in ("uqn", "uqr"):
                        i = c0
                        nh_, w_, lo_ = (4, 128, 0) if src == "uqn" else (8, 64, 128)
                        for j in range(nh_):
                            src_ap = inp["w_uq"][:, nh_ * i + j, lo_:lo_ + w_].rearrange("(kc p) d -> p kc d", p=128)
                            dst_ap = W.t[bi][:, :, j * w_:(j + 1) * w_]
                            P.dma(lambda e, d=dst_ap, s=src_ap: e.dma_start(out=d, in_=s), [], [W.b], W.b, q="pool")
                        continue
                    src_ap = inp[src][r0:r0 + g["K"], c0:c0 + ncols].rearrange("(kc p) c -> p kc c", p=128)
                    dst_ap = W.t[bi][:, :, off:off + ncols]
                    off += ncols
                    P.dma(lambda e, d=dst_ap, s=src_ap: e.dma_start(out=d, in_=s), [], [W.b], W.b, q="pool")

    def norm_rows(self, x_ap, n, g_ap, out_ap, xb, gb, outb, tmp):
        P = self.P
        junk, ss, sd, rstd = tmp["junk"], tmp["ss"], tmp["sd"], tmp["rstd"]
        P.op("act", lambda e: e.activation(out=junk.t[:, 0:n], in_=x_ap, func=AF.Square, accum_out=ss.t[:]),
             [xb], [junk.b, ss.b])
        P.op("act", lambda e: e.activation(out=sd.t[:], in_=ss.t[:], func=AF.Sqrt, bias=EPS, scale=1.0 / n),
             [ss.b], [sd.b])
        P.op("dve", lambda e: e.reciprocal(rstd.t[:], sd.t[:]), [sd.b], [rstd.b])
        P.op("dve", lambda e: e.scalar_tensor_tensor(out=out_ap, in0=x_ap, scalar=rstd.t[:], in1=g_ap,
                                                     op0=ALU.mult, op1=ALU.mult), [xb, rstd.b, gb], [outb])

    def load_norm_T(self, st, tiles, g_tb, hT, xts, hb, tmp, psT, identb, keep_x=None):
        P = self.P
        xall = self.inp["x_all"]
        for j, tile in enumerate(tiles):
            xt = keep_x[j] if keep_x is not None else xts.next()
            self.load(xt, xt.t[:], xall[tile * 128:(tile + 1) * 128, :])
            self.norm_rows(xt.t[:], D, g_tb.t[:], hb.t[:], xt.b, g_tb.b, hb.b, tmp)
            self.transpose_into(hb, hb.t, 16, hT, lambda kc, j=j: hT.t[:, kc, j * 128:(j + 1) * 128], psT, identb)

    def transpose_into(self, src, src_t, nchunks, dst, dst_ap_fn, psT, identb, rows=128):
        P = self.P
        c = 0
        while c < nchunks:
            n = min(4, nchunks - c)
            ps = psT.next()
            for k in range(n):
                self.mm(ps, ps.t[:, k * 128:(k + 1) * 128], src_t[:, (c + k) * 128:(c + k + 1) * 128], identb.t[:],
                        True, True, [src.b, identb.b])
            self._tflip = not getattr(self, "_tflip", False)
            for k in range(n):
                eng = "act" if self._tflip else "dve"
                d_ap = dst_ap_fn(c + k)
                s_ap = ps.t[:, k * 128:(k + 1) * 128]
                if eng == "act":
                    P.op("act", lambda e, d=d_ap, s=s_ap: e.activation(out=d, in_=s, func=AF.Copy), [ps.b], [dst.b])
                else:
                    P.op("dve", lambda e, d=d_ap, s=s_ap: e.tensor_copy(d, s), [ps.b], [dst.b])
            c += n

    def rope(self, out_ap1, out_ap2, x1, x2, cos, sin, xb, csb, outb, tmps):
        P = self.P
        t1, t2 = tmps
        P.op("dve", lambda e: e.tensor_tensor(t1.t_ap, x1, cos, ALU.mult), [xb, csb], [t1.b])
        P.op("dve", lambda e: e.tensor_tensor(t2.t_ap, x2, sin, ALU.mult), [xb, csb], [t2.b])
        P.op("dve", lambda e: e.tensor_tensor(out_ap1, t1.t_ap, t2.t_ap, ALU.subtract), [t1.b, t2.b], [outb])
        P.op("dve", lambda e: e.tensor_tensor(t1.t_ap, x1, sin, ALU.mult), [xb, csb, outb], [t1.b])
        P.op("dve", lambda e: e.tensor_tensor(t2.t_ap, x2, cos, ALU.mult), [xb, csb, outb], [t2.b])
        P.op("dve", lambda e: e.tensor_tensor(out_ap2, t1.t_ap, t2.t_ap, ALU.add), [t1.b, t2.b], [outb])

    def phase_K(self, blocks):
        P = self.P
        self.store_q = "act"
        nc = self.nc
        inp, out, scr = self.inp, self.out, self.scr
        with ExitStack() as st:
            sb = lambda name, shape, dt=F32: self.sb(st, name, shape, dt)
            g_mix = sb("k_gmix", [128, D])
            g_kv = sb("k_gkv", [128, 256])
            identf = sb("k_identf", [128, 128])
            identb = sb("k_identb", [128, 128], BF16)
            ones16 = sb("k_ones16", [128, 16, 1])
            wuk = [sb("k_wuk%d" % i, [128, 2, 512], BF16) for i in range(4)]
            wuv = [sb("k_wuv%d" % i, [128, 2, 512], BF16) for i in range(4)]
            xts = Rot([sb("k_x%d" % i, [128, D]) for i in range(2)])
            hb = sb("k_hb", [128, D], BF16)
            hT = sb("k_hT", [128, 16, 512], BF16)
            wts = Rot([sb("k_wt%d" % i, [128, 16, 512], BF16) for i in range(3)])
            ksts = Rot([sb("k_kst%d" % i, [128, 512], BF16) for i in range(2)])
            vst = [sb("k_vst%d" % i, [128, 16, 130], BF16) for i in range(4)]
            vbst = [sb("k_vbst%d" % i, [128, 16, 130], BF16) for i in range(4)]
            mst = [sb("k_mst%d" % i, [128, 384]) for i in range(4)]
            mo = [sb("k_mo%d" % i, [128, 384]) for i in range(4)]
            km = sb("k_km", [128, 384], BF16)
            ckvT = sb("k_ckvT", [128, 2, 512], BF16)
            ixkrT = sb("k_ixkrT", [128, 512], BF16)
            osts = Rot([sb("k_ost%d" % i, [128, 512]) for i in range(3)])
            cst = sb("k_cs", [128, 4, 64])
            vld = sb("k_vld", [128, 4])
            tmp = dict(junk=sb("k_junk", [128, D], BF16), ss=sb("k_ss", [128, 1]), sd=sb("k_sd", [128, 1]),
                       rstd=sb("k_rstd", [128, 1]))
            rt1 = sb("k_rt1", [128, 32])
            rt2 = sb("k_rt2", [128, 32])
            rt1.t_ap = rt1.t[:]
            rt2.t_ap = rt2.t[:]
            PS = self.PS
            psT = Rot([PS[0], PS[1]])
            psM = Rot([PS[2], PS[3], PS[4], PS[5]])
            psX = Rot([PS[6], PS[7]])

            self.load(g_mix, g_mix.t[:], inp["g_mix"][:, :])
            self.load(g_kv, g_kv.t[:], inp["g_kv"][:, :])
            self.load(identf, identf.t[:], inp["ident"][:, :])
            P.op("dve", lambda e: e.tensor_copy(identb.t[:], identf.t[:]), [identf.b], [identb.b])
            P.op("dve", lambda e: e.memset(ones16.t[:], 1.0), [], [ones16.b])
            for i in range(4):
                self.load(wuk[i], wuk[i].t[:], scr["W_uk"].t[i], reads=[scr["W_uk"].b])
                self.load(wuv[i], wuv[i].t[:], scr["W_uv"].t[i], reads=[scr["W_uv"].b])

            for kb in blocks:
              try:
                tiles = [4 * kb + j for j in range(4)]
                tok0 = 4 * kb * 128
                if kb == 34:
                    own = {0: 16, 1: 17, 2: 18, 3: 19}
                elif kb % 2 == 1:
                    own = {3: (kb - 1) // 2}
                else:
                    own = {}
                self.load(cst, cst.t[:], inp["cs"][tok0:tok0 + 512, :].rearrange("(j p) c -> p j c", p=128))
                self.load(vld, vld.t[:], inp["valid"][:, 4 * kb:4 * kb + 4])
                self.ck(1)
                self.load_norm_T(st, tiles, g_mix, hT, xts, hb, tmp, psT, identb)
                self.ck(2)
                for cb in range(4):
                    wt = wts.next()
                    self.load(wt, wt.t[:], scr["W_ak"].t[cb], reads=[scr["W_ak"].b])
                    for j in range(4):
                        head = 4 * cb + j
                        ps = psM.next()
                        for kc in range(16):
                            self.mm(ps, ps.t[:, :], wt.t[:, kc, j * 128:(j + 1) * 128], hT.t[:, kc, :], kc == 0, kc == 15,
                                    [wt.b, hT.b])
                        kst = ksts.next()
                        P.op("act", lambda e, kst=kst, ps=ps: e.activation(out=kst.t[:], in_=ps.t[:, :], func=AF.Copy),
                             [ps.b], [kst.b])
                        self.store_cols(kst, scr["KT_A"], lambda a, n, head=head: scr["KT_A"].t[head, :, a:a + n], kst.t, tiles)
                    for pos, o in own.items():
                        ps = psM.next()
                        for kc in range(16):
                            self.mm(ps, ps.t[:, :], hT.t[:, kc, pos * 128:(pos + 1) * 128], wt.t[:, kc, :], kc == 0, kc == 15,
                                    [wt.b, hT.b])
                        ost = osts.next()
                        P.op("act", lambda e, ost=ost, ps=ps: e.activation(out=ost.t[:], in_=ps.t[:, :], func=AF.Copy),
                             [ps.b], [ost.b])
                        self.store(ost, out["st_ak"][o * 128:(o + 1) * 128, cb * 512:(cb + 1) * 512], ost.t[:])
                self.ck(3)
                for cb in range(4):
                    wt = wts.next()
                    self.load(wt, wt.t[:], scr["W_av"].t[cb], reads=[scr["W_av"].b])
                    for pos in range(4):
                        ps = psM.next()
                        for kc in range(16):
                            self.mm(ps, ps.t[:, :], hT.t[:, kc, pos * 128:(pos + 1) * 128], wt.t[:, kc, :], kc == 0, kc == 15,
                                    [wt.b, hT.b])
                        v = vst[pos]
                        P.op("dve", lambda e, v=v, ps=ps, cb=cb: e.tensor_copy(
                            v.t[:, 4 * cb:4 * cb + 4, 0:128], ps.t[:, :].rearrange("p (h d) -> p h d", h=4)), [ps.b], [v.b])
                        if pos in own:
                            o = own[pos]
                            ost = osts.next()
                            P.op("dve", lambda e, ost=ost, ps=ps: e.tensor_copy(ost.t[:], ps.t[:, :]), [ps.b], [ost.b])
                            self.store(ost, out["st_av"][o * 128:(o + 1) * 128, cb * 512:(cb + 1) * 512], ost.t[:])
                for pos in range(4):
                    v = vst[pos]
                    P.op("dve", lambda e, v=v, pos=pos: e.tensor_scalar(v.t[:, :, 128:129], ones16.t[:], vld.t[:, pos:pos + 1], None,
                                                                        ALU.mult), [ones16.b, vld.b], [v.b])
                    self.store(v, scr["V_A"].t[:, :, self.tmap(tiles[pos]), 0:129].rearrange("h p c -> p h c"), v.t[:, :, 0:129],
                               scr["V_A"])
                self.ck(4)
                wt = wts.next()
                self.load(wt, wt.t[:, :, 0:384], scr["W_misc"].t[0][:, :, 0:384], reads=[scr["W_misc"].b])
                for pos in range(4):
                    ps = psX.next()
                    for kc in range(16):
                        self.mm(ps, ps.t[:, 0:384], hT.t[:, kc, pos * 128:(pos + 1) * 128], wt.t[:, kc, 0:384], kc == 0, kc == 15,
                                [wt.b, hT.b])
                    m = mst[pos]
                    P.op("act", lambda e, m=m, ps=ps: e.activation(out=m.t[:], in_=ps.t[:, 0:384], func=AF.Copy), [ps.b], [m.b])
                self.ck(41)
                for pos in range(4):
                    m = mst[pos]
                    o_ = mo[pos]
                    self.norm_rows(m.t[:, 0:256], 256, g_kv.t[:], o_.t[:, 0:256], m.b, g_kv.b, o_.b, tmp)
                    self.ck(42)
                    P.op("act", lambda e, m=m, o_=o_: e.activation(out=o_.t[:, 256:320], in_=m.t[:, 256:320], func=AF.Copy),
                         [m.b], [o_.b])
                    self.rope(o_.t[:, 320:352], o_.t[:, 352:384], m.t[:, 320:352], m.t[:, 352:384],
                              cst.t[:, pos, 0:32], cst.t[:, pos, 32:64], m.b, cst.b, o_.b, (rt1, rt2))
                    self.ck(43)
                    if pos in own:
                        o = own[pos]
                        self.store(o_, out["st_ckv"][o * 128:(o + 1) * 128, :], o_.t[:, 0:256])
                        self.store(o_, out["st_ix"][o * 128:(o + 1) * 128, :], o_.t[:, 256:320])
                        self.store(o_, out["st_kr"][o * 128:(o + 1) * 128, :], o_.t[:, 320:384])
                    P.op("act", lambda e, o_=o_: e.activation(out=km.t[:], in_=o_.t[:], func=AF.Copy), [o_.b], [km.b])
                    self.ck(431)
                    ps = psX.next()
                    for k in range(3):
                        self.mm(ps, ps.t[:, k * 128:(k + 1) * 128], km.t[:, k * 128:(k + 1) * 128], identb.t[:], True, True,
                                [km.b, identb.b])
                    self.ck(432)
                    P.op("dve", lambda e, ps=ps, pos=pos: e.tensor_copy(
                        ckvT.t[:, :, pos * 128:(pos + 1) * 128], ps.t[:, 0:256].rearrange("p (c t) -> p c t", c=2)),
                        [ps.b], [ckvT.b])
                    self.ck(4321)
                    P.op("dve", lambda e, ps=ps, pos=pos: e.tensor_copy(ixkrT.t[:, pos * 128:(pos + 1) * 128], ps.t[:, 256:384]),
                         [ps.b], [ixkrT.b])
                    self.ck(433)
                self.ck(44)
                self.store_cols(ixkrT, scr["IXK_T"], lambda a, n: scr["IXK_T"].t[:, a:a + n], ixkrT.t[0:64], tiles)
                self.store_cols(ixkrT, scr["KR_T"], lambda a, n: scr["KR_T"].t[:, a:a + n], ixkrT.t[64:128], tiles)
                self.ck(5)
                if int(os.environ.get("MK_KSTOP", "0")) == 6:
                    self.mla_kside(ckvT, wuk, wuv, psM, ksts, vbst, tiles, tok0, lambda pos: vld.t[:, pos:pos + 1], vld, ones16)
                self.ck(6)
                self.mla_kside(ckvT, wuk, wuv, psM, ksts, vbst, tiles, tok0, lambda pos: vld.t[:, pos:pos + 1], vld, ones16)
              except _Stop:
                break
            self.store_q = "pool"
            P.barrier()
            self.release_scope(locals())

    def mla_kside(self, ckvT, wuk, wuv, psM, ksts, vbst, tiles, tok0, vld_ap_fn, vld, ones16, ncols=512):
        P = self.P
        scr = self.scr
        for h in range(NH):
            ps = psM.next()
            for cc in range(2):
                self.mm(ps, ps.t[:, 0:ncols], wuk[h // 4].t[:, cc, (h % 4) * 128:(h % 4 + 1) * 128], ckvT.t[:, cc, 0:ncols],
                        cc == 0, cc == 1, [wuk[h // 4].b, ckvT.b])
            kst = ksts.next()
            P.op("act", lambda e, kst=kst, ps=ps: e.activation(out=kst.t[:, 0:ncols], in_=ps.t[:, 0:ncols], func=AF.Copy),
                 [ps.b], [kst.b])
            self.store_cols(kst, scr["KT_B"], lambda a, n, h=h: scr["KT_B"].t[h, :, a:a + n], kst.t, tiles)
        for pos in range(len(tiles)):
            v = vbst[pos % len(vbst)]
            for cb in range(4):
                ps = psM.next()
                for cc in range(2):
                    self.mm(ps, ps.t[:, :], ckvT.t[:, cc, pos * 128:(pos + 1) * 128], wuv[cb].t[:, cc, :], cc == 0, cc == 1,
                            [wuv[cb].b, ckvT.b])
                P.op("dve", lambda e, v=v, ps=ps, cb=cb: e.tensor_copy(
                    v.t[:, 4 * cb:4 * cb + 4, 0:128], ps.t[:, :].rearrange("p (h d) -> p h d", h=4)), [ps.b], [v.b])
            if vld is not None:
                P.op("dve", lambda e, v=v, pos=pos: e.tensor_scalar(v.t[:, :, 128:129], ones16.t[:], vld_ap_fn(pos), None,
                                                                    ALU.mult), [ones16.b, vld.b], [v.b])
                self.store(v, scr["V_B"].t[:, :, self.tmap(tiles[pos]), 0:129].rearrange("h p c -> p h c"), v.t[:, :, 0:129], scr["V_B"])
            else:
                self.store(v, scr["V_B"].t[:, :, self.tmap(tiles[pos]), 0:129].rearrange("h p c -> p h c"), v.t[:, :, 0:129], scr["V_B"])

    def release_scope(self, loc):
        bufs = []
        for v in loc.values():
            if isinstance(v, TB):
                bufs.append(v.b)
            elif isinstance(v, (list, tuple)):
                for x in v:
                    if isinstance(x, TB):
                        bufs.append(x.b)
            elif isinstance(v, Rot):
                for x in v.items:
                    if isinstance(x, TB):
                        bufs.append(x.b)
            elif isinstance(v, dict):
                for x in v.values():
                    if isinstance(x, TB):
                        bufs.append(x.b)
        self.P.release(bufs)

    def phase_KC(self):
        P = self.P
        inp, scr = self.inp, self.scr
        with ExitStack() as st:
            sb = lambda name, shape, dt=F32: self.sb(st, name, shape, dt)
            wuk = [sb("c_wuk%d" % i, [128, 2, 512], BF16) for i in range(4)]
            wuv = [sb("c_wuv%d" % i, [128, 2, 512], BF16) for i in range(4)]
            ckvT = sb("c_ckvT", [128, 2, 512], BF16)
            ksts = Rot([sb("c_kst%d" % i, [128, 512], BF16) for i in range(2)])
            vbst = [sb("c_vbst%d" % i, [128, 16, 130], BF16) for i in range(4)]
            PS = self.PS
            psM = Rot([PS[2], PS[3], PS[4], PS[5]])
            for i in range(4):
                self.load(wuk[i], wuk[i].t[:], scr["W_uk"].t[i], reads=[scr["W_uk"].b])
                self.load(wuv[i], wuv[i].t[:], scr["W_uv"].t[i], reads=[scr["W_uv"].b])
                P.op("dve", lambda e, v=vbst[i]: e.memset(v.t[:, :, 128:130], 1.0), [], [vbst[i].b])
            for b in range(2):
                for half in range(2):
                    t0 = 140 + 9 * b + 4 * half
                    self.load(ckvT, ckvT.t[:], inp["c_ckvT"][b, :, half * 512:(half + 1) * 512].rearrange("(c p) s -> p c s", p=128),
                              q="pool")
                    self.mla_kside(ckvT, wuk, wuv, psM, ksts, vbst, [t0 + j for j in range(4)], t0 * 128, None, None, None)
            P.barrier()
            self.release_scope(locals())


    @staticmethod
    def tmap(t):
        return {136: 148, 137: 157}.get(t, t)

    def store_cols(self, src, dst, dst_row_ap_fn, src_t, tiles):
        mapped = [self.tmap(t) for t in tiles]
        if all(mapped[i] == mapped[0] + i for i in range(len(mapped))):
            self.store(src, dst_row_ap_fn(mapped[0] * 128, len(mapped) * 128), src_t[:, 0:len(mapped) * 128], dst)
        else:
            for i, mt in enumerate(mapped):
                self.store(src, dst_row_ap_fn(mt * 128, 128), src_t[:, i * 128:(i + 1) * 128], dst)

    def own_tiles(self, b):
        return [136 + j for j in range(4)] if b == 4 else [8 * (4 * b + j) + 7 for j in range(4)]

    def phase_Q(self):
        P = self.P
        inp, scr = self.inp, self.scr
        with ExitStack() as st:
            sb = lambda name, shape, dt=F32: self.sb(st, name, shape, dt)
            g_mix = sb("q_gmix", [128, D])
            g_q = sb("q_gq", [128, 512])
            identf = sb("q_identf", [128, 128])
            identb = sb("q_identb", [128, 128], BF16)
            xts = Rot([sb("q_x%d" % i, [128, D]) for i in range(2)])
            hb = sb("q_hb", [128, D], BF16)
            hT = sb("q_hT", [128, 16, 512], BF16)
            wts = Rot([sb("q_wt%d" % i, [128, 16, 512], BF16) for i in range(3)])
            qsts = Rot([sb("q_qst%d" % i, [128, 512], BF16) for i in range(3)])
            wuqn = [sb("q_wuqn%d" % i, [128, 4, 512], BF16) for i in range(4)]
            wuqr = [sb("q_wuqr%d" % i, [128, 4, 512], BF16) for i in range(2)]
            cq_f = sb("q_cqf", [128, 512])
            cqb = sb("q_cqb", [128, 512], BF16)
            cqT = sb("q_cqT", [128, 4, 512], BF16)
            qr_f = sb("q_qrf", [128, 1024])
            qr_o = sb("q_qro", [128, 1024])
            qrb = sb("q_qrb", [128, 1024], BF16)
            qrT = sb("q_qrT", [128, 8, 512], BF16)
            ixw = sb("q_ixw", [128, 4, 16])
            cst = sb("q_cs", [128, 4, 64])
            tmp = dict(junk=sb("q_junk", [128, D], BF16), ss=sb("q_ss", [128, 1]), sd=sb("q_sd", [128, 1]),
                       rstd=sb("q_rstd", [128, 1]))
            rt1 = sb("q_rt1", [128, 32])
            rt2 = sb("q_rt2", [128, 32])
            rt1.t_ap = rt1.t[:]
            rt2.t_ap = rt2.t[:]
            PS = self.PS
            psT = Rot([PS[0], PS[1]])
            psM = Rot([PS[2], PS[3], PS[4], PS[5]])
            psX = Rot([PS[6], PS[7]])
            self.load(g_mix, g_mix.t[:], inp["g_mix"][:, :])
            self.load(g_q, g_q.t[:], inp["g_q"][:, :])
            self.load(identf, identf.t[:], inp["ident"][:, :])
            P.op("dve", lambda e: e.tensor_copy(identb.t[:], identf.t[:]), [identf.b], [identb.b])
            for i in range(4):
                self.load(wuqn[i], wuqn[i].t[:], scr["W_uqn"].t[i], reads=[scr["W_uqn"].b])
            for i in range(2):
                self.load(wuqr[i], wuqr[i].t[:], scr["W_uqr"].t[i], reads=[scr["W_uqr"].b])
            for b in range(5):
                tiles = self.own_tiles(b)
                for j, tile in enumerate(tiles):
                    self.load(cst, cst.t[:, j, :], inp["cs"][tile * 128:(tile + 1) * 128, :])
                self.load_norm_T(st, tiles, g_mix, hT, xts, hb, tmp, psT, identb)
                for grp, nblk, dst in (("aq", 4, "QA_T"), ("ixq", 2, "IXQ_T")):
                    for cb in range(nblk):
                        wt = wts.next()
                        self.load(wt, wt.t[:], scr["W_" + grp].t[cb], reads=[scr["W_" + grp].b])
                        for j in range(4):
                            ps = psM.next()
                            for kc in range(16):
                                self.mm(ps, ps.t[:, :], wt.t[:, kc, j * 128:(j + 1) * 128], hT.t[:, kc, :], kc == 0, kc == 15,
                                        [wt.b, hT.b])
                            q = qsts.next()
                            P.op("act", lambda e, q=q, ps=ps: e.activation(out=q.t[:], in_=ps.t[:, :], func=AF.Copy), [ps.b], [q.b])
                            self.store(q, scr[dst].t[b][:, 4 * cb + j, :], q.t[:], scr[dst])
                wt = wts.next()
                self.load(wt, wt.t[:], scr["W_cq"].t[0], reads=[scr["W_cq"].b])
                for pos in range(4):
                    ps = psM.next()
                    for kc in range(16):
                        self.mm(ps, ps.t[:, :], hT.t[:, kc, pos * 128:(pos + 1) * 128], wt.t[:, kc, :], kc == 0, kc == 15, [wt.b, hT.b])
                    P.op("act", lambda e, ps=ps: e.activation(out=cq_f.t[:], in_=ps.t[:, :], func=AF.Copy), [ps.b], [cq_f.b])
                    self.norm_rows(cq_f.t[:], 512, g_q.t[:], cqb.t[:], cq_f.b, g_q.b, cqb.b, tmp)
                    self.transpose_into(cqb, cqb.t, 4, cqT, lambda c, pos=pos: cqT.t[:, c, pos * 128:(pos + 1) * 128], psT, identb)
                wt = wts.next()
                self.load(wt, wt.t[:, :, 0:16], scr["W_cq"].t[1][:, :, 0:16], reads=[scr["W_cq"].b])
                for pos in range(4):
                    ps = psX.next()
                    for kc in range(16):
                        self.mm(ps, ps.t[:, 0:16], hT.t[:, kc, pos * 128:(pos + 1) * 128], wt.t[:, kc, 0:16], kc == 0, kc == 15,
                                [wt.b, hT.b])
                    P.op("act", lambda e, ps=ps, pos=pos: e.activation(out=ixw.t[:, pos, :], in_=ps.t[:, 0:16], func=AF.Copy, scale=0.25),
                         [ps.b], [ixw.b])
                self.store(ixw, scr["IXW"].t[b], ixw.t[:], scr["IXW"])
                for h in range(NH):
                    ps = psM.next()
                    for cc in range(4):
                        self.mm(ps, ps.t[:, :], wuqn[h // 4].t[:, cc, (h % 4) * 128:(h % 4 + 1) * 128], cqT.t[:, cc, :], cc == 0, cc == 3,
                                [wuqn[h // 4].b, cqT.b])
                    q = qsts.next()
                    P.op("act", lambda e, q=q, ps=ps: e.activation(out=q.t[:], in_=ps.t[:, :], func=AF.Copy), [ps.b], [q.b])
                    self.store(q, scr["QN_T"].t[b][:, h, :], q.t[:], scr["QN_T"])
                for pos in range(4):
                    for blk in range(2):
                        ps = psM.next()
                        for cc in range(4):
                            self.mm(ps, ps.t[:, :], cqT.t[:, cc, pos * 128:(pos + 1) * 128], wuqr[blk].t[:, cc, :], cc == 0, cc == 3,
                                    [wuqr[blk].b, cqT.b])
                        P.op("act", lambda e, ps=ps, blk=blk: e.activation(out=qr_f.t[:, blk * 512:(blk + 1) * 512], in_=ps.t[:, :],
                                                                           func=AF.Copy), [ps.b], [qr_f.b])
                    for h in range(NH):
                        c0 = h * 64
                        self.rope(qr_o.t[:, c0:c0 + 32], qr_o.t[:, c0 + 32:c0 + 64], qr_f.t[:, c0:c0 + 32], qr_f.t[:, c0 + 32:c0 + 64],
                                  cst.t[:, pos, 0:32], cst.t[:, pos, 32:64], qr_f.b, cst.b, qr_o.b, (rt1, rt2))
                    P.op("act", lambda e: e.activation(out=qrb.t[:], in_=qr_o.t[:], func=AF.Copy), [qr_o.b], [qrb.b])
                    self.transpose_into(qrb, qrb.t, 8, qrT, lambda c, pos=pos: qrT.t[:, c, pos * 128:(pos + 1) * 128], psT, identb)
                self.store(qrT, scr["QR_T"].t[b], qrT.t[:], scr["QR_T"])
            P.barrier()
            self.release_scope(locals())

    def slots_all(self):
        P = self.P
        inp, scr = self.inp, self.scr
        with ExitStack() as st:
            sb = lambda name, shape, dt=F32: self.sb(st, name, shape, dt)
            C = {}
            C["identf"] = sb("s_identf", [128, 128])
            C["identb"] = sb("s_identb", [128, 128], BF16)
            C["Tb"] = sb("s_Tb", [128, 16, 2, 128])
            C["constb"] = sb("s_constb", [128, 16])
            C["prefmask"] = sb("s_pref", [128, 896])
            C["dsa_diag"] = sb("s_dsadiag", [128, 2, 128])
            C["mlaf"] = sb("s_mlaf", [128, 2, 256])
            C["mlamask"] = sb("s_mlamask", [128, 2, 256], BF16)
            relb = sb("s_relb", [32, 16])
            oh = sb("s_oh", [32, 4096])
            self.load(C["identf"], C["identf"].t[:], inp["ident"][:, :])
            P.op("dve", lambda e: e.tensor_copy(C["identb"].t[:], C["identf"].t[:]), [C["identf"].b], [C["identb"].b])
            self.load(C["constb"], C["constb"].t[:], inp["constb"][:, :])
            self.load(C["prefmask"], C["prefmask"].t[:], inp["prefmask"][:, :])
            self.load(C["dsa_diag"], C["dsa_diag"].t[:], inp["dsa_diag"].rearrange("y t s -> t y s"))
            self.load(C["mlaf"], C["mlaf"].t[:], inp["mla_mask"].rearrange("y s c -> s y c"))
            P.op("dve", lambda e: e.tensor_copy(C["mlamask"].t[:], C["mlaf"].t[:]), [C["mlaf"].b], [C["mlamask"].b])
            self.load(relb, relb.t[:], inp["relb"][:, :])
            PS = self.PS
            for ty in range(2):
                for tg in range(4):
                    self.load(oh, oh.t[:], inp["onehot"][ty][:, tg * 4096:(tg + 1) * 4096])
                    ps = PS[tg % 2]
                    for t in range(32):
                        self.mm(ps, ps.t[:, t * 16:(t + 1) * 16], oh.t[:, t * 128:(t + 1) * 128], relb.t[:, :], True, True, [oh.b, relb.b])
                    P.op("dve", lambda e, ps=ps, ty=ty, tg=tg: e.tensor_copy(
                        C["Tb"].t[:, :, ty, tg * 32:(tg + 1) * 32], ps.t[:, :].rearrange("p (t h) -> p h t", h=16)), [ps.b], [C["Tb"].b])
            P.barrier()
            nslot = int(os.environ.get("MK_NSLOT", "18"))
            order = []
            for o in range(16):
                order.append((o // 4, o % 4, [(0, 8 * o + 8)], 0))
            order.append((4, 0, [(140, 9)], 1))
            order.append((4, 1, [(149, 9)], 1))
            if nslot < 18:
                order = [order[0], order[16], order[1], order[17]][:nslot]
            for (b, j, runs, ty) in order:
                self.slot(C, b, j, runs, ty)
            self.release_scope(dict(C=C, relb=relb, oh=oh))

    def slot(self, C, b, j, runs, ty):
        P = self.P
        inp, scr = self.inp, self.scr
        PS = self.PS
        o = 4 * b + j
        ktl = []
        for (t0, n) in runs:
            ktl += [t0 + i for i in range(n)]
        nk = len(ktl)
        S = nk * 128
        identb = C["identb"]
        with ExitStack() as st:
            sb = lambda name, shape, dt=F32: self.sb(st, name, shape, dt)
            qa = sb("l_qa", [128, 16, 128], BF16)
            ixq = sb("l_ixq", [128, 8, 128], BF16)
            ixw = sb("l_ixw", [128, 16])
            qn = sb("l_qn", [128, 16, 128], BF16)
            qr = sb("l_qr", [128, 8, 128], BF16)
            maskadd = sb("l_maskadd", [128, S], BF16)
            js = slice(j * 128, (j + 1) * 128)
            self.load(qa, qa.t[:], scr["QA_T"].t[b][:, :, js], reads=[scr["QA_T"].b])
            self.load(ixq, ixq.t[:], scr["IXQ_T"].t[b][:, :, js], reads=[scr["IXQ_T"].b])
            self.load(ixw, ixw.t[:], scr["IXW"].t[b][:, j, :], reads=[scr["IXW"].b])
            self.load(qn, qn.t[:], scr["QN_T"].t[b][:, :, js], reads=[scr["QN_T"].b])
            self.load(qr, qr.t[:], scr["QR_T"].t[b][:, :, js], reads=[scr["QR_T"].b])
            with ExitStack() as st2:
                sb2 = lambda name, shape, dt=F32: self.sb(st2, name, shape, dt)
                row = sb2("i_row", [128, S])
                ixks = Rot([sb2("i_ixk%d" % i, [128, 512], BF16) for i in range(2)])
                rbs = Rot([sb2("i_r%d" % i, [128, 512]) for i in range(3)])
                mx = sb2("i_mx", [128, 1])
                mid = sb2("i_mid", [128, 1])
                cnt = sb2("i_cnt", [128, 1])
                tfl = sb2("i_tfl", [128, 1])
                thr = sb2("i_thr", [128, 1])
                psR = Rot([PS[0], PS[1], PS[2], PS[3]])
                col = 0
                for (t0, n) in runs:
                    k = 0
                    while k < n:
                        g = min(4, n - k)
                        W = g * 128
                        tok = (t0 + k) * 128
                        ixk = ixks.next()
                        self.load(ixk, ixk.t[0:64, 0:W], scr["IXK_T"].t[:, tok:tok + W], reads=[scr["IXK_T"].b])
                        self.load(ixk, ixk.t[64:128, 0:W], scr["IXK_T"].t[:, tok:tok + W], reads=[scr["IXK_T"].b])
                        for h in range(16):
                            hf = h % 2
                            ps = psR.next()
                            self.mm(ps, ps.t[:, 0:W], ixq.t[hf * 64:(hf + 1) * 64, h // 2, :], ixk.t[hf * 64:(hf + 1) * 64, 0:W],
                                    True, True, [ixq.b, ixk.b])
                            r = rbs.next()
                            P.op("act", lambda e, r=r, ps=ps, W=W: e.activation(out=r.t[:, 0:W], in_=ps.t[:, 0:W], func=AF.Relu),
                                 [ps.b], [r.b])
                            if h == 0:
                                P.op("dve", lambda e, r=r, W=W, col=col: e.tensor_scalar(
                                    row.t[:, col:col + W], r.t[:, 0:W], ixw.t[:, 0:1], None, ALU.mult), [r.b, ixw.b], [row.b])
                            else:
                                P.op("dve", lambda e, r=r, W=W, col=col, h=h: e.scalar_tensor_tensor(
                                    out=row.t[:, col:col + W], in0=r.t[:, 0:W], scalar=ixw.t[:, h:h + 1], in1=row.t[:, col:col + W],
                                    op0=ALU.mult, op1=ALU.add), [r.b, ixw.b, row.b], [row.b])
                        col += W
                        k += g
                if ty == 0:
                    P.op("dve", lambda e: e.tensor_tensor(row.t[:, 0:896], row.t[:, 0:896], C["prefmask"].t[:], ALU.add),
                         [row.b, C["prefmask"].b], [row.b])
                P.op("dve", lambda e: e.tensor_tensor(row.t[:, S - 128:S], row.t[:, S - 128:S], C["dsa_diag"].t[:, ty, :], ALU.add),
                     [row.b, C["dsa_diag"].b], [row.b])
                P.op("dve", lambda e: e.tensor_reduce(mx.t[:], row.t[:], AX.X, ALU.max), [row.b], [mx.b])
                P.op("dve", lambda e: e.tensor_scalar(mid.t[:], mx.t[:], -BIS_R / 2, None, ALU.add), [mx.b], [mid.b])
                for it in range(BIS_IT):
                    hw = BIS_R / (2 ** (it + 1))
                    P.op("dve", lambda e: e.tensor_scalar(maskadd.t[:], row.t[:], mid.t[:], None, ALU.is_ge, ALU.add, accum_out=cnt.t[:]),
                         [row.b, mid.b], [maskadd.b, cnt.b])
                    P.op("dve", lambda e, hw=hw: e.tensor_scalar(tfl.t[:], cnt.t[:], float(TOPK) - 0.5, hw, ALU.is_ge, ALU.mult),
                         [cnt.b], [tfl.b])
                    P.op("dve", lambda e, hw=hw: e.scalar_tensor_tensor(out=mid.t[:], in0=tfl.t[:], scalar=-hw / 2, in1=mid.t[:],
                                                                        op0=ALU.add, op1=ALU.add), [tfl.b, mid.b], [mid.b])
                hwK = BIS_R / (2 ** (BIS_IT + 1))
                P.op("dve", lambda e: e.tensor_scalar(thr.t[:], mid.t[:], -hwK, None, ALU.add), [mid.b], [thr.b])
                P.op("dve", lambda e: e.tensor_scalar(maskadd.t[:], row.t[:], thr.t[:], MASKNEG, ALU.is_lt, ALU.mult),
                     [row.b, thr.b], [maskadd.b])
                P.barrier()
                self.release_scope(locals())
            with ExitStack() as st3:
                sb3 = lambda name, shape, dt=F32: self.sb(st3, name, shape, dt)
                kcs = Rot([sb3("a_kc%d" % i, [128, 2048], BF16) for i in range(4)])
                vcs = Rot([sb3("a_vc%d" % i, [128, 16, 130], BF16) for i in range(4)])
                krr = sb3("a_kr", [128, S], BF16)
                pts = Rot([sb3("a_p%d" % i, [128, 512], BF16) for i in range(4)])
                p2s = Rot([sb3("a_p2%d" % i, [128, 256], BF16) for i in range(2)])
                tmpn = sb3("a_tmpn", [128, 256])
                rec = sb3("a_rec", [128, 1])
                ost = sb3("a_ost", [128, D])
                psS = Rot([PS[0], PS[1], PS[2], PS[3]])
                psA = Rot([PS[4], PS[5]])
                col = 0
                for (t0, n) in runs:
                    self.load(krr, krr.t[0:64, col:col + n * 128], scr["KR_T"].t[:, t0 * 128:(t0 + n) * 128], reads=[scr["KR_T"].b])
                    self.load(krr, krr.t[64:128, col:col + n * 128], scr["KR_T"].t[:, t0 * 128:(t0 + n) * 128], reads=[scr["KR_T"].b])
                    col += n * 128
                chunks = []
                gk = 0
                for (t0, n) in runs:
                    k = 0
                    while k < n:
                        m = min(16, n - k)
                        chunks.append((t0 + k, m, gk))
                        gk += m
                        k += m
                for kind in ("dsa", "mla"):
                    KT = scr["KT_A"] if kind == "dsa" else scr["KT_B"]
                    VV = scr["V_A"] if kind == "dsa" else scr["V_B"]
                    for h in range(NH):
                        acc = psA.next()
                        hf = h % 2
                        for (t0, m, gk0) in chunks:
                            kc = kcs.next()
                            vc = vcs.next()
                            self.load(kc, kc.t[:, 0:m * 128], KT.t[h, :, t0 * 128:(t0 + m) * 128], reads=[KT.b])
                            self.load(vc, vc.t[:, 0:m, :], VV.t[h, :, t0:t0 + m, :], reads=[VV.b])
                            k = 0
                            while k < m:
                                gkt = gk0 + k
                                if gkt >= nk - 2:
                                    g = nk - gkt
                                    near = True
                                else:
                                    g = min(4, m - k, nk - 2 - gkt)
                                    near = False
                                Sps = psS.next()
                                for gi in range(g):
                                    cs_ = slice(gi * 128, (gi + 1) * 128)
                                    kl = k + gi
                                    if kind == "dsa":
                                        self.mm(Sps, Sps.t[:, cs_], kc.t[:, kl * 128:(kl + 1) * 128], qa.t[:, h, :], True, False,
                                                [kc.b, qa.b])
                                        self.mm(Sps, Sps.t[:, cs_], maskadd.t[:, (gkt + gi) * 128:(gkt + gi + 1) * 128], identb.t[:],
                                                False, True, [maskadd.b, identb.b])
                                    else:
                                        self.mm(Sps, Sps.t[:, cs_], kc.t[:, kl * 128:(kl + 1) * 128], qn.t[:, h, :], True, False,
                                                [kc.b, qn.b])
                                        self.mm(Sps, Sps.t[:, cs_], krr.t[hf * 64:(hf + 1) * 64, (gkt + gi) * 128:(gkt + gi + 1) * 128],
                                                qr.t[hf * 64:(hf + 1) * 64, h // 2, :], False, True, [krr.b, qr.b])
                                W = g * 128
                                p = pts.next()
                                if kind == "dsa":
                                    if near:
                                        P.op("dve", lambda e, Sps=Sps, h=h: e.scalar_tensor_tensor(
                                            out=tmpn.t[:], in0=Sps.t[:, 0:256], scalar=A_SCALE,
                                            in1=C["Tb"].t[:, h, :, :].rearrange("p y t -> p (y t)"), op0=ALU.mult, op1=ALU.add),
                                            [Sps.b, C["Tb"].b], [tmpn.b])
                                        P.op("act", lambda e, p=p: e.activation(out=p.t[:, 0:256], in_=tmpn.t[:], func=AF.Exp),
                                             [tmpn.b], [p.b])
                                    else:
                                        P.op("act", lambda e, p=p, Sps=Sps, W=W, h=h: e.activation(
                                            out=p.t[:, 0:W], in_=Sps.t[:, 0:W], func=AF.Exp, bias=C["constb"].t[:, h:h + 1], scale=A_SCALE),
                                            [Sps.b, C["constb"].b], [p.b])
                                    pp = p
                                else:
                                    P.op("act", lambda e, p=p, Sps=Sps, W=W: e.activation(out=p.t[:, 0:W], in_=Sps.t[:, 0:W], func=AF.Exp,
                                                                                       scale=MLA_SCALE), [Sps.b], [p.b])
                                    pp = p
                                    if near:
                                        p2 = p2s.next()
                                        P.op("dve", lambda e, p=p, p2=p2: e.tensor_tensor(p2.t[:], p.t[:, 0:256], C["mlamask"].t[:, ty, :],
                                                                                         ALU.mult), [p.b, C["mlamask"].b], [p2.b])
                                        pp = p2
                                for gi in range(g):
                                    kl = k + gi
                                    self.mm(acc, acc.t[:, 0:129], pp.t[:, gi * 128:(gi + 1) * 128], vc.t[:, kl, 0:129],
                                            (gkt + gi) == 0, (gkt + gi) == nk - 1, [pp.b, vc.b])
                                k += g
                        P.op("dve", lambda e, acc=acc: e.reciprocal(rec.t[:], acc.t[:, 128:129]), [acc.b], [rec.b])
                        P.op("dve", lambda e, acc=acc, h=h: e.tensor_scalar(ost.t[:, h * 128:(h + 1) * 128], acc.t[:, 0:128], rec.t[:], None,
                                                                            ALU.mult), [acc.b, rec.b], [ost.b])
                    dst = scr["OA"] if kind == "dsa" else scr["OB"]
                    self.store(ost, dst.t[o], ost.t[:], dst)
                P.barrier()
                self.release_scope(locals())
            self.release_scope(dict(qa=qa, ixq=ixq, ixw=ixw, qn=qn, qr=qr, maskadd=maskadd))

    def phase_M(self):
        P = self.P
        inp, out, scr = self.inp, self.out, self.scr
        PS = self.PS
        nb = int(os.environ.get("MK_NMB", "5"))
        for b in list(range(5))[:nb] if nb >= 5 else [0, 4][:nb]:
            tiles = self.own_tiles(b)
            with ExitStack() as so:
                xk = [self.sb(so, "m_x%d" % i, [128, D]) for i in range(4)]
                with ExitStack() as st:
                    sb = lambda name, shape, dt=F32: self.sb(st, name, shape, dt)
                    g_mix = sb("m_gmix", [128, D])
                    identf = sb("m_identf", [128, 128])
                    identb = sb("m_identb", [128, 128], BF16)
                    hb = sb("m_hb", [128, D], BF16)
                    hT = sb("m_hT", [128, 16, 512], BF16)
                    wts = Rot([sb("m_wt%d" % i, [128, 16, 512], BF16) for i in range(2)])
                    gas = Rot([sb("m_ga%d" % i, [128, 512]) for i in range(2)])
                    gbs = Rot([sb("m_gb%d" % i, [128, 512]) for i in range(2)])
                    oas = Rot([sb("m_oa%d" % i, [128, 512]) for i in range(2)])
                    obs = Rot([sb("m_ob%d" % i, [128, 512]) for i in range(2)])
                    mix = [sb("m_mix%d" % i, [128, D], BF16) for i in range(4)]
                    tmp = dict(junk=sb("m_junk", [128, D], BF16), ss=sb("m_ss", [128, 1]), sd=sb("m_sd", [128, 1]),
                               rstd=sb("m_rstd", [128, 1]))
                    psT = Rot([PS[0], PS[1]])
                    psM = Rot([PS[2], PS[3], PS[4], PS[5]])
                    self.load(g_mix, g_mix.t[:], inp["g_mix"][:, :])
                    self.load(identf, identf.t[:], inp["ident"][:, :])
                    P.op("dve", lambda e: e.tensor_copy(identb.t[:], identf.t[:]), [identf.b], [identb.b])
                    self.load_norm_T(st, tiles, g_mix, hT, None, hb, tmp, psT, identb, keep_x=xk)
                    for i in range(4):
                        wa = wts.next()
                        wb = wts.next()
                        self.load(wa, wa.t[:], scr["W_gate"].t[i], reads=[scr["W_gate"].b])
                        self.load(wb, wb.t[:], scr["W_gate"].t[4 + i], reads=[scr["W_gate"].b])
                        cs_ = slice(i * 512, (i + 1) * 512)
                        for pos in range(4):
                            o = 4 * b + pos
                            pa = psM.next()
                            for kc in range(16):
                                self.mm(pa, pa.t[:, :], hT.t[:, kc, pos * 128:(pos + 1) * 128], wa.t[:, kc, :], kc == 0, kc == 15, [wa.b, hT.b])
                            pb = psM.next()
                            for kc in range(16):
                                self.mm(pb, pb.t[:, :], hT.t[:, kc, pos * 128:(pos + 1) * 128], wb.t[:, kc, :], kc == 0, kc == 15, [wb.b, hT.b])
                            ga, gb, oa, ob = gas.next(), gbs.next(), oas.next(), obs.next()
                            P.op("act", lambda e, ga=ga, pa=pa: e.activation(out=ga.t[:], in_=pa.t[:, :], func=AF.Sigmoid), [pa.b], [ga.b])
                            P.op("act", lambda e, gb=gb, pb=pb: e.activation(out=gb.t[:], in_=pb.t[:, :], func=AF.Sigmoid), [pb.b], [gb.b])
                            self.load(oa, oa.t[:], scr["OA"].t[o][:, cs_], reads=[scr["OA"].b])
                            self.load(ob, ob.t[:], scr["OB"].t[o][:, cs_], reads=[scr["OB"].b])
                            P.op("dve", lambda e, ga=ga, oa=oa: e.tensor_tensor(ga.t[:], ga.t[:], oa.t[:], ALU.mult), [ga.b, oa.b], [ga.b])
                            P.op("dve", lambda e, gb=gb, ob=ob: e.tensor_tensor(gb.t[:], gb.t[:], ob.t[:], ALU.mult), [gb.b, ob.b], [gb.b])
                            P.op("dve", lambda e, ga=ga, gb=gb, pos=pos, cs_=cs_: e.tensor_tensor(mix[pos].t[:, cs_], ga.t[:], gb.t[:], ALU.add),
                                 [ga.b, gb.b], [mix[pos].b])
                    for pos in range(4):
                        self.transpose_into(mix[pos], mix[pos].t, 16, hT, lambda kc, pos=pos: hT.t[:, kc, pos * 128:(pos + 1) * 128], psT, identb)
                    for cb in range(4):
                        wt = wts.next()
                        self.load(wt, wt.t[:], scr["W_wo"].t[cb], reads=[scr["W_wo"].b])
                        cs_ = slice(cb * 512, (cb + 1) * 512)
                        for pos in range(4):
                            ps = psM.next()
                            for kc in range(16):
                                self.mm(ps, ps.t[:, :], hT.t[:, kc, pos * 128:(pos + 1) * 128], wt.t[:, kc, :], kc == 0, kc == 15, [wt.b, hT.b])
                            P.op("dve", lambda e, ps=ps, pos=pos, cs_=cs_: e.tensor_tensor(xk[pos].t[:, cs_], xk[pos].t[:, cs_], ps.t[:, :], ALU.add),
                                 [xk[pos].b, ps.b], [xk[pos].b])
                    P.barrier()
                    self.release_scope(locals())
                with ExitStack() as st:
                    sb = lambda name, shape, dt=F32: self.sb(st, name, shape, dt)
                    g_ffn = sb("n_gffn", [128, D])
                    g_fin = sb("n_gfin", [128, D])
                    identf = sb("n_identf", [128, 128])
                    identb = sb("n_identb", [128, 128], BF16)
                    hb = sb("n_hb", [128, D], BF16)
                    h2T = sb("n_h2T", [128, 16, 512], BF16)
                    uT = sb("n_uT", [128, 64, 512], BF16)
                    wts = Rot([sb("n_wt%d" % i, [128, 16, 512], BF16) for i in range(2)])
                    rrs = Rot([sb("n_rr%d" % i, [128, 512]) for i in range(2)])
                    ys = Rot([sb("n_y%d" % i, [128, D]) for i in range(2)])
                    tmp = dict(junk=sb("n_junk", [128, D], BF16), ss=sb("n_ss", [128, 1]), sd=sb("n_sd", [128, 1]),
                               rstd=sb("n_rstd", [128, 1]))
                    psT = Rot([PS[0], PS[1]])
                    psM = Rot([PS[6], PS[7]])
                    self.load(g_ffn, g_ffn.t[:], inp["g_ffn"][:, :])
                    self.load(g_fin, g_fin.t[:], inp["g_fin"][:, :])
                    self.load(identf, identf.t[:], inp["ident"][:, :])
                    P.op("dve", lambda e: e.tensor_copy(identb.t[:], identf.t[:]), [identf.b], [identb.b])
                    for pos in range(4):
                        self.norm_rows(xk[pos].t[:], D, g_ffn.t[:], hb.t[:], xk[pos].b, g_ffn.b, hb.b, tmp)
                        self.transpose_into(hb, hb.t, 16, h2T, lambda kc, pos=pos: h2T.t[:, kc, pos * 128:(pos + 1) * 128], psT, identb)
                    for cb in range(16):
                        wt = wts.next()
                        self.load(wt, wt.t[:], scr["W_up"].t[cb], reads=[scr["W_up"].b])
                        for jj in range(4):
                            ffc = 4 * cb + jj
                            ps = psM.next()
                            for kc in range(16):
                                self.mm(ps, ps.t[:, :], wt.t[:, kc, jj * 128:(jj + 1) * 128], h2T.t[:, kc, :], kc == 0, kc == 15, [wt.b, h2T.b])
                            rr = rrs.next()
                            P.op("act", lambda e, rr=rr, ps=ps: e.activation(out=rr.t[:], in_=ps.t[:, :], func=AF.Relu), [ps.b], [rr.b])
                            P.op("dve", lambda e, rr=rr, ffc=ffc: e.tensor_tensor(uT.t[:, ffc, :], rr.t[:], rr.t[:], ALU.mult), [rr.b], [uT.b])
                    for cb in range(4):
                        cs_ = slice(cb * 512, (cb + 1) * 512)
                        pss = [PS[2], PS[3], PS[4], PS[5]]
                        for kg in range(4):
                            wt = wts.next()
                            self.load(wt, wt.t[:], scr["W_dn"].t[cb * 4 + kg], reads=[scr["W_dn"].b])
                            for pos in range(4):
                                for kc in range(16):
                                    self.mm(pss[pos], pss[pos].t[:, :], uT.t[:, kg * 16 + kc, pos * 128:(pos + 1) * 128], wt.t[:, kc, :],
                                            kg == 0 and kc == 0, kg == 3 and kc == 15, [wt.b, uT.b])
                        for pos in range(4):
                            P.op("dve", lambda e, pos=pos, cs_=cs_, pss=pss: e.tensor_tensor(xk[pos].t[:, cs_], xk[pos].t[:, cs_], pss[pos].t[:, :],
                                                                                         ALU.add), [xk[pos].b, pss[pos].b], [xk[pos].b])
                    for pos in range(4):
                        o = 4 * b + pos
                        y = ys.next()
                        self.norm_rows(xk[pos].t[:], D, g_fin.t[:], y.t[:], xk[pos].b, g_fin.b, y.b, tmp)
                        self.store(y, out["y_own"][o * 128:(o + 1) * 128, :], y.t[:])
                    P.barrier()
                    self.release_scope(locals())
                self.release_scope(dict(xk=xk))

    def build(self):
        nc = self.nc
        self.declare()
        st = self.gstack
        self.PS = [TB(st.enter_context(nc.psum_tensor("ps%d" % i, [128, 512], F32)), "ps%d" % i) for i in range(8)]
        for p_ in self.PS:
            p_.b.excl = True
        self.phase_W()
        if self.upto == "W":
            return self.finish()
        kblocks = list(range(32)) + [34]
        if self.upto == "K1":
            kblocks = [0, 1, 34][:int(os.environ.get("MK_NB", "3"))]
        self.phase_K(kblocks)
        if self.upto in ("K", "K1"):
            return self.finish()
        self.phase_KC()
        if self.upto == "KC":
            return self.finish()
        self.phase_Q()
        if self.upto == "Q":
            return self.finish()
        self.slots_all()
        if self.upto == "S":
            return self.finish()
        self.phase_M()
        return self.finish()

    def finish(self):
        self.P.finish()
        self.gstack.close()
        return self.nc


def t5_bucket_np(rel):
    nb = 16
    ret = (rel > 0).astype(np.int32) * nb
    n = np.abs(rel)
    max_exact = 8
    nf = np.maximum(n, 1).astype(np.float32)
    large = max_exact + (np.log(nf / np.float32(max_exact)) / np.float32(math.log(128 / max_exact))
                         * np.float32(nb - max_exact)).astype(np.int32)
    large = np.minimum(large, nb - 1)
    return ret + np.where(n < max_exact, n, large)


def host_inputs(inputs):
    f32 = np.float32
    xp = np.asarray(inputs["x_prompt"], f32)[0]
    xs = np.asarray(inputs["x_sample"], f32)
    ck = np.asarray(inputs["cache_a_k"], f32)[0]
    cv = np.asarray(inputs["cache_a_v"], f32)[0]
    cix = np.asarray(inputs["cache_a_idx_k"], f32)[0]
    cckv = np.asarray(inputs["cache_b_ckv"], f32)[0]
    ckr = np.asarray(inputs["cache_b_krope"], f32)[0]
    half = 32
    inv_freq = np.power(np.float32(10000.0), -np.arange(half, dtype=f32) / np.float32(half)).astype(f32)
    shared = {
        "ident": np.eye(128, dtype=f32),
        "constb": np.ascontiguousarray(np.broadcast_to(np.asarray(inputs["rel_bias_table"], f32)[15], (128, 16))),
        "relb": np.ascontiguousarray(np.asarray(inputs["rel_bias_table"], f32)),
        "w_in": np.ascontiguousarray(np.asarray(inputs["w_in"], f32)[0]),
        "w_uq": np.ascontiguousarray(np.asarray(inputs["w_uq"], f32)[0]),
        "w_uk": np.ascontiguousarray(np.asarray(inputs["w_uk"], f32)[0].reshape(256, 2048)),
        "w_uv": np.ascontiguousarray(np.asarray(inputs["w_uv"], f32)[0].reshape(256, 2048)),
        "w_out": np.ascontiguousarray(np.asarray(inputs["w_out"], f32)[0]),
        "w_ff_up": np.ascontiguousarray(np.asarray(inputs["w_ff_up"], f32)[0]),
        "w_ff_down": np.ascontiguousarray(np.asarray(inputs["w_ff_down"], f32)[0]),
        "g_mix": np.ascontiguousarray(np.broadcast_to(np.asarray(inputs["norm_mix_g"], f32)[0], (128, D))),
        "g_ffn": np.ascontiguousarray(np.broadcast_to(np.asarray(inputs["norm_ffn_g"], f32)[0], (128, D))),
        "g_fin": np.ascontiguousarray(np.broadcast_to(np.asarray(inputs["final_norm_g"], f32), (128, D))),
        "g_q": np.ascontiguousarray(np.broadcast_to(np.asarray(inputs["q_lora_g"], f32)[0], (128, 512))),
        "g_kv": np.ascontiguousarray(np.broadcast_to(np.asarray(inputs["kv_lora_g"], f32)[0], (128, 256))),
    }
    s = np.arange(128)[None, :]
    t = np.arange(128)[:, None]
    onehot = np.zeros((2, 32, 128, 128), f32)
    for ty, off in enumerate((-128, 0)):
        rel = (s - t + off).astype(np.int32)
        bk = t5_bucket_np(rel)
        for b in range(32):
            onehot[ty, b] = (bk == b)
    shared["onehot"] = onehot.reshape(2, 32, 128 * 128)
    dsa_diag = np.zeros((2, 128, 128), f32)
    dsa_diag[0] = np.where((s // 64) <= (t // 64), 0.0, NEGBIG)
    dsa_diag[1] = np.where(s < 16, 0.0, NEGBIG) * np.ones((128, 1), f32)
    shared["dsa_diag"] = dsa_diag
    mla_mask = np.ones((2, 128, 2, 128), f32)
    ss_ = np.arange(128)[:, None]
    tt_ = np.arange(128)[None, :]
    mla_mask[0, :, 1, :] = ((ss_ // 64) <= (tt_ // 64)).astype(f32)
    mla_mask[1, :, 1, :] = (ss_ < 16).astype(f32) * np.ones((1, 128), f32)
    shared["mla_mask"] = mla_mask.reshape(2, 128, 256)
    maps = []
    for c in range(NCORE):
        m = dict(shared)
        pre = 7 - c
        x_all = np.zeros((NXT * 128, D), f32)
        x_all[pre * 128:pre * 128 + 16384] = xp
        pos = np.zeros((NXT * 128,), f32)
        valid = np.zeros((NXT * 128, 1), f32)
        pos[pre * 128:pre * 128 + 16384] = np.arange(16384, dtype=f32)
        valid[pre * 128:pre * 128 + 16384] = 1.0
        for b in range(2):
            r0 = (136 + b) * 128
            x_all[r0:r0 + 16] = xs[2 * c + b]
            pos[r0:r0 + 16] = 1024 + np.arange(16, dtype=f32)
            valid[r0:r0 + 16] = 1.0
        ang = pos[:, None] * inv_freq[None, :]
        m["x_all"] = x_all
        m["cs"] = np.concatenate([np.cos(ang), np.sin(ang)], axis=1).astype(f32)
        m["valid"] = np.ascontiguousarray(valid.reshape(NXT, 128).T)
        pm = np.zeros((896,), f32)
        pm[:pre * 128] = NEGBIG
        m["prefmask"] = np.ascontiguousarray(np.broadcast_to(pm, (128, 896)))
        sl = slice(2 * c, 2 * c + 2)
        m["c_akT"] = np.ascontiguousarray(ck[sl].transpose(0, 2, 3, 1))
        cve = np.ones((2, 1024, 16, 130), f32)
        cve[..., 0:128] = cv[sl]
        m["c_av"] = cve
        m["c_ixkT"] = np.ascontiguousarray(cix[sl].transpose(0, 2, 1))
        m["c_ckvT"] = np.ascontiguousarray(cckv[sl].transpose(0, 2, 1))
        m["c_krT"] = np.ascontiguousarray(ckr[sl].transpose(0, 2, 1))
        maps.append(m)
    return maps


def assemble(results):
    f32 = np.float32
    y_p = np.zeros((1, 16384, D), f32)
    y_s = np.zeros((16, 16, D), f32)
    a_k_p = np.zeros((1, 1, 16384, 16, 128), f32)
    a_v_p = np.zeros((1, 1, 16384, 16, 128), f32)
    a_ix_p = np.zeros((1, 1, 16384, 64), f32)
    b_ckv_p = np.zeros((1, 1, 16384, 256), f32)
    b_kr_p = np.zeros((1, 1, 16384, 64), f32)
    a_k_s = np.zeros((1, 16, 16, 16, 128), f32)
    a_v_s = np.zeros((1, 16, 16, 16, 128), f32)
    a_ix_s = np.zeros((1, 16, 16, 64), f32)
    b_ckv_s = np.zeros((1, 16, 16, 256), f32)
    b_kr_s = np.zeros((1, 16, 16, 64), f32)
    for c in range(NCORE):
        r = results[c]
        for i in range(16):
            j = c + 8 * i
            rs = slice(i * 128, (i + 1) * 128)
            ps = slice(j * 128, (j + 1) * 128)
            y_p[0, ps] = r["y_own"][rs]
            a_k_p[0, 0, ps] = r["st_ak"][rs].reshape(128, 16, 128)
            a_v_p[0, 0, ps] = r["st_av"][rs].reshape(128, 16, 128)
            a_ix_p[0, 0, ps] = r["st_ix"][rs]
            b_ckv_p[0, 0, ps] = r["st_ckv"][rs]
            b_kr_p[0, 0, ps] = r["st_kr"][rs]
        for b in range(2):
            rs = slice((16 + b) * 128, (16 + b) * 128 + 16)
            sq = 2 * c + b
            y_s[sq] = r["y_own"][rs]
            a_k_s[0, sq] = r["st_ak"][rs].reshape(16, 16, 128)
            a_v_s[0, sq] = r["st_av"][rs].reshape(16, 16, 128)
            a_ix_s[0, sq] = r["st_ix"][rs]
            b_ckv_s[0, sq] = r["st_ckv"][rs]
            b_kr_s[0, sq] = r["st_kr"][rs]
    return (y_p, y_s, a_k_p, a_v_p, a_ix_p, b_ckv_p, b_kr_p, a_k_s, a_v_s, a_ix_s, b_ckv_s, b_kr_s)


def kernel(**inputs):
    upto = os.environ.get("MK_UPTO", "ALL")
    bld = Builder(upto)
    nc = bld.build()
    maps = host_inputs(inputs)
    res = run_bass_kernel_spmd(nc, maps, core_ids=list(range(NCORE)))
    return assemble(res.results)
```

```python
import math
import os
from contextlib import ExitStack

import numpy as np
import concourse.bass as bass
import concourse.mybir as mybir
from concourse.bass_utils import run_bass_kernel_spmd

F32 = mybir.dt.float32
BF16 = mybir.dt.bfloat16
AF = mybir.ActivationFunctionType
ALU = mybir.AluOpType
AX = mybir.AxisListType

D = 2048
NH = 16
HD = 128
NCORE = 8
NPT = 136
NXT = 140
NTILE = 158
NTOK = NTILE * 128
NOWN = 20
EPS = 1e-6
A_SCALE = HD ** -0.5
MLA_SCALE = 192 ** -0.5
NEGBIG = -1.0e30
MASKNEG = -30000.0
TOPK = 256
BIS_R = 128.0
BIS_IT = 16
C_AQ, C_AK, C_AV, C_IXQ, C_IXK, C_IXW, C_CQ, C_CKV, C_KR, C_GA, C_GB = (
    0, 2048, 4096, 6144, 7168, 7232, 7248, 7760, 8016, 8080, 10128)


class Buf:
    __slots__ = ("name", "last_w", "readers", "dsem", "dcount", "excl")

    def __init__(self, name):
        self.excl = False
        self.name = name
        self.last_w = None
        self.readers = []
        self.dsem = None
        self.dcount = 0


class Ins:
    __slots__ = ("eng", "fn", "deps", "is_dma", "dbuf", "need_inc", "seq", "cover")

    def __init__(self, eng, fn, deps, is_dma=False, dbuf=None):
        self.eng = eng
        self.fn = fn
        self.deps = deps
        self.is_dma = is_dma
        self.dbuf = dbuf
        self.need_inc = False
        self.seq = 0
        self.cover = 0


COMPUTE = ("pe", "act", "dve", "pool")


class Prog:
    def __init__(self, nc, stack):
        self.nc = nc
        self.stack = stack
        self.engs = {"pe": nc.tensor, "act": nc.scalar, "dve": nc.vector, "pool": nc.gpsimd, "sp": nc.sync}
        self.batch = []
        self.esem = {e: stack.enter_context(nc.semaphore("e_" + e)) for e in COMPUTE}
        self.ecount = {e: 0 for e in COMPUTE}
        self.waited = {e: {} for e in self.engs}
        self.sempool = []
        self.livesems = {}
        self.nsem = 4
        self.n_ins = 0
        self.n_waits = 0
        self.trace = {e: [] for e in self.engs}

    def _getsem(self, buf):
        if self.sempool:
            sem, cnt = self.sempool.pop()
        else:
            self.nsem += 1
            sem = self.stack.enter_context(self.nc.semaphore("d%d" % self.nsem))
            cnt = 0
        buf.dsem = sem
        buf.dcount = cnt
        self.livesems[id(sem)] = [sem, cnt, cnt]

    def release(self, bufs):
        for b in bufs:
            if b.dsem is not None:
                ent = self.livesems.pop(id(b.dsem))
                self.sempool.append((ent[0], ent[1]))
                b.dsem = None

    def _deps(self, eng, reads, writes, is_dma):
        deps = {}

        def add(p, kind):
            if p is None:
                return
            if (not p.is_dma) and (not is_dma) and p.eng == eng:
                if eng == "pe" or kind != "raw":
                    return
            deps[id(p)] = p

        for b in reads:
            add(b.last_w, "raw")
            if b.excl:
                for r in b.readers:
                    if r.eng != eng:
                        add(r, "war")
        for b in writes:
            add(b.last_w, "waw")
            for r in b.readers:
                add(r, "war")
        return list(deps.values())

    def _post(self, ins, reads, writes):
        for b in reads:
            if not ins.is_dma:
                b.readers = [r for r in b.readers if r.is_dma or r.eng != ins.eng]
            b.readers.append(ins)
        for b in writes:
            b.last_w = ins
            b.readers = []
        self.batch.append(ins)

    def op(self, eng, fn, reads=(), writes=()):
        ins = Ins(eng, fn, self._deps(eng, reads, writes, False))
        self._post(ins, reads, writes)

    def dma(self, fn, reads, writes, dbuf, q="sp"):
        ins = Ins(q, fn, self._deps(q, reads, writes, True), True, dbuf)
        if dbuf.dsem is None:
            self._getsem(dbuf)
        dbuf.dcount += 1
        self.livesems[id(dbuf.dsem)][1] = dbuf.dcount
        self._post(ins, reads, writes)

    def flush(self):
        batch = self.batch
        self.batch = []
        for i in batch:
            for p in i.deps:
                if not p.is_dma:
                    p.need_inc = True
        last = {}
        for i in batch:
            if not i.is_dma:
                last[i.eng] = i
        for i in last.values():
            i.need_inc = True
        for i in batch:
            if not i.is_dma and i.need_inc:
                self.ecount[i.eng] += 1
                i.seq = self.ecount[i.eng]
        nxt = {}
        for i in reversed(batch):
            if not i.is_dma:
                if i.need_inc:
                    nxt[i.eng] = i.seq
                i.cover = nxt[i.eng]
        for i in batch:
            h = self.engs[i.eng]
            need = {}
            for p in i.deps:
                if p.is_dma:
                    ent = self.livesems.get(id(p.dbuf.dsem)) if p.dbuf.dsem is not None else None
                    if ent is None:
                        continue
                    key = id(ent[0])
                    sem = ent[0]
                    val = 16 * ent[2]
                else:
                    key = p.eng
                    sem = self.esem[p.eng]
                    val = p.cover
                if key not in need or need[key][1] < val:
                    need[key] = (sem, val)
            w = self.waited[i.eng]
            for key, (sem, val) in need.items():
                if w.get(key, 0) >= val:
                    continue
                w[key] = val
                h.wait_ge(sem, val)
                self.trace[i.eng].append(("w", id(sem), val))
                self.n_waits += 1
            bi = i.fn(h)
            self.n_ins += 1
            if i.is_dma:
                bi.then_inc(i.dbuf.dsem, 16)
                self.trace[i.eng].append(("i", id(i.dbuf.dsem), 16))
                self.livesems[id(i.dbuf.dsem)][2] += 1
            elif i.need_inc:
                bi.then_inc(self.esem[i.eng], 1)
                self.trace[i.eng].append(("i", id(self.esem[i.eng]), 1))
            else:
                self.trace[i.eng].append(("i", None, 0))

    def barrier(self):
        self.flush()
        for e, h in self.engs.items():
            w = self.waited[e]
            for pe in COMPUTE:
                if pe == e:
                    continue
                val = self.ecount[pe]
                if val > 0 and w.get(pe, 0) < val:
                    w[pe] = val
                    h.wait_ge(self.esem[pe], val)
                    self.trace[e].append(("w", id(self.esem[pe]), val))
            for key, ent in self.livesems.items():
                val = 16 * ent[2]
                if val > 0 and w.get(key, 0) < val:
                    w[key] = val
                    h.wait_ge(ent[0], val)
                    self.trace[e].append(("w", id(ent[0]), val))

    def finish(self):
        self.barrier()


class TB:
    def __init__(self, t, name):
        self.t = t
        self.b = Buf(name)


class Rot:
    def __init__(self, items):
        self.items = items
        self.i = 0

    def next(self):
        x = self.items[self.i % len(self.items)]
        self.i += 1
        return x


def weight_groups():
    g = {}
    g["ak"] = dict(K=2048, blocks=[[("w_in", 0, C_AK + 512 * i, 512)] for i in range(4)])
    g["av"] = dict(K=2048, blocks=[[("w_in", 0, C_AV + 512 * i, 512)] for i in range(4)])
    g["misc"] = dict(K=2048, blocks=[[("w_in", 0, C_CKV, 256), ("w_in", 0, C_IXK, 64), ("w_in", 0, C_KR, 64)]])
    g["aq"] = dict(K=2048, blocks=[[("w_in", 0, C_AQ + 512 * i, 512)] for i in range(4)])
    g["ixq"] = dict(K=2048, blocks=[[("w_in", 0, C_IXQ + 512 * i, 512)] for i in range(2)])
    g["cq"] = dict(K=2048, blocks=[[("w_in", 0, C_CQ, 512)], [("w_in", 0, C_IXW, 16)]])
    g["gate"] = dict(K=2048, blocks=[[("w_in", 0, C_GA + 512 * i, 512)] for i in range(8)])
    g["wo"] = dict(K=2048, blocks=[[("w_out", 0, 512 * i, 512)] for i in range(4)])
    g["up"] = dict(K=2048, blocks=[[("w_ff_up", 0, 512 * i, 512)] for i in range(16)])
    g["dn"] = dict(K=2048, blocks=[[("w_ff_down", 2048 * kg, 512 * cb, 512)] for cb in range(4) for kg in range(4)])
    g["uqn"] = dict(K=512, blocks=[[("uqn", 0, i, 0)] for i in range(4)])
    g["uqr"] = dict(K=512, blocks=[[("uqr", 0, i, 0)] for i in range(2)])
    g["uk"] = dict(K=256, blocks=[[("w_uk", 0, 512 * i, 512)] for i in range(4)])
    g["uv"] = dict(K=256, blocks=[[("w_uv", 0, 512 * i, 512)] for i in range(4)])
    first = ["uk", "uv", "misc", "ak", "av"]
    return {k: g[k] for k in first + [k for k in g if k not in first]}


class _Stop(Exception):
    pass


class Builder:
    def ck(self, n):
        if int(os.environ.get("MK_KSTOP", "0")) == n:
            raise _Stop()

    def __init__(self, upto="ALL"):
        self.upto = upto
        self.nc = bass.Bass("TRN2", target_bir_lowering=False)
        self.gstack = ExitStack()
        self.P = Prog(self.nc, self.gstack)
        self.inp = {}
        self.out = {}
        self.scr = {}

    def din(self, name, shape, dt=F32):
        self.inp[name] = self.nc.dram_tensor(name, list(shape), dt, kind="ExternalInput").ap()
        return self.inp[name]

    def dout(self, name, shape, dt=F32):
        self.out[name] = self.nc.dram_tensor(name, list(shape), dt, kind="ExternalOutput").ap()
        return self.out[name]

    def dscr(self, name, shape, dt=BF16):
        self.scr[name] = TB(self.nc.dram_tensor(name, list(shape), dt, kind="Internal").ap(), name)
        return self.scr[name]

    def sb(self, st, name, shape, dt=F32):
        self._uid = getattr(self, "_uid", 0) + 1
        name = "%s_u%d" % (name, self._uid)
        return TB(st.enter_context(self.nc.sbuf_tensor(name, list(shape), dt)), name)

    def mm(self, ps, out_ap, lhsT, rhs, start, stop, reads):
        self.P.op("pe", lambda e: e.matmul(out_ap, lhsT, rhs, start=start, stop=stop), reads, [ps.b])

    def load(self, dst, dst_ap, src_ap, reads=(), q="sp"):
        self.P.dma(lambda e: e.dma_start(out=dst_ap, in_=src_ap), list(reads), [dst.b], dst.b, q=q)

    def store(self, src, dst_ap, src_ap, dstbuf=None, q=None):
        q = q or getattr(self, "store_q", "pool")
        w = [dstbuf.b] if dstbuf is not None else []
        self.P.dma(lambda e: e.dma_start(out=dst_ap, in_=src_ap), [src.b], w, src.b, q=q)

    def declare(self):
        din, dout, dscr = self.din, self.dout, self.dscr
        din("x_all", [NXT * 128, D])
        din("cs", [NXT * 128, 64])
        din("valid", [128, NXT])
        din("prefmask", [128, 896])
        din("ident", [128, 128])
        din("onehot", [2, 32, 128 * 128])
        din("relb", [32, 16])
        din("constb", [128, 16])
        din("dsa_diag", [2, 128, 128])
        din("mla_mask", [2, 128, 256])
        din("c_akT", [2, 16, 128, 1024])
        din("c_av", [2, 1024, 16, 130])
        din("c_ixkT", [2, 64, 1024])
        din("c_ckvT", [2, 256, 1024])
        din("c_krT", [2, 64, 1024])
        din("w_in", [D, 12176])
        din("w_uq", [512, 16, 192])
        din("w_uk", [256, 2048])
        din("w_uv", [256, 2048])
        din("w_out", [D, D])
        din("w_ff_up", [D, 8192])
        din("w_ff_down", [8192, D])
        din("g_mix", [128, D])
        din("g_ffn", [128, D])
        din("g_fin", [128, D])
        din("g_q", [128, 512])
        din("g_kv", [128, 256])
        dout("y_own", [NOWN * 128, D])
        dout("st_ak", [NOWN * 128, D])
        dout("st_av", [NOWN * 128, D])
        dout("st_ix", [NOWN * 128, 64])
        dout("st_ckv", [NOWN * 128, 256])
        dout("st_kr", [NOWN * 128, 64])
        dscr("KT_A", [NH, 128, NTOK])
        dscr("V_A", [NH, 128, NTILE, 130])
        dscr("IXK_T", [64, NTOK])
        dscr("KT_B", [NH, 128, NTOK])
        dscr("KR_T", [64, NTOK])
        dscr("V_B", [NH, 128, NTILE, 130])
        dscr("QA_T", [5, 128, 16, 512])
        dscr("IXQ_T", [5, 128, 8, 512])
        dscr("IXW", [5, 128, 4, 16], F32)
        dscr("QN_T", [5, 128, 16, 512])
        dscr("QR_T", [5, 128, 8, 512])
        dscr("OA", [NOWN, 128, D], F32)
        dscr("OB", [NOWN, 128, D], F32)
        self.wg = weight_groups()
        for name, g in self.wg.items():
            dscr("W_" + name, [len(g["blocks"]), 128, g["K"] // 128, 512])

    def phase_W(self):
        P = self.P
        inp = self.inp
        urgent = ["uk", "uv", "misc", "ak", "av"]
        self._precast(urgent)
        KT_A, V_A, IXK_T, KR_T = self.scr["KT_A"], self.scr["V_A"], self.scr["IXK_T"], self.scr["KR_T"]
        for b in range(2):
            t0 = 140 + 9 * b
            for h in range(NH):
                P.dma(lambda e, b=b, h=h, t0=t0: e.dma_start(out=KT_A.t[h, :, t0 * 128:t0 * 128 + 1024], in_=inp["c_akT"][b, h]),
                      [], [KT_A.b], KT_A.b, q="pool")
                P.dma(lambda e, b=b, h=h, t0=t0: e.dma_start(
                    out=V_A.t[h, :, t0:t0 + 8, :],
                    in_=inp["c_av"][b, :, h, :].rearrange("(kt p) d -> p kt d", p=128)), [], [V_A.b], V_A.b, q="pool")
            P.dma(lambda e, b=b, t0=t0: e.dma_start(out=IXK_T.t[:, t0 * 128:t0 * 128 + 1024], in_=inp["c_ixkT"][b]),
                  [], [IXK_T.b], IXK_T.b, q="pool")
            P.dma(lambda e, b=b, t0=t0: e.dma_start(out=KR_T.t[:, t0 * 128:t0 * 128 + 1024], in_=inp["c_krT"][b]),
                  [], [KR_T.b], KR_T.b, q="pool")
        self._precast([k for k in self.wg if k not in urgent])

    def _precast(self, names):
        P = self.P
        inp = self.inp
        for name in names:
            g = self.wg[name]
            W = self.scr["W_" + name]
            KC = g["K"] // 128
            for bi, pieces in enumerate(g["blocks"]):
                off = 0
                for (src, r0, c0, ncols) in pieces:
                    if src in ("uqn", "uqr"):
                        i = c0
                        nh_, w_, lo_ = (4, 128, 0) if src == "uqn" else (8, 64, 128)
                        for j in range(nh_):
                            src_ap = inp["w_uq"][:, nh_ * i + j, lo_:lo_ + w_].rearrange("(kc p) d -> p kc d", p=128)
                            dst_ap = W.t[bi][:, :, j * w_:(j + 1) * w_]
                            P.dma(lambda e, d=dst_ap, s=src_ap: e.dma_start(out=d, in_=s), [], [W.b], W.b, q="pool")
                        continue
                    src_ap = inp[src][r0:r0 + g["K"], c0:c0 + ncols].rearrange("(kc p) c -> p kc c", p=128)
                    dst_ap = W.t[bi][:, :, off:off + ncols]
                    off += ncols
                    P.dma(lambda e, d=dst_ap, s=src_ap: e.dma_start(out=d, in_=s), [], [W.b], W.b, q="pool")

    def norm_rows(self, x_ap, n, g_ap, out_ap, xb, gb, outb, tmp):
        P = self.P
        junk, ss, sd, rstd = tmp["junk"], tmp["ss"], tmp["sd"], tmp["rstd"]
        P.op("act", lambda e: e.activation(out=junk.t[:, 0:n], in_=x_ap, func=AF.Square, accum_out=ss.t[:]),
             [xb], [junk.b, ss.b])
        P.op("act", lambda e: e.activation(out=sd.t[:], in_=ss.t[:], func=AF.Sqrt, bias=EPS, scale=1.0 / n),
             [ss.b], [sd.b])
        P.op("dve", lambda e: e.reciprocal(rstd.t[:], sd.t[:]), [sd.b], [rstd.b])
        P.op("dve", lambda e: e.scalar_tensor_tensor(out=out_ap, in0=x_ap, scalar=rstd.t[:], in1=g_ap,
                                                     op0=ALU.mult, op1=ALU.mult), [xb, rstd.b, gb], [outb])

    def load_norm_T(self, st, tiles, g_tb, hT, xts, hb, tmp, psT, identb, keep_x=None):
        P = self.P
        xall = self.inp["x_all"]
        for j, tile in enumerate(tiles):
            xt = keep_x[j] if keep_x is not None else xts.next()
            self.load(xt, xt.t[:], xall[tile * 128:(tile + 1) * 128, :])
            self.norm_rows(xt.t[:], D, g_tb.t[:], hb.t[:], xt.b, g_tb.b, hb.b, tmp)
            self.transpose_into(hb, hb.t, 16, hT, lambda kc, j=j: hT.t[:, kc, j * 128:(j + 1) * 128], psT, identb)

    def transpose_into(self, src, src_t, nchunks, dst, dst_ap_fn, psT, identb, rows=128):
        P = self.P
        c = 0
        while c < nchunks:
            n = min(4, nchunks - c)
            ps = psT.next()
            for k in range(n):
                self.mm(ps, ps.t[:, k * 128:(k + 1) * 128], src_t[:, (c + k) * 128:(c + k + 1) * 128], identb.t[:],
                        True, True, [src.b, identb.b])
            self._tflip = not getattr(self, "_tflip", False)
            for k in range(n):
                eng = "act" if self._tflip else "dve"
                d_ap = dst_ap_fn(c + k)
                s_ap = ps.t[:, k * 128:(k + 1) * 128]
                if eng == "act":
                    P.op("act", lambda e, d=d_ap, s=s_ap: e.activation(out=d, in_=s, func=AF.Copy), [ps.b], [dst.b])
                else:
                    P.op("dve", lambda e, d=d_ap, s=s_ap: e.tensor_copy(d, s), [ps.b], [dst.b])
            c += n

    def rope(self, out_ap1, out_ap2, x1, x2, cos, sin, xb, csb, outb, tmps):
        P = self.P
        t1, t2 = tmps
        P.op("dve", lambda e: e.tensor_tensor(t1.t_ap, x1, cos, ALU.mult), [xb, csb], [t1.b])
        P.op("dve", lambda e: e.tensor_tensor(t2.t_ap, x2, sin, ALU.mult), [xb, csb], [t2.b])
        P.op("dve", lambda e: e.tensor_tensor(out_ap1, t1.t_ap, t2.t_ap, ALU.subtract), [t1.b, t2.b], [outb])
        P.op("dve", lambda e: e.tensor_tensor(t1.t_ap, x1, sin, ALU.mult), [xb, csb, outb], [t1.b])
        P.op("dve", lambda e: e.tensor_tensor(t2.t_ap, x2, cos, ALU.mult), [xb, csb, outb], [t2.b])
        P.op("dve", lambda e: e.tensor_tensor(out_ap2, t1.t_ap, t2.t_ap, ALU.add), [t1.b, t2.b], [outb])

    def phase_K(self, blocks):
        P = self.P
        self.store_q = "act"
        nc = self.nc
        inp, out, scr = self.inp, self.out, self.scr
        with ExitStack() as st:
            sb = lambda name, shape, dt=F32: self.sb(st, name, shape, dt)
            g_mix = sb("k_gmix", [128, D])
            g_kv = sb("k_gkv", [128, 256])
            identf = sb("k_identf", [128, 128])
            identb = sb("k_identb", [128, 128], BF16)
            ones16 = sb("k_ones16", [128, 16, 1])
            wuk = [sb("k_wuk%d" % i, [128, 2, 512], BF16) for i in range(4)]
            wuv = [sb("k_wuv%d" % i, [128, 2, 512], BF16) for i in range(4)]
            xts = Rot([sb("k_x%d" % i, [128, D]) for i in range(2)])
            hb = sb("k_hb", [128, D], BF16)
            hT = sb("k_hT", [128, 16, 512], BF16)
            wts = Rot([sb("k_wt%d" % i, [128, 16, 512], BF16) for i in range(3)])
            ksts = Rot([sb("k_kst%d" % i, [128, 512], BF16) for i in range(2)])
            vst = [sb("k_vst%d" % i, [128, 16, 130], BF16) for i in range(4)]
            vbst = [sb("k_vbst%d" % i, [128, 16, 130], BF16) for i in range(4)]
            mst = [sb("k_mst%d" % i, [128, 384]) for i in range(4)]
            mo = [sb("k_mo%d" % i, [128, 384]) for i in range(4)]
            km = sb("k_km", [128, 384], BF16)
            ckvT = sb("k_ckvT", [128, 2, 512], BF16)
            ixkrT = sb("k_ixkrT", [128, 512], BF16)
            osts = Rot([sb("k_ost%d" % i, [128, 512]) for i in range(3)])
            cst = sb("k_cs", [128, 4, 64])
            vld = sb("k_vld", [128, 4])
            tmp = dict(junk=sb("k_junk", [128, D], BF16), ss=sb("k_ss", [128, 1]), sd=sb("k_sd", [128, 1]),
                       rstd=sb("k_rstd", [128, 1]))
            rt1 = sb("k_rt1", [128, 32])
            rt2 = sb("k_rt2", [128, 32])
            rt1.t_ap = rt1.t[:]
            rt2.t_ap = rt2.t[:]
            PS = self.PS
            psT = Rot([PS[0], PS[1]])
            psM = Rot([PS[2], PS[3], PS[4], PS[5]])
            psX = Rot([PS[6], PS[7]])

            self.load(g_mix, g_mix.t[:], inp["g_mix"][:, :])
            self.load(g_kv, g_kv.t[:], inp["g_kv"][:, :])
            self.load(identf, identf.t[:], inp["ident"][:, :])
            P.op("dve", lambda e: e.tensor_copy(identb.t[:], identf.t[:]), [identf.b], [identb.b])
            P.op("dve", lambda e: e.memset(ones16.t[:], 1.0), [], [ones16.b])
            for i in range(4):
                self.load(wuk[i], wuk[i].t[:], scr["W_uk"].t[i], reads=[scr["W_uk"].b])
                self.load(wuv[i], wuv[i].t[:], scr["W_uv"].t[i], reads=[scr["W_uv"].b])

            for kb in blocks:
              try:
                tiles = [4 * kb + j for j in range(4)]
                tok0 = 4 * kb * 128
                if kb == 34:
                    own = {0: 16, 1: 17, 2: 18, 3: 19}
                elif kb % 2 == 1:
                    own = {3: (kb - 1) // 2}
                else:
                    own = {}
                self.load(cst, cst.t[:], inp["cs"][tok0:tok0 + 512, :].rearrange("(j p) c -> p j c", p=128))
                self.load(vld, vld.t[:], inp["valid"][:, 4 * kb:4 * kb + 4])
                self.ck(1)
                self.load_norm_T(st, tiles, g_mix, hT, xts, hb, tmp, psT, identb)
                self.ck(2)
                for cb in range(4):
                    wt = wts.next()
                    self.load(wt, wt.t[:], scr["W_ak"].t[cb], reads=[scr["W_ak"].b])
                    for j in range(4):
                        head = 4 * cb + j
                        ps = psM.next()
                        for kc in range(16):
                            self.mm(ps, ps.t[:, :], wt.t[:, kc, j * 128:(j + 1) * 128], hT.t[:, kc, :], kc == 0, kc == 15,
                                    [wt.b, hT.b])
                        kst = ksts.next()
                        P.op("act", lambda e, kst=kst, ps=ps: e.activation(out=kst.t[:], in_=ps.t[:, :], func=AF.Copy),
                             [ps.b], [kst.b])
                        self.store_cols(kst, scr["KT_A"], lambda a, n, head=head: scr["KT_A"].t[head, :, a:a + n], kst.t, tiles)
                    for pos, o in own.items():
                        ps = psM.next()
                        for kc in range(16):
                            self.mm(ps, ps.t[:, :], hT.t[:, kc, pos * 128:(pos + 1) * 128], wt.t[:, kc, :], kc == 0, kc == 15,
                                    [wt.b, hT.b])
                        ost = osts.next()
                        P.op("act", lambda e, ost=ost, ps=ps: e.activation(out=ost.t[:], in_=ps.t[:, :], func=AF.Copy),
                             [ps.b], [ost.b])
                        self.store(ost, out["st_ak"][o * 128:(o + 1) * 128, cb * 512:(cb + 1) * 512], ost.t[:])
                self.ck(3)
                for cb in range(4):
                    wt = wts.next()
                    self.load(wt, wt.t[:], scr["W_av"].t[cb], reads=[scr["W_av"].b])
                    for pos in range(4):
                        ps = psM.next()
                        for kc in range(16):
                            self.mm(ps, ps.t[:, :], hT.t[:, kc, pos * 128:(pos + 1) * 128], wt.t[:, kc, :], kc == 0, kc == 15,
                                    [wt.b, hT.b])
                        v = vst[pos]
                        P.op("dve", lambda e, v=v, ps=ps, cb=cb: e.tensor_copy(
                            v.t[:, 4 * cb:4 * cb + 4, 0:128], ps.t[:, :].rearrange("p (h d) -> p h d", h=4)), [ps.b], [v.b])
                        if pos in own:
                            o = own[pos]
                            ost = osts.next()
                            P.op("dve", lambda e, ost=ost, ps=ps: e.tensor_copy(ost.t[:], ps.t[:, :]), [ps.b], [ost.b])
                            self.store(ost, out["st_av"][o * 128:(o + 1) * 128, cb * 512:(cb + 1) * 512], ost.t[:])
                for pos in range(4):
                    v = vst[pos]
                    P.op("dve", lambda e, v=v, pos=pos: e.tensor_scalar(v.t[:, :, 128:129], ones16.t[:], vld.t[:, pos:pos + 1], None,
                                                                        ALU.mult), [ones16.b, vld.b], [v.b])
                    self.store(v, scr["V_A"].t[:, :, self.tmap(tiles[pos]), 0:129].rearrange("h p c -> p h c"), v.t[:, :, 0:129],
                               scr["V_A"])
                self.ck(4)
                wt = wts.next()
                self.load(wt, wt.t[:, :, 0:384], scr["W_misc"].t[0][:, :, 0:384], reads=[scr["W_misc"].b])
                for pos in range(4):
                    ps = psX.next()
                    for kc in range(16):
                        self.mm(ps, ps.t[:, 0:384], hT.t[:, kc, pos * 128:(pos + 1) * 128], wt.t[:, kc, 0:384], kc == 0, kc == 15,
                                [wt.b, hT.b])
                    m = mst[pos]
                    P.op("act", lambda e, m=m, ps=ps: e.activation(out=m.t[:], in_=ps.t[:, 0:384], func=AF.Copy), [ps.b], [m.b])
                self.ck(41)
                for pos in range(4):
                    m = mst[pos]
                    o_ = mo[pos]
                    self.norm_rows(m.t[:, 0:256], 256, g_kv.t[:], o_.t[:, 0:256], m.b, g_kv.b, o_.b, tmp)
                    self.ck(42)
                    P.op("act", lambda e, m=m, o_=o_: e.activation(out=o_.t[:, 256:320], in_=m.t[:, 256:320], func=AF.Copy),
                         [m.b], [o_.b])
                    self.rope(o_.t[:, 320:352], o_.t[:, 352:384], m.t[:, 320:352], m.t[:, 352:384],
                              cst.t[:, pos, 0:32], cst.t[:, pos, 32:64], m.b, cst.b, o_.b, (rt1, rt2))
                    self.ck(43)
                    if pos in own:
                        o = own[pos]
                        self.store(o_, out["st_ckv"][o * 128:(o + 1) * 128, :], o_.t[:, 0:256])
                        self.store(o_, out["st_ix"][o * 128:(o + 1) * 128, :], o_.t[:, 256:320])
                        self.store(o_, out["st_kr"][o * 128:(o + 1) * 128, :], o_.t[:, 320:384])
                    P.op("act", lambda e, o_=o_: e.activation(out=km.t[:], in_=o_.t[:], func=AF.Copy), [o_.b], [km.b])
                    self.ck(431)
                    ps = psX.next()
                    for k in range(3):
                        self.mm(ps, ps.t[:, k * 128:(k + 1) * 128], km.t[:, k * 128:(k + 1) * 128], identb.t[:], True, True,
                                [km.b, identb.b])
                    self.ck(432)
                    P.op("dve", lambda e, ps=ps, pos=pos: e.tensor_copy(
                        ckvT.t[:, :, pos * 128:(pos + 1) * 128], ps.t[:, 0:256].rearrange("p (c t) -> p c t", c=2)),
                        [ps.b], [ckvT.b])
                    self.ck(4321)
                    P.op("dve", lambda e, ps=ps, pos=pos: e.tensor_copy(ixkrT.t[:, pos * 128:(pos + 1) * 128], ps.t[:, 256:384]),
                         [ps.b], [ixkrT.b])
                    self.ck(433)
                self.ck(44)
                self.store_cols(ixkrT, scr["IXK_T"], lambda a, n: scr["IXK_T"].t[:, a:a + n], ixkrT.t[0:64], tiles)
                self.store_cols(ixkrT, scr["KR_T"], lambda a, n: scr["KR_T"].t[:, a:a + n], ixkrT.t[64:128], tiles)
                self.ck(5)
                if int(os.environ.get("MK_KSTOP", "0")) == 6:
                    self.mla_kside(ckvT, wuk, wuv, psM, ksts, vbst, tiles, tok0, lambda pos: vld.t[:, pos:pos + 1], vld, ones16)
                self.ck(6)
                self.mla_kside(ckvT, wuk, wuv, psM, ksts, vbst, tiles, tok0, lambda pos: vld.t[:, pos:pos + 1], vld, ones16)
              except _Stop:
                break
            self.store_q = "pool"
            P.barrier()
            self.release_scope(locals())

    def mla_kside(self, ckvT, wuk, wuv, psM, ksts, vbst, tiles, tok0, vld_ap_fn, vld, ones16, ncols=512):
        P = self.P
        scr = self.scr
        for h in range(NH):
            ps = psM.next()
            for cc in range(2):
                self.mm(ps, ps.t[:, 0:ncols], wuk[h // 4].t[:, cc, (h % 4) * 128:(h % 4 + 1) * 128], ckvT.t[:, cc, 0:ncols],
                        cc == 0, cc == 1, [wuk[h // 4].b, ckvT.b])
            kst = ksts.next()
            P.op("act", lambda e, kst=kst, ps=ps: e.activation(out=kst.t[:, 0:ncols], in_=ps.t[:, 0:ncols], func=AF.Copy),
                 [ps.b], [kst.b])
            self.store_cols(kst, scr["KT_B"], lambda a, n, h=h: scr["KT_B"].t[h, :, a:a + n], kst.t, tiles)
        for pos in range(len(tiles)):
            v = vbst[pos % len(vbst)]
            for cb in range(4):
                ps = psM.next()
                for cc in range(2):
                    self.mm(ps, ps.t[:, :], ckvT.t[:, cc, pos * 128:(pos + 1) * 128], wuv[cb].t[:, cc, :], cc == 0, cc == 1,
                            [wuv[cb].b, ckvT.b])
                P.op("dve", lambda e, v=v, ps=ps, cb=cb: e.tensor_copy(
                    v.t[:, 4 * cb:4 * cb + 4, 0:128], ps.t[:, :].rearrange("p (h d) -> p h d", h=4)), [ps.b], [v.b])
            if vld is not None:
                P.op("dve", lambda e, v=v, pos=pos: e.tensor_scalar(v.t[:, :, 128:129], ones16.t[:], vld_ap_fn(pos), None,
                                                                    ALU.mult), [ones16.b, vld.b], [v.b])
                self.store(v, scr["V_B"].t[:, :, self.tmap(tiles[pos]), 0:129].rearrange("h p c -> p h c"), v.t[:, :, 0:129], scr["V_B"])
            else:
                self.store(v, scr["V_B"].t[:, :, self.tmap(tiles[pos]), 0:129].rearrange("h p c -> p h c"), v.t[:, :, 0:129], scr["V_B"])

    def release_scope(self, loc):
        bufs = []
        for v in loc.values():
            if isinstance(v, TB):
                bufs.append(v.b)
            elif isinstance(v, (list, tuple)):
                for x in v:
                    if isinstance(x, TB):
                        bufs.append(x.b)
            elif isinstance(v, Rot):
                for x in v.items:
                    if isinstance(x, TB):
                        bufs.append(x.b)
            elif isinstance(v, dict):
                for x in v.values():
                    if isinstance(x, TB):
                        bufs.append(x.b)
        self.P.release(bufs)

    def phase_KC(self):
        P = self.P
        inp, scr = self.inp, self.scr
        with ExitStack() as st:
            sb = lambda name, shape, dt=F32: self.sb(st, name, shape, dt)
            wuk = [sb("c_wuk%d" % i, [128, 2, 512], BF16) for i in range(4)]
            wuv = [sb("c_wuv%d" % i, [128, 2, 512], BF16) for i in range(4)]
            ckvT = sb("c_ckvT", [128, 2, 512], BF16)
            ksts = Rot([sb("c_kst%d" % i, [128, 512], BF16) for i in range(2)])
            vbst = [sb("c_vbst%d" % i, [128, 16, 130], BF16) for i in range(4)]
            PS = self.PS
            psM = Rot([PS[2], PS[3], PS[4], PS[5]])
            for i in range(4):
                self.load(wuk[i], wuk[i].t[:], scr["W_uk"].t[i], reads=[scr["W_uk"].b])
                self.load(wuv[i], wuv[i].t[:], scr["W_uv"].t[i], reads=[scr["W_uv"].b])
                P.op("dve", lambda e, v=vbst[i]: e.memset(v.t[:, :, 128:130], 1.0), [], [vbst[i].b])
            for b in range(2):
                for half in range(2):
                    t0 = 140 + 9 * b + 4 * half
                    self.load(ckvT, ckvT.t[:], inp["c_ckvT"][b, :, half * 512:(half + 1) * 512].rearrange("(c p) s -> p c s", p=128),
                              q="pool")
                    self.mla_kside(ckvT, wuk, wuv, psM, ksts, vbst, [t0 + j for j in range(4)], t0 * 128, None, None, None)
            P.barrier()
            self.release_scope(locals())


    @staticmethod
    def tmap(t):
        return {136: 148, 137: 157}.get(t, t)

    def store_cols(self, src, dst, dst_row_ap_fn, src_t, tiles):
        mapped = [self.tmap(t) for t in tiles]
        if all(mapped[i] == mapped[0] + i for i in range(len(mapped))):
            self.store(src, dst_row_ap_fn(mapped[0] * 128, len(mapped) * 128), src_t[:, 0:len(mapped) * 128], dst)
        else:
            for i, mt in enumerate(mapped):
                self.store(src, dst_row_ap_fn(mt * 128, 128), src_t[:, i * 128:(i + 1) * 128], dst)

    def own_tiles(self, b):
        return [136 + j for j in range(4)] if b == 4 else [8 * (4 * b + j) + 7 for j in range(4)]

    def phase_Q(self):
        P = self.P
        inp, scr = self.inp, self.scr
        with ExitStack() as st:
            sb = lambda name, shape, dt=F32: self.sb(st, name, shape, dt)
            g_mix = sb("q_gmix", [128, D])
            g_q = sb("q_gq", [128, 512])
            identf = sb("q_identf", [128, 128])
            identb = sb("q_identb", [128, 128], BF16)
            xts = Rot([sb("q_x%d" % i, [128, D]) for i in range(2)])
            hb = sb("q_hb", [128, D], BF16)
            hT = sb("q_hT", [128, 16, 512], BF16)
            wts = Rot([sb("q_wt%d" % i, [128, 16, 512], BF16) for i in range(3)])
            qsts = Rot([sb("q_qst%d" % i, [128, 512], BF16) for i in range(3)])
            wuqn = [sb("q_wuqn%d" % i, [128, 4, 512], BF16) for i in range(4)]
            wuqr = [sb("q_wuqr%d" % i, [128, 4, 512], BF16) for i in range(2)]
            cq_f = sb("q_cqf", [128, 512])
            cqb = sb("q_cqb", [128, 512], BF16)
            cqT = sb("q_cqT", [128, 4, 512], BF16)
            qr_f = sb("q_qrf", [128, 1024])
            qr_o = sb("q_qro", [128, 1024])
            qrb = sb("q_qrb", [128, 1024], BF16)
            qrT = sb("q_qrT", [128, 8, 512], BF16)
            ixw = sb("q_ixw", [128, 4, 16])
            cst = sb("q_cs", [128, 4, 64])
            tmp = dict(junk=sb("q_junk", [128, D], BF16), ss=sb("q_ss", [128, 1]), sd=sb("q_sd", [128, 1]),
                       rstd=sb("q_rstd", [128, 1]))
            rt1 = sb("q_rt1", [128, 32])
            rt2 = sb("q_rt2", [128, 32])
            rt1.t_ap = rt1.t[:]
            rt2.t_ap = rt2.t[:]
            PS = self.PS
            psT = Rot([PS[0], PS[1]])
            psM = Rot([PS[2], PS[3], PS[4], PS[5]])
            psX = Rot([PS[6], PS[7]])
            self.load(g_mix, g_mix.t[:], inp["g_mix"][:, :])
            self.load(g_q, g_q.t[:], inp["g_q"][:, :])
            self.load(identf, identf.t[:], inp["ident"][:, :])
            P.op("dve", lambda e: e.tensor_copy(identb.t[:], identf.t[:]), [identf.b], [identb.b])
            for i in range(4):
                self.load(wuqn[i], wuqn[i].t[:], scr["W_uqn"].t[i], reads=[scr["W_uqn"].b])
            for i in range(2):
                self.load(wuqr[i], wuqr[i].t[:], scr["W_uqr"].t[i], reads=[scr["W_uqr"].b])
            for b in range(5):
                tiles = self.own_tiles(b)
                for j, tile in enumerate(tiles):
                    self.load(cst, cst.t[:, j, :], inp["cs"][tile * 128:(tile + 1) * 128, :])
                self.load_norm_T(st, tiles, g_mix, hT, xts, hb, tmp, psT, identb)
                for grp, nblk, dst in (("aq", 4, "QA_T"), ("ixq", 2, "IXQ_T")):
                    for cb in range(nblk):
                        wt = wts.next()
                        self.load(wt, wt.t[:], scr["W_" + grp].t[cb], reads=[scr["W_" + grp].b])
                        for j in range(4):
                            ps = psM.next()
                            for kc in range(16):
                                self.mm(ps, ps.t[:, :], wt.t[:, kc, j * 128:(j + 1) * 128], hT.t[:, kc, :], kc == 0, kc == 15,
                                        [wt.b, hT.b])
                            q = qsts.next()
                            P.op("act", lambda e, q=q, ps=ps: e.activation(out=q.t[:], in_=ps.t[:, :], func=AF.Copy), [ps.b], [q.b])
                            self.store(q, scr[dst].t[b][:, 4 * cb + j, :], q.t[:], scr[dst])
                wt = wts.next()
                self.load(wt, wt.t[:], scr["W_cq"].t[0], reads=[scr["W_cq"].b])
                for pos in range(4):
                    ps = psM.next()
                    for kc in range(16):
                        self.mm(ps, ps.t[:, :], hT.t[:, kc, pos * 128:(pos + 1) * 128], wt.t[:, kc, :], kc == 0, kc == 15, [wt.b, hT.b])
                    P.op("act", lambda e, ps=ps: e.activation(out=cq_f.t[:], in_=ps.t[:, :], func=AF.Copy), [ps.b], [cq_f.b])
                    self.norm_rows(cq_f.t[:], 512, g_q.t[:], cqb.t[:], cq_f.b, g_q.b, cqb.b, tmp)
                    self.transpose_into(cqb, cqb.t, 4, cqT, lambda c, pos=pos: cqT.t[:, c, pos * 128:(pos + 1) * 128], psT, identb)
                wt = wts.next()
                self.load(wt, wt.t[:, :, 0:16], scr["W_cq"].t[1][:, :, 0:16], reads=[scr["W_cq"].b])
                for pos in range(4):
                    ps = psX.next()
                    for kc in range(16):
                        self.mm(ps, ps.t[:, 0:16], hT.t[:, kc, pos * 128:(pos + 1) * 128], wt.t[:, kc, 0:16], kc == 0, kc == 15,
                                [wt.b, hT.b])
                    P.op("act", lambda e, ps=ps, pos=pos: e.activation(out=ixw.t[:, pos, :], in_=ps.t[:, 0:16], func=AF.Copy, scale=0.25),
                         [ps.b], [ixw.b])
                self.store(ixw, scr["IXW"].t[b], ixw.t[:], scr["IXW"])
                for h in range(NH):
                    ps = psM.next()
                    for cc in range(4):
                        self.mm(ps, ps.t[:, :], wuqn[h // 4].t[:, cc, (h % 4) * 128:(h % 4 + 1) * 128], cqT.t[:, cc, :], cc == 0, cc == 3,
                                [wuqn[h // 4].b, cqT.b])
                    q = qsts.next()
                    P.op("act", lambda e, q=q, ps=ps: e.activation(out=q.t[:], in_=ps.t[:, :], func=AF.Copy), [ps.b], [q.b])
                    self.store(q, scr["QN_T"].t[b][:, h, :], q.t[:], scr["QN_T"])
                for pos in range(4):
                    for blk in range(2):
                        ps = psM.next()
                        for cc in range(4):
                            self.mm(ps, ps.t[:, :], cqT.t[:, cc, pos * 128:(pos + 1) * 128], wuqr[blk].t[:, cc, :], cc == 0, cc == 3,
                                    [wuqr[blk].b, cqT.b])
                        P.op("act", lambda e, ps=ps, blk=blk: e.activation(out=qr_f.t[:, blk * 512:(blk + 1) * 512], in_=ps.t[:, :],
                                                                           func=AF.Copy), [ps.b], [qr_f.b])
                    for h in range(NH):
                        c0 = h * 64
                        self.rope(qr_o.t[:, c0:c0 + 32], qr_o.t[:, c0 + 32:c0 + 64], qr_f.t[:, c0:c0 + 32], qr_f.t[:, c0 + 32:c0 + 64],
                                  cst.t[:, pos, 0:32], cst.t[:, pos, 32:64], qr_f.b, cst.b, qr_o.b, (rt1, rt2))
                    P.op("act", lambda e: e.activation(out=qrb.t[:], in_=qr_o.t[:], func=AF.Copy), [qr_o.b], [qrb.b])
                    self.transpose_into(qrb, qrb.t, 8, qrT, lambda c, pos=pos: qrT.t[:, c, pos * 128:(pos + 1) * 128], psT, identb)
                self.store(qrT, scr["QR_T"].t[b], qrT.t[:], scr["QR_T"])
            P.barrier()
            self.release_scope(locals())

    def slots_all(self):
        P = self.P
        inp, scr = self.inp, self.scr
        with ExitStack() as st:
            sb = lambda name, shape, dt=F32: self.sb(st, name, shape, dt)
            C = {}
            C["identf"] = sb("s_identf", [128, 128])
            C["identb"] = sb("s_identb", [128, 128], BF16)
            C["Tb"] = sb("s_Tb", [128, 16, 2, 128])
            C["constb"] = sb("s_constb", [128, 16])
            C["prefmask"] = sb("s_pref", [128, 896])
            C["dsa_diag"] = sb("s_dsadiag", [128, 2, 128])
            C["mlaf"] = sb("s_mlaf", [128, 2, 256])
            C["mlamask"] = sb("s_mlamask", [128, 2, 256], BF16)
            relb = sb("s_relb", [32, 16])
            oh = sb("s_oh", [32, 4096])
            self.load(C["identf"], C["identf"].t[:], inp["ident"][:, :])
            P.op("dve", lambda e: e.tensor_copy(C["identb"].t[:], C["identf"].t[:]), [C["identf"].b], [C["identb"].b])
            self.load(C["constb"], C["constb"].t[:], inp["constb"][:, :])
            self.load(C["prefmask"], C["prefmask"].t[:], inp["prefmask"][:, :])
            self.load(C["dsa_diag"], C["dsa_diag"].t[:], inp["dsa_diag"].rearrange("y t s -> t y s"))
            self.load(C["mlaf"], C["mlaf"].t[:], inp["mla_mask"].rearrange("y s c -> s y c"))
            P.op("dve", lambda e: e.tensor_copy(C["mlamask"].t[:], C["mlaf"].t[:]), [C["mlaf"].b], [C["mlamask"].b])
            self.load(relb, relb.t[:], inp["relb"][:, :])
            PS = self.PS
            for ty in range(2):
                for tg in range(4):
                    self.load(oh, oh.t[:], inp["onehot"][ty][:, tg * 4096:(tg + 1) * 4096])
                    ps = PS[tg % 2]
                    for t in range(32):
                        self.mm(ps, ps.t[:, t * 16:(t + 1) * 16], oh.t[:, t * 128:(t + 1) * 128], relb.t[:, :], True, True, [oh.b, relb.b])
                    P.op("dve", lambda e, ps=ps, ty=ty, tg=tg: e.tensor_copy(
                        C["Tb"].t[:, :, ty, tg * 32:(tg + 1) * 32], ps.t[:, :].rearrange("p (t h) -> p h t", h=16)), [ps.b], [C["Tb"].b])
            P.barrier()
            nslot = int(os.environ.get("MK_NSLOT", "18"))
            order = []
            for o in range(16):
                order.append((o // 4, o % 4, [(0, 8 * o + 8)], 0))
            order.append((4, 0, [(140, 9)], 1))
            order.append((4, 1, [(149, 9)], 1))
            if nslot < 18:
                order = [order[0], order[16], order[1], order[17]][:nslot]
            for (b, j, runs, ty) in order:
                self.slot(C, b, j, runs, ty)
            self.release_scope(dict(C=C, relb=relb, oh=oh))

    def slot(self, C, b, j, runs, ty):
        P = self.P
        inp, scr = self.inp, self.scr
        PS = self.PS
        o = 4 * b + j
        ktl = []
        for (t0, n) in runs:
            ktl += [t0 + i for i in range(n)]
        nk = len(ktl)
        S = nk * 128
        identb = C["identb"]
        with ExitStack() as st:
            sb = lambda name, shape, dt=F32: self.sb(st, name, shape, dt)
            qa = sb("l_qa", [128, 16, 128], BF16)
            ixq = sb("l_ixq", [128, 8, 128], BF16)
            ixw = sb("l_ixw", [128, 16])
            qn = sb("l_qn", [128, 16, 128], BF16)
            qr = sb("l_qr", [128, 8, 128], BF16)
            maskadd = sb("l_maskadd", [128, S], BF16)
            js = slice(j * 128, (j + 1) * 128)
            self.load(qa, qa.t[:], scr["QA_T"].t[b][:, :, js], reads=[scr["QA_T"].b])
            self.load(ixq, ixq.t[:], scr["IXQ_T"].t[b][:, :, js], reads=[scr["IXQ_T"].b])
            self.load(ixw, ixw.t[:], scr["IXW"].t[b][:, j, :], reads=[scr["IXW"].b])
            self.load(qn, qn.t[:], scr["QN_T"].t[b][:, :, js], reads=[scr["QN_T"].b])
            self.load(qr, qr.t[:], scr["QR_T"].t[b][:, :, js], reads=[scr["QR_T"].b])
            with ExitStack() as st2:
                sb2 = lambda name, shape, dt=F32: self.sb(st2, name, shape, dt)
                row = sb2("i_row", [128, S])
                ixks = Rot([sb2("i_ixk%d" % i, [128, 512], BF16) for i in range(2)])
                rbs = Rot([sb2("i_r%d" % i, [128, 512]) for i in range(3)])
                mx = sb2("i_mx", [128, 1])
                mid = sb2("i_mid", [128, 1])
                cnt = sb2("i_cnt", [128, 1])
                tfl = sb2("i_tfl", [128, 1])
                thr = sb2("i_thr", [128, 1])
                psR = Rot([PS[0], PS[1], PS[2], PS[3]])
                col = 0
                for (t0, n) in runs:
                    k = 0
                    while k < n:
                        g = min(4, n - k)
                        W = g * 128
                        tok = (t0 + k) * 128
                        ixk = ixks.next()
                        self.load(ixk, ixk.t[0:64, 0:W], scr["IXK_T"].t[:, tok:tok + W], reads=[scr["IXK_T"].b])
                        self.load(ixk, ixk.t[64:128, 0:W], scr["IXK_T"].t[:, tok:tok + W], reads=[scr["IXK_T"].b])
                        for h in range(16):
                            hf = h % 2
                            ps = psR.next()
                            self.mm(ps, ps.t[:, 0:W], ixq.t[hf * 64:(hf + 1) * 64, h // 2, :], ixk.t[hf * 64:(hf + 1) * 64, 0:W],
                                    True, True, [ixq.b, ixk.b])
                            r = rbs.next()
                            P.op("act", lambda e, r=r, ps=ps, W=W: e.activation(out=r.t[:, 0:W], in_=ps.t[:, 0:W], func=AF.Relu),
                                 [ps.b], [r.b])
                            if h == 0:
                                P.op("dve", lambda e, r=r, W=W, col=col: e.tensor_scalar(
                                    row.t[:, col:col + W], r.t[:, 0:W], ixw.t[:, 0:1], None, ALU.mult), [r.b, ixw.b], [row.b])
                            else:
                                P.op("dve", lambda e, r=r, W=W, col=col, h=h: e.scalar_tensor_tensor(
                                    out=row.t[:, col:col + W], in0=r.t[:, 0:W], scalar=ixw.t[:, h:h + 1], in1=row.t[:, col:col + W],
                                    op0=ALU.mult, op1=ALU.add), [r.b, ixw.b, row.b], [row.b])
                        col += W
                        k += g
                if ty == 0:
                    P.op("dve", lambda e: e.tensor_tensor(row.t[:, 0:896], row.t[:, 0:896], C["prefmask"].t[:], ALU.add),
                         [row.b, C["prefmask"].b], [row.b])
                P.op("dve", lambda e: e.tensor_tensor(row.t[:, S - 128:S], row.t[:, S - 128:S], C["dsa_diag"].t[:, ty, :], ALU.add),
                     [row.b, C["dsa_diag"].b], [row.b])
                P.op("dve", lambda e: e.tensor_reduce(mx.t[:], row.t[:], AX.X, ALU.max), [row.b], [mx.b])
                P.op("dve", lambda e: e.tensor_scalar(mid.t[:], mx.t[:], -BIS_R / 2, None, ALU.add), [mx.b], [mid.b])
                for it in range(BIS_IT):
                    hw = BIS_R / (2 ** (it + 1))
                    P.op("dve", lambda e: e.tensor_scalar(maskadd.t[:], row.t[:], mid.t[:], None, ALU.is_ge, ALU.add, accum_out=cnt.t[:]),
                         [row.b, mid.b], [maskadd.b, cnt.b])
                    P.op("dve", lambda e, hw=hw: e.tensor_scalar(tfl.t[:], cnt.t[:], float(TOPK) - 0.5, hw, ALU.is_ge, ALU.mult),
                         [cnt.b], [tfl.b])
                    P.op("dve", lambda e, hw=hw: e.scalar_tensor_tensor(out=mid.t[:], in0=tfl.t[:], scalar=-hw / 2, in1=mid.t[:],
                                                                        op0=ALU.add, op1=ALU.add), [tfl.b, mid.b], [mid.b])
                hwK = BIS_R / (2 ** (BIS_IT + 1))
                P.op("dve", lambda e: e.tensor_scalar(thr.t[:], mid.t[:], -hwK, None, ALU.add), [mid.b], [thr.b])
                P.op("dve", lambda e: e.tensor_scalar(maskadd.t[:], row.t[:], thr.t[:], MASKNEG, ALU.is_lt, ALU.mult),
                     [row.b, thr.b], [maskadd.b])
                P.barrier()
                self.release_scope(locals())
            with ExitStack() as st3:
                sb3 = lambda name, shape, dt=F32: self.sb(st3, name, shape, dt)
                kcs = Rot([sb3("a_kc%d" % i, [128, 2048], BF16) for i in range(4)])
                vcs = Rot([sb3("a_vc%d" % i, [128, 16, 130], BF16) for i in range(4)])
                krr = sb3("a_kr", [128, S], BF16)
                pts = Rot([sb3("a_p%d" % i, [128, 512], BF16) for i in range(4)])
                p2s = Rot([sb3("a_p2%d" % i, [128, 256], BF16) for i in range(2)])
                tmpn = sb3("a_tmpn", [128, 256])
                rec = sb3("a_rec", [128, 1])
                ost = sb3("a_ost", [128, D])
                psS = Rot([PS[0], PS[1], PS[2], PS[3]])
                psA = Rot([PS[4], PS[5]])
                col = 0
                for (t0, n) in runs:
                    self.load(krr, krr.t[0:64, col:col + n * 128], scr["KR_T"].t[:, t0 * 128:(t0 + n) * 128], reads=[scr["KR_T"].b])
                    self.load(krr, krr.t[64:128, col:col + n * 128], scr["KR_T"].t[:, t0 * 128:(t0 + n) * 128], reads=[scr["KR_T"].b])
                    col += n * 128
                chunks = []
                gk = 0
                for (t0, n) in runs:
                    k = 0
                    while k < n:
                        m = min(16, n - k)
                        chunks.append((t0 + k, m, gk))
                        gk += m
                        k += m
                for kind in ("dsa", "mla"):
                    KT = scr["KT_A"] if kind == "dsa" else scr["KT_B"]
                    VV = scr["V_A"] if kind == "dsa" else scr["V_B"]
                    for h in range(NH):
                        acc = psA.next()
                        hf = h % 2
                        pending = None
                        for (t0, m, gk0) in chunks:
                            kc = kcs.next()
                            vc = vcs.next()
                            self.load(kc, kc.t[:, 0:m * 128], KT.t[h, :, t0 * 128:(t0 + m) * 128], reads=[KT.b])
                            self.load(vc, vc.t[:, 0:m, :], VV.t[h, :, t0:t0 + m, :], reads=[VV.b])
                            k = 0
                            while k < m:
                                gkt = gk0 + k
                                if gkt >= nk - 2:
                                    g = nk - gkt
                                    near = True
                                else:
                                    g = min(4, m - k, nk - 2 - gkt)
                                    near = False
                                Sps = psS.next()
                                for gi in range(g):
                                    cs_ = slice(gi * 128, (gi + 1) * 128)
                                    kl = k + gi
                                    if kind == "dsa":
                                        self.mm(Sps, Sps.t[:, cs_], kc.t[:, kl * 128:(kl + 1) * 128], qa.t[:, h, :], True, False,
                                                [kc.b, qa.b])
                                        self.mm(Sps, Sps.t[:, cs_], maskadd.t[:, (gkt + gi) * 128:(gkt + gi + 1) * 128], identb.t[:],
                                                False, True, [maskadd.b, identb.b])
                                    else:
                                        self.mm(Sps, Sps.t[:, cs_], kc.t[:, kl * 128:(kl + 1) * 128], qn.t[:, h, :], True, False,
                                                [kc.b, qn.b])
                                        self.mm(Sps, Sps.t[:, cs_], krr.t[hf * 64:(hf + 1) * 64, (gkt + gi) * 128:(gkt + gi + 1) * 128],
                                                qr.t[hf * 64:(hf + 1) * 64, h // 2, :], False, True, [krr.b, qr.b])
                                W = g * 128
                                p = pts.next()
                                if kind == "dsa":
                                    if near:
                                        P.op("dve", lambda e, Sps=Sps, h=h: e.scalar_tensor_tensor(
                                            out=tmpn.t[:], in0=Sps.t[:, 0:256], scalar=A_SCALE,
                                            in1=C["Tb"].t[:, h, :, :].rearrange("p y t -> p (y t)"), op0=ALU.mult, op1=ALU.add),
                                            [Sps.b, C["Tb"].b], [tmpn.b])
                                        P.op("act", lambda e, p=p: e.activation(out=p.t[:, 0:256], in_=tmpn.t[:], func=AF.Exp),
                                             [tmpn.b], [p.b])
                                    else:
                                        P.op("act", lambda e, p=p, Sps=Sps, W=W, h=h: e.activation(
                                            out=p.t[:, 0:W], in_=Sps.t[:, 0:W], func=AF.Exp, bias=C["constb"].t[:, h:h + 1], scale=A_SCALE),
                                            [Sps.b, C["constb"].b], [p.b])
                                    pp = p
                                else:
                                    P.op("act", lambda e, p=p, Sps=Sps, W=W: e.activation(out=p.t[:, 0:W], in_=Sps.t[:, 0:W], func=AF.Exp,
                                                                                       scale=MLA_SCALE), [Sps.b], [p.b])
                                    pp = p
                                    if near:
                                        p2 = p2s.next()
                                        P.op("dve", lambda e, p=p, p2=p2: e.tensor_tensor(p2.t[:], p.t[:, 0:256], C["mlamask"].t[:, ty, :],
                                                                                         ALU.mult), [p.b, C["mlamask"].b], [p2.b])
                                        pp = p2
                                if pending is not None:
                                    pending()

                                def pv(acc=acc, pp=pp, vc=vc, k=k, g=g, gkt=gkt):
                                    for gi in range(g):
                                        kl = k + gi
                                        self.mm(acc, acc.t[:, 0:129], pp.t[:, gi * 128:(gi + 1) * 128], vc.t[:, kl, 0:129],
                                                (gkt + gi) == 0, (gkt + gi) == nk - 1, [pp.b, vc.b])
                                pending = pv
                                k += g
                        if pending is not None:
                            pending()
                            pending = None
                        P.op("dve", lambda e, acc=acc: e.reciprocal(rec.t[:], acc.t[:, 128:129]), [acc.b], [rec.b])
                        P.op("dve", lambda e, acc=acc, h=h: e.tensor_scalar(ost.t[:, h * 128:(h + 1) * 128], acc.t[:, 0:128], rec.t[:], None,
                                                                            ALU.mult), [acc.b, rec.b], [ost.b])
                    dst = scr["OA"] if kind == "dsa" else scr["OB"]
                    self.store(ost, dst.t[o], ost.t[:], dst)
                P.barrier()
                self.release_scope(locals())
            self.release_scope(dict(qa=qa, ixq=ixq, ixw=ixw, qn=qn, qr=qr, maskadd=maskadd))

    def phase_M(self):
        P = self.P
        inp, out, scr = self.inp, self.out, self.scr
        PS = self.PS
        nb = int(os.environ.get("MK_NMB", "5"))
        for b in list(range(5))[:nb] if nb >= 5 else [0, 4][:nb]:
            tiles = self.own_tiles(b)
            with ExitStack() as so:
                xk = [self.sb(so, "m_x%d" % i, [128, D]) for i in range(4)]
                with ExitStack() as st:
                    sb = lambda name, shape, dt=F32: self.sb(st, name, shape, dt)
                    g_mix = sb("m_gmix", [128, D])
                    identf = sb("m_identf", [128, 128])
                    identb = sb("m_identb", [128, 128], BF16)
                    hb = sb("m_hb", [128, D], BF16)
                    hT = sb("m_hT", [128, 16, 512], BF16)
                    wts = Rot([sb("m_wt%d" % i, [128, 16, 512], BF16) for i in range(2)])
                    gas = Rot([sb("m_ga%d" % i, [128, 512]) for i in range(2)])
                    gbs = Rot([sb("m_gb%d" % i, [128, 512]) for i in range(2)])
                    oas = Rot([sb("m_oa%d" % i, [128, 512]) for i in range(2)])
                    obs = Rot([sb("m_ob%d" % i, [128, 512]) for i in range(2)])
                    mix = [sb("m_mix%d" % i, [128, D], BF16) for i in range(4)]
                    tmp = dict(junk=sb("m_junk", [128, D], BF16), ss=sb("m_ss", [128, 1]), sd=sb("m_sd", [128, 1]),
                               rstd=sb("m_rstd", [128, 1]))
                    psT = Rot([PS[0], PS[1]])
                    psM = Rot([PS[2], PS[3], PS[4], PS[5]])
                    self.load(g_mix, g_mix.t[:], inp["g_mix"][:, :])
                    self.load(identf, identf.t[:], inp["ident"][:, :])
                    P.op("dve", lambda e: e.tensor_copy(identb.t[:], identf.t[:]), [identf.b], [identb.b])
                    self.load_norm_T(st, tiles, g_mix, hT, None, hb, tmp, psT, identb, keep_x=xk)
                    for i in range(4):
                        wa = wts.next()
                        wb = wts.next()
                        self.load(wa, wa.t[:], scr["W_gate"].t[i], reads=[scr["W_gate"].b])
                        self.load(wb, wb.t[:], scr["W_gate"].t[4 + i], reads=[scr["W_gate"].b])
                        cs_ = slice(i * 512, (i + 1) * 512)
                        for pos in range(4):
                            o = 4 * b + pos
                            pa = psM.next()
                            for kc in range(16):
                                self.mm(pa, pa.t[:, :], hT.t[:, kc, pos * 128:(pos + 1) * 128], wa.t[:, kc, :], kc == 0, kc == 15, [wa.b, hT.b])
                            pb = psM.next()
                            for kc in range(16):
                                self.mm(pb, pb.t[:, :], hT.t[:, kc, pos * 128:(pos + 1) * 128], wb.t[:, kc, :], kc == 0, kc == 15, [wb.b, hT.b])
                            ga, gb, oa, ob = gas.next(), gbs.next(), oas.next(), obs.next()
                            P.op("act", lambda e, ga=ga, pa=pa: e.activation(out=ga.t[:], in_=pa.t[:, :], func=AF.Sigmoid), [pa.b], [ga.b])
                            P.op("act", lambda e, gb=gb, pb=pb: e.activation(out=gb.t[:], in_=pb.t[:, :], func=AF.Sigmoid), [pb.b], [gb.b])
                            self.load(oa, oa.t[:], scr["OA"].t[o][:, cs_], reads=[scr["OA"].b])
                            self.load(ob, ob.t[:], scr["OB"].t[o][:, cs_], reads=[scr["OB"].b])
                            P.op("dve", lambda e, ga=ga, oa=oa: e.tensor_tensor(ga.t[:], ga.t[:], oa.t[:], ALU.mult), [ga.b, oa.b], [ga.b])
                            P.op("dve", lambda e, gb=gb, ob=ob: e.tensor_tensor(gb.t[:], gb.t[:], ob.t[:], ALU.mult), [gb.b, ob.b], [gb.b])
                            P.op("dve", lambda e, ga=ga, gb=gb, pos=pos, cs_=cs_: e.tensor_tensor(mix[pos].t[:, cs_], ga.t[:], gb.t[:], ALU.add),
                                 [ga.b, gb.b], [mix[pos].b])
                    for pos in range(4):
                        self.transpose_into(mix[pos], mix[pos].t, 16, hT, lambda kc, pos=pos: hT.t[:, kc, pos * 128:(pos + 1) * 128], psT, identb)
                    for cb in range(4):
                        wt = wts.next()
                        self.load(wt, wt.t[:], scr["W_wo"].t[cb], reads=[scr["W_wo"].b])
                        cs_ = slice(cb * 512, (cb + 1) * 512)
                        for pos in range(4):
                            ps = psM.next()
                            for kc in range(16):
                                self.mm(ps, ps.t[:, :], hT.t[:, kc, pos * 128:(pos + 1) * 128], wt.t[:, kc, :], kc == 0, kc == 15, [wt.b, hT.b])
                            P.op("dve", lambda e, ps=ps, pos=pos, cs_=cs_: e.tensor_tensor(xk[pos].t[:, cs_], xk[pos].t[:, cs_], ps.t[:, :], ALU.add),
                                 [xk[pos].b, ps.b], [xk[pos].b])
                    P.barrier()
                    self.release_scope(locals())
                with ExitStack() as st:
                    sb = lambda name, shape, dt=F32: self.sb(st, name, shape, dt)
                    g_ffn = sb("n_gffn", [128, D])
                    g_fin = sb("n_gfin", [128, D])
                    identf = sb("n_identf", [128, 128])
                    identb = sb("n_identb", [128, 128], BF16)
                    hb = sb("n_hb", [128, D], BF16)
                    h2T = sb("n_h2T", [128, 16, 512], BF16)
                    uT = sb("n_uT", [128, 64, 512], BF16)
                    wts = Rot([sb("n_wt%d" % i, [128, 16, 512], BF16) for i in range(2)])
                    rrs = Rot([sb("n_rr%d" % i, [128, 512]) for i in range(2)])
                    ys = Rot([sb("n_y%d" % i, [128, D]) for i in range(2)])
                    tmp = dict(junk=sb("n_junk", [128, D], BF16), ss=sb("n_ss", [128, 1]), sd=sb("n_sd", [128, 1]),
                               rstd=sb("n_rstd", [128, 1]))
                    psT = Rot([PS[0], PS[1]])
                    psM = Rot([PS[6], PS[7]])
                    self.load(g_ffn, g_ffn.t[:], inp["g_ffn"][:, :])
                    self.load(g_fin, g_fin.t[:], inp["g_fin"][:, :])
                    self.load(identf, identf.t[:], inp["ident"][:, :])
                    P.op("dve", lambda e: e.tensor_copy(identb.t[:], identf.t[:]), [identf.b], [identb.b])
                    for pos in range(4):
                        self.norm_rows(xk[pos].t[:], D, g_ffn.t[:], hb.t[:], xk[pos].b, g_ffn.b, hb.b, tmp)
                        self.transpose_into(hb, hb.t, 16, h2T, lambda kc, pos=pos: h2T.t[:, kc, pos * 128:(pos + 1) * 128], psT, identb)
                    for cb in range(16):
                        wt = wts.next()
                        self.load(wt, wt.t[:], scr["W_up"].t[cb], reads=[scr["W_up"].b])
                        for jj in range(4):
                            ffc = 4 * cb + jj
                            ps = psM.next()
                            for kc in range(16):
                                self.mm(ps, ps.t[:, :], wt.t[:, kc, jj * 128:(jj + 1) * 128], h2T.t[:, kc, :], kc == 0, kc == 15, [wt.b, h2T.b])
                            rr = rrs.next()
                            P.op("act", lambda e, rr=rr, ps=ps: e.activation(out=rr.t[:], in_=ps.t[:, :], func=AF.Relu), [ps.b], [rr.b])
                            P.op("dve", lambda e, rr=rr, ffc=ffc: e.tensor_tensor(uT.t[:, ffc, :], rr.t[:], rr.t[:], ALU.mult), [rr.b], [uT.b])
                    for cb in range(4):
                        cs_ = slice(cb * 512, (cb + 1) * 512)
                        pss = [PS[2], PS[3], PS[4], PS[5]]
                        for kg in range(4):
                            wt = wts.next()
                            self.load(wt, wt.t[:], scr["W_dn"].t[cb * 4 + kg], reads=[scr["W_dn"].b])
                            for pos in range(4):
                                for kc in range(16):
                                    self.mm(pss[pos], pss[pos].t[:, :], uT.t[:, kg * 16 + kc, pos * 128:(pos + 1) * 128], wt.t[:, kc, :],
                                            kg == 0 and kc == 0, kg == 3 and kc == 15, [wt.b, uT.b])
                        for pos in range(4):
                            P.op("dve", lambda e, pos=pos, cs_=cs_, pss=pss: e.tensor_tensor(xk[pos].t[:, cs_], xk[pos].t[:, cs_], pss[pos].t[:, :],
                                                                                         ALU.add), [xk[pos].b, pss[pos].b], [xk[pos].b])
                    for pos in range(4):
                        o = 4 * b + pos
                        y = ys.next()
                        self.norm_rows(xk[pos].t[:], D, g_fin.t[:], y.t[:], xk[pos].b, g_fin.b, y.b, tmp)
                        self.store(y, out["y_own"][o * 128:(o + 1) * 128, :], y.t[:])
                    P.barrier()
                    self.release_scope(locals())
                self.release_scope(dict(xk=xk))

    def build(self):
        nc = self.nc
        self.declare()
        st = self.gstack
        self.PS = [TB(st.enter_context(nc.psum_tensor("ps%d" % i, [128, 512], F32)), "ps%d" % i) for i in range(8)]
        for p_ in self.PS:
            p_.b.excl = True
        self.phase_W()
        if self.upto == "W":
            return self.finish()
        kblocks = list(range(32)) + [34]
        if self.upto == "K1":
            kblocks = [0, 1, 34][:int(os.environ.get("MK_NB", "3"))]
        self.phase_K(kblocks)
        if self.upto in ("K", "K1"):
            return self.finish()
        self.phase_KC()
        if self.upto == "KC":
            return self.finish()
        self.phase_Q()
        if self.upto == "Q":
            return self.finish()
        self.slots_all()
        if self.upto == "S":
            return self.finish()
        self.phase_M()
        return self.finish()

    def finish(self):
        self.P.finish()
        self.gstack.close()
        return self.nc


def t5_bucket_np(rel):
    nb = 16
    ret = (rel > 0).astype(np.int32) * nb
    n = np.abs(rel)
    max_exact = 8
    nf = np.maximum(n, 1).astype(np.float32)
    large = max_exact + (np.log(nf / np.float32(max_exact)) / np.float32(math.log(128 / max_exact))
                         * np.float32(nb - max_exact)).astype(np.int32)
    large = np.minimum(large, nb - 1)
    return ret + np.where(n < max_exact, n, large)


def host_inputs(inputs):
    f32 = np.float32
    xp = np.asarray(inputs["x_prompt"], f32)[0]
    xs = np.asarray(inputs["x_sample"], f32)
    ck = np.asarray(inputs["cache_a_k"], f32)[0]
    cv = np.asarray(inputs["cache_a_v"], f32)[0]
    cix = np.asarray(inputs["cache_a_idx_k"], f32)[0]
    cckv = np.asarray(inputs["cache_b_ckv"], f32)[0]
    ckr = np.asarray(inputs["cache_b_krope"], f32)[0]
    half = 32
    inv_freq = np.power(np.float32(10000.0), -np.arange(half, dtype=f32) / np.float32(half)).astype(f32)
    shared = {
        "ident": np.eye(128, dtype=f32),
        "constb": np.ascontiguousarray(np.broadcast_to(np.asarray(inputs["rel_bias_table"], f32)[15], (128, 16))),
        "relb": np.ascontiguousarray(np.asarray(inputs["rel_bias_table"], f32)),
        "w_in": np.ascontiguousarray(np.asarray(inputs["w_in"], f32)[0]),
        "w_uq": np.ascontiguousarray(np.asarray(inputs["w_uq"], f32)[0]),
        "w_uk": np.ascontiguousarray(np.asarray(inputs["w_uk"], f32)[0].reshape(256, 2048)),
        "w_uv": np.ascontiguousarray(np.asarray(inputs["w_uv"], f32)[0].reshape(256, 2048)),
        "w_out": np.ascontiguousarray(np.asarray(inputs["w_out"], f32)[0]),
        "w_ff_up": np.ascontiguousarray(np.asarray(inputs["w_ff_up"], f32)[0]),
        "w_ff_down": np.ascontiguousarray(np.asarray(inputs["w_ff_down"], f32)[0]),
        "g_mix": np.ascontiguousarray(np.broadcast_to(np.asarray(inputs["norm_mix_g"], f32)[0], (128, D))),
        "g_ffn": np.ascontiguousarray(np.broadcast_to(np.asarray(inputs["norm_ffn_g"], f32)[0], (128, D))),
        "g_fin": np.ascontiguousarray(np.broadcast_to(np.asarray(inputs["final_norm_g"], f32), (128, D))),
        "g_q": np.ascontiguousarray(np.broadcast_to(np.asarray(inputs["q_lora_g"], f32)[0], (128, 512))),
        "g_kv": np.ascontiguousarray(np.broadcast_to(np.asarray(inputs["kv_lora_g"], f32)[0], (128, 256))),
    }
    s = np.arange(128)[None, :]
    t = np.arange(128)[:, None]
    onehot = np.zeros((2, 32, 128, 128), f32)
    for ty, off in enumerate((-128, 0)):
        rel = (s - t + off).astype(np.int32)
        bk = t5_bucket_np(rel)
        for b in range(32):
            onehot[ty, b] = (bk == b)
    shared["onehot"] = onehot.reshape(2, 32, 128 * 128)
    dsa_diag = np.zeros((2, 128, 128), f32)
    dsa_diag[0] = np.where((s // 64) <= (t // 64), 0.0, NEGBIG)
    dsa_diag[1] = np.where(s < 16, 0.0, NEGBIG) * np.ones((128, 1), f32)
    shared["dsa_diag"] = dsa_diag
    mla_mask = np.ones((2, 128, 2, 128), f32)
    ss_ = np.arange(128)[:, None]
    tt_ = np.arange(128)[None, :]
    mla_mask[0, :, 1, :] = ((ss_ // 64) <= (tt_ // 64)).astype(f32)
    mla_mask[1, :, 1, :] = (ss_ < 16).astype(f32) * np.ones((1, 128), f32)
    shared["mla_mask"] = mla_mask.reshape(2, 128, 256)
    maps = []
    for c in range(NCORE):
        m = dict(shared)
        pre = 7 - c
        x_all = np.zeros((NXT * 128, D), f32)
        x_all[pre * 128:pre * 128 + 16384] = xp
        pos = np.zeros((NXT * 128,), f32)
        valid = np.zeros((NXT * 128, 1), f32)
        pos[pre * 128:pre * 128 + 16384] = np.arange(16384, dtype=f32)
        valid[pre * 128:pre * 128 + 16384] = 1.0
        for b in range(2):
            r0 = (136 + b) * 128
            x_all[r0:r0 + 16] = xs[2 * c + b]
            pos[r0:r0 + 16] = 1024 + np.arange(16, dtype=f32)
            valid[r0:r0 + 16] = 1.0
        ang = pos[:, None] * inv_freq[None, :]
        m["x_all"] = x_all
        m["cs"] = np.concatenate([np.cos(ang), np.sin(ang)], axis=1).astype(f32)
        m["valid"] = np.ascontiguousarray(valid.reshape(NXT, 128).T)
        pm = np.zeros((896,), f32)
        pm[:pre * 128] = NEGBIG
        m["prefmask"] = np.ascontiguousarray(np.broadcast_to(pm, (128, 896)))
        sl = slice(2 * c, 2 * c + 2)
        m["c_akT"] = np.ascontiguousarray(ck[sl].transpose(0, 2, 3, 1))
        cve = np.ones((2, 1024, 16, 130), f32)
        cve[..., 0:128] = cv[sl]
        m["c_av"] = cve
        m["c_ixkT"] = np.ascontiguousarray(cix[sl].transpose(0, 2, 1))
        m["c_ckvT"] = np.ascontiguousarray(cckv[sl].transpose(0, 2, 1))
        m["c_krT"] = np.ascontiguousarray(ckr[sl].transpose(0, 2, 1))
        maps.append(m)
    return maps


def assemble(results):
    f32 = np.float32
    y_p = np.zeros((1, 16384, D), f32)
    y_s = np.zeros((16, 16, D), f32)
    a_k_p = np.zeros((1, 1, 16384, 16, 128), f32)
    a_v_p = np.zeros((1, 1, 16384, 16, 128), f32)
    a_ix_p = np.zeros((1, 1, 16384, 64), f32)
    b_ckv_p = np.zeros((1, 1, 16384, 256), f32)
    b_kr_p = np.zeros((1, 1, 16384, 64), f32)
    a_k_s = np.zeros((1, 16, 16, 16, 128), f32)
    a_v_s = np.zeros((1, 16, 16, 16, 128), f32)
    a_ix_s = np.zeros((1, 16, 16, 64), f32)
    b_ckv_s = np.zeros((1, 16, 16, 256), f32)
    b_kr_s = np.zeros((1, 16, 16, 64), f32)
    for c in range(NCORE):
        r = results[c]
        for i in range(16):
            j = c + 8 * i
            rs = slice(i * 128, (i + 1) * 128)
            ps = slice(j * 128, (j + 1) * 128)
            y_p[0, ps] = r["y_own"][rs]
            a_k_p[0, 0, ps] = r["st_ak"][rs].reshape(128, 16, 128)
            a_v_p[0, 0, ps] = r["st_av"][rs].reshape(128, 16, 128)
            a_ix_p[0, 0, ps] = r["st_ix"][rs]
            b_ckv_p[0, 0, ps] = r["st_ckv"][rs]
            b_kr_p[0, 0, ps] = r["st_kr"][rs]
        for b in range(2):
            rs = slice((16 + b) * 128, (16 + b) * 128 + 16)
            sq = 2 * c + b
            y_s[sq] = r["y_own"][rs]
            a_k_s[0, sq] = r["st_ak"][rs].reshape(16, 16, 128)
            a_v_s[0, sq] = r["st_av"][rs].reshape(16, 16, 128)
            a_ix_s[0, sq] = r["st_ix"][rs]
            b_ckv_s[0, sq] = r["st_ckv"][rs]
            b_kr_s[0, sq] = r["st_kr"][rs]
    return (y_p, y_s, a_k_p, a_v_p, a_ix_p, b_ckv_p, b_kr_p, a_k_s, a_v_s, a_ix_s, b_ckv_s, b_kr_s)


def kernel(**inputs):
    upto = os.environ.get("MK_UPTO", "ALL")
    bld = Builder(upto)
    nc = bld.build()
    maps = host_inputs(inputs)
    res = run_bass_kernel_spmd(nc, maps, core_ids=list(range(NCORE)))
    return assemble(res.results)
```

```python
import math
import os
from contextlib import ExitStack

import numpy as np
import concourse.bass as bass
import concourse.mybir as mybir
from concourse.bass_utils import run_bass_kernel_spmd

F32 = mybir.dt.float32
BF16 = mybir.dt.bfloat16
AF = mybir.ActivationFunctionType
ALU = mybir.AluOpType
AX = mybir.AxisListType

D = 2048
NH = 16
HD = 128
NCORE = 8
NPT = 136
NXT = 140
NTILE = 158
NTOK = NTILE * 128
NOWN = 20
EPS = 1e-6
A_SCALE = HD ** -0.5
MLA_SCALE = 192 ** -0.5
NEGBIG = -1.0e30
MASKNEG = -30000.0
TOPK = 256
BIS_R = 128.0
BIS_IT = 13
C_AQ, C_AK, C_AV, C_IXQ, C_IXK, C_IXW, C_CQ, C_CKV, C_KR, C_GA, C_GB = (
    0, 2048, 4096, 6144, 7168, 7232, 7248, 7760, 8016, 8080, 10128)


class Buf:
    __slots__ = ("name", "last_w", "readers", "dsem", "dcount", "excl")

    def __init__(self, name):
        self.excl = False
        self.name = name
        self.last_w = None
        self.readers = []
        self.dsem = None
        self.dcount = 0


class Ins:
    __slots__ = ("eng", "fn", "deps", "is_dma", "dbuf", "need_inc", "seq", "cover")

    def __init__(self, eng, fn, deps, is_dma=False, dbuf=None):
        self.eng = eng
        self.fn = fn
        self.deps = deps
        self.is_dma = is_dma
        self.dbuf = dbuf
        self.need_inc = False
        self.seq = 0
        self.cover = 0


COMPUTE = ("pe", "act", "dve", "pool")


class Prog:
    def __init__(self, nc, stack):
        self.nc = nc
        self.stack = stack
        self.engs = {"pe": nc.tensor, "act": nc.scalar, "dve": nc.vector, "pool": nc.gpsimd, "sp": nc.sync}
        self.batch = []
        self.esem = {e: stack.enter_context(nc.semaphore("e_" + e)) for e in COMPUTE}
        self.ecount = {e: 0 for e in COMPUTE}
        self.waited = {e: {} for e in self.engs}
        self.sempool = []
        self.livesems = {}
        self.nsem = 4
        self.n_ins = 0
        self.n_waits = 0
        self.trace = {e: [] for e in self.engs}

    def _getsem(self, buf):
        if self.sempool:
            sem, cnt = self.sempool.pop()
        else:
            self.nsem += 1
            sem = self.stack.enter_context(self.nc.semaphore("d%d" % self.nsem))
            cnt = 0
        buf.dsem = sem
        buf.dcount = cnt
        self.livesems[id(sem)] = [sem, cnt, cnt]

    def release(self, bufs):
        for b in bufs:
            if b.dsem is not None:
                ent = self.livesems.pop(id(b.dsem))
                self.sempool.append((ent[0], ent[1]))
                b.dsem = None

    def _deps(self, eng, reads, writes, is_dma):
        deps = {}

        def add(p, kind):
            if p is None:
                return
            if (not p.is_dma) and (not is_dma) and p.eng == eng:
                if eng == "pe" or kind != "raw":
                    return
            deps[id(p)] = p

        for b in reads:
            add(b.last_w, "raw")
            if b.excl:
                for r in b.readers:
                    if r.eng != eng:
                        add(r, "war")
        for b in writes:
            add(b.last_w, "waw")
            for r in b.readers:
                add(r, "war")
        return list(deps.values())

    def _post(self, ins, reads, writes):
        for b in reads:
            if not ins.is_dma:
                b.readers = [r for r in b.readers if r.is_dma or r.eng != ins.eng]
            b.readers.append(ins)
        for b in writes:
            b.last_w = ins
            b.readers = []
        self.batch.append(ins)

    def op(self, eng, fn, reads=(), writes=()):
        ins = Ins(eng, fn, self._deps(eng, reads, writes, False))
        self._post(ins, reads, writes)

    def dma(self, fn, reads, writes, dbuf, q="sp"):
        ins = Ins(q, fn, self._deps(q, reads, writes, True), True, dbuf)
        if dbuf.dsem is None:
            self._getsem(dbuf)
        dbuf.dcount += 1
        self.livesems[id(dbuf.dsem)][1] = dbuf.dcount
        self._post(ins, reads, writes)

    def flush(self):
        batch = self.batch
        self.batch = []
        for i in batch:
            for p in i.deps:
                if not p.is_dma:
                    p.need_inc = True
        last = {}
        for i in batch:
            if not i.is_dma:
                last[i.eng] = i
        for i in last.values():
            i.need_inc = True
        for i in batch:
            if not i.is_dma and i.need_inc:
                self.ecount[i.eng] += 1
                i.seq = self.ecount[i.eng]
        nxt = {}
        for i in reversed(batch):
            if not i.is_dma:
                if i.need_inc:
                    nxt[i.eng] = i.seq
                i.cover = nxt[i.eng]
        for i in batch:
            h = self.engs[i.eng]
            need = {}
            for p in i.deps:
                if p.is_dma:
                    ent = self.livesems.get(id(p.dbuf.dsem)) if p.dbuf.dsem is not None else None
                    if ent is None:
                        continue
                    key = id(ent[0])
                    sem = ent[0]
                    val = 16 * ent[2]
                else:
                    key = p.eng
                    sem = self.esem[p.eng]
                    val = p.cover
                if key not in need or need[key][1] < val:
                    need[key] = (sem, val)
            w = self.waited[i.eng]
            for key, (sem, val) in need.items():
                if w.get(key, 0) >= val:
                    continue
                w[key] = val
                h.wait_ge(sem, val)
                self.trace[i.eng].append(("w", id(sem), val))
                self.n_waits += 1
            bi = i.fn(h)
            self.n_ins += 1
            if i.is_dma:
                bi.then_inc(i.dbuf.dsem, 16)
                self.trace[i.eng].append(("i", id(i.dbuf.dsem), 16))
                self.livesems[id(i.dbuf.dsem)][2] += 1
            elif i.need_inc:
                bi.then_inc(self.esem[i.eng], 1)
                self.trace[i.eng].append(("i", id(self.esem[i.eng]), 1))
            else:
                self.trace[i.eng].append(("i", None, 0))

    def barrier(self):
        self.flush()
        for e, h in self.engs.items():
            w = self.waited[e]
            for pe in COMPUTE:
                if pe == e:
                    continue
                val = self.ecount[pe]
                if val > 0 and w.get(pe, 0) < val:
                    w[pe] = val
                    h.wait_ge(self.esem[pe], val)
                    self.trace[e].append(("w", id(self.esem[pe]), val))
            for key, ent in self.livesems.items():
                val = 16 * ent[2]
                if val > 0 and w.get(key, 0) < val:
                    w[key] = val
                    h.wait_ge(ent[0], val)
                    self.trace[e].append(("w", id(ent[0]), val))

    def finish(self):
        self.barrier()


class TB:
    def __init__(self, t, name):
        self.t = t
        self.b = Buf(name)


class Rot:
    def __init__(self, items):
        self.items = items
        self.i = 0

    def next(self):
        x = self.items[self.i % len(self.items)]
        self.i += 1
        return x


def weight_groups():
    g = {}
    g["ak"] = dict(K=2048, blocks=[[("w_in", 0, C_AK + 512 * i, 512)] for i in range(4)])
    g["av"] = dict(K=2048, blocks=[[("w_in", 0, C_AV + 512 * i, 512)] for i in range(4)])
    g["misc"] = dict(K=2048, blocks=[[("w_in", 0, C_CKV, 256), ("w_in", 0, C_IXK, 64), ("w_in", 0, C_KR, 64)]])
    g["aq"] = dict(K=2048, blocks=[[("w_in", 0, C_AQ + 512 * i, 512)] for i in range(4)])
    g["ixq"] = dict(K=2048, blocks=[[("w_in", 0, C_IXQ + 512 * i, 512)] for i in range(2)])
    g["cq"] = dict(K=2048, blocks=[[("w_in", 0, C_CQ, 512)], [("w_in", 0, C_IXW, 16)]])
    g["gate"] = dict(K=2048, blocks=[[("w_in", 0, C_GA + 512 * i, 512)] for i in range(8)])
    g["wo"] = dict(K=2048, blocks=[[("w_out", 0, 512 * i, 512)] for i in range(4)])
    g["up"] = dict(K=2048, blocks=[[("w_ff_up", 0, 512 * i, 512)] for i in range(16)])
    g["dn"] = dict(K=2048, blocks=[[("w_ff_down", 2048 * kg, 512 * cb, 512)] for cb in range(4) for kg in range(4)])
    g["uqn"] = dict(K=512, blocks=[[("uqn", 0, i, 0)] for i in range(4)])
    g["uqr"] = dict(K=512, blocks=[[("uqr", 0, i, 0)] for i in range(2)])
    g["uk"] = dict(K=256, blocks=[[("w_uk", 0, 512 * i, 512)] for i in range(4)])
    g["uv"] = dict(K=256, blocks=[[("w_uv", 0, 512 * i, 512)] for i in range(4)])
    first = ["uk", "uv", "misc", "ak", "av"]
    return {k: g[k] for k in first + [k for k in g if k not in first]}


class _Stop(Exception):
    pass


class Builder:
    def ck(self, n):
        if int(os.environ.get("MK_KSTOP", "0")) == n:
            raise _Stop()

    def __init__(self, upto="ALL"):
        self.upto = upto
        self.nc = bass.Bass("TRN2", target_bir_lowering=False)
        self.gstack = ExitStack()
        self.P = Prog(self.nc, self.gstack)
        self.inp = {}
        self.out = {}
        self.scr = {}

    def din(self, name, shape, dt=F32):
        self.inp[name] = self.nc.dram_tensor(name, list(shape), dt, kind="ExternalInput").ap()
        return self.inp[name]

    def dout(self, name, shape, dt=F32):
        self.out[name] = self.nc.dram_tensor(name, list(shape), dt, kind="ExternalOutput").ap()
        return self.out[name]

    def dscr(self, name, shape, dt=BF16):
        self.scr[name] = TB(self.nc.dram_tensor(name, list(shape), dt, kind="Internal").ap(), name)
        return self.scr[name]

    def sb(self, st, name, shape, dt=F32):
        self._uid = getattr(self, "_uid", 0) + 1
        name = "%s_u%d" % (name, self._uid)
        return TB(st.enter_context(self.nc.sbuf_tensor(name, list(shape), dt)), name)

    def mm(self, ps, out_ap, lhsT, rhs, start, stop, reads):
        self.P.op("pe", lambda e: e.matmul(out_ap, lhsT, rhs, start=start, stop=stop), reads, [ps.b])

    def load(self, dst, dst_ap, src_ap, reads=(), q="sp"):
        self.P.dma(lambda e: e.dma_start(out=dst_ap, in_=src_ap), list(reads), [dst.b], dst.b, q=q)

    def store(self, src, dst_ap, src_ap, dstbuf=None, q=None):
        q = q or getattr(self, "store_q", "pool")
        w = [dstbuf.b] if dstbuf is not None else []
        self.P.dma(lambda e: e.dma_start(out=dst_ap, in_=src_ap), [src.b], w, src.b, q=q)

    def declare(self):
        din, dout, dscr = self.din, self.dout, self.dscr
        din("x_all", [NXT * 128, D])
        din("cs", [NXT * 128, 64])
        din("valid", [128, NXT])
        din("prefmask", [128, 896])
        din("ident", [128, 128])
        din("onehot", [2, 32, 128 * 128])
        din("relb", [32, 16])
        din("constb", [128, 16])
        din("dsa_diag", [2, 128, 128])
        din("mla_mask", [2, 128, 256])
        din("c_akT", [2, 16, 128, 1024])
        din("c_av", [2, 1024, 16, 130])
        din("c_ixkT", [2, 64, 1024])
        din("c_ckvT", [2, 256, 1024])
        din("c_krT", [2, 64, 1024])
        din("w_in", [D, 12176])
        din("w_uq", [512, 16, 192])
        din("w_uk", [256, 2048])
        din("w_uv", [256, 2048])
        din("w_out", [D, D])
        din("w_ff_up", [D, 8192])
        din("w_ff_down", [8192, D])
        din("g_mix", [128, D])
        din("g_ffn", [128, D])
        din("g_fin", [128, D])
        din("g_q", [128, 512])
        din("g_kv", [128, 256])
        dout("y_own", [NOWN * 128, D])
        dout("st_ak", [NOWN * 128, D])
        dout("st_av", [NOWN * 128, D])
        dout("st_ix", [NOWN * 128, 64])
        dout("st_ckv", [NOWN * 128, 256])
        dout("st_kr", [NOWN * 128, 64])
        dscr("KT_A", [NH, 128, NTOK])
        dscr("V_A", [NH, 128, NTILE, 130])
        dscr("IXK_T", [64, NTOK])
        dscr("KT_B", [NH, 128, NTOK])
        dscr("KR_T", [64, NTOK])
        dscr("V_B", [NH, 128, NTILE, 130])
        dscr("QA_T", [5, 128, 16, 512])
        dscr("IXQ_T", [5, 128, 8, 512])
        dscr("IXW", [5, 128, 4, 16], F32)
        dscr("QN_T", [5, 128, 16, 512])
        dscr("QR_T", [5, 128, 8, 512])
        dscr("OA", [NOWN, 128, D], F32)
        dscr("OB", [NOWN, 128, D], F32)
        self.wg = weight_groups()
        for name, g in self.wg.items():
            dscr("W_" + name, [len(g["blocks"]), 128, g["K"] // 128, 512])

    def phase_W(self):
        P = self.P
        inp = self.inp
        urgent = ["uk", "uv", "misc", "ak", "av"]
        self._precast(urgent)
        KT_A, V_A, IXK_T, KR_T = self.scr["KT_A"], self.scr["V_A"], self.scr["IXK_T"], self.scr["KR_T"]
        for b in range(2):
            t0 = 140 + 9 * b
            for h in range(NH):
                P.dma(lambda e, b=b, h=h, t0=t0: e.dma_start(out=KT_A.t[h, :, t0 * 128:t0 * 128 + 1024], in_=inp["c_akT"][b, h]),
                      [], [KT_A.b], KT_A.b, q="pool")
                P.dma(lambda e, b=b, h=h, t0=t0: e.dma_start(
                    out=V_A.t[h, :, t0:t0 + 8, :],
                    in_=inp["c_av"][b, :, h, :].rearrange("(kt p) d -> p kt d", p=128)), [], [V_A.b], V_A.b, q="pool")
            P.dma(lambda e, b=b, t0=t0: e.dma_start(out=IXK_T.t[:, t0 * 128:t0 * 128 + 1024], in_=inp["c_ixkT"][b]),
                  [], [IXK_T.b], IXK_T.b, q="pool")
            P.dma(lambda e, b=b, t0=t0: e.dma_start(out=KR_T.t[:, t0 * 128:t0 * 128 + 1024], in_=inp["c_krT"][b]),
                  [], [KR_T.b], KR_T.b, q="pool")
        self._precast([k for k in self.wg if k not in urgent])

    def _precast(self, names):
        P = self.P
        inp = self.inp
        for name in names:
            g = self.wg[name]
            W = self.scr["W_" + name]
            KC = g["K"] // 128
            for bi, pieces in enumerate(g["blocks"]):
                off = 0
                for (src, r0, c0, ncols) in pieces:
                    if src in ("uqn", "uqr"):
                        i = c0
                        nh_, w_, lo_ = (4, 128, 0) if src == "uqn" else (8, 64, 128)
                        for j in range(nh_):
                            src_ap = inp["w_uq"][:, nh_ * i + j, lo_:lo_ + w_].rearrange("(kc p) d -> p kc d", p=128)
                            dst_ap = W.t[bi][:, :, j * w_:(j + 1) * w_]
                            P.dma(lambda e, d=dst_ap, s=src_ap: e.dma_start(out=d, in_=s), [], [W.b], W.b, q="pool")
                        continue
                    src_ap = inp[src][r0:r0 + g["K"], c0:c0 + ncols].rearrange("(kc p) c -> p kc c", p=128)
                    dst_ap = W.t[bi][:, :, off:off + ncols]
                    off += ncols
                    P.dma(lambda e, d=dst_ap, s=src_ap: e.dma_start(out=d, in_=s), [], [W.b], W.b, q="pool")

    def norm_rows(self, x_ap, n, g_ap, out_ap, xb, gb, outb, tmp):
        P = self.P
        junk, ss, sd, rstd = tmp["junk"], tmp["ss"], tmp["sd"], tmp["rstd"]
        P.op("act", lambda e: e.activation(out=junk.t[:, 0:n], in_=x_ap, func=AF.Square, accum_out=ss.t[:]),
             [xb], [junk.b, ss.b])
        P.op("act", lambda e: e.activation(out=sd.t[:], in_=ss.t[:], func=AF.Sqrt, bias=EPS, scale=1.0 / n),
             [ss.b], [sd.b])
        P.op("dve", lambda e: e.reciprocal(rstd.t[:], sd.t[:]), [sd.b], [rstd.b])
        P.op("dve", lambda e: e.scalar_tensor_tensor(out=out_ap, in0=x_ap, scalar=rstd.t[:], in1=g_ap,
                                                     op0=ALU.mult, op1=ALU.mult), [xb, rstd.b, gb], [outb])

    def load_norm_T(self, st, tiles, g_tb, hT, xts, hb, tmp, psT, identb, keep_x=None):
        P = self.P
        xall = self.inp["x_all"]
        for j, tile in enumerate(tiles):
            xt = keep_x[j] if keep_x is not None else xts.next()
            self.load(xt, xt.t[:], xall[tile * 128:(tile + 1) * 128, :])
            self.norm_rows(xt.t[:], D, g_tb.t[:], hb.t[:], xt.b, g_tb.b, hb.b, tmp)
            self.transpose_into(hb, hb.t, 16, hT, lambda kc, j=j: hT.t[:, kc, j * 128:(j + 1) * 128], psT, identb)

    def transpose_into(self, src, src_t, nchunks, dst, dst_ap_fn, psT, identb, rows=128):
        P = self.P
        c = 0
        while c < nchunks:
            n = min(4, nchunks - c)
            ps = psT.next()
            for k in range(n):
                self.mm(ps, ps.t[:, k * 128:(k + 1) * 128], src_t[:, (c + k) * 128:(c + k + 1) * 128], identb.t[:],
                        True, True, [src.b, identb.b])
            self._tflip = not getattr(self, "_tflip", False)
            for k in range(n):
                eng = "act" if self._tflip else "dve"
                d_ap = dst_ap_fn(c + k)
                s_ap = ps.t[:, k * 128:(k + 1) * 128]
                if eng == "act":
                    P.op("act", lambda e, d=d_ap, s=s_ap: e.activation(out=d, in_=s, func=AF.Copy), [ps.b], [dst.b])
                else:
                    P.op("dve", lambda e, d=d_ap, s=s_ap: e.tensor_copy(d, s), [ps.b], [dst.b])
            c += n

    def rope(self, out_ap1, out_ap2, x1, x2, cos, sin, xb, csb, outb, tmps):
        P = self.P
        t1, t2 = tmps
        P.op("dve", lambda e: e.tensor_tensor(t1.t_ap, x1, cos, ALU.mult), [xb, csb], [t1.b])
        P.op("dve", lambda e: e.tensor_tensor(t2.t_ap, x2, sin, ALU.mult), [xb, csb], [t2.b])
        P.op("dve", lambda e: e.tensor_tensor(out_ap1, t1.t_ap, t2.t_ap, ALU.subtract), [t1.b, t2.b], [outb])
        P.op("dve", lambda e: e.tensor_tensor(t1.t_ap, x1, sin, ALU.mult), [xb, csb, outb], [t1.b])
        P.op("dve", lambda e: e.tensor_tensor(t2.t_ap, x2, cos, ALU.mult), [xb, csb, outb], [t2.b])
        P.op("dve", lambda e: e.tensor_tensor(out_ap2, t1.t_ap, t2.t_ap, ALU.add), [t1.b, t2.b], [outb])

    def phase_K(self, blocks):
        P = self.P
        self.store_q = "act"
        nc = self.nc
        inp, out, scr = self.inp, self.out, self.scr
        with ExitStack() as st:
            sb = lambda name, shape, dt=F32: self.sb(st, name, shape, dt)
            g_mix = sb("k_gmix", [128, D])
            g_kv = sb("k_gkv", [128, 256])
            identf = sb("k_identf", [128, 128])
            identb = sb("k_identb", [128, 128], BF16)
            ones16 = sb("k_ones16", [128, 16, 1])
            wuk = [sb("k_wuk%d" % i, [128, 2, 512], BF16) for i in range(4)]
            wuv = [sb("k_wuv%d" % i, [128, 2, 512], BF16) for i in range(4)]
            xts = Rot([sb("k_x%d" % i, [128, D]) for i in range(2)])
            hb = sb("k_hb", [128, D], BF16)
            hT = sb("k_hT", [128, 16, 512], BF16)
            wts = Rot([sb("k_wt%d" % i, [128, 16, 512], BF16) for i in range(3)])
            ksts = Rot([sb("k_kst%d" % i, [128, 512], BF16) for i in range(2)])
            vst = [sb("k_vst%d" % i, [128, 16, 130], BF16) for i in range(4)]
            vbst = [sb("k_vbst%d" % i, [128, 16, 130], BF16) for i in range(4)]
            mst = [sb("k_mst%d" % i, [128, 384]) for i in range(4)]
            mo = [sb("k_mo%d" % i, [128, 384]) for i in range(4)]
            km = sb("k_km", [128, 384], BF16)
            ckvT = sb("k_ckvT", [128, 2, 512], BF16)
            ixkrT = sb("k_ixkrT", [128, 512], BF16)
            osts = Rot([sb("k_ost%d" % i, [128, 512]) for i in range(3)])
            cst = sb("k_cs", [128, 4, 64])
            vld = sb("k_vld", [128, 4])
            tmp = dict(junk=sb("k_junk", [128, D], BF16), ss=sb("k_ss", [128, 1]), sd=sb("k_sd", [128, 1]),
                       rstd=sb("k_rstd", [128, 1]))
            rt1 = sb("k_rt1", [128, 32])
            rt2 = sb("k_rt2", [128, 32])
            rt1.t_ap = rt1.t[:]
            rt2.t_ap = rt2.t[:]
            PS = self.PS
            psT = Rot([PS[0], PS[1]])
            psM = Rot([PS[2], PS[3], PS[4], PS[5]])
            psX = Rot([PS[6], PS[7]])

            self.load(g_mix, g_mix.t[:], inp["g_mix"][:, :])
            self.load(g_kv, g_kv.t[:], inp["g_kv"][:, :])
            self.load(identf, identf.t[:], inp["ident"][:, :])
            P.op("dve", lambda e: e.tensor_copy(identb.t[:], identf.t[:]), [identf.b], [identb.b])
            P.op("dve", lambda e: e.memset(ones16.t[:], 1.0), [], [ones16.b])
            for i in range(4):
                self.load(wuk[i], wuk[i].t[:], scr["W_uk"].t[i], reads=[scr["W_uk"].b])
                self.load(wuv[i], wuv[i].t[:], scr["W_uv"].t[i], reads=[scr["W_uv"].b])

            for kb in blocks:
              try:
                tiles = [4 * kb + j for j in range(4)]
                tok0 = 4 * kb * 128
                if kb == 34:
                    own = {0: 16, 1: 17, 2: 18, 3: 19}
                elif kb % 2 == 1:
                    own = {3: (kb - 1) // 2}
                else:
                    own = {}
                self.load(cst, cst.t[:], inp["cs"][tok0:tok0 + 512, :].rearrange("(j p) c -> p j c", p=128))
                self.load(vld, vld.t[:], inp["valid"][:, 4 * kb:4 * kb + 4])
                self.ck(1)
                self.load_norm_T(st, tiles, g_mix, hT, xts, hb, tmp, psT, identb)
                self.ck(2)
                for cb in range(4):
                    wt = wts.next()
                    self.load(wt, wt.t[:], scr["W_ak"].t[cb], reads=[scr["W_ak"].b])
                    for j in range(4):
                        head = 4 * cb + j
                        ps = psM.next()
                        for kc in range(16):
                            self.mm(ps, ps.t[:, :], wt.t[:, kc, j * 128:(j + 1) * 128], hT.t[:, kc, :], kc == 0, kc == 15,
                                    [wt.b, hT.b])
                        kst = ksts.next()
                        P.op("act", lambda e, kst=kst, ps=ps: e.activation(out=kst.t[:], in_=ps.t[:, :], func=AF.Copy),
                             [ps.b], [kst.b])
                        self.store_cols(kst, scr["KT_A"], lambda a, n, head=head: scr["KT_A"].t[head, :, a:a + n], kst.t, tiles)
                    for pos, o in own.items():
                        ps = psM.next()
                        for kc in range(16):
                            self.mm(ps, ps.t[:, :], hT.t[:, kc, pos * 128:(pos + 1) * 128], wt.t[:, kc, :], kc == 0, kc == 15,
                                    [wt.b, hT.b])
                        ost = osts.next()
                        P.op("act", lambda e, ost=ost, ps=ps: e.activation(out=ost.t[:], in_=ps.t[:, :], func=AF.Copy),
                             [ps.b], [ost.b])
                        self.store(ost, out["st_ak"][o * 128:(o + 1) * 128, cb * 512:(cb + 1) * 512], ost.t[:])
                self.ck(3)
                for cb in range(4):
                    wt = wts.next()
                    self.load(wt, wt.t[:], scr["W_av"].t[cb], reads=[scr["W_av"].b])
                    for pos in range(4):
                        ps = psM.next()
                        for kc in range(16):
                            self.mm(ps, ps.t[:, :], hT.t[:, kc, pos * 128:(pos + 1) * 128], wt.t[:, kc, :], kc == 0, kc == 15,
                                    [wt.b, hT.b])
                        v = vst[pos]
                        P.op("dve", lambda e, v=v, ps=ps, cb=cb: e.tensor_copy(
                            v.t[:, 4 * cb:4 * cb + 4, 0:128], ps.t[:, :].rearrange("p (h d) -> p h d", h=4)), [ps.b], [v.b])
                        if pos in own:
                            o = own[pos]
                            ost = osts.next()
                            P.op("dve", lambda e, ost=ost, ps=ps: e.tensor_copy(ost.t[:], ps.t[:, :]), [ps.b], [ost.b])
                            self.store(ost, out["st_av"][o * 128:(o + 1) * 128, cb * 512:(cb + 1) * 512], ost.t[:])
                for pos in range(4):
                    v = vst[pos]
                    P.op("dve", lambda e, v=v, pos=pos: e.tensor_scalar(v.t[:, :, 128:129], ones16.t[:], vld.t[:, pos:pos + 1], None,
                                                                        ALU.mult), [ones16.b, vld.b], [v.b])
                    self.store(v, scr["V_A"].t[:, :, self.tmap(tiles[pos]), 0:129].rearrange("h p c -> p h c"), v.t[:, :, 0:129],
                               scr["V_A"])
                self.ck(4)
                wt = wts.next()
                self.load(wt, wt.t[:, :, 0:384], scr["W_misc"].t[0][:, :, 0:384], reads=[scr["W_misc"].b])
                for pos in range(4):
                    ps = psX.next()
                    for kc in range(16):
                        self.mm(ps, ps.t[:, 0:384], hT.t[:, kc, pos * 128:(pos + 1) * 128], wt.t[:, kc, 0:384], kc == 0, kc == 15,
                                [wt.b, hT.b])
                    m = mst[pos]
                    P.op("act", lambda e, m=m, ps=ps: e.activation(out=m.t[:], in_=ps.t[:, 0:384], func=AF.Copy), [ps.b], [m.b])
                self.ck(41)
                for pos in range(4):
                    m = mst[pos]
                    o_ = mo[pos]
                    self.norm_rows(m.t[:, 0:256], 256, g_kv.t[:], o_.t[:, 0:256], m.b, g_kv.b, o_.b, tmp)
                    self.ck(42)
                    P.op("act", lambda e, m=m, o_=o_: e.activation(out=o_.t[:, 256:320], in_=m.t[:, 256:320], func=AF.Copy),
                         [m.b], [o_.b])
                    self.rope(o_.t[:, 320:352], o_.t[:, 352:384], m.t[:, 320:352], m.t[:, 352:384],
                              cst.t[:, pos, 0:32], cst.t[:, pos, 32:64], m.b, cst.b, o_.b, (rt1, rt2))
                    self.ck(43)
                    if pos in own:
                        o = own[pos]
                        self.store(o_, out["st_ckv"][o * 128:(o + 1) * 128, :], o_.t[:, 0:256])
                        self.store(o_, out["st_ix"][o * 128:(o + 1) * 128, :], o_.t[:, 256:320])
                        self.store(o_, out["st_kr"][o * 128:(o + 1) * 128, :], o_.t[:, 320:384])
                    P.op("act", lambda e, o_=o_: e.activation(out=km.t[:], in_=o_.t[:], func=AF.Copy), [o_.b], [km.b])
                    self.ck(431)
                    ps = psX.next()
                    for k in range(3):
                        self.mm(ps, ps.t[:, k * 128:(k + 1) * 128], km.t[:, k * 128:(k + 1) * 128], identb.t[:], True, True,
                                [km.b, identb.b])
                    self.ck(432)
                    P.op("dve", lambda e, ps=ps, pos=pos: e.tensor_copy(
                        ckvT.t[:, :, pos * 128:(pos + 1) * 128], ps.t[:, 0:256].rearrange("p (c t) -> p c t", c=2)),
                        [ps.b], [ckvT.b])
                    self.ck(4321)
                    P.op("dve", lambda e, ps=ps, pos=pos: e.tensor_copy(ixkrT.t[:, pos * 128:(pos + 1) * 128], ps.t[:, 256:384]),
                         [ps.b], [ixkrT.b])
                    self.ck(433)
                self.ck(44)
                self.store_cols(ixkrT, scr["IXK_T"], lambda a, n: scr["IXK_T"].t[:, a:a + n], ixkrT.t[0:64], tiles)
                self.store_cols(ixkrT, scr["KR_T"], lambda a, n: scr["KR_T"].t[:, a:a + n], ixkrT.t[64:128], tiles)
                self.ck(5)
                if int(os.environ.get("MK_KSTOP", "0")) == 6:
                    self.mla_kside(ckvT, wuk, wuv, psM, ksts, vbst, tiles, tok0, lambda pos: vld.t[:, pos:pos + 1], vld, ones16)
                self.ck(6)
                self.mla_kside(ckvT, wuk, wuv, psM, ksts, vbst, tiles, tok0, lambda pos: vld.t[:, pos:pos + 1], vld, ones16)
              except _Stop:
                break
            self.store_q = "pool"
            P.barrier()
            self.release_scope(locals())

    def mla_kside(self, ckvT, wuk, wuv, psM, ksts, vbst, tiles, tok0, vld_ap_fn, vld, ones16, ncols=512):
        P = self.P
        scr = self.scr
        for h in range(NH):
            ps = psM.next()
            for cc in range(2):
                self.mm(ps, ps.t[:, 0:ncols], wuk[h // 4].t[:, cc, (h % 4) * 128:(h % 4 + 1) * 128], ckvT.t[:, cc, 0:ncols],
                        cc == 0, cc == 1, [wuk[h // 4].b, ckvT.b])
            kst = ksts.next()
            P.op("act", lambda e, kst=kst, ps=ps: e.activation(out=kst.t[:, 0:ncols], in_=ps.t[:, 0:ncols], func=AF.Copy),
                 [ps.b], [kst.b])
            self.store_cols(kst, scr["KT_B"], lambda a, n, h=h: scr["KT_B"].t[h, :, a:a + n], kst.t, tiles)
        for pos in range(len(tiles)):
            v = vbst[pos % len(vbst)]
            for cb in range(4):
                ps = psM.next()
                for cc in range(2):
                    self.mm(ps, ps.t[:, :], ckvT.t[:, cc, pos * 128:(pos + 1) * 128], wuv[cb].t[:, cc, :], cc == 0, cc == 1,
                            [wuv[cb].b, ckvT.b])
                P.op("dve", lambda e, v=v, ps=ps, cb=cb: e.tensor_copy(
                    v.t[:, 4 * cb:4 * cb + 4, 0:128], ps.t[:, :].rearrange("p (h d) -> p h d", h=4)), [ps.b], [v.b])
            if vld is not None:
                P.op("dve", lambda e, v=v, pos=pos: e.tensor_scalar(v.t[:, :, 128:129], ones16.t[:], vld_ap_fn(pos), None,
                                                                    ALU.mult), [ones16.b, vld.b], [v.b])
                self.store(v, scr["V_B"].t[:, :, self.tmap(tiles[pos]), 0:129].rearrange("h p c -> p h c"), v.t[:, :, 0:129], scr["V_B"])
            else:
                self.store(v, scr["V_B"].t[:, :, self.tmap(tiles[pos]), 0:129].rearrange("h p c -> p h c"), v.t[:, :, 0:129], scr["V_B"])

    def release_scope(self, loc):
        bufs = []
        for v in loc.values():
            if isinstance(v, TB):
                bufs.append(v.b)
            elif isinstance(v, (list, tuple)):
                for x in v:
                    if isinstance(x, TB):
                        bufs.append(x.b)
            elif isinstance(v, Rot):
                for x in v.items:
                    if isinstance(x, TB):
                        bufs.append(x.b)
            elif isinstance(v, dict):
                for x in v.values():
                    if isinstance(x, TB):
                        bufs.append(x.b)
        self.P.release(bufs)

    def phase_KC(self):
        P = self.P
        inp, scr = self.inp, self.scr
        with ExitStack() as st:
            sb = lambda name, shape, dt=F32: self.sb(st, name, shape, dt)
            wuk = [sb("c_wuk%d" % i, [128, 2, 512], BF16) for i in range(4)]
            wuv = [sb("c_wuv%d" % i, [128, 2, 512], BF16) for i in range(4)]
            ckvT = sb("c_ckvT", [128, 2, 512], BF16)
            ksts = Rot([sb("c_kst%d" % i, [128, 512], BF16) for i in range(2)])
            vbst = [sb("c_vbst%d" % i, [128, 16, 130], BF16) for i in range(4)]
            PS = self.PS
            psM = Rot([PS[2], PS[3], PS[4], PS[5]])
            for i in range(4):
                self.load(wuk[i], wuk[i].t[:], scr["W_uk"].t[i], reads=[scr["W_uk"].b])
                self.load(wuv[i], wuv[i].t[:], scr["W_uv"].t[i], reads=[scr["W_uv"].b])
                P.op("dve", lambda e, v=vbst[i]: e.memset(v.t[:, :, 128:130], 1.0), [], [vbst[i].b])
            for b in range(2):
                for half in range(2):
                    t0 = 140 + 9 * b + 4 * half
                    self.load(ckvT, ckvT.t[:], inp["c_ckvT"][b, :, half * 512:(half + 1) * 512].rearrange("(c p) s -> p c s", p=128),
                              q="pool")
                    self.mla_kside(ckvT, wuk, wuv, psM, ksts, vbst, [t0 + j for j in range(4)], t0 * 128, None, None, None)
            P.barrier()
            self.release_scope(locals())


    @staticmethod
    def tmap(t):
        return {136: 148, 137: 157}.get(t, t)

    def store_cols(self, src, dst, dst_row_ap_fn, src_t, tiles):
        mapped = [self.tmap(t) for t in tiles]
        if all(mapped[i] == mapped[0] + i for i in range(len(mapped))):
            self.store(src, dst_row_ap_fn(mapped[0] * 128, len(mapped) * 128), src_t[:, 0:len(mapped) * 128], dst)
        else:
            for i, mt in enumerate(mapped):
                self.store(src, dst_row_ap_fn(mt * 128, 128), src_t[:, i * 128:(i + 1) * 128], dst)

    def own_tiles(self, b):
        return [136 + j for j in range(4)] if b == 4 else [8 * (4 * b + j) + 7 for j in range(4)]

    def phase_Q(self):
        P = self.P
        inp, scr = self.inp, self.scr
        with ExitStack() as st:
            sb = lambda name, shape, dt=F32: self.sb(st, name, shape, dt)
            g_mix = sb("q_gmix", [128, D])
            g_q = sb("q_gq", [128, 512])
            identf = sb("q_identf", [128, 128])
            identb = sb("q_identb", [128, 128], BF16)
            xts = Rot([sb("q_x%d" % i, [128, D]) for i in range(2)])
            hb = sb("q_hb", [128, D], BF16)
            hT = sb("q_hT", [128, 16, 512], BF16)
            wts = Rot([sb("q_wt%d" % i, [128, 16, 512], BF16) for i in range(3)])
            qsts = Rot([sb("q_qst%d" % i, [128, 512], BF16) for i in range(3)])
            wuqn = [sb("q_wuqn%d" % i, [128, 4, 512], BF16) for i in range(4)]
            wuqr = [sb("q_wuqr%d" % i, [128, 4, 512], BF16) for i in range(2)]
            cq_f = sb("q_cqf", [128, 512])
            cqb = sb("q_cqb", [128, 512], BF16)
            cqT = sb("q_cqT", [128, 4, 512], BF16)
            qr_f = sb("q_qrf", [128, 1024])
            qr_o = sb("q_qro", [128, 1024])
            qrb = sb("q_qrb", [128, 1024], BF16)
            qrT = sb("q_qrT", [128, 8, 512], BF16)
            ixw = sb("q_ixw", [128, 4, 16])
            cst = sb("q_cs", [128, 4, 64])
            tmp = dict(junk=sb("q_junk", [128, D], BF16), ss=sb("q_ss", [128, 1]), sd=sb("q_sd", [128, 1]),
                       rstd=sb("q_rstd", [128, 1]))
            rt1 = sb("q_rt1", [128, 32])
            rt2 = sb("q_rt2", [128, 32])
            rt1.t_ap = rt1.t[:]
            rt2.t_ap = rt2.t[:]
            PS = self.PS
            psT = Rot([PS[0], PS[1]])
            psM = Rot([PS[2], PS[3], PS[4], PS[5]])
            psX = Rot([PS[6], PS[7]])
            self.load(g_mix, g_mix.t[:], inp["g_mix"][:, :])
            self.load(g_q, g_q.t[:], inp["g_q"][:, :])
            self.load(identf, identf.t[:], inp["ident"][:, :])
            P.op("dve", lambda e: e.tensor_copy(identb.t[:], identf.t[:]), [identf.b], [identb.b])
            for i in range(4):
                self.load(wuqn[i], wuqn[i].t[:], scr["W_uqn"].t[i], reads=[scr["W_uqn"].b])
            for i in range(2):
                self.load(wuqr[i], wuqr[i].t[:], scr["W_uqr"].t[i], reads=[scr["W_uqr"].b])
            for b in range(5):
                tiles = self.own_tiles(b)
                for j, tile in enumerate(tiles):
                    self.load(cst, cst.t[:, j, :], inp["cs"][tile * 128:(tile + 1) * 128, :])
                self.load_norm_T(st, tiles, g_mix, hT, xts, hb, tmp, psT, identb)
                for grp, nblk, dst in (("aq", 4, "QA_T"), ("ixq", 2, "IXQ_T")):
                    for cb in range(nblk):
                        wt = wts.next()
                        self.load(wt, wt.t[:], scr["W_" + grp].t[cb], reads=[scr["W_" + grp].b])
                        for j in range(4):
                            ps = psM.next()
                            for kc in range(16):
                                self.mm(ps, ps.t[:, :], wt.t[:, kc, j * 128:(j + 1) * 128], hT.t[:, kc, :], kc == 0, kc == 15,
                                        [wt.b, hT.b])
                            q = qsts.next()
                            P.op("act", lambda e, q=q, ps=ps: e.activation(out=q.t[:], in_=ps.t[:, :], func=AF.Copy), [ps.b], [q.b])
                            self.store(q, scr[dst].t[b][:, 4 * cb + j, :], q.t[:], scr[dst])
                wt = wts.next()
                self.load(wt, wt.t[:], scr["W_cq"].t[0], reads=[scr["W_cq"].b])
                for pos in range(4):
                    ps = psM.next()
                    for kc in range(16):
                        self.mm(ps, ps.t[:, :], hT.t[:, kc, pos * 128:(pos + 1) * 128], wt.t[:, kc, :], kc == 0, kc == 15, [wt.b, hT.b])
                    P.op("act", lambda e, ps=ps: e.activation(out=cq_f.t[:], in_=ps.t[:, :], func=AF.Copy), [ps.b], [cq_f.b])
                    self.norm_rows(cq_f.t[:], 512, g_q.t[:], cqb.t[:], cq_f.b, g_q.b, cqb.b, tmp)
                    self.transpose_into(cqb, cqb.t, 4, cqT, lambda c, pos=pos: cqT.t[:, c, pos * 128:(pos + 1) * 128], psT, identb)
                wt = wts.next()
                self.load(wt, wt.t[:, :, 0:16], scr["W_cq"].t[1][:, :, 0:16], reads=[scr["W_cq"].b])
                for pos in range(4):
                    ps = psX.next()
                    for kc in range(16):
                        self.mm(ps, ps.t[:, 0:16], hT.t[:, kc, pos * 128:(pos + 1) * 128], wt.t[:, kc, 0:16], kc == 0, kc == 15,
                                [wt.b, hT.b])
                    P.op("act", lambda e, ps=ps, pos=pos: e.activation(out=ixw.t[:, pos, :], in_=ps.t[:, 0:16], func=AF.Copy, scale=0.25),
                         [ps.b], [ixw.b])
                self.store(ixw, scr["IXW"].t[b], ixw.t[:], scr["IXW"])
                for h in range(NH):
                    ps = psM.next()
                    for cc in range(4):
                        self.mm(ps, ps.t[:, :], wuqn[h // 4].t[:, cc, (h % 4) * 128:(h % 4 + 1) * 128], cqT.t[:, cc, :], cc == 0, cc == 3,
                                [wuqn[h // 4].b, cqT.b])
                    q = qsts.next()
                    P.op("act", lambda e, q=q, ps=ps: e.activation(out=q.t[:], in_=ps.t[:, :], func=AF.Copy), [ps.b], [q.b])
                    self.store(q, scr["QN_T"].t[b][:, h, :], q.t[:], scr["QN_T"])
                for pos in range(4):
                    for blk in range(2):
                        ps = psM.next()
                        for cc in range(4):
                            self.mm(ps, ps.t[:, :], cqT.t[:, cc, pos * 128:(pos + 1) * 128], wuqr[blk].t[:, cc, :], cc == 0, cc == 3,
                                    [wuqr[blk].b, cqT.b])
                        P.op("act", lambda e, ps=ps, blk=blk: e.activation(out=qr_f.t[:, blk * 512:(blk + 1) * 512], in_=ps.t[:, :],
                                                                           func=AF.Copy), [ps.b], [qr_f.b])
                    for h in range(NH):
                        c0 = h * 64
                        self.rope(qr_o.t[:, c0:c0 + 32], qr_o.t[:, c0 + 32:c0 + 64], qr_f.t[:, c0:c0 + 32], qr_f.t[:, c0 + 32:c0 + 64],
                                  cst.t[:, pos, 0:32], cst.t[:, pos, 32:64], qr_f.b, cst.b, qr_o.b, (rt1, rt2))
                    P.op("act", lambda e: e.activation(out=qrb.t[:], in_=qr_o.t[:], func=AF.Copy), [qr_o.b], [qrb.b])
                    self.transpose_into(qrb, qrb.t, 8, qrT, lambda c, pos=pos: qrT.t[:, c, pos * 128:(pos + 1) * 128], psT, identb)
                self.store(qrT, scr["QR_T"].t[b], qrT.t[:], scr["QR_T"])
            P.barrier()
            self.release_scope(locals())

    def slots_all(self):
        P = self.P
        inp, scr = self.inp, self.scr
        with ExitStack() as st:
            sb = lambda name, shape, dt=F32: self.sb(st, name, shape, dt)
            C = {}
            C["identf"] = sb("s_identf", [128, 128])
            C["identb"] = sb("s_identb", [128, 128], BF16)
            C["Tb"] = sb("s_Tb", [128, 16, 2, 128])
            C["constb"] = sb("s_constb", [128, 16])
            C["prefmask"] = sb("s_pref", [128, 896])
            C["dsa_diag"] = sb("s_dsadiag", [128, 2, 128])
            C["mlaf"] = sb("s_mlaf", [128, 2, 256])
            C["mlamask"] = sb("s_mlamask", [128, 2, 256], BF16)
            relb = sb("s_relb", [32, 16])
            oh = sb("s_oh", [32, 4096])
            self.load(C["identf"], C["identf"].t[:], inp["ident"][:, :])
            P.op("dve", lambda e: e.tensor_copy(C["identb"].t[:], C["identf"].t[:]), [C["identf"].b], [C["identb"].b])
            self.load(C["constb"], C["constb"].t[:], inp["constb"][:, :])
            self.load(C["prefmask"], C["prefmask"].t[:], inp["prefmask"][:, :])
            self.load(C["dsa_diag"], C["dsa_diag"].t[:], inp["dsa_diag"].rearrange("y t s -> t y s"))
            self.load(C["mlaf"], C["mlaf"].t[:], inp["mla_mask"].rearrange("y s c -> s y c"))
            P.op("dve", lambda e: e.tensor_copy(C["mlamask"].t[:], C["mlaf"].t[:]), [C["mlaf"].b], [C["mlamask"].b])
            self.load(relb, relb.t[:], inp["relb"][:, :])
            PS = self.PS
            for ty in range(2):
                for tg in range(4):
                    self.load(oh, oh.t[:], inp["onehot"][ty][:, tg * 4096:(tg + 1) * 4096])
                    ps = PS[tg % 2]
                    for t in range(32):
                        self.mm(ps, ps.t[:, t * 16:(t + 1) * 16], oh.t[:, t * 128:(t + 1) * 128], relb.t[:, :], True, True, [oh.b, relb.b])
                    P.op("dve", lambda e, ps=ps, ty=ty, tg=tg: e.tensor_copy(
                        C["Tb"].t[:, :, ty, tg * 32:(tg + 1) * 32], ps.t[:, :].rearrange("p (t h) -> p h t", h=16)), [ps.b], [C["Tb"].b])
            P.barrier()
            nslot = int(os.environ.get("MK_NSLOT", "18"))
            order = []
            for o in range(16):
                order.append((o // 4, o % 4, [(0, 8 * o + 8)], 0))
            order.append((4, 0, [(140, 9)], 1))
            order.append((4, 1, [(149, 9)], 1))
            if nslot < 18:
                order = [order[0], order[16], order[1], order[17]][:nslot]
            for (b, j, runs, ty) in order:
                self.slot(C, b, j, runs, ty)
            self.release_scope(dict(C=C, relb=relb, oh=oh))

    def slot(self, C, b, j, runs, ty):
        P = self.P
        inp, scr = self.inp, self.scr
        PS = self.PS
        o = 4 * b + j
        ktl = []
        for (t0, n) in runs:
            ktl += [t0 + i for i in range(n)]
        nk = len(ktl)
        S = nk * 128
        identb = C["identb"]
        with ExitStack() as st:
            sb = lambda name, shape, dt=F32: self.sb(st, name, shape, dt)
            qa = sb("l_qa", [128, 16, 128], BF16)
            ixq = sb("l_ixq", [128, 8, 128], BF16)
            ixw = sb("l_ixw", [128, 16])
            qn = sb("l_qn", [128, 16, 128], BF16)
            qr = sb("l_qr", [128, 8, 128], BF16)
            maskadd = sb("l_maskadd", [128, S], BF16)
            js = slice(j * 128, (j + 1) * 128)
            self.load(qa, qa.t[:], scr["QA_T"].t[b][:, :, js], reads=[scr["QA_T"].b])
            self.load(ixq, ixq.t[:], scr["IXQ_T"].t[b][:, :, js], reads=[scr["IXQ_T"].b])
            self.load(ixw, ixw.t[:], scr["IXW"].t[b][:, j, :], reads=[scr["IXW"].b])
            self.load(qn, qn.t[:], scr["QN_T"].t[b][:, :, js], reads=[scr["QN_T"].b])
            self.load(qr, qr.t[:], scr["QR_T"].t[b][:, :, js], reads=[scr["QR_T"].b])
            with ExitStack() as st2:
                sb2 = lambda name, shape, dt=F32: self.sb(st2, name, shape, dt)
                row = sb2("i_row", [128, S])
                ixks = Rot([sb2("i_ixk%d" % i, [128, 512], BF16) for i in range(2)])
                rbs = Rot([sb2("i_r%d" % i, [128, 512]) for i in range(3)])
                mx = sb2("i_mx", [128, 1])
                mid = sb2("i_mid", [128, 1])
                cnt = sb2("i_cnt", [128, 1])
                tfl = sb2("i_tfl", [128, 1])
                thr = sb2("i_thr", [128, 1])
                psR = Rot([PS[0], PS[1], PS[2], PS[3]])
                col = 0
                for (t0, n) in runs:
                    k = 0
                    while k < n:
                        g = min(4, n - k)
                        W = g * 128
                        tok = (t0 + k) * 128
                        ixk = ixks.next()
                        self.load(ixk, ixk.t[0:64, 0:W], scr["IXK_T"].t[:, tok:tok + W], reads=[scr["IXK_T"].b])
                        self.load(ixk, ixk.t[64:128, 0:W], scr["IXK_T"].t[:, tok:tok + W], reads=[scr["IXK_T"].b])
                        for h in range(16):
                            hf = h % 2
                            ps = psR.next()
                            self.mm(ps, ps.t[:, 0:W], ixq.t[hf * 64:(hf + 1) * 64, h // 2, :], ixk.t[hf * 64:(hf + 1) * 64, 0:W],
                                    True, True, [ixq.b, ixk.b])
                            r = rbs.next()
                            P.op("act", lambda e, r=r, ps=ps, W=W: e.activation(out=r.t[:, 0:W], in_=ps.t[:, 0:W], func=AF.Relu),
                                 [ps.b], [r.b])
                            if h == 0:
                                P.op("dve", lambda e, r=r, W=W, col=col: e.tensor_scalar(
                                    row.t[:, col:col + W], r.t[:, 0:W], ixw.t[:, 0:1], None, ALU.mult), [r.b, ixw.b], [row.b])
                            else:
                                P.op("dve", lambda e, r=r, W=W, col=col, h=h: e.scalar_tensor_tensor(
                                    out=row.t[:, col:col + W], in0=r.t[:, 0:W], scalar=ixw.t[:, h:h + 1], in1=row.t[:, col:col + W],
                                    op0=ALU.mult, op1=ALU.add), [r.b, ixw.b, row.b], [row.b])
                        col += W
                        k += g
                if ty == 0:
                    P.op("dve", lambda e: e.tensor_tensor(row.t[:, 0:896], row.t[:, 0:896], C["prefmask"].t[:], ALU.add),
                         [row.b, C["prefmask"].b], [row.b])
                P.op("dve", lambda e: e.tensor_tensor(row.t[:, S - 128:S], row.t[:, S - 128:S], C["dsa_diag"].t[:, ty, :], ALU.add),
                     [row.b, C["dsa_diag"].b], [row.b])
                P.op("dve", lambda e: e.tensor_reduce(mx.t[:], row.t[:], AX.X, ALU.max), [row.b], [mx.b])
                P.op("dve", lambda e: e.tensor_scalar(mid.t[:], mx.t[:], -BIS_R / 2, None, ALU.add), [mx.b], [mid.b])
                for it in range(BIS_IT):
                    hw = BIS_R / (2 ** (it + 1))
                    P.op("dve", lambda e: e.tensor_scalar(maskadd.t[:], row.t[:], mid.t[:], None, ALU.is_ge, ALU.add, accum_out=cnt.t[:]),
                         [row.b, mid.b], [maskadd.b, cnt.b])
                    P.op("dve", lambda e, hw=hw: e.tensor_scalar(tfl.t[:], cnt.t[:], float(TOPK) - 0.5, hw, ALU.is_ge, ALU.mult),
                         [cnt.b], [tfl.b])
                    P.op("dve", lambda e, hw=hw: e.scalar_tensor_tensor(out=mid.t[:], in0=tfl.t[:], scalar=-hw / 2, in1=mid.t[:],
                                                                        op0=ALU.add, op1=ALU.add), [tfl.b, mid.b], [mid.b])
                hwK = BIS_R / (2 ** (BIS_IT + 1))
                P.op("dve", lambda e: e.tensor_scalar(thr.t[:], mid.t[:], -hwK, None, ALU.add), [mid.b], [thr.b])
                P.op("dve", lambda e: e.tensor_scalar(maskadd.t[:], row.t[:], thr.t[:], MASKNEG, ALU.is_lt, ALU.mult),
                     [row.b, thr.b], [maskadd.b])
                P.barrier()
                self.release_scope(locals())
            with ExitStack() as st3:
                sb3 = lambda name, shape, dt=F32: self.sb(st3, name, shape, dt)
                kcs = Rot([sb3("a_kc%d" % i, [128, 2048], BF16) for i in range(4)])
                vcs = Rot([sb3("a_vc%d" % i, [128, 16, 130], BF16) for i in range(4)])
                krr = sb3("a_kr", [128, S], BF16)
                pts = Rot([sb3("a_p%d" % i, [128, 512], BF16) for i in range(4)])
                p2s = Rot([sb3("a_p2%d" % i, [128, 256], BF16) for i in range(2)])
                tmpn = sb3("a_tmpn", [128, 256])
                rec = sb3("a_rec", [128, 1])
                ost = sb3("a_ost", [128, D])
                psS = Rot([PS[0], PS[1], PS[2], PS[3]])
                psA = Rot([PS[4], PS[5]])
                col = 0
                for (t0, n) in runs:
                    self.load(krr, krr.t[0:64, col:col + n * 128], scr["KR_T"].t[:, t0 * 128:(t0 + n) * 128], reads=[scr["KR_T"].b])
                    self.load(krr, krr.t[64:128, col:col + n * 128], scr["KR_T"].t[:, t0 * 128:(t0 + n) * 128], reads=[scr["KR_T"].b])
                    col += n * 128
                chunks = []
                gk = 0
                for (t0, n) in runs:
                    k = 0
                    while k < n:
                        m = min(16, n - k)
                        chunks.append((t0 + k, m, gk))
                        gk += m
                        k += m
                for kind in ("dsa", "mla"):
                    KT = scr["KT_A"] if kind == "dsa" else scr["KT_B"]
                    VV = scr["V_A"] if kind == "dsa" else scr["V_B"]
                    for h in range(NH):
                        acc = psA.next()
                        hf = h % 2
                        pending = None
                        for (t0, m, gk0) in chunks:
                            kc = kcs.next()
                            vc = vcs.next()
                            self.load(kc, kc.t[:, 0:m * 128], KT.t[h, :, t0 * 128:(t0 + m) * 128], reads=[KT.b])
                            self.load(vc, vc.t[:, 0:m, :], VV.t[h, :, t0:t0 + m, :], reads=[VV.b])
                            k = 0
                            while k < m:
                                gkt = gk0 + k
                                if gkt >= nk - 2:
                                    g = nk - gkt
                                    near = True
                                else:
                                    g = min(4, m - k, nk - 2 - gkt)
                                    near = False
                                Sps = psS.next()
                                for gi in range(g):
                                    cs_ = slice(gi * 128, (gi + 1) * 128)
                                    kl = k + gi
                                    if kind == "dsa":
                                        self.mm(Sps, Sps.t[:, cs_], kc.t[:, kl * 128:(kl + 1) * 128], qa.t[:, h, :], True, False,
                                                [kc.b, qa.b])
                                        self.mm(Sps, Sps.t[:, cs_], maskadd.t[:, (gkt + gi) * 128:(gkt + gi + 1) * 128], identb.t[:],
                                                False, True, [maskadd.b, identb.b])
                                    else:
                                        self.mm(Sps, Sps.t[:, cs_], kc.t[:, kl * 128:(kl + 1) * 128], qn.t[:, h, :], True, False,
                                                [kc.b, qn.b])
                                        self.mm(Sps, Sps.t[:, cs_], krr.t[hf * 64:(hf + 1) * 64, (gkt + gi) * 128:(gkt + gi + 1) * 128],
                                                qr.t[hf * 64:(hf + 1) * 64, h // 2, :], False, True, [krr.b, qr.b])
                                W = g * 128
                                p = pts.next()
                                if kind == "dsa":
                                    if near:
                                        P.op("dve", lambda e, Sps=Sps, h=h: e.scalar_tensor_tensor(
                                            out=tmpn.t[:], in0=Sps.t[:, 0:256], scalar=A_SCALE,
                                            in1=C["Tb"].t[:, h, :, :].rearrange("p y t -> p (y t)"), op0=ALU.mult, op1=ALU.add),
                                            [Sps.b, C["Tb"].b], [tmpn.b])
                                        P.op("act", lambda e, p=p: e.activation(out=p.t[:, 0:256], in_=tmpn.t[:], func=AF.Exp),
                                             [tmpn.b], [p.b])
                                    else:
                                        P.op("act", lambda e, p=p, Sps=Sps, W=W, h=h: e.activation(
                                            out=p.t[:, 0:W], in_=Sps.t[:, 0:W], func=AF.Exp, bias=C["constb"].t[:, h:h + 1], scale=A_SCALE),
                                            [Sps.b, C["constb"].b], [p.b])
                                    pp = p
                                else:
                                    P.op("act", lambda e, p=p, Sps=Sps, W=W: e.activation(out=p.t[:, 0:W], in_=Sps.t[:, 0:W], func=AF.Exp,
                                                                                       scale=MLA_SCALE), [Sps.b], [p.b])
                                    pp = p
                                    if near:
                                        p2 = p2s.next()
                                        P.op("dve", lambda e, p=p, p2=p2: e.tensor_tensor(p2.t[:], p.t[:, 0:256], C["mlamask"].t[:, ty, :],
                                                                                         ALU.mult), [p.b, C["mlamask"].b], [p2.b])
                                        pp = p2
                                if pending is not None:
                                    pending()

                                def pv(acc=acc, pp=pp, vc=vc, k=k, g=g, gkt=gkt):
                                    for gi in range(g):
                                        kl = k + gi
                                        self.mm(acc, acc.t[:, 0:129], pp.t[:, gi * 128:(gi + 1) * 128], vc.t[:, kl, 0:129],
                                                (gkt + gi) == 0, (gkt + gi) == nk - 1, [pp.b, vc.b])
                                pending = pv
                                k += g
                        if pending is not None:
                            pending()
                            pending = None
                        P.op("dve", lambda e, acc=acc: e.reciprocal(rec.t[:], acc.t[:, 128:129]), [acc.b], [rec.b])
                        P.op("dve", lambda e, acc=acc, h=h: e.tensor_scalar(ost.t[:, h * 128:(h + 1) * 128], acc.t[:, 0:128], rec.t[:], None,
                                                                            ALU.mult), [acc.b, rec.b], [ost.b])
                    dst = scr["OA"] if kind == "dsa" else scr["OB"]
                    self.store(ost, dst.t[o], ost.t[:], dst)
                P.barrier()
                self.release_scope(locals())
            self.release_scope(dict(qa=qa, ixq=ixq, ixw=ixw, qn=qn, qr=qr, maskadd=maskadd))

    def phase_M(self):
        P = self.P
        inp, out, scr = self.inp, self.out, self.scr
        PS = self.PS
        nb = int(os.environ.get("MK_NMB", "5"))
        for b in list(range(5))[:nb] if nb >= 5 else [0, 4][:nb]:
            tiles = self.own_tiles(b)
            with ExitStack() as so:
                xk = [self.sb(so, "m_x%d" % i, [128, D]) for i in range(4)]
                with ExitStack() as st:
                    sb = lambda name, shape, dt=F32: self.sb(st, name, shape, dt)
                    g_mix = sb("m_gmix", [128, D])
                    identf = sb("m_identf", [128, 128])
                    identb = sb("m_identb", [128, 128], BF16)
                    hb = sb("m_hb", [128, D], BF16)
                    hT = sb("m_hT", [128, 16, 512], BF16)
                    wts = Rot([sb("m_wt%d" % i, [128, 16, 512], BF16) for i in range(2)])
                    gas = Rot([sb("m_ga%d" % i, [128, 512]) for i in range(2)])
                    gbs = Rot([sb("m_gb%d" % i, [128, 512]) for i in range(2)])
                    oas = Rot([sb("m_oa%d" % i, [128, 512]) for i in range(2)])
                    obs = Rot([sb("m_ob%d" % i, [128, 512]) for i in range(2)])
                    mix = [sb("m_mix%d" % i, [128, D], BF16) for i in range(4)]
                    tmp = dict(junk=sb("m_junk", [128, D], BF16), ss=sb("m_ss", [128, 1]), sd=sb("m_sd", [128, 1]),
                               rstd=sb("m_rstd", [128, 1]))
                    psT = Rot([PS[0], PS[1]])
                    psM = Rot([PS[2], PS[3], PS[4], PS[5]])
                    self.load(g_mix, g_mix.t[:], inp["g_mix"][:, :])
                    self.load(identf, identf.t[:], inp["ident"][:, :])
                    P.op("dve", lambda e: e.tensor_copy(identb.t[:], identf.t[:]), [identf.b], [identb.b])
                    self.load_norm_T(st, tiles, g_mix, hT, None, hb, tmp, psT, identb, keep_x=xk)
                    for i in range(4):
                        wa = wts.next()
                        wb = wts.next()
                        self.load(wa, wa.t[:], scr["W_gate"].t[i], reads=[scr["W_gate"].b])
                        self.load(wb, wb.t[:], scr["W_gate"].t[4 + i], reads=[scr["W_gate"].b])
                        cs_ = slice(i * 512, (i + 1) * 512)
                        for pos in range(4):
                            o = 4 * b + pos
                            pa = psM.next()
                            for kc in range(16):
                                self.mm(pa, pa.t[:, :], hT.t[:, kc, pos * 128:(pos + 1) * 128], wa.t[:, kc, :], kc == 0, kc == 15, [wa.b, hT.b])
                            pb = psM.next()
                            for kc in range(16):
                                self.mm(pb, pb.t[:, :], hT.t[:, kc, pos * 128:(pos + 1) * 128], wb.t[:, kc, :], kc == 0, kc == 15, [wb.b, hT.b])
                            ga, gb, oa, ob = gas.next(), gbs.next(), oas.next(), obs.next()
                            P.op("act", lambda e, ga=ga, pa=pa: e.activation(out=ga.t[:], in_=pa.t[:, :], func=AF.Sigmoid), [pa.b], [ga.b])
                            P.op("act", lambda e, gb=gb, pb=pb: e.activation(out=gb.t[:], in_=pb.t[:, :], func=AF.Sigmoid), [pb.b], [gb.b])
                            self.load(oa, oa.t[:], scr["OA"].t[o][:, cs_], reads=[scr["OA"].b])
                            self.load(ob, ob.t[:], scr["OB"].t[o][:, cs_], reads=[scr["OB"].b])
                            P.op("dve", lambda e, ga=ga, oa=oa: e.tensor_tensor(ga.t[:], ga.t[:], oa.t[:], ALU.mult), [ga.b, oa.b], [ga.b])
                            P.op("dve", lambda e, gb=gb, ob=ob: e.tensor_tensor(gb.t[:], gb.t[:], ob.t[:], ALU.mult), [gb.b, ob.b], [gb.b])
                            P.op("dve", lambda e, ga=ga, gb=gb, pos=pos, cs_=cs_: e.tensor_tensor(mix[pos].t[:, cs_], ga.t[:], gb.t[:], ALU.add),
                                 [ga.b, gb.b], [mix[pos].b])
                    for pos in range(4):
                        self.transpose_into(mix[pos], mix[pos].t, 16, hT, lambda kc, pos=pos: hT.t[:, kc, pos * 128:(pos + 1) * 128], psT, identb)
                    for cb in range(4):
                        wt = wts.next()
                        self.load(wt, wt.t[:], scr["W_wo"].t[cb], reads=[scr["W_wo"].b])
                        cs_ = slice(cb * 512, (cb + 1) * 512)
                        for pos in range(4):
                            ps = psM.next()
                            for kc in range(16):
                                self.mm(ps, ps.t[:, :], hT.t[:, kc, pos * 128:(pos + 1) * 128], wt.t[:, kc, :], kc == 0, kc == 15, [wt.b, hT.b])
                            P.op("dve", lambda e, ps=ps, pos=pos, cs_=cs_: e.tensor_tensor(xk[pos].t[:, cs_], xk[pos].t[:, cs_], ps.t[:, :], ALU.add),
                                 [xk[pos].b, ps.b], [xk[pos].b])
                    P.barrier()
                    self.release_scope(locals())
                with ExitStack() as st:
                    sb = lambda name, shape, dt=F32: self.sb(st, name, shape, dt)
                    g_ffn = sb("n_gffn", [128, D])
                    g_fin = sb("n_gfin", [128, D])
                    identf = sb("n_identf", [128, 128])
                    identb = sb("n_identb", [128, 128], BF16)
                    hb = sb("n_hb", [128, D], BF16)
                    h2T = sb("n_h2T", [128, 16, 512], BF16)
                    uT = sb("n_uT", [128, 64, 512], BF16)
                    wts = Rot([sb("n_wt%d" % i, [128, 16, 512], BF16) for i in range(2)])
                    rrs = Rot([sb("n_rr%d" % i, [128, 512]) for i in range(2)])
                    ys = Rot([sb("n_y%d" % i, [128, D]) for i in range(2)])
                    tmp = dict(junk=sb("n_junk", [128, D], BF16), ss=sb("n_ss", [128, 1]), sd=sb("n_sd", [128, 1]),
                               rstd=sb("n_rstd", [128, 1]))
                    psT = Rot([PS[0], PS[1]])
                    psM = Rot([PS[6], PS[7]])
                    self.load(g_ffn, g_ffn.t[:], inp["g_ffn"][:, :])
                    self.load(g_fin, g_fin.t[:], inp["g_fin"][:, :])
                    self.load(identf, identf.t[:], inp["ident"][:, :])
                    P.op("dve", lambda e: e.tensor_copy(identb.t[:], identf.t[:]), [identf.b], [identb.b])
                    for pos in range(4):
                        self.norm_rows(xk[pos].t[:], D, g_ffn.t[:], hb.t[:], xk[pos].b, g_ffn.b, hb.b, tmp)
                        self.transpose_into(hb, hb.t, 16, h2T, lambda kc, pos=pos: h2T.t[:, kc, pos * 128:(pos + 1) * 128], psT, identb)
                    for cb in range(16):
                        wt = wts.next()
                        self.load(wt, wt.t[:], scr["W_up"].t[cb], reads=[scr["W_up"].b])
                        for jj in range(4):
                            ffc = 4 * cb + jj
                            ps = psM.next()
                            for kc in range(16):
                                self.mm(ps, ps.t[:, :], wt.t[:, kc, jj * 128:(jj + 1) * 128], h2T.t[:, kc, :], kc == 0, kc == 15, [wt.b, h2T.b])
                            rr = rrs.next()
                            P.op("act", lambda e, rr=rr, ps=ps: e.activation(out=rr.t[:], in_=ps.t[:, :], func=AF.Relu), [ps.b], [rr.b])
                            P.op("dve", lambda e, rr=rr, ffc=ffc: e.tensor_tensor(uT.t[:, ffc, :], rr.t[:], rr.t[:], ALU.mult), [rr.b], [uT.b])
                    for cb in range(4):
                        cs_ = slice(cb * 512, (cb + 1) * 512)
                        pss = [PS[2], PS[3], PS[4], PS[5]]
                        for kg in range(4):
                            wt = wts.next()
                            self.load(wt, wt.t[:], scr["W_dn"].t[cb * 4 + kg], reads=[scr["W_dn"].b])
                            for pos in range(4):
                                for kc in range(16):
                                    self.mm(pss[pos], pss[pos].t[:, :], uT.t[:, kg * 16 + kc, pos * 128:(pos + 1) * 128], wt.t[:, kc, :],
                                            kg == 0 and kc == 0, kg == 3 and kc == 15, [wt.b, uT.b])
                        for pos in range(4):
                            P.op("dve", lambda e, pos=pos, cs_=cs_, pss=pss: e.tensor_tensor(xk[pos].t[:, cs_], xk[pos].t[:, cs_], pss[pos].t[:, :],
                                                                                         ALU.add), [xk[pos].b, pss[pos].b], [xk[pos].b])
                    for pos in range(4):
                        o = 4 * b + pos
                        y = ys.next()
                        self.norm_rows(xk[pos].t[:], D, g_fin.t[:], y.t[:], xk[pos].b, g_fin.b, y.b, tmp)
                        self.store(y, out["y_own"][o * 128:(o + 1) * 128, :], y.t[:])
                    P.barrier()
                    self.release_scope(locals())
                self.release_scope(dict(xk=xk))

    def build(self):
        nc = self.nc
        self.declare()
        st = self.gstack
        self.PS = [TB(st.enter_context(nc.psum_tensor("ps%d" % i, [128, 512], F32)), "ps%d" % i) for i in range(8)]
        for p_ in self.PS:
            p_.b.excl = True
        self.phase_W()
        if self.upto == "W":
            return self.finish()
        kblocks = list(range(32)) + [34]
        if self.upto == "K1":
            kblocks = [0, 1, 34][:int(os.environ.get("MK_NB", "3"))]
        self.phase_K(kblocks)
        if self.upto in ("K", "K1"):
            return self.finish()
        self.phase_KC()
        if self.upto == "KC":
            return self.finish()
        self.phase_Q()
        if self.upto == "Q":
            return self.finish()
        self.slots_all()
        if self.upto == "S":
            return self.finish()
        self.phase_M()
        return self.finish()

    def finish(self):
        self.P.finish()
        self.gstack.close()
        return self.nc


def t5_bucket_np(rel):
    nb = 16
    ret = (rel > 0).astype(np.int32) * nb
    n = np.abs(rel)
    max_exact = 8
    nf = np.maximum(n, 1).astype(np.float32)
    large = max_exact + (np.log(nf / np.float32(max_exact)) / np.float32(math.log(128 / max_exact))
                         * np.float32(nb - max_exact)).astype(np.int32)
    large = np.minimum(large, nb - 1)
    return ret + np.where(n < max_exact, n, large)


def host_inputs(inputs):
    f32 = np.float32
    xp = np.asarray(inputs["x_prompt"], f32)[0]
    xs = np.asarray(inputs["x_sample"], f32)
    ck = np.asarray(inputs["cache_a_k"], f32)[0]
    cv = np.asarray(inputs["cache_a_v"], f32)[0]
    cix = np.asarray(inputs["cache_a_idx_k"], f32)[0]
    cckv = np.asarray(inputs["cache_b_ckv"], f32)[0]
    ckr = np.asarray(inputs["cache_b_krope"], f32)[0]
    half = 32
    inv_freq = np.power(np.float32(10000.0), -np.arange(half, dtype=f32) / np.float32(half)).astype(f32)
    shared = {
        "ident": np.eye(128, dtype=f32),
        "constb": np.ascontiguousarray(np.broadcast_to(np.asarray(inputs["rel_bias_table"], f32)[15], (128, 16))),
        "relb": np.ascontiguousarray(np.asarray(inputs["rel_bias_table"], f32)),
        "w_in": np.ascontiguousarray(np.asarray(inputs["w_in"], f32)[0]),
        "w_uq": np.ascontiguousarray(np.asarray(inputs["w_uq"], f32)[0]),
        "w_uk": np.ascontiguousarray(np.asarray(inputs["w_uk"], f32)[0].reshape(256, 2048)),
        "w_uv": np.ascontiguousarray(np.asarray(inputs["w_uv"], f32)[0].reshape(256, 2048)),
        "w_out": np.ascontiguousarray(np.asarray(inputs["w_out"], f32)[0]),
        "w_ff_up": np.ascontiguousarray(np.asarray(inputs["w_ff_up"], f32)[0]),
        "w_ff_down": np.ascontiguousarray(np.asarray(inputs["w_ff_down"], f32)[0]),
        "g_mix": np.ascontiguousarray(np.broadcast_to(np.asarray(inputs["norm_mix_g"], f32)[0], (128, D))),
        "g_ffn": np.ascontiguousarray(np.broadcast_to(np.asarray(inputs["norm_ffn_g"], f32)[0], (128, D))),
        "g_fin": np.ascontiguousarray(np.broadcast_to(np.asarray(inputs["final_norm_g"], f32), (128, D))),
        "g_q": np.ascontiguousarray(np.broadcast_to(np.asarray(inputs["q_lora_g"], f32)[0], (128, 512))),
        "g_kv": np.ascontiguousarray(np.broadcast_to(np.asarray(inputs["kv_lora_g"], f32)[0], (128, 256))),
    }
    s = np.arange(128)[None, :]
    t = np.arange(128)[:, None]
    onehot = np.zeros((2, 32, 128, 128), f32)
    for ty, off in enumerate((-128, 0)):
        rel = (s - t + off).astype(np.int32)
        bk = t5_bucket_np(rel)
        for b in range(32):
            onehot[ty, b] = (bk == b)
    shared["onehot"] = onehot.reshape(2, 32, 128 * 128)
    dsa_diag = np.zeros((2, 128, 128), f32)
    dsa_diag[0] = np.where((s // 64) <= (t // 64), 0.0, NEGBIG)
    dsa_diag[1] = np.where(s < 16, 0.0, NEGBIG) * np.ones((128, 1), f32)
    shared["dsa_diag"] = dsa_diag
    mla_mask = np.ones((2, 128, 2, 128), f32)
    ss_ = np.arange(128)[:, None]
    tt_ = np.arange(128)[None, :]
    mla_mask[0, :, 1, :] = ((ss_ // 64) <= (tt_ // 64)).astype(f32)
    mla_mask[1, :, 1, :] = (ss_ < 16).astype(f32) * np.ones((1, 128), f32)
    shared["mla_mask"] = mla_mask.reshape(2, 128, 256)
    maps = []
    for c in range(NCORE):
        m = dict(shared)
        pre = 7 - c
        x_all = np.zeros((NXT * 128, D), f32)
        x_all[pre * 128:pre * 128 + 16384] = xp
        pos = np.zeros((NXT * 128,), f32)
        valid = np.zeros((NXT * 128, 1), f32)
        pos[pre * 128:pre * 128 + 16384] = np.arange(16384, dtype=f32)
        valid[pre * 128:pre * 128 + 16384] = 1.0
        for b in range(2):
            r0 = (136 + b) * 128
            x_all[r0:r0 + 16] = xs[2 * c + b]
            pos[r0:r0 + 16] = 1024 + np.arange(16, dtype=f32)
            valid[r0:r0 + 16] = 1.0
        ang = pos[:, None] * inv_freq[None, :]
        m["x_all"] = x_all
        m["cs"] = np.concatenate([np.cos(ang), np.sin(ang)], axis=1).astype(f32)
        m["valid"] = np.ascontiguousarray(valid.reshape(NXT, 128).T)
        pm = np.zeros((896,), f32)
        pm[:pre * 128] = NEGBIG
        m["prefmask"] = np.ascontiguousarray(np.broadcast_to(pm, (128, 896)))
        sl = slice(2 * c, 2 * c + 2)
        m["c_akT"] = np.ascontiguousarray(ck[sl].transpose(0, 2, 3, 1))
        cve = np.ones((2, 1024, 16, 130), f32)
        cve[..., 0:128] = cv[sl]
        m["c_av"] = cve
        m["c_ixkT"] = np.ascontiguousarray(cix[sl].transpose(0, 2, 1))
        m["c_ckvT"] = np.ascontiguousarray(cckv[sl].transpose(0, 2, 1))
        m["c_krT"] = np.ascontiguousarray(ckr[sl].transpose(0, 2, 1))
        maps.append(m)
    return maps


def assemble(results):
    f32 = np.float32
    y_p = np.zeros((1, 16384, D), f32)
    y_s = np.zeros((16, 16, D), f32)
    a_k_p = np.zeros((1, 1, 16384, 16, 128), f32)
    a_v_p = np.zeros((1, 1, 16384, 16, 128), f32)
    a_ix_p = np.zeros((1, 1, 16384, 64), f32)
    b_ckv_p = np.zeros((1, 1, 16384, 256), f32)
    b_kr_p = np.zeros((1, 1, 16384, 64), f32)
    a_k_s = np.zeros((1, 16, 16, 16, 128), f32)
    a_v_s = np.zeros((1, 16, 16, 16, 128), f32)
    a_ix_s = np.zeros((1, 16, 16, 64), f32)
    b_ckv_s = np.zeros((1, 16, 16, 256), f32)
    b_kr_s = np.zeros((1, 16, 16, 64), f32)
    for c in range(NCORE):
        r = results[c]
        for i in range(16):
            j = c + 8 * i
            rs = slice(i * 128, (i + 1) * 128)
            ps = slice(j * 128, (j + 1) * 128)
            y_p[0, ps] = r["y_own"][rs]
            a_k_p[0, 0, ps] = r["st_ak"][rs].reshape(128, 16, 128)
            a_v_p[0, 0, ps] = r["st_av"][rs].reshape(128, 16, 128)
            a_ix_p[0, 0, ps] = r["st_ix"][rs]
            b_ckv_p[0, 0, ps] = r["st_ckv"][rs]
            b_kr_p[0, 0, ps] = r["st_kr"][rs]
        for b in range(2):
            rs = slice((16 + b) * 128, (16 + b) * 128 + 16)
            sq = 2 * c + b
            y_s[sq] = r["y_own"][rs]
            a_k_s[0, sq] = r["st_ak"][rs].reshape(16, 16, 128)
            a_v_s[0, sq] = r["st_av"][rs].reshape(16, 16, 128)
            a_ix_s[0, sq] = r["st_ix"][rs]
            b_ckv_s[0, sq] = r["st_ckv"][rs]
            b_kr_s[0, sq] = r["st_kr"][rs]
    return (y_p, y_s, a_k_p, a_v_p, a_ix_p, b_ckv_p, b_kr_p, a_k_s, a_v_s, a_ix_s, b_ckv_s, b_kr_s)


def kernel(**inputs):
    upto = os.environ.get("MK_UPTO", "ALL")
    bld = Builder(upto)
    nc = bld.build()
    maps = host_inputs(inputs)
    res = run_bass_kernel_spmd(nc, maps, core_ids=list(range(NCORE)))
    return assemble(res.results)
```

```python
import math
import os
from contextlib import ExitStack

import numpy as np
import concourse.bass as bass
import concourse.mybir as mybir
from concourse.bass_utils import run_bass_kernel_spmd

F32 = mybir.dt.float32
BF16 = mybir.dt.bfloat16
AF = mybir.ActivationFunctionType
ALU = mybir.AluOpType
AX = mybir.AxisListType

D = 2048
NH = 16
HD = 128
NCORE = 8
NPT = 136
NXT = 140
NTILE = 158
NTOK = NTILE * 128
NOWN = 20
EPS = 1e-6
A_SCALE = HD ** -0.5
MLA_SCALE = 192 ** -0.5
NEGBIG = -1.0e30
MASKNEG = -30000.0
TOPK = 256
BIS_R = 128.0
BIS_IT = 11
C_AQ, C_AK, C_AV, C_IXQ, C_IXK, C_IXW, C_CQ, C_CKV, C_KR, C_GA, C_GB = (
    0, 2048, 4096, 6144, 7168, 7232, 7248, 7760, 8016, 8080, 10128)


class Buf:
    __slots__ = ("name", "last_w", "readers", "dsem", "dcount", "excl")

    def __init__(self, name):
        self.excl = False
        self.name = name
        self.last_w = None
        self.readers = []
        self.dsem = None
        self.dcount = 0


class Ins:
    __slots__ = ("eng", "fn", "deps", "is_dma", "dbuf", "need_inc", "seq", "cover")

    def __init__(self, eng, fn, deps, is_dma=False, dbuf=None):
        self.eng = eng
        self.fn = fn
        self.deps = deps
        self.is_dma = is_dma
        self.dbuf = dbuf
        self.need_inc = False
        self.seq = 0
        self.cover = 0


COMPUTE = ("pe", "act", "dve", "pool")


class Prog:
    def __init__(self, nc, stack):
        self.nc = nc
        self.stack = stack
        self.engs = {"pe": nc.tensor, "act": nc.scalar, "dve": nc.vector, "pool": nc.gpsimd, "sp": nc.sync}
        self.batch = []
        self.esem = {e: stack.enter_context(nc.semaphore("e_" + e)) for e in COMPUTE}
        self.ecount = {e: 0 for e in COMPUTE}
        self.waited = {e: {} for e in self.engs}
        self.sempool = []
        self.livesems = {}
        self.nsem = 4
        self.n_ins = 0
        self.n_waits = 0
        self.trace = {e: [] for e in self.engs}

    def _getsem(self, buf):
        if self.sempool:
            sem, cnt = self.sempool.pop()
        else:
            self.nsem += 1
            sem = self.stack.enter_context(self.nc.semaphore("d%d" % self.nsem))
            cnt = 0
        buf.dsem = sem
        buf.dcount = cnt
        self.livesems[id(sem)] = [sem, cnt, cnt]

    def release(self, bufs):
        for b in bufs:
            if b.dsem is not None:
                ent = self.livesems.pop(id(b.dsem))
                self.sempool.append((ent[0], ent[1]))
                b.dsem = None

    def _deps(self, eng, reads, writes, is_dma):
        deps = {}

        def add(p, kind):
            if p is None:
                return
            if (not p.is_dma) and (not is_dma) and p.eng == eng:
                if eng == "pe" or kind != "raw":
                    return
            deps[id(p)] = p

        for b in reads:
            add(b.last_w, "raw")
            if b.excl:
                for r in b.readers:
                    if r.eng != eng:
                        add(r, "war")
        for b in writes:
            add(b.last_w, "waw")
            for r in b.readers:
                add(r, "war")
        return list(deps.values())

    def _post(self, ins, reads, writes):
        for b in reads:
            if not ins.is_dma:
                b.readers = [r for r in b.readers if r.is_dma or r.eng != ins.eng]
            b.readers.append(ins)
        for b in writes:
            b.last_w = ins
            b.readers = []
        self.batch.append(ins)

    def op(self, eng, fn, reads=(), writes=()):
        ins = Ins(eng, fn, self._deps(eng, reads, writes, False))
        self._post(ins, reads, writes)

    def dma(self, fn, reads, writes, dbuf, q="sp"):
        ins = Ins(q, fn, self._deps(q, reads, writes, True), True, dbuf)
        if dbuf.dsem is None:
            self._getsem(dbuf)
        dbuf.dcount += 1
        self.livesems[id(dbuf.dsem)][1] = dbuf.dcount
        self._post(ins, reads, writes)

    def flush(self):
        batch = self.batch
        self.batch = []
        for i in batch:
            for p in i.deps:
                if not p.is_dma:
                    p.need_inc = True
        last = {}
        for i in batch:
            if not i.is_dma:
                last[i.eng] = i
        for i in last.values():
            i.need_inc = True
        for i in batch:
            if not i.is_dma and i.need_inc:
                self.ecount[i.eng] += 1
                i.seq = self.ecount[i.eng]
        nxt = {}
        for i in reversed(batch):
            if not i.is_dma:
                if i.need_inc:
                    nxt[i.eng] = i.seq
                i.cover = nxt[i.eng]
        for i in batch:
            h = self.engs[i.eng]
            need = {}
            for p in i.deps:
                if p.is_dma:
                    ent = self.livesems.get(id(p.dbuf.dsem)) if p.dbuf.dsem is not None else None
                    if ent is None:
                        continue
                    key = id(ent[0])
                    sem = ent[0]
                    val = 16 * ent[2]
                else:
                    key = p.eng
                    sem = self.esem[p.eng]
                    val = p.cover
                if key not in need or need[key][1] < val:
                    need[key] = (sem, val)
            w = self.waited[i.eng]
            for key, (sem, val) in need.items():
                if w.get(key, 0) >= val:
                    continue
                w[key] = val
                h.wait_ge(sem, val)
                self.trace[i.eng].append(("w", id(sem), val))
                self.n_waits += 1
            bi = i.fn(h)
            self.n_ins += 1
            if i.is_dma:
                bi.then_inc(i.dbuf.dsem, 16)
                self.trace[i.eng].append(("i", id(i.dbuf.dsem), 16))
                self.livesems[id(i.dbuf.dsem)][2] += 1
            elif i.need_inc:
                bi.then_inc(self.esem[i.eng], 1)
                self.trace[i.eng].append(("i", id(self.esem[i.eng]), 1))
            else:
                self.trace[i.eng].append(("i", None, 0))

    def barrier(self):
        self.flush()
        for e, h in self.engs.items():
            w = self.waited[e]
            for pe in COMPUTE:
                if pe == e:
                    continue
                val = self.ecount[pe]
                if val > 0 and w.get(pe, 0) < val:
                    w[pe] = val
                    h.wait_ge(self.esem[pe], val)
                    self.trace[e].append(("w", id(self.esem[pe]), val))
            for key, ent in self.livesems.items():
                val = 16 * ent[2]
                if val > 0 and w.get(key, 0) < val:
                    w[key] = val
                    h.wait_ge(ent[0], val)
                    self.trace[e].append(("w", id(ent[0]), val))

    def finish(self):
        self.barrier()


class TB:
    def __init__(self, t, name):
        self.t = t
        self.b = Buf(name)


class Rot:
    def __init__(self, items):
        self.items = items
        self.i = 0

    def next(self):
        x = self.items[self.i % len(self.items)]
        self.i += 1
        return x


def weight_groups():
    g = {}
    g["ak"] = dict(K=2048, blocks=[[("w_in", 0, C_AK + 512 * i, 512)] for i in range(4)])
    g["av"] = dict(K=2048, blocks=[[("w_in", 0, C_AV + 512 * i, 512)] for i in range(4)])
    g["misc"] = dict(K=2048, blocks=[[("w_in", 0, C_CKV, 256), ("w_in", 0, C_IXK, 64), ("w_in", 0, C_KR, 64)]])
    g["aq"] = dict(K=2048, blocks=[[("w_in", 0, C_AQ + 512 * i, 512)] for i in range(4)])
    g["ixq"] = dict(K=2048, blocks=[[("w_in", 0, C_IXQ + 512 * i, 512)] for i in range(2)])
    g["cq"] = dict(K=2048, blocks=[[("w_in", 0, C_CQ, 512)], [("w_in", 0, C_IXW, 16)]])
    g["gate"] = dict(K=2048, blocks=[[("w_in", 0, C_GA + 512 * i, 512)] for i in range(8)])
    g["wo"] = dict(K=2048, blocks=[[("w_out", 0, 512 * i, 512)] for i in range(4)])
    g["up"] = dict(K=2048, blocks=[[("w_ff_up", 0, 512 * i, 512)] for i in range(16)])
    g["dn"] = dict(K=2048, blocks=[[("w_ff_down", 2048 * kg, 512 * cb, 512)] for cb in range(4) for kg in range(4)])
    g["uqn"] = dict(K=512, blocks=[[("uqn", 0, i, 0)] for i in range(4)])
    g["uqr"] = dict(K=512, blocks=[[("uqr", 0, i, 0)] for i in range(2)])
    g["uk"] = dict(K=256, blocks=[[("w_uk", 0, 512 * i, 512)] for i in range(4)])
    g["uv"] = dict(K=256, blocks=[[("w_uv", 0, 512 * i, 512)] for i in range(4)])
    first = ["uk", "uv", "misc", "ak", "av"]
    return {k: g[k] for k in first + [k for k in g if k not in first]}


class _Stop(Exception):
    pass


class Builder:
    def ck(self, n):
        if int(os.environ.get("MK_KSTOP", "0")) == n:
            raise _Stop()

    def __init__(self, upto="ALL"):
        self.upto = upto
        self.nc = bass.Bass("TRN2", target_bir_lowering=False)
        self.gstack = ExitStack()
        self.P = Prog(self.nc, self.gstack)
        self.inp = {}
        self.out = {}
        self.scr = {}

    def din(self, name, shape, dt=F32):
        self.inp[name] = self.nc.dram_tensor(name, list(shape), dt, kind="ExternalInput").ap()
        return self.inp[name]

    def dout(self, name, shape, dt=F32):
        self.out[name] = self.nc.dram_tensor(name, list(shape), dt, kind="ExternalOutput").ap()
        return self.out[name]

    def dscr(self, name, shape, dt=BF16):
        self.scr[name] = TB(self.nc.dram_tensor(name, list(shape), dt, kind="Internal").ap(), name)
        return self.scr[name]

    def sb(self, st, name, shape, dt=F32):
        self._uid = getattr(self, "_uid", 0) + 1
        name = "%s_u%d" % (name, self._uid)
        return TB(st.enter_context(self.nc.sbuf_tensor(name, list(shape), dt)), name)

    def mm(self, ps, out_ap, lhsT, rhs, start, stop, reads):
        self.P.op("pe", lambda e: e.matmul(out_ap, lhsT, rhs, start=start, stop=stop), reads, [ps.b])

    def load(self, dst, dst_ap, src_ap, reads=(), q="sp"):
        self.P.dma(lambda e: e.dma_start(out=dst_ap, in_=src_ap), list(reads), [dst.b], dst.b, q=q)

    def store(self, src, dst_ap, src_ap, dstbuf=None, q=None):
        q = q or getattr(self, "store_q", "pool")
        w = [dstbuf.b] if dstbuf is not None else []
        self.P.dma(lambda e: e.dma_start(out=dst_ap, in_=src_ap), [src.b], w, src.b, q=q)

    def declare(self):
        din, dout, dscr = self.din, self.dout, self.dscr
        din("x_all", [NXT * 128, D])
        din("cs", [NXT * 128, 64])
        din("valid", [128, NXT])
        din("prefmask", [128, 896])
        din("ident", [128, 128])
        din("onehot", [2, 32, 128 * 128])
        din("relb", [32, 16])
        din("constb", [128, 16])
        din("dsa_diag", [2, 128, 128])
        din("mla_mask", [2, 128, 256])
        din("c_akT", [2, 16, 128, 1024])
        din("c_av", [2, 1024, 16, 130])
        din("c_ixkT", [2, 64, 1024])
        din("c_ckvT", [2, 256, 1024])
        din("c_krT", [2, 64, 1024])
        din("w_in", [D, 12176])
        din("w_uq", [512, 16, 192])
        din("w_uk", [256, 2048])
        din("w_uv", [256, 2048])
        din("w_out", [D, D])
        din("w_ff_up", [D, 8192])
        din("w_ff_down", [8192, D])
        din("g_mix", [128, D])
        din("g_ffn", [128, D])
        din("g_fin", [128, D])
        din("g_q", [128, 512])
        din("g_kv", [128, 256])
        dout("y_own", [NOWN * 128, D])
        dout("st_ak", [NOWN * 128, D])
        dout("st_av", [NOWN * 128, D])
        dout("st_ix", [NOWN * 128, 64])
        dout("st_ckv", [NOWN * 128, 256])
        dout("st_kr", [NOWN * 128, 64])
        dscr("KT_A", [NH, 128, NTOK])
        dscr("V_A", [NH, 128, NTILE, 130])
        dscr("IXK_T", [64, NTOK])
        dscr("KT_B", [NH, 128, NTOK])
        dscr("KR_T", [64, NTOK])
        dscr("V_B", [NH, 128, NTILE, 130])
        dscr("QA_T", [5, 128, 16, 512])
        dscr("IXQ_T", [5, 128, 8, 512])
        dscr("IXW", [5, 128, 4, 16], F32)
        dscr("QN_T", [5, 128, 16, 512])
        dscr("QR_T", [5, 128, 8, 512])
        dscr("OA", [NOWN, 128, D], F32)
        dscr("OB", [NOWN, 128, D], F32)
        self.wg = weight_groups()
        for name, g in self.wg.items():
            dscr("W_" + name, [len(g["blocks"]), 128, g["K"] // 128, 512])

    def phase_W(self):
        P = self.P
        inp = self.inp
        urgent = ["uk", "uv", "misc", "ak", "av"]
        self._precast(urgent)
        KT_A, V_A, IXK_T, KR_T = self.scr["KT_A"], self.scr["V_A"], self.scr["IXK_T"], self.scr["KR_T"]
        for b in range(2):
            t0 = 140 + 9 * b
            for h in range(NH):
                P.dma(lambda e, b=b, h=h, t0=t0: e.dma_start(out=KT_A.t[h, :, t0 * 128:t0 * 128 + 1024], in_=inp["c_akT"][b, h]),
                      [], [KT_A.b], KT_A.b, q="pool")
                P.dma(lambda e, b=b, h=h, t0=t0: e.dma_start(
                    out=V_A.t[h, :, t0:t0 + 8, :],
                    in_=inp["c_av"][b, :, h, :].rearrange("(kt p) d -> p kt d", p=128)), [], [V_A.b], V_A.b, q="pool")
            P.dma(lambda e, b=b, t0=t0: e.dma_start(out=IXK_T.t[:, t0 * 128:t0 * 128 + 1024], in_=inp["c_ixkT"][b]),
                  [], [IXK_T.b], IXK_T.b, q="pool")
            P.dma(lambda e, b=b, t0=t0: e.dma_start(out=KR_T.t[:, t0 * 128:t0 * 128 + 1024], in_=inp["c_krT"][b]),
                  [], [KR_T.b], KR_T.b, q="pool")
        self._precast([k for k in self.wg if k not in urgent])

    def _precast(self, names):
        P = self.P
        inp = self.inp
        for name in names:
            g = self.wg[name]
            W = self.scr["W_" + name]
            KC = g["K"] // 128
            for bi, pieces in enumerate(g["blocks"]):
                off = 0
                for (src, r0, c0, ncols) in pieces:
                    if src in ("uqn", "uqr"):
                        i = c0
                        nh_, w_, lo_ = (4, 128, 0) if src == "uqn" else (8, 64, 128)
                        for j in range(nh_):
                            src_ap = inp["w_uq"][:, nh_ * i + j, lo_:lo_ + w_].rearrange("(kc p) d -> p kc d", p=128)
                            dst_ap = W.t[bi][:, :, j * w_:(j + 1) * w_]
                            P.dma(lambda e, d=dst_ap, s=src_ap: e.dma_start(out=d, in_=s), [], [W.b], W.b, q="pool")
                        continue
                    src_ap = inp[src][r0:r0 + g["K"], c0:c0 + ncols].rearrange("(kc p) c -> p kc c", p=128)
                    dst_ap = W.t[bi][:, :, off:off + ncols]
                    off += ncols
                    P.dma(lambda e, d=dst_ap, s=src_ap: e.dma_start(out=d, in_=s), [], [W.b], W.b, q="pool")

    def norm_rows(self, x_ap, n, g_ap, out_ap, xb, gb, outb, tmp):
        P = self.P
        junk, ss, sd, rstd = tmp["junk"], tmp["ss"], tmp["sd"], tmp["rstd"]
        P.op("act", lambda e: e.activation(out=junk.t[:, 0:n], in_=x_ap, func=AF.Square, accum_out=ss.t[:]),
             [xb], [junk.b, ss.b])
        P.op("act", lambda e: e.activation(out=sd.t[:], in_=ss.t[:], func=AF.Sqrt, bias=EPS, scale=1.0 / n),
             [ss.b], [sd.b])
        P.op("dve", lambda e: e.reciprocal(rstd.t[:], sd.t[:]), [sd.b], [rstd.b])
        P.op("dve", lambda e: e.scalar_tensor_tensor(out=out_ap, in0=x_ap, scalar=rstd.t[:], in1=g_ap,
                                                     op0=ALU.mult, op1=ALU.mult), [xb, rstd.b, gb], [outb])

    def load_norm_T(self, st, tiles, g_tb, hT, xts, hb, tmp, psT, identb, keep_x=None):
        P = self.P
        xall = self.inp["x_all"]
        for j, tile in enumerate(tiles):
            xt = keep_x[j] if keep_x is not None else xts.next()
            self.load(xt, xt.t[:], xall[tile * 128:(tile + 1) * 128, :])
            self.norm_rows(xt.t[:], D, g_tb.t[:], hb.t[:], xt.b, g_tb.b, hb.b, tmp)
            self.transpose_into(hb, hb.t, 16, hT, lambda kc, j=j: hT.t[:, kc, j * 128:(j + 1) * 128], psT, identb)

    def transpose_into(self, src, src_t, nchunks, dst, dst_ap_fn, psT, identb, rows=128):
        P = self.P
        c = 0
        while c < nchunks:
            n = min(4, nchunks - c)
            ps = psT.next()
            for k in range(n):
                self.mm(ps, ps.t[:, k * 128:(k + 1) * 128], src_t[:, (c + k) * 128:(c + k + 1) * 128], identb.t[:],
                        True, True, [src.b, identb.b])
            self._tflip = not getattr(self, "_tflip", False)
            for k in range(n):
                eng = "act" if self._tflip else "dve"
                d_ap = dst_ap_fn(c + k)
                s_ap = ps.t[:, k * 128:(k + 1) * 128]
                if eng == "act":
                    P.op("act", lambda e, d=d_ap, s=s_ap: e.activation(out=d, in_=s, func=AF.Copy), [ps.b], [dst.b])
                else:
                    P.op("dve", lambda e, d=d_ap, s=s_ap: e.tensor_copy(d, s), [ps.b], [dst.b])
            c += n

    def rope(self, out_ap1, out_ap2, x1, x2, cos, sin, xb, csb, outb, tmps):
        P = self.P
        t1, t2 = tmps
        P.op("dve", lambda e: e.tensor_tensor(t1.t_ap, x1, cos, ALU.mult), [xb, csb], [t1.b])
        P.op("dve", lambda e: e.tensor_tensor(t2.t_ap, x2, sin, ALU.mult), [xb, csb], [t2.b])
        P.op("dve", lambda e: e.tensor_tensor(out_ap1, t1.t_ap, t2.t_ap, ALU.subtract), [t1.b, t2.b], [outb])
        P.op("dve", lambda e: e.tensor_tensor(t1.t_ap, x1, sin, ALU.mult), [xb, csb, outb], [t1.b])
        P.op("dve", lambda e: e.tensor_tensor(t2.t_ap, x2, cos, ALU.mult), [xb, csb, outb], [t2.b])
        P.op("dve", lambda e: e.tensor_tensor(out_ap2, t1.t_ap, t2.t_ap, ALU.add), [t1.b, t2.b], [outb])

    def phase_K(self, blocks):
        P = self.P
        self.store_q = "act"
        nc = self.nc
        inp, out, scr = self.inp, self.out, self.scr
        with ExitStack() as st:
            sb = lambda name, shape, dt=F32: self.sb(st, name, shape, dt)
            g_mix = sb("k_gmix", [128, D])
            g_kv = sb("k_gkv", [128, 256])
            identf = sb("k_identf", [128, 128])
            identb = sb("k_identb", [128, 128], BF16)
            ones16 = sb("k_ones16", [128, 16, 1])
            wuk = [sb("k_wuk%d" % i, [128, 2, 512], BF16) for i in range(4)]
            wuv = [sb("k_wuv%d" % i, [128, 2, 512], BF16) for i in range(4)]
            xts = Rot([sb("k_x%d" % i, [128, D]) for i in range(2)])
            hb = sb("k_hb", [128, D], BF16)
            hT = sb("k_hT", [128, 16, 512], BF16)
            wts = Rot([sb("k_wt%d" % i, [128, 16, 512], BF16) for i in range(3)])
            ksts = Rot([sb("k_kst%d" % i, [128, 512], BF16) for i in range(2)])
            vst = [sb("k_vst%d" % i, [128, 16, 130], BF16) for i in range(4)]
            vbst = [sb("k_vbst%d" % i, [128, 16, 130], BF16) for i in range(4)]
            mst = [sb("k_mst%d" % i, [128, 384]) for i in range(4)]
            mo = [sb("k_mo%d" % i, [128, 384]) for i in range(4)]
            km = sb("k_km", [128, 384], BF16)
            ckvT = sb("k_ckvT", [128, 2, 512], BF16)
            ixkrT = sb("k_ixkrT", [128, 512], BF16)
            osts = Rot([sb("k_ost%d" % i, [128, 512]) for i in range(3)])
            cst = sb("k_cs", [128, 4, 64])
            vld = sb("k_vld", [128, 4])
            tmp = dict(junk=sb("k_junk", [128, D], BF16), ss=sb("k_ss", [128, 1]), sd=sb("k_sd", [128, 1]),
                       rstd=sb("k_rstd", [128, 1]))
            rt1 = sb("k_rt1", [128, 32])
            rt2 = sb("k_rt2", [128, 32])
            rt1.t_ap = rt1.t[:]
            rt2.t_ap = rt2.t[:]
            PS = self.PS
            psT = Rot([PS[0], PS[1]])
            psM = Rot([PS[2], PS[3], PS[4], PS[5]])
            psX = Rot([PS[6], PS[7]])

            self.load(g_mix, g_mix.t[:], inp["g_mix"][:, :])
            self.load(g_kv, g_kv.t[:], inp["g_kv"][:, :])
            self.load(identf, identf.t[:], inp["ident"][:, :])
            P.op("dve", lambda e: e.tensor_copy(identb.t[:], identf.t[:]), [identf.b], [identb.b])
            P.op("dve", lambda e: e.memset(ones16.t[:], 1.0), [], [ones16.b])
            for i in range(4):
                self.load(wuk[i], wuk[i].t[:], scr["W_uk"].t[i], reads=[scr["W_uk"].b])
                self.load(wuv[i], wuv[i].t[:], scr["W_uv"].t[i], reads=[scr["W_uv"].b])

            for kb in blocks:
              try:
                tiles = [4 * kb + j for j in range(4)]
                tok0 = 4 * kb * 128
                if kb == 34:
                    own = {0: 16, 1: 17, 2: 18, 3: 19}
                elif kb % 2 == 1:
                    own = {3: (kb - 1) // 2}
                else:
                    own = {}
                self.load(cst, cst.t[:], inp["cs"][tok0:tok0 + 512, :].rearrange("(j p) c -> p j c", p=128))
                self.load(vld, vld.t[:], inp["valid"][:, 4 * kb:4 * kb + 4])
                self.ck(1)
                self.load_norm_T(st, tiles, g_mix, hT, xts, hb, tmp, psT, identb)
                self.ck(2)
                for cb in range(4):
                    wt = wts.next()
                    self.load(wt, wt.t[:], scr["W_ak"].t[cb], reads=[scr["W_ak"].b])
                    for j in range(4):
                        head = 4 * cb + j
                        ps = psM.next()
                        for kc in range(16):
                            self.mm(ps, ps.t[:, :], wt.t[:, kc, j * 128:(j + 1) * 128], hT.t[:, kc, :], kc == 0, kc == 15,
                                    [wt.b, hT.b])
                        kst = ksts.next()
                        P.op("act", lambda e, kst=kst, ps=ps: e.activation(out=kst.t[:], in_=ps.t[:, :], func=AF.Copy),
                             [ps.b], [kst.b])
                        self.store_cols(kst, scr["KT_A"], lambda a, n, head=head: scr["KT_A"].t[head, :, a:a + n], kst.t, tiles)
                    for pos, o in own.items():
                        ps = psM.next()
                        for kc in range(16):
                            self.mm(ps, ps.t[:, :], hT.t[:, kc, pos * 128:(pos + 1) * 128], wt.t[:, kc, :], kc == 0, kc == 15,
                                    [wt.b, hT.b])
                        ost = osts.next()
                        P.op("act", lambda e, ost=ost, ps=ps: e.activation(out=ost.t[:], in_=ps.t[:, :], func=AF.Copy),
                             [ps.b], [ost.b])
                        self.store(ost, out["st_ak"][o * 128:(o + 1) * 128, cb * 512:(cb + 1) * 512], ost.t[:])
                self.ck(3)
                for cb in range(4):
                    wt = wts.next()
                    self.load(wt, wt.t[:], scr["W_av"].t[cb], reads=[scr["W_av"].b])
                    for pos in range(4):
                        ps = psM.next()
                        for kc in range(16):
                            self.mm(ps, ps.t[:, :], hT.t[:, kc, pos * 128:(pos + 1) * 128], wt.t[:, kc, :], kc == 0, kc == 15,
                                    [wt.b, hT.b])
                        v = vst[pos]
                        P.op("dve", lambda e, v=v, ps=ps, cb=cb: e.tensor_copy(
                            v.t[:, 4 * cb:4 * cb + 4, 0:128], ps.t[:, :].rearrange("p (h d) -> p h d", h=4)), [ps.b], [v.b])
                        if pos in own:
                            o = own[pos]
                            ost = osts.next()
                            P.op("dve", lambda e, ost=ost, ps=ps: e.tensor_copy(ost.t[:], ps.t[:, :]), [ps.b], [ost.b])
                            self.store(ost, out["st_av"][o * 128:(o + 1) * 128, cb * 512:(cb + 1) * 512], ost.t[:])
                for pos in range(4):
                    v = vst[pos]
                    P.op("dve", lambda e, v=v, pos=pos: e.tensor_scalar(v.t[:, :, 128:129], ones16.t[:], vld.t[:, pos:pos + 1], None,
                                                                        ALU.mult), [ones16.b, vld.b], [v.b])
                    self.store(v, scr["V_A"].t[:, :, self.tmap(tiles[pos]), 0:129].rearrange("h p c -> p h c"), v.t[:, :, 0:129],
                               scr["V_A"])
                self.ck(4)
                wt = wts.next()
                self.load(wt, wt.t[:, :, 0:384], scr["W_misc"].t[0][:, :, 0:384], reads=[scr["W_misc"].b])
                for pos in range(4):
                    ps = psX.next()
                    for kc in range(16):
                        self.mm(ps, ps.t[:, 0:384], hT.t[:, kc, pos * 128:(pos + 1) * 128], wt.t[:, kc, 0:384], kc == 0, kc == 15,
                                [wt.b, hT.b])
                    m = mst[pos]
                    P.op("act", lambda e, m=m, ps=ps: e.activation(out=m.t[:], in_=ps.t[:, 0:384], func=AF.Copy), [ps.b], [m.b])
                self.ck(41)
                for pos in range(4):
                    m = mst[pos]
                    o_ = mo[pos]
                    self.norm_rows(m.t[:, 0:256], 256, g_kv.t[:], o_.t[:, 0:256], m.b, g_kv.b, o_.b, tmp)
                    self.ck(42)
                    P.op("act", lambda e, m=m, o_=o_: e.activation(out=o_.t[:, 256:320], in_=m.t[:, 256:320], func=AF.Copy),
                         [m.b], [o_.b])
                    self.rope(o_.t[:, 320:352], o_.t[:, 352:384], m.t[:, 320:352], m.t[:, 352:384],
                              cst.t[:, pos, 0:32], cst.t[:, pos, 32:64], m.b, cst.b, o_.b, (rt1, rt2))
                    self.ck(43)
                    if pos in own:
                        o = own[pos]
                        self.store(o_, out["st_ckv"][o * 128:(o + 1) * 128, :], o_.t[:, 0:256])
                        self.store(o_, out["st_ix"][o * 128:(o + 1) * 128, :], o_.t[:, 256:320])
                        self.store(o_, out["st_kr"][o * 128:(o + 1) * 128, :], o_.t[:, 320:384])
                    P.op("act", lambda e, o_=o_: e.activation(out=km.t[:], in_=o_.t[:], func=AF.Copy), [o_.b], [km.b])
                    self.ck(431)
                    ps = psX.next()
                    for k in range(3):
                        self.mm(ps, ps.t[:, k * 128:(k + 1) * 128], km.t[:, k * 128:(k + 1) * 128], identb.t[:], True, True,
                                [km.b, identb.b])
                    self.ck(432)
                    P.op("dve", lambda e, ps=ps, pos=pos: e.tensor_copy(
                        ckvT.t[:, :, pos * 128:(pos + 1) * 128], ps.t[:, 0:256].rearrange("p (c t) -> p c t", c=2)),
                        [ps.b], [ckvT.b])
                    self.ck(4321)
                    P.op("dve", lambda e, ps=ps, pos=pos: e.tensor_copy(ixkrT.t[:, pos * 128:(pos + 1) * 128], ps.t[:, 256:384]),
                         [ps.b], [ixkrT.b])
                    self.ck(433)
                self.ck(44)
                self.store_cols(ixkrT, scr["IXK_T"], lambda a, n: scr["IXK_T"].t[:, a:a + n], ixkrT.t[0:64], tiles)
                self.store_cols(ixkrT, scr["KR_T"], lambda a, n: scr["KR_T"].t[:, a:a + n], ixkrT.t[64:128], tiles)
                self.ck(5)
                if int(os.environ.get("MK_KSTOP", "0")) == 6:
                    self.mla_kside(ckvT, wuk, wuv, psM, ksts, vbst, tiles, tok0, lambda pos: vld.t[:, pos:pos + 1], vld, ones16)
                self.ck(6)
                self.mla_kside(ckvT, wuk, wuv, psM, ksts, vbst, tiles, tok0, lambda pos: vld.t[:, pos:pos + 1], vld, ones16)
              except _Stop:
                break
            self.store_q = "pool"
            P.barrier()
            self.release_scope(locals())

    def mla_kside(self, ckvT, wuk, wuv, psM, ksts, vbst, tiles, tok0, vld_ap_fn, vld, ones16, ncols=512):
        P = self.P
        scr = self.scr
        for h in range(NH):
            ps = psM.next()
            for cc in range(2):
                self.mm(ps, ps.t[:, 0:ncols], wuk[h // 4].t[:, cc, (h % 4) * 128:(h % 4 + 1) * 128], ckvT.t[:, cc, 0:ncols],
                        cc == 0, cc == 1, [wuk[h // 4].b, ckvT.b])
            kst = ksts.next()
            P.op("act", lambda e, kst=kst, ps=ps: e.activation(out=kst.t[:, 0:ncols], in_=ps.t[:, 0:ncols], func=AF.Copy),
                 [ps.b], [kst.b])
            self.store_cols(kst, scr["KT_B"], lambda a, n, h=h: scr["KT_B"].t[h, :, a:a + n], kst.t, tiles)
        for pos in range(len(tiles)):
            v = vbst[pos % len(vbst)]
            for cb in range(4):
                ps = psM.next()
                for cc in range(2):
                    self.mm(ps, ps.t[:, :], ckvT.t[:, cc, pos * 128:(pos + 1) * 128], wuv[cb].t[:, cc, :], cc == 0, cc == 1,
                            [wuv[cb].b, ckvT.b])
                P.op("dve", lambda e, v=v, ps=ps, cb=cb: e.tensor_copy(
                    v.t[:, 4 * cb:4 * cb + 4, 0:128], ps.t[:, :].rearrange("p (h d) -> p h d", h=4)), [ps.b], [v.b])
            if vld is not None:
                P.op("dve", lambda e, v=v, pos=pos: e.tensor_scalar(v.t[:, :, 128:129], ones16.t[:], vld_ap_fn(pos), None,
                                                                    ALU.mult), [ones16.b, vld.b], [v.b])
                self.store(v, scr["V_B"].t[:, :, self.tmap(tiles[pos]), 0:129].rearrange("h p c -> p h c"), v.t[:, :, 0:129], scr["V_B"])
            else:
                self.store(v, scr["V_B"].t[:, :, self.tmap(tiles[pos]), 0:129].rearrange("h p c -> p h c"), v.t[:, :, 0:129], scr["V_B"])

    def release_scope(self, loc):
        bufs = []
        for v in loc.values():
            if isinstance(v, TB):
                bufs.append(v.b)
            elif isinstance(v, (list, tuple)):
                for x in v:
                    if isinstance(x, TB):
                        bufs.append(x.b)
            elif isinstance(v, Rot):
                for x in v.items:
                    if isinstance(x, TB):
                        bufs.append(x.b)
            elif isinstance(v, dict):
                for x in v.values():
                    if isinstance(x, TB):
                        bufs.append(x.b)
        self.P.release(bufs)

    def phase_KC(self):
        P = self.P
        inp, scr = self.inp, self.scr
        with ExitStack() as st:
            sb = lambda name, shape, dt=F32: self.sb(st, name, shape, dt)
            wuk = [sb("c_wuk%d" % i, [128, 2, 512], BF16) for i in range(4)]
            wuv = [sb("c_wuv%d" % i, [128, 2, 512], BF16) for i in range(4)]
            ckvT = sb("c_ckvT", [128, 2, 512], BF16)
            ksts = Rot([sb("c_kst%d" % i, [128, 512], BF16) for i in range(2)])
            vbst = [sb("c_vbst%d" % i, [128, 16, 130], BF16) for i in range(4)]
            PS = self.PS
            psM = Rot([PS[2], PS[3], PS[4], PS[5]])
            for i in range(4):
                self.load(wuk[i], wuk[i].t[:], scr["W_uk"].t[i], reads=[scr["W_uk"].b])
                self.load(wuv[i], wuv[i].t[:], scr["W_uv"].t[i], reads=[scr["W_uv"].b])
                P.op("dve", lambda e, v=vbst[i]: e.memset(v.t[:, :, 128:130], 1.0), [], [vbst[i].b])
            for b in range(2):
                for half in range(2):
                    t0 = 140 + 9 * b + 4 * half
                    self.load(ckvT, ckvT.t[:], inp["c_ckvT"][b, :, half * 512:(half + 1) * 512].rearrange("(c p) s -> p c s", p=128),
                              q="pool")
                    self.mla_kside(ckvT, wuk, wuv, psM, ksts, vbst, [t0 + j for j in range(4)], t0 * 128, None, None, None)
            P.barrier()
            self.release_scope(locals())


    @staticmethod
    def tmap(t):
        return {136: 148, 137: 157}.get(t, t)

    def store_cols(self, src, dst, dst_row_ap_fn, src_t, tiles):
        mapped = [self.tmap(t) for t in tiles]
        if all(mapped[i] == mapped[0] + i for i in range(len(mapped))):
            self.store(src, dst_row_ap_fn(mapped[0] * 128, len(mapped) * 128), src_t[:, 0:len(mapped) * 128], dst)
        else:
            for i, mt in enumerate(mapped):
                self.store(src, dst_row_ap_fn(mt * 128, 128), src_t[:, i * 128:(i + 1) * 128], dst)

    def own_tiles(self, b):
        return [136 + j for j in range(4)] if b == 4 else [8 * (4 * b + j) + 7 for j in range(4)]

    def phase_Q(self):
        P = self.P
        inp, scr = self.inp, self.scr
        with ExitStack() as st:
            sb = lambda name, shape, dt=F32: self.sb(st, name, shape, dt)
            g_mix = sb("q_gmix", [128, D])
            g_q = sb("q_gq", [128, 512])
            identf = sb("q_identf", [128, 128])
            identb = sb("q_identb", [128, 128], BF16)
            xts = Rot([sb("q_x%d" % i, [128, D]) for i in range(2)])
            hb = sb("q_hb", [128, D], BF16)
            hT = sb("q_hT", [128, 16, 512], BF16)
            wts = Rot([sb("q_wt%d" % i, [128, 16, 512], BF16) for i in range(3)])
            qsts = Rot([sb("q_qst%d" % i, [128, 512], BF16) for i in range(3)])
            wuqn = [sb("q_wuqn%d" % i, [128, 4, 512], BF16) for i in range(4)]
            wuqr = [sb("q_wuqr%d" % i, [128, 4, 512], BF16) for i in range(2)]
            cq_f = sb("q_cqf", [128, 512])
            cqb = sb("q_cqb", [128, 512], BF16)
            cqT = sb("q_cqT", [128, 4, 512], BF16)
            qr_f = sb("q_qrf", [128, 1024])
            qr_o = sb("q_qro", [128, 1024])
            qrb = sb("q_qrb", [128, 1024], BF16)
            qrT = sb("q_qrT", [128, 8, 512], BF16)
            ixw = sb("q_ixw", [128, 4, 16])
            cst = sb("q_cs", [128, 4, 64])
            tmp = dict(junk=sb("q_junk", [128, D], BF16), ss=sb("q_ss", [128, 1]), sd=sb("q_sd", [128, 1]),
                       rstd=sb("q_rstd", [128, 1]))
            rt1 = sb("q_rt1", [128, 32])
            rt2 = sb("q_rt2", [128, 32])
            rt1.t_ap = rt1.t[:]
            rt2.t_ap = rt2.t[:]
            PS = self.PS
            psT = Rot([PS[0], PS[1]])
            psM = Rot([PS[2], PS[3], PS[4], PS[5]])
            psX = Rot([PS[6], PS[7]])
            self.load(g_mix, g_mix.t[:], inp["g_mix"][:, :])
            self.load(g_q, g_q.t[:], inp["g_q"][:, :])
            self.load(identf, identf.t[:], inp["ident"][:, :])
            P.op("dve", lambda e: e.tensor_copy(identb.t[:], identf.t[:]), [identf.b], [identb.b])
            for i in range(4):
                self.load(wuqn[i], wuqn[i].t[:], scr["W_uqn"].t[i], reads=[scr["W_uqn"].b])
            for i in range(2):
                self.load(wuqr[i], wuqr[i].t[:], scr["W_uqr"].t[i], reads=[scr["W_uqr"].b])
            for b in range(5):
                tiles = self.own_tiles(b)
                for j, tile in enumerate(tiles):
                    self.load(cst, cst.t[:, j, :], inp["cs"][tile * 128:(tile + 1) * 128, :])
                self.load_norm_T(st, tiles, g_mix, hT, xts, hb, tmp, psT, identb)
                for grp, nblk, dst in (("aq", 4, "QA_T"), ("ixq", 2, "IXQ_T")):
                    for cb in range(nblk):
                        wt = wts.next()
                        self.load(wt, wt.t[:], scr["W_" + grp].t[cb], reads=[scr["W_" + grp].b])
                        for j in range(4):
                            ps = psM.next()
                            for kc in range(16):
                                self.mm(ps, ps.t[:, :], wt.t[:, kc, j * 128:(j + 1) * 128], hT.t[:, kc, :], kc == 0, kc == 15,
                                        [wt.b, hT.b])
                            q = qsts.next()
                            P.op("act", lambda e, q=q, ps=ps: e.activation(out=q.t[:], in_=ps.t[:, :], func=AF.Copy), [ps.b], [q.b])
                            self.store(q, scr[dst].t[b][:, 4 * cb + j, :], q.t[:], scr[dst])
                wt = wts.next()
                self.load(wt, wt.t[:], scr["W_cq"].t[0], reads=[scr["W_cq"].b])
                for pos in range(4):
                    ps = psM.next()
                    for kc in range(16):
                        self.mm(ps, ps.t[:, :], hT.t[:, kc, pos * 128:(pos + 1) * 128], wt.t[:, kc, :], kc == 0, kc == 15, [wt.b, hT.b])
                    P.op("act", lambda e, ps=ps: e.activation(out=cq_f.t[:], in_=ps.t[:, :], func=AF.Copy), [ps.b], [cq_f.b])
                    self.norm_rows(cq_f.t[:], 512, g_q.t[:], cqb.t[:], cq_f.b, g_q.b, cqb.b, tmp)
                    self.transpose_into(cqb, cqb.t, 4, cqT, lambda c, pos=pos: cqT.t[:, c, pos * 128:(pos + 1) * 128], psT, identb)
                wt = wts.next()
                self.load(wt, wt.t[:, :, 0:16], scr["W_cq"].t[1][:, :, 0:16], reads=[scr["W_cq"].b])
                for pos in range(4):
                    ps = psX.next()
                    for kc in range(16):
                        self.mm(ps, ps.t[:, 0:16], hT.t[:, kc, pos * 128:(pos + 1) * 128], wt.t[:, kc, 0:16], kc == 0, kc == 15,
                                [wt.b, hT.b])
                    P.op("act", lambda e, ps=ps, pos=pos: e.activation(out=ixw.t[:, pos, :], in_=ps.t[:, 0:16], func=AF.Copy, scale=0.25),
                         [ps.b], [ixw.b])
                self.store(ixw, scr["IXW"].t[b], ixw.t[:], scr["IXW"])
                for h in range(NH):
                    ps = psM.next()
                    for cc in range(4):
                        self.mm(ps, ps.t[:, :], wuqn[h // 4].t[:, cc, (h % 4) * 128:(h % 4 + 1) * 128], cqT.t[:, cc, :], cc == 0, cc == 3,
                                [wuqn[h // 4].b, cqT.b])
                    q = qsts.next()
                    P.op("act", lambda e, q=q, ps=ps: e.activation(out=q.t[:], in_=ps.t[:, :], func=AF.Copy), [ps.b], [q.b])
                    self.store(q, scr["QN_T"].t[b][:, h, :], q.t[:], scr["QN_T"])
                for pos in range(4):
                    for blk in range(2):
                        ps = psM.next()
                        for cc in range(4):
                            self.mm(ps, ps.t[:, :], cqT.t[:, cc, pos * 128:(pos + 1) * 128], wuqr[blk].t[:, cc, :], cc == 0, cc == 3,
                                    [wuqr[blk].b, cqT.b])
                        P.op("act", lambda e, ps=ps, blk=blk: e.activation(out=qr_f.t[:, blk * 512:(blk + 1) * 512], in_=ps.t[:, :],
                                                                           func=AF.Copy), [ps.b], [qr_f.b])
                    for h in range(NH):
                        c0 = h * 64
                        self.rope(qr_o.t[:, c0:c0 + 32], qr_o.t[:, c0 + 32:c0 + 64], qr_f.t[:, c0:c0 + 32], qr_f.t[:, c0 + 32:c0 + 64],
                                  cst.t[:, pos, 0:32], cst.t[:, pos, 32:64], qr_f.b, cst.b, qr_o.b, (rt1, rt2))
                    P.op("act", lambda e: e.activation(out=qrb.t[:], in_=qr_o.t[:], func=AF.Copy), [qr_o.b], [qrb.b])
                    self.transpose_into(qrb, qrb.t, 8, qrT, lambda c, pos=pos: qrT.t[:, c, pos * 128:(pos + 1) * 128], psT, identb)
                self.store(qrT, scr["QR_T"].t[b], qrT.t[:], scr["QR_T"])
            P.barrier()
            self.release_scope(locals())

    def slots_all(self):
        P = self.P
        inp, scr = self.inp, self.scr
        with ExitStack() as st:
            sb = lambda name, shape, dt=F32: self.sb(st, name, shape, dt)
            C = {}
            C["identf"] = sb("s_identf", [128, 128])
            C["identb"] = sb("s_identb", [128, 128], BF16)
            C["Tb"] = sb("s_Tb", [128, 16, 2, 128])
            C["constb"] = sb("s_constb", [128, 16])
            C["prefmask"] = sb("s_pref", [128, 896])
            C["dsa_diag"] = sb("s_dsadiag", [128, 2, 128])
            C["mlaf"] = sb("s_mlaf", [128, 2, 256])
            C["mlamask"] = sb("s_mlamask", [128, 2, 256], BF16)
            relb = sb("s_relb", [32, 16])
            oh = sb("s_oh", [32, 4096])
            self.load(C["identf"], C["identf"].t[:], inp["ident"][:, :])
            P.op("dve", lambda e: e.tensor_copy(C["identb"].t[:], C["identf"].t[:]), [C["identf"].b], [C["identb"].b])
            self.load(C["constb"], C["constb"].t[:], inp["constb"][:, :])
            self.load(C["prefmask"], C["prefmask"].t[:], inp["prefmask"][:, :])
            self.load(C["dsa_diag"], C["dsa_diag"].t[:], inp["dsa_diag"].rearrange("y t s -> t y s"))
            self.load(C["mlaf"], C["mlaf"].t[:], inp["mla_mask"].rearrange("y s c -> s y c"))
            P.op("dve", lambda e: e.tensor_copy(C["mlamask"].t[:], C["mlaf"].t[:]), [C["mlaf"].b], [C["mlamask"].b])
            self.load(relb, relb.t[:], inp["relb"][:, :])
            PS = self.PS
            for ty in range(2):
                for tg in range(4):
                    self.load(oh, oh.t[:], inp["onehot"][ty][:, tg * 4096:(tg + 1) * 4096])
                    ps = PS[tg % 2]
                    for t in range(32):
                        self.mm(ps, ps.t[:, t * 16:(t + 1) * 16], oh.t[:, t * 128:(t + 1) * 128], relb.t[:, :], True, True, [oh.b, relb.b])
                    P.op("dve", lambda e, ps=ps, ty=ty, tg=tg: e.tensor_copy(
                        C["Tb"].t[:, :, ty, tg * 32:(tg + 1) * 32], ps.t[:, :].rearrange("p (t h) -> p h t", h=16)), [ps.b], [C["Tb"].b])
            P.barrier()
            nslot = int(os.environ.get("MK_NSLOT", "18"))
            order = []
            for o in range(16):
                order.append((o // 4, o % 4, [(0, 8 * o + 8)], 0))
            order.append((4, 0, [(140, 9)], 1))
            order.append((4, 1, [(149, 9)], 1))
            if nslot < 18:
                order = [order[0], order[16], order[1], order[17]][:nslot]
            for (b, j, runs, ty) in order:
                self.slot(C, b, j, runs, ty)
            self.release_scope(dict(C=C, relb=relb, oh=oh))

    def slot(self, C, b, j, runs, ty):
        P = self.P
        inp, scr = self.inp, self.scr
        PS = self.PS
        o = 4 * b + j
        ktl = []
        for (t0, n) in runs:
            ktl += [t0 + i for i in range(n)]
        nk = len(ktl)
        S = nk * 128
        identb = C["identb"]
        with ExitStack() as st:
            sb = lambda name, shape, dt=F32: self.sb(st, name, shape, dt)
            qa = sb("l_qa", [128, 16, 128], BF16)
            ixq = sb("l_ixq", [128, 8, 128], BF16)
            ixw = sb("l_ixw", [128, 16])
            qn = sb("l_qn", [128, 16, 128], BF16)
            qr = sb("l_qr", [128, 8, 128], BF16)
            maskadd = sb("l_maskadd", [128, S], BF16)
            js = slice(j * 128, (j + 1) * 128)
            self.load(qa, qa.t[:], scr["QA_T"].t[b][:, :, js], reads=[scr["QA_T"].b])
            self.load(ixq, ixq.t[:], scr["IXQ_T"].t[b][:, :, js], reads=[scr["IXQ_T"].b])
            self.load(ixw, ixw.t[:], scr["IXW"].t[b][:, j, :], reads=[scr["IXW"].b])
            self.load(qn, qn.t[:], scr["QN_T"].t[b][:, :, js], reads=[scr["QN_T"].b])
            self.load(qr, qr.t[:], scr["QR_T"].t[b][:, :, js], reads=[scr["QR_T"].b])
            with ExitStack() as st2:
                sb2 = lambda name, shape, dt=F32: self.sb(st2, name, shape, dt)
                row = sb2("i_row", [128, S])
                ixks = Rot([sb2("i_ixk%d" % i, [128, 512], BF16) for i in range(2)])
                rbs = Rot([sb2("i_r%d" % i, [128, 512]) for i in range(3)])
                mx = sb2("i_mx", [128, 1])
                mid = sb2("i_mid", [128, 1])
                cnt = sb2("i_cnt", [128, 1])
                tfl = sb2("i_tfl", [128, 1])
                thr = sb2("i_thr", [128, 1])
                psR = Rot([PS[0], PS[1], PS[2], PS[3]])
                col = 0
                for (t0, n) in runs:
                    k = 0
                    while k < n:
                        g = min(4, n - k)
                        W = g * 128
                        tok = (t0 + k) * 128
                        ixk = ixks.next()
                        self.load(ixk, ixk.t[0:64, 0:W], scr["IXK_T"].t[:, tok:tok + W], reads=[scr["IXK_T"].b])
                        self.load(ixk, ixk.t[64:128, 0:W], scr["IXK_T"].t[:, tok:tok + W], reads=[scr["IXK_T"].b])
                        for h in range(16):
                            hf = h % 2
                            ps = psR.next()
                            self.mm(ps, ps.t[:, 0:W], ixq.t[hf * 64:(hf + 1) * 64, h // 2, :], ixk.t[hf * 64:(hf + 1) * 64, 0:W],
                                    True, True, [ixq.b, ixk.b])
                            r = rbs.next()
                            P.op("act", lambda e, r=r, ps=ps, W=W: e.activation(out=r.t[:, 0:W], in_=ps.t[:, 0:W], func=AF.Relu),
                                 [ps.b], [r.b])
                            if h == 0:
                                P.op("dve", lambda e, r=r, W=W, col=col: e.tensor_scalar(
                                    row.t[:, col:col + W], r.t[:, 0:W], ixw.t[:, 0:1], None, ALU.mult), [r.b, ixw.b], [row.b])
                            else:
                                P.op("dve", lambda e, r=r, W=W, col=col, h=h: e.scalar_tensor_tensor(
                                    out=row.t[:, col:col + W], in0=r.t[:, 0:W], scalar=ixw.t[:, h:h + 1], in1=row.t[:, col:col + W],
                                    op0=ALU.mult, op1=ALU.add), [r.b, ixw.b, row.b], [row.b])
                        col += W
                        k += g
                if ty == 0:
                    P.op("dve", lambda e: e.tensor_tensor(row.t[:, 0:896], row.t[:, 0:896], C["prefmask"].t[:], ALU.add),
                         [row.b, C["prefmask"].b], [row.b])
                P.op("dve", lambda e: e.tensor_tensor(row.t[:, S - 128:S], row.t[:, S - 128:S], C["dsa_diag"].t[:, ty, :], ALU.add),
                     [row.b, C["dsa_diag"].b], [row.b])
                P.op("dve", lambda e: e.tensor_reduce(mx.t[:], row.t[:], AX.X, ALU.max), [row.b], [mx.b])
                P.op("dve", lambda e: e.tensor_scalar(mid.t[:], mx.t[:], -BIS_R / 2, None, ALU.add), [mx.b], [mid.b])
                for it in range(BIS_IT):
                    hw = BIS_R / (2 ** (it + 1))
                    P.op("dve", lambda e: e.tensor_scalar(maskadd.t[:], row.t[:], mid.t[:], None, ALU.is_ge, ALU.add, accum_out=cnt.t[:]),
                         [row.b, mid.b], [maskadd.b, cnt.b])
                    P.op("dve", lambda e, hw=hw: e.tensor_scalar(tfl.t[:], cnt.t[:], float(TOPK) - 0.5, hw, ALU.is_ge, ALU.mult),
                         [cnt.b], [tfl.b])
                    P.op("dve", lambda e, hw=hw: e.scalar_tensor_tensor(out=mid.t[:], in0=tfl.t[:], scalar=-hw / 2, in1=mid.t[:],
                                                                        op0=ALU.add, op1=ALU.add), [tfl.b, mid.b], [mid.b])
                hwK = BIS_R / (2 ** (BIS_IT + 1))
                P.op("dve", lambda e: e.tensor_scalar(thr.t[:], mid.t[:], -hwK, None, ALU.add), [mid.b], [thr.b])
                P.op("dve", lambda e: e.tensor_scalar(maskadd.t[:], row.t[:], thr.t[:], MASKNEG, ALU.is_lt, ALU.mult),
                     [row.b, thr.b], [maskadd.b])
                P.barrier()
                self.release_scope(locals())
            with ExitStack() as st3:
                sb3 = lambda name, shape, dt=F32: self.sb(st3, name, shape, dt)
                kcs = Rot([sb3("a_kc%d" % i, [128, 2048], BF16) for i in range(4)])
                vcs = Rot([sb3("a_vc%d" % i, [128, 16, 130], BF16) for i in range(4)])
                krr = sb3("a_kr", [128, S], BF16)
                pts = Rot([sb3("a_p%d" % i, [128, 512], BF16) for i in range(4)])
                p2s = Rot([sb3("a_p2%d" % i, [128, 256], BF16) for i in range(2)])
                tmpn = sb3("a_tmpn", [128, 256])
                rec = sb3("a_rec", [128, 1])
                ost = sb3("a_ost", [128, D])
                psS = Rot([PS[0], PS[1], PS[2], PS[3]])
                psA = Rot([PS[4], PS[5]])
                col = 0
                for (t0, n) in runs:
                    self.load(krr, krr.t[0:64, col:col + n * 128], scr["KR_T"].t[:, t0 * 128:(t0 + n) * 128], reads=[scr["KR_T"].b])
                    self.load(krr, krr.t[64:128, col:col + n * 128], scr["KR_T"].t[:, t0 * 128:(t0 + n) * 128], reads=[scr["KR_T"].b])
                    col += n * 128
                chunks = []
                gk = 0
                for (t0, n) in runs:
                    k = 0
                    while k < n:
                        m = min(16, n - k)
                        chunks.append((t0 + k, m, gk))
                        gk += m
                        k += m
                for kind in ("dsa", "mla"):
                    KT = scr["KT_A"] if kind == "dsa" else scr["KT_B"]
                    VV = scr["V_A"] if kind == "dsa" else scr["V_B"]
                    pending = None
                    fin = None
                    for h in range(NH):
                        acc = psA.next()
                        hf = h % 2
                        for (t0, m, gk0) in chunks:
                            kc = kcs.next()
                            vc = vcs.next()
                            self.load(kc, kc.t[:, 0:m * 128], KT.t[h, :, t0 * 128:(t0 + m) * 128], reads=[KT.b])
                            self.load(vc, vc.t[:, 0:m, :], VV.t[h, :, t0:t0 + m, :], reads=[VV.b])
                            k = 0
                            while k < m:
                                gkt = gk0 + k
                                if gkt >= nk - 2:
                                    g = nk - gkt
                                    near = True
                                else:
                                    g = min(4, m - k, nk - 2 - gkt)
                                    near = False
                                Sps = psS.next()
                                for gi in range(g):
                                    cs_ = slice(gi * 128, (gi + 1) * 128)
                                    kl = k + gi
                                    if kind == "dsa":
                                        self.mm(Sps, Sps.t[:, cs_], kc.t[:, kl * 128:(kl + 1) * 128], qa.t[:, h, :], True, False,
                                                [kc.b, qa.b])
                                        self.mm(Sps, Sps.t[:, cs_], maskadd.t[:, (gkt + gi) * 128:(gkt + gi + 1) * 128], identb.t[:],
                                                False, True, [maskadd.b, identb.b])
                                    else:
                                        self.mm(Sps, Sps.t[:, cs_], kc.t[:, kl * 128:(kl + 1) * 128], qn.t[:, h, :], True, False,
                                                [kc.b, qn.b])
                                        self.mm(Sps, Sps.t[:, cs_], krr.t[hf * 64:(hf + 1) * 64, (gkt + gi) * 128:(gkt + gi + 1) * 128],
                                                qr.t[hf * 64:(hf + 1) * 64, h // 2, :], False, True, [krr.b, qr.b])
                                W = g * 128
                                p = pts.next()
                                if kind == "dsa":
                                    if near:
                                        P.op("dve", lambda e, Sps=Sps, h=h: e.scalar_tensor_tensor(
                                            out=tmpn.t[:], in0=Sps.t[:, 0:256], scalar=A_SCALE,
                                            in1=C["Tb"].t[:, h, :, :].rearrange("p y t -> p (y t)"), op0=ALU.mult, op1=ALU.add),
                                            [Sps.b, C["Tb"].b], [tmpn.b])
                                        P.op("act", lambda e, p=p: e.activation(out=p.t[:, 0:256], in_=tmpn.t[:], func=AF.Exp),
                                             [tmpn.b], [p.b])
                                    else:
                                        P.op("act", lambda e, p=p, Sps=Sps, W=W, h=h: e.activation(
                                            out=p.t[:, 0:W], in_=Sps.t[:, 0:W], func=AF.Exp, bias=C["constb"].t[:, h:h + 1], scale=A_SCALE),
                                            [Sps.b, C["constb"].b], [p.b])
                                    pp = p
                                else:
                                    P.op("act", lambda e, p=p, Sps=Sps, W=W: e.activation(out=p.t[:, 0:W], in_=Sps.t[:, 0:W], func=AF.Exp,
                                                                                       scale=MLA_SCALE), [Sps.b], [p.b])
                                    pp = p
                                    if near:
                                        p2 = p2s.next()
                                        P.op("dve", lambda e, p=p, p2=p2: e.tensor_tensor(p2.t[:], p.t[:, 0:256], C["mlamask"].t[:, ty, :],
                                                                                         ALU.mult), [p.b, C["mlamask"].b], [p2.b])
                                        pp = p2
                                if pending is not None:
                                    pending()
                                if fin is not None:
                                    fin()
                                    fin = None

                                def pv(acc=acc, pp=pp, vc=vc, k=k, g=g, gkt=gkt):
                                    for gi in range(g):
                                        kl = k + gi
                                        self.mm(acc, acc.t[:, 0:129], pp.t[:, gi * 128:(gi + 1) * 128], vc.t[:, kl, 0:129],
                                                (gkt + gi) == 0, (gkt + gi) == nk - 1, [pp.b, vc.b])
                                pending = pv
                                k += g
                        def fin_(acc=acc, h=h):
                            P.op("dve", lambda e: e.reciprocal(rec.t[:], acc.t[:, 128:129]), [acc.b], [rec.b])
                            P.op("dve", lambda e: e.tensor_scalar(ost.t[:, h * 128:(h + 1) * 128], acc.t[:, 0:128], rec.t[:], None,
                                                                  ALU.mult), [acc.b, rec.b], [ost.b])
                        fin = fin_
                    if pending is not None:
                        pending()
                        pending = None
                    if fin is not None:
                        fin()
                        fin = None
                    dst = scr["OA"] if kind == "dsa" else scr["OB"]
                    self.store(ost, dst.t[o], ost.t[:], dst)
                P.barrier()
                self.release_scope(locals())
            self.release_scope(dict(qa=qa, ixq=ixq, ixw=ixw, qn=qn, qr=qr, maskadd=maskadd))

    def phase_M(self):
        P = self.P
        inp, out, scr = self.inp, self.out, self.scr
        PS = self.PS
        nb = int(os.environ.get("MK_NMB", "5"))
        for b in list(range(5))[:nb] if nb >= 5 else [0, 4][:nb]:
            tiles = self.own_tiles(b)
            with ExitStack() as so:
                xk = [self.sb(so, "m_x%d" % i, [128, D]) for i in range(4)]
                with ExitStack() as st:
                    sb = lambda name, shape, dt=F32: self.sb(st, name, shape, dt)
                    g_mix = sb("m_gmix", [128, D])
                    identf = sb("m_identf", [128, 128])
                    identb = sb("m_identb", [128, 128], BF16)
                    hb = sb("m_hb", [128, D], BF16)
                    hT = sb("m_hT", [128, 16, 512], BF16)
                    wts = Rot([sb("m_wt%d" % i, [128, 16, 512], BF16) for i in range(2)])
                    gas = Rot([sb("m_ga%d" % i, [128, 512]) for i in range(2)])
                    gbs = Rot([sb("m_gb%d" % i, [128, 512]) for i in range(2)])
                    oas = Rot([sb("m_oa%d" % i, [128, 512]) for i in range(2)])
                    obs = Rot([sb("m_ob%d" % i, [128, 512]) for i in range(2)])
                    mix = [sb("m_mix%d" % i, [128, D], BF16) for i in range(4)]
                    tmp = dict(junk=sb("m_junk", [128, D], BF16), ss=sb("m_ss", [128, 1]), sd=sb("m_sd", [128, 1]),
                               rstd=sb("m_rstd", [128, 1]))
                    psT = Rot([PS[0], PS[1]])
                    psM = Rot([PS[2], PS[3], PS[4], PS[5]])
                    self.load(g_mix, g_mix.t[:], inp["g_mix"][:, :])
                    self.load(identf, identf.t[:], inp["ident"][:, :])
                    P.op("dve", lambda e: e.tensor_copy(identb.t[:], identf.t[:]), [identf.b], [identb.b])
                    self.load_norm_T(st, tiles, g_mix, hT, None, hb, tmp, psT, identb, keep_x=xk)
                    for i in range(4):
                        wa = wts.next()
                        wb = wts.next()
                        self.load(wa, wa.t[:], scr["W_gate"].t[i], reads=[scr["W_gate"].b])
                        self.load(wb, wb.t[:], scr["W_gate"].t[4 + i], reads=[scr["W_gate"].b])
                        cs_ = slice(i * 512, (i + 1) * 512)
                        for pos in range(4):
                            o = 4 * b + pos
                            pa = psM.next()
                            for kc in range(16):
                                self.mm(pa, pa.t[:, :], hT.t[:, kc, pos * 128:(pos + 1) * 128], wa.t[:, kc, :], kc == 0, kc == 15, [wa.b, hT.b])
                            pb = psM.next()
                            for kc in range(16):
                                self.mm(pb, pb.t[:, :], hT.t[:, kc, pos * 128:(pos + 1) * 128], wb.t[:, kc, :], kc == 0, kc == 15, [wb.b, hT.b])
                            ga, gb, oa, ob = gas.next(), gbs.next(), oas.next(), obs.next()
                            P.op("act", lambda e, ga=ga, pa=pa: e.activation(out=ga.t[:], in_=pa.t[:, :], func=AF.Sigmoid), [pa.b], [ga.b])
                            P.op("act", lambda e, gb=gb, pb=pb: e.activation(out=gb.t[:], in_=pb.t[:, :], func=AF.Sigmoid), [pb.b], [gb.b])
                            self.load(oa, oa.t[:], scr["OA"].t[o][:, cs_], reads=[scr["OA"].b])
                            self.load(ob, ob.t[:], scr["OB"].t[o][:, cs_], reads=[scr["OB"].b])
                            P.op("dve", lambda e, ga=ga, oa=oa: e.tensor_tensor(ga.t[:], ga.t[:], oa.t[:], ALU.mult), [ga.b, oa.b], [ga.b])
                            P.op("dve", lambda e, gb=gb, ob=ob: e.tensor_tensor(gb.t[:], gb.t[:], ob.t[:], ALU.mult), [gb.b, ob.b], [gb.b])
                            P.op("dve", lambda e, ga=ga, gb=gb, pos=pos, cs_=cs_: e.tensor_tensor(mix[pos].t[:, cs_], ga.t[:], gb.t[:], ALU.add),
                                 [ga.b, gb.b], [mix[pos].b])
                    for pos in range(4):
                        self.transpose_into(mix[pos], mix[pos].t, 16, hT, lambda kc, pos=pos: hT.t[:, kc, pos * 128:(pos + 1) * 128], psT, identb)
                    for cb in range(4):
                        wt = wts.next()
                        self.load(wt, wt.t[:], scr["W_wo"].t[cb], reads=[scr["W_wo"].b])
                        cs_ = slice(cb * 512, (cb + 1) * 512)
                        for pos in range(4):
                            ps = psM.next()
                            for kc in range(16):
                                self.mm(ps, ps.t[:, :], hT.t[:, kc, pos * 128:(pos + 1) * 128], wt.t[:, kc, :], kc == 0, kc == 15, [wt.b, hT.b])
                            P.op("dve", lambda e, ps=ps, pos=pos, cs_=cs_: e.tensor_tensor(xk[pos].t[:, cs_], xk[pos].t[:, cs_], ps.t[:, :], ALU.add),
                                 [xk[pos].b, ps.b], [xk[pos].b])
                    P.barrier()
                    self.release_scope(locals())
                with ExitStack() as st:
                    sb = lambda name, shape, dt=F32: self.sb(st, name, shape, dt)
                    g_ffn = sb("n_gffn", [128, D])
                    g_fin = sb("n_gfin", [128, D])
                    identf = sb("n_identf", [128, 128])
                    identb = sb("n_identb", [128, 128], BF16)
                    hb = sb("n_hb", [128, D], BF16)
                    h2T = sb("n_h2T", [128, 16, 512], BF16)
                    uT = sb("n_uT", [128, 64, 512], BF16)
                    wts = Rot([sb("n_wt%d" % i, [128, 16, 512], BF16) for i in range(2)])
                    rrs = Rot([sb("n_rr%d" % i, [128, 512]) for i in range(2)])
                    ys = Rot([sb("n_y%d" % i, [128, D]) for i in range(2)])
                    tmp = dict(junk=sb("n_junk", [128, D], BF16), ss=sb("n_ss", [128, 1]), sd=sb("n_sd", [128, 1]),
                               rstd=sb("n_rstd", [128, 1]))
                    psT = Rot([PS[0], PS[1]])
                    psM = Rot([PS[6], PS[7]])
                    self.load(g_ffn, g_ffn.t[:], inp["g_ffn"][:, :])
                    self.load(g_fin, g_fin.t[:], inp["g_fin"][:, :])
                    self.load(identf, identf.t[:], inp["ident"][:, :])
                    P.op("dve", lambda e: e.tensor_copy(identb.t[:], identf.t[:]), [identf.b], [identb.b])
                    for pos in range(4):
                        self.norm_rows(xk[pos].t[:], D, g_ffn.t[:], hb.t[:], xk[pos].b, g_ffn.b, hb.b, tmp)
                        self.transpose_into(hb, hb.t, 16, h2T, lambda kc, pos=pos: h2T.t[:, kc, pos * 128:(pos + 1) * 128], psT, identb)
                    for cb in range(16):
                        wt = wts.next()
                        self.load(wt, wt.t[:], scr["W_up"].t[cb], reads=[scr["W_up"].b])
                        for jj in range(4):
                            ffc = 4 * cb + jj
                            ps = psM.next()
                            for kc in range(16):
                                self.mm(ps, ps.t[:, :], wt.t[:, kc, jj * 128:(jj + 1) * 128], h2T.t[:, kc, :], kc == 0, kc == 15, [wt.b, h2T.b])
                            rr = rrs.next()
                            P.op("act", lambda e, rr=rr, ps=ps: e.activation(out=rr.t[:], in_=ps.t[:, :], func=AF.Relu), [ps.b], [rr.b])
                            P.op("dve", lambda e, rr=rr, ffc=ffc: e.tensor_tensor(uT.t[:, ffc, :], rr.t[:], rr.t[:], ALU.mult), [rr.b], [uT.b])
                    for cb in range(4):
                        cs_ = slice(cb * 512, (cb + 1) * 512)
                        pss = [PS[2], PS[3], PS[4], PS[5]]
                        for kg in range(4):
                            wt = wts.next()
                            self.load(wt, wt.t[:], scr["W_dn"].t[cb * 4 + kg], reads=[scr["W_dn"].b])
                            for pos in range(4):
                                for kc in range(16):
                                    self.mm(pss[pos], pss[pos].t[:, :], uT.t[:, kg * 16 + kc, pos * 128:(pos + 1) * 128], wt.t[:, kc, :],
                                            kg == 0 and kc == 0, kg == 3 and kc == 15, [wt.b, uT.b])
                        for pos in range(4):
                            P.op("dve", lambda e, pos=pos, cs_=cs_, pss=pss: e.tensor_tensor(xk[pos].t[:, cs_], xk[pos].t[:, cs_], pss[pos].t[:, :],
                                                                                         ALU.add), [xk[pos].b, pss[pos].b], [xk[pos].b])
                    for pos in range(4):
                        o = 4 * b + pos
                        y = ys.next()
                        self.norm_rows(xk[pos].t[:], D, g_fin.t[:], y.t[:], xk[pos].b, g_fin.b, y.b, tmp)
                        self.store(y, out["y_own"][o * 128:(o + 1) * 128, :], y.t[:])
                    P.barrier()
                    self.release_scope(locals())
                self.release_scope(dict(xk=xk))

    def build(self):
        nc = self.nc
        self.declare()
        st = self.gstack
        self.PS = [TB(st.enter_context(nc.psum_tensor("ps%d" % i, [128, 512], F32)), "ps%d" % i) for i in range(8)]
        for p_ in self.PS:
            p_.b.excl = True
        self.phase_W()
        if self.upto == "W":
            return self.finish()
        kblocks = list(range(32)) + [34]
        if self.upto == "K1":
            kblocks = [0, 1, 34][:int(os.environ.get("MK_NB", "3"))]
        self.phase_K(kblocks)
        if self.upto in ("K", "K1"):
            return self.finish()
        self.phase_KC()
        if self.upto == "KC":
            return self.finish()
        self.phase_Q()
        if self.upto == "Q":
            return self.finish()
        self.slots_all()
        if self.upto == "S":
            return self.finish()
        self.phase_M()
        return self.finish()

    def finish(self):
        self.P.finish()
        self.gstack.close()
        return self.nc


def t5_bucket_np(rel):
    nb = 16
    ret = (rel > 0).astype(np.int32) * nb
    n = np.abs(rel)
    max_exact = 8
    nf = np.maximum(n, 1).astype(np.float32)
    large = max_exact + (np.log(nf / np.float32(max_exact)) / np.float32(math.log(128 / max_exact))
                         * np.float32(nb - max_exact)).astype(np.int32)
    large = np.minimum(large, nb - 1)
    return ret + np.where(n < max_exact, n, large)


def host_inputs(inputs):
    f32 = np.float32
    xp = np.asarray(inputs["x_prompt"], f32)[0]
    xs = np.asarray(inputs["x_sample"], f32)
    ck = np.asarray(inputs["cache_a_k"], f32)[0]
    cv = np.asarray(inputs["cache_a_v"], f32)[0]
    cix = np.asarray(inputs["cache_a_idx_k"], f32)[0]
    cckv = np.asarray(inputs["cache_b_ckv"], f32)[0]
    ckr = np.asarray(inputs["cache_b_krope"], f32)[0]
    half = 32
    inv_freq = np.power(np.float32(10000.0), -np.arange(half, dtype=f32) / np.float32(half)).astype(f32)
    shared = {
        "ident": np.eye(128, dtype=f32),
        "constb": np.ascontiguousarray(np.broadcast_to(np.asarray(inputs["rel_bias_table"], f32)[15], (128, 16))),
        "relb": np.ascontiguousarray(np.asarray(inputs["rel_bias_table"], f32)),
        "w_in": np.ascontiguousarray(np.asarray(inputs["w_in"], f32)[0]),
        "w_uq": np.ascontiguousarray(np.asarray(inputs["w_uq"], f32)[0]),
        "w_uk": np.ascontiguousarray(np.asarray(inputs["w_uk"], f32)[0].reshape(256, 2048)),
        "w_uv": np.ascontiguousarray(np.asarray(inputs["w_uv"], f32)[0].reshape(256, 2048)),
        "w_out": np.ascontiguousarray(np.asarray(inputs["w_out"], f32)[0]),
        "w_ff_up": np.ascontiguousarray(np.asarray(inputs["w_ff_up"], f32)[0]),
        "w_ff_down": np.ascontiguousarray(np.asarray(inputs["w_ff_down"], f32)[0]),
        "g_mix": np.ascontiguousarray(np.broadcast_to(np.asarray(inputs["norm_mix_g"], f32)[0], (128, D))),
        "g_ffn": np.ascontiguousarray(np.broadcast_to(np.asarray(inputs["norm_ffn_g"], f32)[0], (128, D))),
        "g_fin": np.ascontiguousarray(np.broadcast_to(np.asarray(inputs["final_norm_g"], f32), (128, D))),
        "g_q": np.ascontiguousarray(np.broadcast_to(np.asarray(inputs["q_lora_g"], f32)[0], (128, 512))),
        "g_kv": np.ascontiguousarray(np.broadcast_to(np.asarray(inputs["kv_lora_g"], f32)[0], (128, 256))),
    }
    s = np.arange(128)[None, :]
    t = np.arange(128)[:, None]
    onehot = np.zeros((2, 32, 128, 128), f32)
    for ty, off in enumerate((-128, 0)):
        rel = (s - t + off).astype(np.int32)
        bk = t5_bucket_np(rel)
        for b in range(32):
            onehot[ty, b] = (bk == b)
    shared["onehot"] = onehot.reshape(2, 32, 128 * 128)
    dsa_diag = np.zeros((2, 128, 128), f32)
    dsa_diag[0] = np.where((s // 64) <= (t // 64), 0.0, NEGBIG)
    dsa_diag[1] = np.where(s < 16, 0.0, NEGBIG) * np.ones((128, 1), f32)
    shared["dsa_diag"] = dsa_diag
    mla_mask = np.ones((2, 128, 2, 128), f32)
    ss_ = np.arange(128)[:, None]
    tt_ = np.arange(128)[None, :]
    mla_mask[0, :, 1, :] = ((ss_ // 64) <= (tt_ // 64)).astype(f32)
    mla_mask[1, :, 1, :] = (ss_ < 16).astype(f32) * np.ones((1, 128), f32)
    shared["mla_mask"] = mla_mask.reshape(2, 128, 256)
    maps = []
    for c in range(NCORE):
        m = dict(shared)
        pre = 7 - c
        x_all = np.zeros((NXT * 128, D), f32)
        x_all[pre * 128:pre * 128 + 16384] = xp
        pos = np.zeros((NXT * 128,), f32)
        valid = np.zeros((NXT * 128, 1), f32)
        pos[pre * 128:pre * 128 + 16384] = np.arange(16384, dtype=f32)
        valid[pre * 128:pre * 128 + 16384] = 1.0
        for b in range(2):
            r0 = (136 + b) * 128
            x_all[r0:r0 + 16] = xs[2 * c + b]
            pos[r0:r0 + 16] = 1024 + np.arange(16, dtype=f32)
            valid[r0:r0 + 16] = 1.0
        ang = pos[:, None] * inv_freq[None, :]
        m["x_all"] = x_all
        m["cs"] = np.concatenate([np.cos(ang), np.sin(ang)], axis=1).astype(f32)
        m["valid"] = np.ascontiguousarray(valid.reshape(NXT, 128).T)
        pm = np.zeros((896,), f32)
        pm[:pre * 128] = NEGBIG
        m["prefmask"] = np.ascontiguousarray(np.broadcast_to(pm, (128, 896)))
        sl = slice(2 * c, 2 * c + 2)
        m["c_akT"] = np.ascontiguousarray(ck[sl].transpose(0, 2, 3, 1))
        cve = np.ones((2, 1024, 16, 130), f32)
        cve[..., 0:128] = cv[sl]
        m["c_av"] = cve
        m["c_ixkT"] = np.ascontiguousarray(cix[sl].transpose(0, 2, 1))
        m["c_ckvT"] = np.ascontiguousarray(cckv[sl].transpose(0, 2, 1))
        m["c_krT"] = np.ascontiguousarray(ckr[sl].transpose(0, 2, 1))
        maps.append(m)
    return maps


def assemble(results):
    f32 = np.float32
    y_p = np.zeros((1, 16384, D), f32)
    y_s = np.zeros((16, 16, D), f32)
    a_k_p = np.zeros((1, 1, 16384, 16, 128), f32)
    a_v_p = np.zeros((1, 1, 16384, 16, 128), f32)
    a_ix_p = np.zeros((1, 1, 16384, 64), f32)
    b_ckv_p = np.zeros((1, 1, 16384, 256), f32)
    b_kr_p = np.zeros((1, 1, 16384, 64), f32)
    a_k_s = np.zeros((1, 16, 16, 16, 128), f32)
    a_v_s = np.zeros((1, 16, 16, 16, 128), f32)
    a_ix_s = np.zeros((1, 16, 16, 64), f32)
    b_ckv_s = np.zeros((1, 16, 16, 256), f32)
    b_kr_s = np.zeros((1, 16, 16, 64), f32)
    for c in range(NCORE):
        r = results[c]
        for i in range(16):
            j = c + 8 * i
            rs = slice(i * 128, (i + 1) * 128)
            ps = slice(j * 128, (j + 1) * 128)
            y_p[0, ps] = r["y_own"][rs]
            a_k_p[0, 0, ps] = r["st_ak"][rs].reshape(128, 16, 128)
            a_v_p[0, 0, ps] = r["st_av"][rs].reshape(128, 16, 128)
            a_ix_p[0, 0, ps] = r["st_ix"][rs]
            b_ckv_p[0, 0, ps] = r["st_ckv"][rs]
            b_kr_p[0, 0, ps] = r["st_kr"][rs]
        for b in range(2):
            rs = slice((16 + b) * 128, (16 + b) * 128 + 16)
            sq = 2 * c + b
            y_s[sq] = r["y_own"][rs]
            a_k_s[0, sq] = r["st_ak"][rs].reshape(16, 16, 128)
            a_v_s[0, sq] = r["st_av"][rs].reshape(16, 16, 128)
            a_ix_s[0, sq] = r["st_ix"][rs]
            b_ckv_s[0, sq] = r["st_ckv"][rs]
            b_kr_s[0, sq] = r["st_kr"][rs]
    return (y_p, y_s, a_k_p, a_v_p, a_ix_p, b_ckv_p, b_kr_p, a_k_s, a_v_s, a_ix_s, b_ckv_s, b_kr_s)


def kernel(**inputs):
    upto = os.environ.get("MK_UPTO", "ALL")
    bld = Builder(upto)
    nc = bld.build()
    maps = host_inputs(inputs)
    res = run_bass_kernel_spmd(nc, maps, core_ids=list(range(NCORE)))
    return assemble(res.results)
```
